# Optimizing a Trainium2 kernel written in Bass

```python
import math
import jax, jax.numpy as jnp
from jax import lax
import numpy as np

D_MODEL = 1024
BATCH = 16
SEQ = 256
DEPTH = 4
DEC_BATCH = 4
DEC_SEQ = 4096
PAST_LEN = 256

GRID_W = 64
H_A = 4
DK_A = 64
DV_A = 128
H_B = 4
DK_B = 64
DV_B = 128
H_C = 4
DK_C = 128
DV_C = 128
W_A = H_A * DV_A
W_B = H_B * DV_B
W_C = H_C * DV_C
N_DIR = 2
CONV_K = 5
CHUNK = 64
Q_BLOCK = 128
ROPE_BASE = 10000.0
EPS = 1e-6
IN_WIDTHS = (H_A * 2 * DK_A, H_A * 2 * DK_A, W_A, W_A,
             H_B * DK_B, H_B * DK_B, W_B, W_B,
             H_C * DK_C, H_C * DK_C, W_C, W_C, N_DIR * H_C, N_DIR * H_C,
             3 * D_MODEL)
D_IN = sum(IN_WIDTHS)

kernel_name = 'hybrid_diff_ret_gdn_dit_step'


def rms_norm(x, w):
    xf = x.astype(jnp.float32)
    y = xf * lax.rsqrt(jnp.mean(xf * xf, axis=-1, keepdims=True) + EPS)
    return (y * w.astype(jnp.float32)).astype(x.dtype)


def l2_norm(x):
    xf = x.astype(jnp.float32)
    return (xf * lax.rsqrt(jnp.sum(xf * xf, axis=-1, keepdims=True) + EPS)).astype(x.dtype)


def axial_rope_tables(n_tokens, dim, dtype):
    n_rows = n_tokens // GRID_W
    row = jnp.repeat(jnp.arange(n_rows, dtype=jnp.float32), GRID_W)
    col = jnp.tile(jnp.arange(GRID_W, dtype=jnp.float32), n_rows)
    n_freq = dim // 4
    inv = 1.0 / (ROPE_BASE ** (jnp.arange(n_freq, dtype=jnp.float32) / n_freq))
    ar = row[:, None] * inv
    ac = col[:, None] * inv
    ang = jnp.concatenate([ar, ar, ac, ac], axis=-1)
    return jnp.cos(ang).astype(dtype), jnp.sin(ang).astype(dtype)


def apply_axial_rope(x, cos, sin):
    extra = x.ndim - 3
    c = cos.reshape(cos.shape[:1] + (1,) * extra + cos.shape[1:])
    s = sin.reshape(sin.shape[:1] + (1,) * extra + sin.shape[1:])
    d = x.shape[-1]
    xs = x.reshape(x.shape[:-1] + (2, 2, d // 4))
    rot = jnp.stack([-xs[..., 1, :], xs[..., 0, :]], axis=-2).reshape(x.shape)
    return x * c + rot * s


def depthwise_conv(x, w):
    k, ch = w.shape
    return lax.conv_general_dilated(x, w[:, None, :].astype(x.dtype), (1,), ((k // 2, k // 2),),
                                    dimension_numbers=('NWC', 'WIO', 'NWC'), feature_group_count=ch)


def diff_attention(q, k, v, lam):
    b, t, h, _, d = q.shape
    dv = v.shape[-1]
    nb = t // Q_BLOCK
    qb = jnp.moveaxis(q.reshape(b, nb, Q_BLOCK, h, 2, d), 1, 0)
    scale = d ** -0.5

    def one_block(qi):
        s = jnp.einsum('bqhmd,bshmd->bmhqs', qi, k).astype(jnp.float32) * scale
        p = jax.nn.softmax(s, axis=-1)
        wts = (p[:, 0] - lam * p[:, 1]).astype(v.dtype)
        return jnp.einsum('bhqs,bshe->bqhe', wts, v)

    o = lax.map(one_block, qb)
    return jnp.moveaxis(o, 0, 1).reshape(b, t, h, dv)


def _to_chunks(a):
    b, t = a.shape[:2]
    a = a.reshape((b, t // CHUNK, CHUNK) + a.shape[2:]).astype(jnp.float32)
    return jnp.moveaxis(a, (1, 3), (0, 2))


def _from_chunks(o):
    o = jnp.moveaxis(o, (0, 2), (1, 3))
    return o.reshape((o.shape[0], o.shape[1] * o.shape[2]) + o.shape[3:])


def retention_scan(q, k, v, log_gamma, s0):
    dk = q.shape[-1]
    qc = _to_chunks(q)
    kc = _to_chunks(k) * dk ** -0.5
    vc = _to_chunks(v)
    i = jnp.arange(CHUNK, dtype=jnp.float32)
    rel = i[:, None] - i[None, :]
    causal = rel >= 0
    dmat = jnp.where(causal, jnp.exp(jnp.where(causal, rel, 0.0)[None] * log_gamma[:, None, None]), 0.0)
    q_dec = jnp.exp((i + 1.0)[None] * log_gamma[:, None])
    k_dec = jnp.exp((CHUNK - 1.0 - i)[None] * log_gamma[:, None])
    c_dec = jnp.exp(CHUNK * log_gamma)

    def step(s, inp):
        q_i, k_i, v_i = inp
        intra = jnp.einsum('bhid,bhjd->bhij', q_i, k_i) * dmat
        o = (jnp.einsum('bhij,bhje->bhie', intra, v_i)
             + jnp.einsum('bhid,bhde->bhie', q_i, s) * q_dec[:, :, None])
        s = s * c_dec[:, None, None] + jnp.einsum('bhjd,bhje->bhde', k_i * k_dec[:, :, None], v_i)
        return s, o

    s, o = lax.scan(step, s0.astype(jnp.float32), (qc, kc, vc))
    return _from_chunks(o), s


def gated_delta_scan(q, k, v, g, beta, s0):
    dk = q.shape[-1]
    qc = _to_chunks(q) * dk ** -0.5
    kc = _to_chunks(k)
    vc = _to_chunks(v)
    gc = jnp.cumsum(_to_chunks(g), axis=-1)
    bc = _to_chunks(beta)[..., None]
    kb = kc * bc
    lower = jnp.tril(jnp.ones((CHUNK, CHUNK), dtype=bool))
    strict = jnp.tril(jnp.ones((CHUNK, CHUNK), dtype=jnp.float32), -1)
    decay = jnp.exp(jnp.where(lower, gc[..., :, None] - gc[..., None, :], -jnp.inf))
    a = jnp.einsum('nbhid,nbhjd->nbhij', kb, kc) * decay * strict
    eye = jnp.eye(CHUNK, dtype=jnp.float32)
    t_inv = lax.linalg.triangular_solve(a + eye, jnp.broadcast_to(eye, a.shape), left_side=True,
                                        lower=True, unit_diagonal=True)
    u = jnp.einsum('nbhij,nbhjd->nbhid', t_inv, vc * bc)
    w = jnp.einsum('nbhij,nbhjd->nbhid', t_inv, kb * jnp.exp(gc)[..., None])
    qk = jnp.einsum('nbhid,nbhjd->nbhij', qc, kc) * decay

    def step(s, inp):
        q_i, k_i, u_i, w_i, g_i, qk_i = inp
        v_new = u_i - jnp.einsum('bhcd,bhde->bhce', w_i, s)
        o = (jnp.einsum('bhcd,bhde->bhce', q_i * jnp.exp(g_i)[..., None], s)
             + jnp.einsum('bhij,bhje->bhie', qk_i, v_new))
        g_last = g_i[..., -1:]
        s = s * jnp.exp(g_last)[..., None] + jnp.einsum(
            'bhcd,bhce->bhde', k_i * jnp.exp(g_last - g_i)[..., None], v_new)
        return s, o

    s, o = lax.scan(step, s0.astype(jnp.float32), (qc, kc, u, w, gc, qk))
    return _from_chunks(o), s


def trunk_layer(x, mod, layer_idx, p, rope, ctx):
    b, t, _ = x.shape
    f32 = jnp.float32
    shift, scale, gate = jnp.split(mod, 3, axis=-1)
    h = rms_norm(x, p['norm_w']) * (1.0 + scale) + shift
    proj = h @ p['w_in']
    split_pts = np.cumsum(IN_WIDTHS)[:-1].tolist()
    (aq, ak, av, az, bq, bk, bv, bz, cq, ck, cv, cz, c_beta, c_alpha, merge_logits) = jnp.split(proj, split_pts, axis=-1)

    lam_init = 0.8 - 0.6 * math.exp(-0.3 * layer_idx)
    qa = rms_norm(aq.reshape(b, t, H_A, 2, DK_A), p['qk_norm_w'][0])
    ka = rms_norm(ak.reshape(b, t, H_A, 2, DK_A), p['qk_norm_w'][1])
    va = av.reshape(b, t, H_A, DV_A)
    if rope is not None:
        qa = apply_axial_rope(qa, rope[0], rope[1])
        ka_att = apply_axial_rope(ka, rope[0], rope[1])
    else:
        ka_att = ka
    lv = p['diff_lambda'].astype(f32)
    lam = jnp.exp(jnp.sum(lv[0] * lv[1])) - jnp.exp(jnp.sum(lv[2] * lv[3])) + lam_init
    if ctx is None:
        k_all, v_all = ka_att, va
    else:
        k_all = jnp.concatenate([ka_att, ctx[0].astype(ka_att.dtype)], axis=1)
        v_all = jnp.concatenate([va, ctx[1].astype(va.dtype)], axis=1)
    oa = rms_norm(diff_attention(qa, k_all, v_all, lam), p['subln_w']) * (1.0 - lam_init)
    ya = (oa.reshape(b, t, W_A) * jax.nn.silu(az)) @ p['w_branch'][0]

    qr = bq.reshape(b, t, H_B, DK_B)
    kr = bk.reshape(b, t, H_B, DK_B)
    vr = bv.reshape(b, t, H_B, DV_B)
    if rope is not None:
        qr = apply_axial_rope(qr, rope[0], rope[1])
        kr = apply_axial_rope(kr, rope[0], rope[1])
    log_gamma = jax.nn.log_sigmoid(p['ret_decay'].astype(f32))
    s_ret = jnp.zeros((b, N_DIR, H_B, DK_B, DV_B), f32) if ctx is None else ctx[2]
    or_f, sr_f = retention_scan(qr, kr, vr, log_gamma[0], s_ret[:, 0])
    or_b, sr_b = retention_scan(jnp.flip(qr, 1), jnp.flip(kr, 1), jnp.flip(vr, 1), log_gamma[1], s_ret[:, 1])
    orr = rms_norm((or_f + jnp.flip(or_b, 1)).astype(x.dtype), p['ret_norm_w'])
    yb = (orr.reshape(b, t, W_B) * jax.nn.silu(bz)) @ p['w_branch'][1]

    qkv = jax.nn.silu(depthwise_conv(jnp.concatenate([cq, ck, cv], axis=-1), p['conv_w']))
    cq, ck, cv = jnp.split(qkv, [H_C * DK_C, 2 * H_C * DK_C], axis=-1)
    qd = l2_norm(cq.reshape(b, t, H_C, DK_C))
    kd = l2_norm(ck.reshape(b, t, H_C, DK_C))
    vd = cv.reshape(b, t, H_C, DV_C)
    beta = jax.nn.sigmoid(c_beta.reshape(b, t, N_DIR, H_C).astype(f32))
    g = -jnp.exp(p['gdn_a_log'].astype(f32)) * jax.nn.softplus(
        c_alpha.reshape(b, t, N_DIR, H_C).astype(f32) + p['gdn_dt_bias'].astype(f32))
    s_gdn = jnp.zeros((b, N_DIR, H_C, DK_C, DV_C), f32) if ctx is None else ctx[3]
    od_f, sd_f = gated_delta_scan(qd, kd, vd, g[:, :, 0], beta[:, :, 0], s_gdn[:, 0])
    od_b, sd_b = gated_delta_scan(jnp.flip(qd, 1), jnp.flip(kd, 1), jnp.flip(vd, 1),
                                  jnp.flip(g[:, :, 1], 1), jnp.flip(beta[:, :, 1], 1), s_gdn[:, 1])
    od = rms_norm((od_f + jnp.flip(od_b, 1)).astype(x.dtype), p['gdn_norm_w'])
    yc = (od.reshape(b, t, W_C) * jax.nn.silu(cz)) @ p['w_branch'][2]

    g_a, g_b, g_c = jnp.split(jax.nn.sigmoid(merge_logits), 3, axis=-1)
    y = (g_a * ya + g_b * yb + g_c * yc) @ p['w_out']
    x = x + gate * y
    if ctx is None:
        return x, (ka, va, jnp.stack([sr_f, sr_b], axis=1).astype(x.dtype),
                   jnp.stack([sd_f, sd_b], axis=1).astype(x.dtype))
    return x, None


def setup_inputs(seed: int = 0) -> dict:
    key = jax.random.key(seed)
    ks = jax.random.split(key, 24)
    f32 = jnp.float32

    def nrm(k, shape, s):
        return s * jax.random.normal(k, shape, f32)

    ret_base = 1.0 - 2.0 ** (-5.0 - jnp.arange(H_B, dtype=f32))
    ret_logit = jnp.log(ret_base) - jnp.log1p(-ret_base)
    dt = jnp.exp(jax.random.uniform(ks[20], (DEPTH, N_DIR, H_C), f32, math.log(1e-3), math.log(1e-1)))
    return {
        'x_prompt': nrm(ks[0], (BATCH, SEQ, D_MODEL), 1.0),
        'x_sample': nrm(ks[1], (DEC_BATCH, DEC_SEQ, D_MODEL), 1.0),
        'cache_attn_k': nrm(ks[2], (DEC_BATCH, DEPTH, PAST_LEN, H_A, 2, DK_A), 1.0),
        'cache_attn_v': nrm(ks[3], (DEC_BATCH, DEPTH, PAST_LEN, H_A, DV_A), 1.0),
        'state_ret': nrm(ks[4], (DEC_BATCH, DEPTH, N_DIR, H_B, DK_B, DV_B), 1.0),
        'state_gdn': nrm(ks[5], (DEC_BATCH, DEPTH, N_DIR, H_C, DK_C, DV_C), 0.1),
        'c': nrm(ks[6], (DEC_BATCH, D_MODEL), 1.0),
        'c_ctx': nrm(ks[7], (D_MODEL,), 1.0),
        'norm_w': 1.0 + nrm(ks[8], (DEPTH, D_MODEL), 0.02),
        'w_ada': nrm(ks[9], (DEPTH, D_MODEL, 3 * D_MODEL), 0.5 * D_MODEL ** -0.5),
        'b_ada': nrm(ks[10], (DEPTH, 3 * D_MODEL), 0.01),
        'w_in': nrm(ks[11], (DEPTH, D_MODEL, D_IN), D_MODEL ** -0.5),
        'qk_norm_w': 1.0 + nrm(ks[12], (DEPTH, 2, DK_A), 0.02),
        'diff_lambda': nrm(ks[13], (DEPTH, 4, DK_A), 0.1),
        'subln_w': 1.0 + nrm(ks[14], (DEPTH, DV_A), 0.02),
        'ret_decay': ret_logit + nrm(ks[15], (DEPTH, N_DIR, H_B), 0.1),
        'ret_norm_w': 1.0 + nrm(ks[16], (DEPTH, DV_B), 0.02),
        'conv_w': nrm(ks[17], (DEPTH, CONV_K, 3 * W_C), CONV_K ** -0.5),
        'gdn_a_log': jnp.log(jax.random.uniform(ks[18], (DEPTH, N_DIR, H_C), f32, 1.0, 16.0)),
        'gdn_dt_bias': dt + jnp.log(-jnp.expm1(-dt)),
        'gdn_norm_w': 1.0 + nrm(ks[19], (DEPTH, DV_C), 0.02),
        'w_branch': nrm(ks[21], (DEPTH, 3, W_A, D_MODEL), W_A ** -0.5),
        'w_out': nrm(ks[22], (DEPTH, D_MODEL, D_MODEL), D_MODEL ** -0.5),
    }


def reference(x_prompt, x_sample, cache_attn_k, cache_attn_v, state_ret, state_gdn, c, c_ctx,
              norm_w, w_ada, b_ada, w_in, qk_norm_w, diff_lambda, subln_w, ret_decay, ret_norm_w,
              conv_w, gdn_a_log, gdn_dt_bias, gdn_norm_w, w_branch, w_out):
    ctx_cond = jax.nn.silu(c_ctx)
    lat_cond = jax.nn.silu(c)
    cos, sin = axial_rope_tables(x_sample.shape[1], DK_A, x_sample.dtype)
    y_p, y_s = x_prompt, x_sample
    ks_out, vs_out, rs_out, gs_out = [], [], [], []
    for l in range(DEPTH):
        p = {'norm_w': norm_w[l], 'w_in': w_in[l], 'qk_norm_w': qk_norm_w[l],
             'diff_lambda': diff_lambda[l], 'subln_w': subln_w[l], 'ret_decay': ret_decay[l],
             'ret_norm_w': ret_norm_w[l], 'conv_w': conv_w[l], 'gdn_a_log': gdn_a_log[l],
             'gdn_dt_bias': gdn_dt_bias[l], 'gdn_norm_w': gdn_norm_w[l],
             'w_branch': w_branch[l], 'w_out': w_out[l]}
        m_ctx = (ctx_cond @ w_ada[l] + b_ada[l])[None, None, :]
        m_lat = (lat_cond @ w_ada[l] + b_ada[l])[:, None, :]
        y_p, (k_l, v_l, r_l, g_l) = trunk_layer(y_p, m_ctx, l, p, None, None)
        y_s, _ = trunk_layer(y_s, m_lat, l, p, (cos, sin),
                             (cache_attn_k[:, l], cache_attn_v[:, l], state_ret[:, l], state_gdn[:, l]))
        ks_out.append(k_l)
        vs_out.append(v_l)
        rs_out.append(r_l)
        gs_out.append(g_l)
    return (y_p, y_s, jnp.stack(ks_out, axis=1), jnp.stack(vs_out, axis=1),
            jnp.stack(rs_out, axis=1), jnp.stack(gs_out, axis=1))
```

```python
import math
from contextlib import ExitStack
import numpy as np
import ml_dtypes
import concourse.bass as bass
import concourse.mybir as mybir
from concourse.bass_utils import run_bass_kernel_spmd

F32 = mybir.dt.float32
BF16 = mybir.dt.bfloat16
AF = mybir.ActivationFunctionType
ALU = mybir.AluOpType
AX = mybir.AxisListType

D = 1024
DEPTH = 4
TP = 256
NPR = 2
PAST = 256
D_IN = 8720
EPS = 1e-6
CH = 64


import os


class StopBuild(Exception):
    pass


def ckpt(name):
    if os.environ.get("KSTOP", "") == name:
        raise StopBuild(name)


class Res:
    __slots__ = ("name", "w", "r", "excl")

    def __init__(self, name=""):
        self.name = name
        self.w = None
        self.r = {}
        self.excl = False


class Tile:
    def __init__(self, h, name, psum=False):
        self.h = h
        self.res = Res(name)
        self.res.excl = psum

    def __getitem__(self, k):
        return self.h[k]


def _res(x):
    out = []
    for t in x:
        if isinstance(t, (list, tuple)):
            out.extend(_res(t))
        elif isinstance(t, Res):
            out.append(t)
        else:
            out.append(t.res)
    return out


class Sched:
    ENG = ("pe", "act", "dve", "pool", "sp")

    def __init__(self, nc, n_dma_slots=8):
        self.nc = nc
        self.streams = {e: [] for e in self.ENG}
        self.sems = {}
        self.cnt = {}
        for e in ("pe", "act", "dve", "pool"):
            self.sems[e] = nc.alloc_semaphore("s_" + e)
            self.cnt[e] = 0
        self.nslots = n_dma_slots
        self.dq = {}
        for q in ("sp", "pool", "act"):
            slots = []
            for i in range(n_dma_slots):
                k = "d_%s%d" % (q, i)
                self.sems[k] = nc.alloc_semaphore(k)
                slots.append([k, 0])
            self.dq[q] = [slots, 0]
        self.seen = {e: {} for e in self.ENG}
        self.n_instr = 0
        self.n_wait = 0

    def _wait(self, eng, key, val):
        if val is None or val <= 0:
            return
        if eng == "pe" and key == "pe":
            return
        s = self.seen[eng]
        if s.get(key, 0) >= val:
            return
        s[key] = val
        sem = self.sems[key]
        self.streams[eng].append(lambda e, sem=sem, val=val: e.wait_ge(sem, val))
        self.n_wait += 1

    def _deps(self, eng, reads, writes, is_dma=False):
        for r in reads:
            if r.w is not None:
                self._wait(eng, r.w[0], r.w[1])
        for w in writes:
            if w.w is not None:
                if is_dma or not (w.w[0] == eng):
                    self._wait(eng, w.w[0], w.w[1])
            for k, v in w.r.items():
                if (not is_dma) and k == eng:
                    continue
                self._wait(eng, k, v)

    def _mark(self, key, val, reads, writes):
        for r in reads:
            if r.r.get(key, 0) < val:
                r.r[key] = val
        for w in writes:
            w.w = (key, val)
            w.r = {}

    def op(self, eng, method, R=(), W=(), **kw):
        reads = _res(R)
        writes = _res(W)
        ex = [r for r in reads if r.excl]
        if ex:
            reads = [r for r in reads if not r.excl]
            writes = writes + [r for r in ex if r not in writes]
        self._deps(eng, reads, writes)
        self.cnt[eng] += 1
        val = self.cnt[eng]
        sem = self.sems[eng]
        import traceback
        org = traceback.extract_stack(limit=3)[0]
        org = "%s:%d" % (org.name, org.lineno)

        def _f(e, m=method, kw=kw, sem=sem, org=org):
            try:
                return getattr(e, m)(**kw).then_inc(sem, 1)
            except Exception as ex:
                raise RuntimeError("emit failed at %s (%s): %s" % (org, m, ex)) from ex
        self.streams[eng].append(_f)
        self._mark(eng, val, reads, writes)
        self.n_instr += 1

    def dma(self, q, out, in_, R=(), W=(), **kw):
        reads = _res(R)
        writes = _res(W)
        slots, idx = self.dq[q]
        slot = slots[idx % self.nslots]
        self.dq[q][1] = idx + 1
        key = slot[0]
        if slot[1] > 0:
            self._wait(q, key, slot[1])
        self._deps(q, reads, writes, is_dma=True)
        slot[1] += 16
        val = slot[1]
        sem = self.sems[key]
        import traceback
        org = traceback.extract_stack(limit=3)[0]
        org = "%s:%d" % (org.name, org.lineno)

        def _f(e, out=out, in_=in_, sem=sem, kw=kw, org=org):
            try:
                return e.dma_start(out=out, in_=in_, **kw).then_inc(sem, 16)
            except Exception as ex:
                raise RuntimeError("dma emit failed at %s: %s" % (org, ex)) from ex
        self.streams[q].append(_f)
        self._mark(key, val, reads, writes)
        self.n_instr += 1

    def barrier(self):
        for e in self.ENG:
            for k in ("pe", "act", "dve", "pool"):
                if k != e:
                    self._wait(e, k, self.cnt[k])
            for q in self.dq:
                for slot in self.dq[q][0]:
                    if slot[1] > 0:
                        self._wait(e, slot[0], slot[1])

    def emit(self):
        nc = self.nc
        st = self.streams
        with nc.Block() as block:
            @block.sync
            def _(e):
                for f in st["sp"]:
                    f(e)

            @block.tensor
            def _(e):
                for f in st["pe"]:
                    f(e)

            @block.scalar
            def _(e):
                for f in st["act"]:
                    f(e)

            @block.vector
            def _(e):
                for f in st["dve"]:
                    f(e)

            @block.gpsimd
            def _(e):
                for f in st["pool"]:
                    f(e)


def make_consts(T_S):
    c = {}
    c["ident"] = np.eye(128, dtype=np.float32)
    n_rows = T_S // 64
    row = np.repeat(np.arange(n_rows, dtype=np.float32), 64)
    col = np.tile(np.arange(64, dtype=np.float32), n_rows)
    inv = (1.0 / (10000.0 ** (np.arange(16, dtype=np.float32) / 16))).astype(np.float32)
    ar = row[:, None] * inv
    ac = col[:, None] * inv
    ang = np.concatenate([ar, ar, ac, ac], axis=-1).astype(np.float32)
    cos = np.cos(ang).astype(np.float32)
    sin = np.sin(ang).astype(np.float32)
    sgn = np.tile(np.concatenate([-np.ones(16), np.ones(16)]), 2).astype(np.float32)
    c["ropec"] = cos
    c["ropes"] = (sin * sgn).astype(np.float32)
    a = np.arange(64)[:, None]
    b = np.arange(64)[None, :]
    low = (a > b).astype(np.float32)
    up = (b > a).astype(np.float32)
    upi = (b >= a).astype(np.float32)
    lowi = (a >= b).astype(np.float32)
    gm = np.zeros((64, 2, 3, 64), np.float32)
    gm[:, 0, 0] = -low
    gm[:, 0, 1] = -up
    gm[:, 0, 2] = upi
    gm[:, 1, 0] = -up
    gm[:, 1, 1] = -low
    gm[:, 1, 2] = lowi
    c["gmask"] = gm
    a2 = np.arange(128)[:, None]
    b2 = np.arange(128)[None, :]
    same = (a2 // 64 == b2 // 64)
    gm2 = np.zeros((128, 2, 3, 128), np.float32)
    gm2[:, 0, 0] = -1.0 * ((a2 > b2) & same)
    gm2[:, 0, 1] = -1.0 * ((b2 > a2) & same)
    gm2[:, 0, 2] = ((b2 >= a2) & same)
    gm2[:, 1, 0] = -1.0 * ((b2 > a2) & same)
    gm2[:, 1, 1] = -1.0 * ((a2 > b2) & same)
    gm2[:, 1, 2] = ((a2 >= b2) & same)
    c["gmask2"] = gm2
    ut2 = np.zeros((128, 2, 128), np.float32)
    ut2[:, 0] = ((a2 <= b2) & same)
    ut2[:, 1] = ((a2 >= b2) & same)
    c["utri2"] = ut2
    c["bd1"] = same.astype(np.float32)
    sel = np.zeros((128, 2, 128), np.float32)
    sel[0:64, 0, :] = 1.0
    sel[64:128, 1, :] = 1.0
    c["selc"] = sel
    ut = np.zeros((64, 2, 64), np.float32)
    ut[:, 0] = (a <= b).astype(np.float32)
    ut[:, 1] = (a >= b).astype(np.float32)
    c["utri"] = ut
    j = np.arange(128)[:, None].astype(np.float32)
    i = np.arange(128)[None, :].astype(np.float32)
    rr = np.zeros((128, 2, 128), np.float32)
    rm = np.zeros((128, 2, 128), np.float32)
    rr[:, 0] = np.maximum(i - j, 0)
    rm[:, 0] = (i >= j)
    rr[:, 1] = np.maximum(j - i, 0)
    rm[:, 1] = (j >= i)
    c["rrel"] = rr
    c["rmask"] = rm
    rq = np.zeros((64, 2, 128), np.float32)
    rq[:, 0] = (np.arange(128) + 1.0)[None, :]
    rq[:, 1] = (128.0 - np.arange(128))[None, :]
    c["rqexp"] = rq
    rk = np.zeros((128, 2), np.float32)
    rk[:, 0] = 127.0 - np.arange(128)
    rk[:, 1] = np.arange(128)
    c["rkexp"] = rk
    return c


CONST_SHAPES = lambda T_S: {k: v.shape for k, v in make_consts(T_S).items()}


def build(T_S=4096, depth=DEPTH, debug=False):
    nc = bass.Bass("TRN2", target_bir_lowering=False)
    S = Sched(nc)
    TT = T_S + NPR * TP
    NT = TT // 128
    NTS = T_S // 128
    seqs = [(0, T_S, True)] + [(T_S + i * TP, TP, False) for i in range(NPR)]
    lam_inits = [0.8 - 0.6 * math.exp(-0.3 * l) for l in range(depth)]

    def din(name, shape, dt=F32):
        return nc.dram_tensor(name, list(shape), dt, kind="ExternalInput").ap()

    def dout(name, shape, dt=F32):
        return nc.dram_tensor(name, list(shape), dt, kind="ExternalOutput").ap()

    def scr(name, shape, dt):
        if debug:
            return nc.dram_tensor(name, list(shape), dt, kind="ExternalOutput").ap()
        return nc.dram_tensor(name, list(shape), dt).ap()

    x_in = din("x_in", [TT, D])
    cond = din("cond", [2, D])
    cache_k = din("cache_k", [depth, PAST, 512])
    cache_v = din("cache_v", [depth, PAST, 512])
    st_ret = din("st_ret", [depth, 2, 4, 64, 128])
    st_gdn = din("st_gdn", [depth, 2, 4, 128, 128])
    norm_w = din("norm_w", [depth, D])
    w_ada = din("w_ada", [depth, D, 3 * D])
    b_ada = din("b_ada", [depth, 3 * D])
    w_in = din("w_in", [depth, D, D_IN])
    qk_norm_w = din("qk_norm_w", [depth, 2, 64])
    diff_lambda = din("diff_lambda", [depth, 4, 64])
    subln_w = din("subln_w", [depth, 128])
    ret_decay = din("ret_decay", [depth, 8])
    ret_norm_w = din("ret_norm_w", [depth, 128])
    conv_w = din("conv_w", [depth, 5, 1536])
    gdn_a_log = din("gdn_a_log", [depth, 8])
    gdn_dt_bias = din("gdn_dt_bias", [depth, 8])
    gdn_norm_w = din("gdn_norm_w", [depth, 128])
    w_branch = din("w_branch", [depth, 3, 512, D])
    w_out = din("w_out", [depth, D, D])
    cst = {k: din("c_" + k, shp) for k, shp in CONST_SHAPES(T_S).items()}
    y_out = dout("y", [TT, D])
    nk_out = dout("nk", [NPR, depth, TP, 512])
    nv_out = dout("nv", [NPR, depth, TP, 512])
    nret_out = dout("nret", [NPR, depth, 2, 4, 64, 128])
    ngdn_out = dout("ngdn", [NPR, depth, 2, 4, 128, 128])
    xcur = scr("xcur", [TT, D], F32)
    aqT = scr("aqT", [4, 128, TT], BF16)
    akT = scr("akT", [4, 128, TT], BF16)
    av = scr("av", [TT, 512], BF16)
    zg = scr("zg", [3, TT, 512], BF16)
    bqT = scr("bqT", [4, 64, TT], BF16)
    bkT = scr("bkT", [4, 64, TT], BF16)
    bk = scr("bk", [TT, 256], BF16)
    bv = scr("bv", [TT, 512], BF16)
    cT = scr("cT", [1536, TT], F32)
    gqT = scr("gqT", [4, 128, TT], BF16)
    gkT = scr("gkT", [4, 128, TT], BF16)
    gk = scr("gk", [TT, 512], BF16)
    gv = scr("gv", [TT, 512], BF16)
    bgs = scr("bgs", [TT, 16], F32)
    mg = scr("mg", [TT, 3 * D], BF16)
    ofs = scr("ofs", [TT, 512], F32)
    ofb = scr("ofb", [TT, 512], F32)
    gT = scr("gT", [3, 512, TT], BF16)

    def tres(n):
        return [Res("%s%d" % (n, i)) for i in range(NT)]
    RX = {n: tres(n) for n in ("x", "aqT", "akT", "av", "zg0", "zg1", "zg2", "bqT", "bkT", "bk", "bv", "cT",
                               "gqT", "gkT", "gk", "gv", "bgs", "mg", "ofs", "ofb", "gT0", "gT1", "gT2")}
    R_out = Res("outs")

    uid = [0]

    def sb(es, name, shape, dt):
        uid[0] += 1
        nm = "%s_u%d" % (name, uid[0])
        return Tile(es.enter_context(nc.sbuf_tensor(nm, list(shape), dt)), nm)

    class Pool_:
        def __init__(self, es, name, shape, dt, n):
            self.t = [sb(es, "%s_%d" % (name, i), shape, dt) for i in range(n)]
            self.i = 0

        def next(self):
            t = self.t[self.i % len(self.t)]
            self.i += 1
            return t

    root = ExitStack()
    PSF = [Tile(nc.alloc_psum_tensor("psf%d" % i, [128, 512], F32), "psf%d" % i, True) for i in range(6)]
    PSB = [Tile(nc.alloc_psum_tensor("psb%d" % i, [128, 1024], BF16), "psb%d" % i, True) for i in range(2)]
    psc = [0, 0]

    psf_banks = [[0, 1, 2, 3]]

    def psf():
        psc[0] += 1
        bk = psf_banks[0]
        return PSF[bk[psc[0] % len(bk)]]

    def psb():
        psc[1] += 1
        return PSB[psc[1] % 2]

    ident_f = sb(root, "ident_f", [128, 128], F32)
    ident_b = sb(root, "ident_b", [128, 128], BF16)
    ones_f = sb(root, "ones_f", [128, 128], F32)
    ropec = sb(root, "ropec", [128, NTS, 64], F32)
    ropes = sb(root, "ropes", [128, NTS, 64], F32)
    gmask = sb(root, "gmask", [64, 2, 3, 64], F32)
    utri = sb(root, "utri", [64, 2, 64], F32)
    gmask2 = sb(root, "gmask2", [128, 2, 3, 128], F32)
    utri2 = sb(root, "utri2", [128, 2, 128], F32)
    bd1 = sb(root, "bd1", [128, 128], F32)
    selc = sb(root, "selc", [128, 2, 128], F32)
    rrel = sb(root, "rrel", [128, 2, 128], F32)
    rmask = sb(root, "rmask", [128, 2, 128], F32)
    rqexp = sb(root, "rqexp", [64, 2, 128], F32)
    rkexp = sb(root, "rkexp", [128, 2], F32)
    gate_bc = sb(root, "gate_bc", [128, 2, D], F32)
    wq_bc = sb(root, "wq_bc", [128, 2, 64], F32)
    subw_bc = sb(root, "subw_bc", [128, 128], F32)
    retw_bc = sb(root, "retw_bc", [128, 128], F32)
    gdnw_bc = sb(root, "gdnw_bc", [128, 128], F32)
    lamt = sb(root, "lamt", [128, 8], F32)
    dl_bc = sb(root, "dl_bc", [128, 4, 64], F32)
    lg_bc = sb(root, "lg_bc", [128, 8], F32)
    rdmat = sb(root, "rdmat", [128, 2, 4, 128], F32)
    rqdec = sb(root, "rqdec", [64, 2, 4, 128], F32)
    rkdec = sb(root, "rkdec", [128, 2, 4], F32)
    rcdec = sb(root, "rcdec", [128, 8], F32)
    negA_bc = sb(root, "negA_bc", [128, 8], F32)
    dtb_bc = sb(root, "dtb_bc", [128, 8], F32)
    convw = sb(root, "convw", [128, 12, 5], F32)
    tmp8 = sb(root, "tmp8", [128, 8], F32)

    S.dma("sp", ident_f[:], cst["ident"], W=[ident_f])
    S.op("dve", "tensor_copy", R=[ident_f], W=[ident_b], out=ident_b[:], in_=ident_f[:])
    S.op("dve", "memset", W=[ones_f], ap=ones_f[:], constant=1.0)
    S.dma("sp", ropec[:], cst["ropec"].rearrange("(n p) d -> p n d", p=128), W=[ropec])
    S.dma("sp", ropes[:], cst["ropes"].rearrange("(n p) d -> p n d", p=128), W=[ropes])
    S.dma("sp", gmask[:], cst["gmask"], W=[gmask])
    S.dma("sp", utri[:], cst["utri"], W=[utri])
    S.dma("sp", gmask2[:], cst["gmask2"], W=[gmask2])
    S.dma("sp", utri2[:], cst["utri2"], W=[utri2])
    S.dma("sp", bd1[:], cst["bd1"], W=[bd1])
    S.dma("sp", selc[:], cst["selc"], W=[selc])
    S.dma("sp", rrel[:], cst["rrel"], W=[rrel])
    S.dma("sp", rmask[:], cst["rmask"], W=[rmask])
    S.dma("sp", rqexp[:], cst["rqexp"], W=[rqexp])
    S.dma("sp", rkexp[:], cst["rkexp"], W=[rkexp])

    def rsqrt_inplace(t, ap, scale, eps):
        S.op("dve", "tensor_scalar", R=[t], W=[t], out=ap, in0=ap, scalar1=scale, scalar2=eps,
             op0=ALU.mult, op1=ALU.add)
        S.op("act", "activation", R=[t], W=[t], out=ap, in_=ap, func=AF.Sqrt)
        S.op("dve", "reciprocal", R=[t], W=[t], out=ap, in_=ap)

    def transposes_to(ps_t, src_t, blocks, rows=128):
        for (sap, w, off) in blocks:
            S.op("pe", "transpose", R=[src_t, ident_b], W=[ps_t], out=ps_t[0:w, off:off + rows], in_=sap,
                 identity=ident_b[0:rows, 0:rows])

    def layer(l):
        last = (l == depth - 1)
        xsrc = x_in if l == 0 else xcur
        xdst = y_out if last else xcur
        esH = ExitStack()
        hT = sb(esH, "hT", [128, 8, TT], BF16)
        esA = ExitStack()
        A_bc = sb(esA, "A_bc", [128, 2, D], F32)
        sh_bc = sb(esA, "sh_bc", [128, 2, D], F32)
        with ExitStack() as es:
            nw_bc = sb(es, "nw_bc", [128, D], F32)
            cs = sb(es, "cs", [128, 2, 8], F32)
            rep = sb(es, "rep", [128, 16, 128], F32)
            bada = sb(es, "bada", [1, 3 * D], F32)
            wa = Pool_(es, "wa", [128, 8, 512], F32, 2)
            S.dma("sp", nw_bc[:], norm_w[l].partition_broadcast(128), W=[nw_bc])
            S.dma("sp", cs[:], cond.rearrange("j (p c) -> p j c", c=8), W=[cs])
            S.dma("sp", bada[:], b_ada[l:l + 1, :], W=[bada])
            S.dma("sp", wq_bc[:],
                  qk_norm_w[l].rearrange("a d -> (a d)").partition_broadcast(128).rearrange("p (a d) -> p a d", a=2),
                  W=[wq_bc])
            S.dma("sp", subw_bc[:], subln_w[l].partition_broadcast(128), W=[subw_bc])
            S.dma("sp", retw_bc[:], ret_norm_w[l].partition_broadcast(128), W=[retw_bc])
            S.dma("sp", gdnw_bc[:], gdn_norm_w[l].partition_broadcast(128), W=[gdnw_bc])
            S.dma("sp", dl_bc[:], diff_lambda[l].rearrange("a d -> (a d)").partition_broadcast(128)
                  .rearrange("p (a d) -> p a d", a=4), W=[dl_bc])
            S.dma("sp", lg_bc[:], ret_decay[l].partition_broadcast(128), W=[lg_bc])
            S.dma("sp", negA_bc[:], gdn_a_log[l].partition_broadcast(128), W=[negA_bc])
            S.dma("sp", dtb_bc[:], gdn_dt_bias[l].partition_broadcast(128), W=[dtb_bc])
            for k in range(5):
                S.dma("sp", convw[:, :, k:k + 1], conv_w[l, k].rearrange("(c p o) -> p c o", p=128, o=1), W=[convw],
                      allow_slow_non_contiguous=True)
            S.op("dve", "tensor_scalar", R=[wq_bc], W=[wq_bc], out=wq_bc[:, 0, :], in0=wq_bc[:, 0, :],
                 scalar1=0.125, scalar2=None, op0=ALU.mult)
            S.op("dve", "tensor_scalar", R=[subw_bc], W=[subw_bc], out=subw_bc[:], in0=subw_bc[:],
                 scalar1=float(1.0 - lam_inits[l]), scalar2=None, op0=ALU.mult)
            S.op("dve", "tensor_tensor", R=[dl_bc], W=[dl_bc], out=dl_bc[:, 0, :], in0=dl_bc[:, 0, :],
                 in1=dl_bc[:, 1, :], op=ALU.mult)
            S.op("dve", "tensor_tensor", R=[dl_bc], W=[dl_bc], out=dl_bc[:, 2, :], in0=dl_bc[:, 2, :],
                 in1=dl_bc[:, 3, :], op=ALU.mult)
            S.op("dve", "tensor_reduce", R=[dl_bc], W=[lamt], out=lamt[:, 0:4], in_=dl_bc[:], axis=AX.X, op=ALU.add)
            S.op("act", "activation", R=[lamt], W=[lamt], out=lamt[:, 4:8], in_=lamt[:, 0:4], func=AF.Exp)
            S.op("dve", "tensor_tensor", R=[lamt], W=[lamt], out=lamt[:, 1:2], in0=lamt[:, 6:7], in1=lamt[:, 4:5],
                 op=ALU.subtract)
            S.op("dve", "tensor_scalar", R=[lamt], W=[lamt], out=lamt[:, 0:1], in0=lamt[:, 1:2],
                 scalar1=float(-lam_inits[l]), scalar2=None, op0=ALU.add)
            S.op("act", "activation", R=[lg_bc], W=[lg_bc], out=lg_bc[:], in_=lg_bc[:], func=AF.Exp, scale=-1.0)
            S.op("act", "activation", R=[lg_bc], W=[lg_bc], out=lg_bc[:], in_=lg_bc[:], func=AF.Ln, bias=1.0)
            S.op("dve", "tensor_scalar", R=[lg_bc], W=[lg_bc], out=lg_bc[:], in0=lg_bc[:], scalar1=-1.0,
                 scalar2=None, op0=ALU.mult)
            for d in range(2):
                for h in range(4):
                    u = d * 4 + h
                    S.op("act", "activation", R=[rrel, lg_bc], W=[rdmat], out=rdmat[:, d, h, :], in_=rrel[:, d, :],
                         func=AF.Exp, scale=lg_bc[:, u:u + 1])
                    S.op("dve", "tensor_tensor", R=[rdmat, rmask], W=[rdmat], out=rdmat[:, d, h, :],
                         in0=rdmat[:, d, h, :], in1=rmask[:, d, :], op=ALU.mult)
                    S.op("act", "activation", R=[rqexp, lg_bc], W=[rqdec], out=rqdec[:, d, h, :], in_=rqexp[:, d, :],
                         func=AF.Exp, scale=lg_bc[0:64, u:u + 1])
                    S.op("act", "activation", R=[rkexp, lg_bc], W=[rkdec], out=rkdec[:, d, h:h + 1],
                         in_=rkexp[:, d:d + 1], func=AF.Exp, scale=lg_bc[:, u:u + 1])
            S.op("act", "activation", R=[lg_bc], W=[rcdec], out=rcdec[:], in_=lg_bc[:], func=AF.Exp, scale=128.0)
            S.op("act", "activation", R=[negA_bc], W=[negA_bc], out=negA_bc[:], in_=negA_bc[:], func=AF.Exp)
            S.op("dve", "tensor_scalar", R=[negA_bc], W=[negA_bc], out=negA_bc[:], in0=negA_bc[:], scalar1=-1.0,
                 scalar2=None, op0=ALU.mult)
            S.op("act", "activation", R=[cs], W=[cs], out=cs[:], in_=cs[:], func=AF.Silu)
            S.op("dve", "tensor_copy", R=[cs], W=[rep], out=rep[:],
                 in_=cs[:].rearrange("p j c -> p (j c)").unsqueeze(2).to_broadcast([128, 16, 128]))
            for nb in range(6):
                w = wa.next()
                S.dma("sp", w[:], w_ada[l].rearrange("(p c) n -> p c n", c=8)[:, :, nb * 512:(nb + 1) * 512], W=[w])
                for j in range(2):
                    ps = psf()
                    for c in range(8):
                        S.op("pe", "matmul", R=[rep, w], W=[ps], out=ps[:], lhsT=rep[:, j * 8 + c, :], rhs=w[:, c, :],
                             start=(c == 0), stop=False)
                    S.op("pe", "matmul", R=[ones_f, bada], W=[ps], out=ps[:], lhsT=ones_f[0:1, :],
                         rhs=bada[0:1, nb * 512:(nb + 1) * 512], start=False, stop=True)
                    cols = slice((nb % 2) * 512, (nb % 2) * 512 + 512)
                    if nb < 2:
                        S.op("act", "copy", R=[ps], W=[sh_bc], out=sh_bc[:, j, cols], in_=ps[:])
                    elif nb < 4:
                        S.op("dve", "scalar_tensor_tensor", R=[ps, nw_bc], W=[A_bc], out=A_bc[:, j, cols], in0=ps[:],
                             scalar=1.0, in1=nw_bc[:, cols], op0=ALU.add, op1=ALU.mult)
                    else:
                        S.op("act", "copy", R=[ps], W=[gate_bc], out=gate_bc[:, j, cols], in_=ps[:])
            S.barrier()
        ckpt("A")
        with ExitStack() as es:
            xp = Pool_(es, "xB", [128, D], F32, 4)
            hp = Pool_(es, "hB", [128, D], F32, 4)
            hbp = Pool_(es, "hbB", [128, D], BF16, 4)
            junk = sb(es, "junkB", [128, D], F32)
            ssp = Pool_(es, "ssB", [128, 1], F32, 4)
            def tileB(tt):
                j = 0 if tt < NTS else 1
                xt = xp.next()
                ht = hp.next()
                hb = hbp.next()
                ss = ssp.next()
                S.dma("sp", xt[:], xsrc[tt * 128:(tt + 1) * 128, :], R=[RX["x"][tt]] if l > 0 else [], W=[xt])
                S.op("act", "activation", R=[xt], W=[junk, ss], out=junk[:], in_=xt[:], func=AF.Square, accum_out=ss[:])
                yield
                rsqrt_inplace(ss, ss[:], 1.0 / D, EPS)
                S.op("dve", "scalar_tensor_tensor", R=[xt, ss, A_bc], W=[ht], out=ht[:], in0=xt[:], scalar=ss[:, 0:1],
                     in1=A_bc[:, j, :], op0=ALU.mult, op1=ALU.mult)
                S.op("pool", "tensor_tensor", R=[ht, sh_bc], W=[hb], out=hb[:], in0=ht[:], in1=sh_bc[:, j, :], op=ALU.add)
                yield
                pb = psb()
                transposes_to(pb, hb, [(hb[:, c * 128:(c + 1) * 128], 128, c * 128) for c in range(8)])
                S.op("act", "copy", R=[pb], W=[hT], out=hT[:, :, tt * 128:(tt + 1) * 128],
                     in_=pb[:].rearrange("p (c t) -> p c t", c=8))
            pipeline([tileB(tt) for tt in range(NT)], 3)
            S.barrier()
        esA.close()
        ckpt("B")
        with ExitStack() as es:
            wf = Pool_(es, "wfC", [128, 4, 512], F32, 2)
            wbp = Pool_(es, "wbC", [128, 8, 512], BF16, 2)
            f1 = Pool_(es, "f1C", [128, 512], F32, 3)
            f2 = Pool_(es, "f2C", [128, 512], F32, 2)
            b1 = Pool_(es, "b1C", [128, 512], BF16, 3)
            st8 = Pool_(es, "st8C", [128, 16], F32, 2)
            stg = Pool_(es, "stgC", [128, 1024], BF16, 2)
            cfp = Pool_(es, "cfC", [128, 512], F32, 3)

            def load_wblock(c0, ncols):
                wb = wbp.next()
                for half in range(2):
                    w = wf.next()
                    S.dma("sp", w[:, :, 0:ncols],
                          w_in[l].rearrange("(c p) n -> p c n", p=128)[:, half * 4:half * 4 + 4, c0:c0 + ncols], W=[w])
                    S.op("dve" if half == 0 else "pool", "tensor_copy", R=[w], W=[wb],
                         out=wb[:, half * 4:half * 4 + 4, 0:ncols], in_=w[:, :, 0:ncols])
                return wb

            def mm_tok(wb, tt, ncols):
                ps = psf()
                for c in range(8):
                    S.op("pe", "matmul", R=[hT, wb], W=[ps], out=ps[:, 0:ncols], lhsT=hT[:, c, tt * 128:(tt + 1) * 128],
                         rhs=wb[:, c, 0:ncols], start=(c == 0), stop=(c == 7))
                return ps

            def rope(src, tt, ngrp, dst_pool):
                t1 = dst_pool.next()
                t2 = dst_pool.next()
                s4 = src[:, 0:ngrp * 64].rearrange("p (g a h f) -> p g a h f", a=2, h=2, f=16)
                S.op("dve", "tensor_tensor", R=[src, ropec], W=[t1],
                     out=t1[:, 0:ngrp * 64].rearrange("p (g d) -> p g d", d=64),
                     in0=src[:, 0:ngrp * 64].rearrange("p (g d) -> p g d", d=64),
                     in1=ropec[:, tt:tt + 1, :].to_broadcast([128, ngrp, 64]), op=ALU.mult)
                t24 = t2[:, 0:ngrp * 64].rearrange("p (g a h f) -> p g a h f", a=2, h=2, f=16)
                sn4 = ropes[:, tt, :].rearrange("p (a h f) -> p a h f", a=2, h=2)
                for hh in range(2):
                    S.op("pool", "tensor_tensor", R=[src, ropes], W=[t2], out=t24[:, :, :, hh, :],
                         in0=s4[:, :, :, 1 - hh, :],
                         in1=sn4[:, :, hh, :].unsqueeze(1).to_broadcast([128, ngrp, 2, 16]), op=ALU.mult)
                S.op("dve", "tensor_tensor", R=[t1, t2], W=[t1], out=t1[:, 0:ngrp * 64], in0=t1[:, 0:ngrp * 64],
                     in1=t2[:, 0:ngrp * 64], op=ALU.add)
                return t1

            for blk in range(2):
                wb = load_wblock(blk * 512, 512)
                for tt in range(NT):
                    is_s = tt < NTS
                    ps = mm_tok(wb, tt, 512)
                    sq = f2.next()
                    s8 = st8.next()
                    qn = f1.next()
                    S.op("act", "activation", R=[ps], W=[sq], out=sq[:], in_=ps[:], func=AF.Square)
                    S.op("dve", "tensor_reduce", R=[sq], W=[s8], out=s8[:, 0:8],
                         in_=sq[:].rearrange("p (g d) -> p g d", d=64), axis=AX.X, op=ALU.add)
                    rsqrt_inplace(s8, s8[:, 0:8], 1.0 / 64, EPS)
                    S.op("dve", "tensor_tensor", R=[ps, s8], W=[qn], out=qn[:].rearrange("p (g d) -> p g d", d=64),
                         in0=ps[:].rearrange("p (g d) -> p g d", d=64),
                         in1=s8[:, 0:8].unsqueeze(2).to_broadcast([128, 8, 64]), op=ALU.mult)
                    S.op("pool", "tensor_tensor", R=[qn, wq_bc], W=[qn], out=qn[:].rearrange("p (g d) -> p g d", d=64),
                         in0=qn[:].rearrange("p (g d) -> p g d", d=64),
                         in1=wq_bc[:, blk:blk + 1, :].to_broadcast([128, 8, 64]), op=ALU.mult)
                    if blk == 1 and not is_s:
                        pi = (tt - NTS) // 2
                        r0 = ((tt - NTS) % 2) * 128
                        S.dma("pool", nk_out[pi, l, r0:r0 + 128, :], qn[:], R=[qn])
                    if is_s:
                        qn = rope(qn, tt, 8, f1)
                    qb = b1.next()
                    S.op("act", "copy", R=[qn], W=[qb], out=qb[:], in_=qn[:])
                    pb = psb()
                    transposes_to(pb, qb, [(qb[:, h * 128:(h + 1) * 128], 128, h * 128) for h in range(4)])
                    sg = stg.next()
                    S.op("dve", "tensor_copy", R=[pb], W=[sg], out=sg[:, 0:512], in_=pb[:, 0:512])
                    dst = aqT if blk == 0 else akT
                    S.dma("pool", dst[:, :, tt * 128:(tt + 1) * 128].rearrange("h p t -> p h t"),
                          sg[:, 0:512].rearrange("p (h t) -> p h t", h=4), R=[sg],
                          W=[RX["aqT" if blk == 0 else "akT"][tt]])
            ckpt("C0")
            for (c0, kind) in ((1024, "av"), (1536, "z0"), (2560, "bv"), (3072, "z1"), (5120, "z2")):
                wb = load_wblock(c0, 512)
                for tt in range(NT):
                    is_s = tt < NTS
                    ps = mm_tok(wb, tt, 512)
                    ob = b1.next()
                    if kind in ("av", "bv"):
                        S.op("act", "copy", R=[ps], W=[ob], out=ob[:], in_=ps[:])
                        dst, rn = (av, "av") if kind == "av" else (bv, "bv")
                        S.dma("pool", dst[tt * 128:(tt + 1) * 128, :], ob[:], R=[ob], W=[RX[rn][tt]])
                        if kind == "av" and not is_s:
                            of_ = f1.next()
                            S.op("dve", "tensor_copy", R=[ps], W=[of_], out=of_[:], in_=ps[:])
                            pi = (tt - NTS) // 2
                            r0 = ((tt - NTS) % 2) * 128
                            S.dma("pool", nv_out[pi, l, r0:r0 + 128, :], of_[:], R=[of_])
                    else:
                        zi = int(kind[1])
                        S.op("act", "activation", R=[ps], W=[ob], out=ob[:], in_=ps[:], func=AF.Silu)
                        S.dma("pool", zg[zi, tt * 128:(tt + 1) * 128, :], ob[:], R=[ob], W=[RX["zg%d" % zi][tt]])
                ckpt("C1_" + kind)
            ckpt("C1")
            wb = load_wblock(2048, 512)
            for tt in range(NT):
                is_s = tt < NTS
                ps = mm_tok(wb, tt, 512)
                qk = f1.next()
                S.op("act", "copy", R=[ps], W=[qk], out=qk[:, 0:256], in_=ps[:, 0:256])
                S.op("act", "mul", R=[ps], W=[qk], out=qk[:, 256:512], in_=ps[:, 256:512], mul=0.125)
                if is_s:
                    qk = rope(qk, tt, 8, f1)
                qb = b1.next()
                S.op("act", "copy", R=[qk], W=[qb], out=qb[:], in_=qk[:])
                S.dma("pool", bk[tt * 128:(tt + 1) * 128, :], qb[:, 256:512], R=[qb], W=[RX["bk"][tt]])
                pb = psb()
                transposes_to(pb, qb, [(qb[:, g * 64:(g + 1) * 64], 64, g * 128) for g in range(8)])
                sg = stg.next()
                S.op("dve", "tensor_copy", R=[pb], W=[sg], out=sg[0:64, :], in_=pb[0:64, :])
                S.dma("pool", bqT[:, :, tt * 128:(tt + 1) * 128].rearrange("h p t -> p h t"),
                      sg[0:64, 0:512].rearrange("p (h t) -> p h t", h=4), R=[sg], W=[RX["bqT"][tt]])
                S.dma("pool", bkT[:, :, tt * 128:(tt + 1) * 128].rearrange("h p t -> p h t"),
                      sg[0:64, 512:1024].rearrange("p (h t) -> p h t", h=4), R=[sg], W=[RX["bkT"][tt]])
            ckpt("C2")
            for blk in range(3):
                wb = load_wblock(3584 + blk * 512, 512)
                for cc in range(4):
                    for (t0, T, _) in seqs:
                        for g0 in range(0, T, 512):
                            n = min(512, T - g0)
                            ps = psf()
                            for c in range(8):
                                S.op("pe", "matmul", R=[hT, wb], W=[ps], out=ps[:, 0:n],
                                     lhsT=wb[:, c, cc * 128:(cc + 1) * 128], rhs=hT[:, c, t0 + g0:t0 + g0 + n],
                                     start=(c == 0), stop=(c == 7))
                            cf = cfp.next()
                            S.op("act", "copy", R=[ps], W=[cf], out=cf[:, 0:n], in_=ps[:, 0:n])
                            ch0 = (blk * 4 + cc) * 128
                            tiles = range((t0 + g0) // 128, (t0 + g0 + n) // 128)
                            S.dma("pool", cT[ch0:ch0 + 128, t0 + g0:t0 + g0 + n], cf[:, 0:n], R=[cf],
                                  W=[RX["cT"][i] for i in tiles])
            ckpt("C3")
            wb = load_wblock(5632, 16)
            for tt in range(NT):
                ps = mm_tok(wb, tt, 16)
                o16 = st8.next()
                S.op("act", "activation", R=[ps], W=[o16], out=o16[:, 0:8], in_=ps[:, 0:8], func=AF.Sigmoid)
                S.op("dve", "tensor_tensor", R=[ps, dtb_bc], W=[o16], out=o16[:, 8:16], in0=ps[:, 8:16], in1=dtb_bc[:],
                     op=ALU.add)
                S.op("act", "activation", R=[o16], W=[o16], out=o16[:, 8:16], in_=o16[:, 8:16], func=AF.Exp)
                S.op("act", "activation", R=[o16], W=[o16], out=o16[:, 8:16], in_=o16[:, 8:16], func=AF.Ln, bias=1.0)
                S.op("dve", "tensor_tensor", R=[o16, negA_bc], W=[o16], out=o16[:, 8:16], in0=o16[:, 8:16],
                     in1=negA_bc[:], op=ALU.mult)
                S.dma("pool", bgs[tt * 128:(tt + 1) * 128, :], o16[:], R=[o16], W=[RX["bgs"][tt]])
            ckpt("C4")
            for blk in range(6):
                wb = load_wblock(5648 + blk * 512, 512)
                for tt in range(NT):
                    ps = mm_tok(wb, tt, 512)
                    ob = b1.next()
                    S.op("act", "activation", R=[ps], W=[ob], out=ob[:], in_=ps[:], func=AF.Sigmoid)
                    S.dma("pool", mg[tt * 128:(tt + 1) * 128, blk * 512:(blk + 1) * 512], ob[:], R=[ob],
                          W=[RX["mg"][tt]] if blk == 5 else [])
            S.barrier()
        esH.close()
        ckpt("C")
        for si, (t0, T, is_s) in enumerate(seqs):
            if is_s and not os.environ.get("KNOOVL"):
                sample_mixers_overlapped(l, si, t0, T, is_s)
                ckpt("gpre%d" % si)
            else:
                attention(l, si, t0, T, is_s)
                ckpt("att%d" % si)
                retention(l, si, t0, T, is_s)
                ckpt("ret%d" % si)
                gdn_pre(l, si, t0, T, is_s)
                ckpt("gpre%d" % si)
            gdn_scan(l, si, t0, T, is_s)
            ckpt("gscan%d" % si)
        phaseE(l, xsrc, xdst)

    def norm_gate_store(es_tiles, o_t, o_ap, w_bc, zi, mi, tt, rows, col_lo=None):
        sq, s4, zt, gb, sg = es_tiles
        tok0 = tt * 128 + (col_lo or 0)
        S.op("act", "activation", R=[o_t], W=[sq], out=sq[0:rows, :], in_=o_ap, func=AF.Square)
        S.op("dve", "tensor_reduce", R=[sq], W=[s4], out=s4[0:rows, 0:4],
             in_=sq[0:rows, :].rearrange("p (g d) -> p g d", d=128), axis=AX.X, op=ALU.add)
        rsqrt_inplace(s4, s4[0:rows, 0:4], 1.0 / 128, EPS)
        S.dma("sp", zt[0:rows, :], zg[zi, tok0:tok0 + rows, :], R=[RX["zg%d" % zi][tt]], W=[zt])
        S.op("dve", "tensor_tensor", R=[o_t, s4], W=[o_t], out=o_ap.rearrange("p (g d) -> p g d", d=128),
             in0=o_ap.rearrange("p (g d) -> p g d", d=128),
             in1=s4[0:rows, 0:4].unsqueeze(2).to_broadcast([rows, 4, 128]), op=ALU.mult)
        S.op("pool", "tensor_tensor", R=[o_t, w_bc], W=[o_t], out=o_ap.rearrange("p (g d) -> p g d", d=128),
             in0=o_ap.rearrange("p (g d) -> p g d", d=128),
             in1=w_bc[0:rows, :].unsqueeze(1).to_broadcast([rows, 4, 128]), op=ALU.mult)
        S.op("dve", "tensor_tensor", R=[o_t, zt], W=[gb], out=gb[0:rows, :], in0=o_ap, in1=zt[0:rows, :], op=ALU.mult)
        pb = psb()
        for g in range(4):
            S.op("pe", "transpose", R=[gb, ident_b], W=[pb], out=pb[:, g * 128:g * 128 + rows],
                 in_=gb[0:rows, g * 128:(g + 1) * 128], identity=ident_b[0:rows, 0:rows])
        pv = pb[:, 0:512].rearrange("p (g t) -> p g t", g=4)[:, :, 0:rows]
        S.op("act", "copy", R=[pb], W=[sg], out=sg[:, :, 0:rows], in_=pv)
        S.dma("pool", gT[mi, :, tok0:tok0 + rows].rearrange("(g p) t -> p g t", p=128), sg[:, :, 0:rows], R=[sg],
              W=[RX["gT%d" % mi][tt]])

    def ng_tiles(es, pfx):
        return (sb(es, pfx + "sq", [128, 512], F32), sb(es, pfx + "s4", [128, 4], F32),
                sb(es, pfx + "zt", [128, 512], BF16), sb(es, pfx + "gb", [128, 512], BF16),
                sb(es, pfx + "sg", [128, 4, 128], BF16))

    def attention_gen(l, si, t0, T, is_s, es, acc_sets):
        Sk = T + (PAST if is_s else 0)
        nst = Sk // 128
        ntl = T // 128
        QB = min(512, T)
        kT = sb(es, "at_kT", [128, 4, Sk], BF16)
        V1 = sb(es, "at_V1", [128, nst, 4, 130], BF16)
        qTp = Pool_(es, "at_qT", [128, 4, QB], BF16, 2)
        ex = Pool_(es, "at_ex", [128, QB], BF16, 3)
        osb = [sb(es, "at_os%d" % qs, [128, 512], F32) for qs in range(QB // 128)]
        rc = Pool_(es, "at_rc", [128, 2], F32, 4)
        ngt = ng_tiles(es, "at_")
        grp = [0]
        scn = [0]
        S.dma("sp", kT[:, :, 0:T], akT[:, :, t0:t0 + T].rearrange("h p t -> p h t"),
              R=[RX["akT"][i] for i in range(t0 // 128, (t0 + T) // 128)], W=[kT])
        S.op("pool", "memset", W=[V1], ap=V1[:, :, :, 128:130], constant=1.0)
        for i in range(ntl):
            S.dma("sp", V1[:, i, :, 0:128], av[t0 + i * 128:t0 + (i + 1) * 128, :].rearrange("p (h e) -> p h e", h=4),
                  R=[RX["av"][t0 // 128 + i]], W=[V1])
        if is_s:
            with ExitStack() as es2:
                ck = sb(es2, "at_ck", [128, 2, 512], F32)
                cv = sb(es2, "at_cv", [128, 2, 512], F32)
                ckb = sb(es2, "at_ckb", [128, 2, 512], BF16)
                S.dma("sp", ck[:], cache_k[l].rearrange("(n p) f -> p n f", p=128), W=[ck])
                S.dma("sp", cv[:], cache_v[l].rearrange("(n p) f -> p n f", p=128), W=[cv])
                S.op("dve", "tensor_copy", R=[ck], W=[ckb], out=ckb[:], in_=ck[:])
                for n in range(2):
                    S.op("pool", "tensor_copy", R=[cv], W=[V1], out=V1[:, ntl + n, :, 0:128],
                         in_=cv[:, n, :].rearrange("p (h e) -> p h e", h=4))
                    pb = psb()
                    transposes_to(pb, ckb, [(ckb[:, n, h * 128:(h + 1) * 128], 128, h * 128) for h in range(4)])
                    S.op("act", "copy", R=[pb], W=[kT], out=kT[:, :, T + n * 128:T + (n + 1) * 128],
                         in_=pb[:, 0:512].rearrange("p (h t) -> p h t", h=4))
                S.barrier()
        for qb0 in range(0, T, QB):
            qT = qTp.next()
            S.dma("sp", qT[:], aqT[:, :, t0 + qb0:t0 + qb0 + QB].rearrange("h p t -> p h t"),
                  R=[RX["aqT"][i] for i in range((t0 + qb0) // 128, (t0 + qb0 + QB) // 128)], W=[qT])
            nqs = QB // 128
            for h in range(4):
                for m in range(2):
                    grp[0] += 1
                    acc = acc_sets[grp[0] % len(acc_sets)]

                    def pv(st, e, acc=acc, h=h):
                        for qs in range(nqs):
                            a = acc[qs // 2]
                            S.op("pe", "matmul", R=[e, V1], W=[a], out=a[:, (qs % 2) * 256:(qs % 2) * 256 + 129],
                                 lhsT=e[:, qs * 128:(qs + 1) * 128], rhs=V1[:, st, h, 0:129],
                                 start=(st == 0 and qs % 2 == 0), stop=(st == nst - 1), skip_group_check=True)
                    pend = None
                    for st in range(nst):
                        scn[0] += 1
                        ps = PSF[scn[0] % 2]
                        S.op("pe", "matmul", R=[kT, qT], W=[ps], out=ps[:, 0:QB],
                             lhsT=kT[m * 64:(m + 1) * 64, h, st * 128:(st + 1) * 128],
                             rhs=qT[m * 64:(m + 1) * 64, h, :], start=True, stop=True)
                        e = ex.next()
                        S.op("act", "activation", R=[ps], W=[e], out=e[:, 0:QB], in_=ps[:, 0:QB], func=AF.Exp)
                        if pend is not None:
                            pv(*pend)
                        pend = (st, e)
                        yield
                    pv(*pend)
                    for qs in range(nqs):
                        a = acc[qs // 2]
                        c0 = (qs % 2) * 256
                        r = rc.next()
                        S.op("dve", "reciprocal", R=[a], W=[r], out=r[:, 0:1], in_=a[:, c0 + 128:c0 + 129])
                        if m == 0:
                            S.op("dve", "tensor_scalar", R=[a, r], W=[osb[qs]], out=osb[qs][:, h * 128:(h + 1) * 128],
                                 in0=a[:, c0:c0 + 128], scalar1=r[:, 0:1], scalar2=None, op0=ALU.mult)
                        else:
                            S.op("dve", "tensor_tensor", R=[r, lamt], W=[r], out=r[:, 1:2], in0=r[:, 0:1],
                                 in1=lamt[:, 0:1], op=ALU.mult)
                            S.op("dve", "scalar_tensor_tensor", R=[a, r, osb[qs]], W=[osb[qs]],
                                 out=osb[qs][:, h * 128:(h + 1) * 128], in0=a[:, c0:c0 + 128], scalar=r[:, 1:2],
                                 in1=osb[qs][:, h * 128:(h + 1) * 128], op0=ALU.mult, op1=ALU.add)
            for qs in range(nqs):
                tt = (t0 + qb0) // 128 + qs
                norm_gate_store(ngt, osb[qs], osb[qs][:], subw_bc, 0, 0, tt, 128)

    def attention(l, si, t0, T, is_s):
        with ExitStack() as es:
            for _ in attention_gen(l, si, t0, T, is_s, es, [[PSF[2], PSF[3]], [PSF[4], PSF[5]]]):
                pass
            S.barrier()

    def retention(l, si, t0, T, is_s):
        with ExitStack() as es:
            run_rr([ret_chain(l, si, t0, T, is_s, d, es) for d in range(2)])
            S.barrier()
        combine(t0, T, retw_bc, 1, 1)

    def rr_gen(gens):
        gens = list(gens)
        while gens:
            for g in list(gens):
                try:
                    next(g)
                    yield
                except StopIteration:
                    gens.remove(g)

    def side_gen(l, si, t0, T, is_s):
        with ExitStack() as es2:
            yield from gdn_pre_gen(l, si, t0, T, is_s, es2, 256, 1)
            S.barrier()
        with ExitStack() as es3:
            yield from rr_gen([ret_chain(l, si, t0, T, is_s, d, es3) for d in range(2)])
            S.barrier()
        yield from combine_gen(t0, T, retw_bc, 1, 1)

    def sample_mixers_overlapped(l, si, t0, T, is_s):
        with ExitStack() as es:
            att = attention_gen(l, si, t0, T, is_s, es, [[PSF[2], PSF[3]]])
            next(att)
            old = psf_banks[0]
            psf_banks[0] = [4, 5]
            run_rr([att, side_gen(l, si, t0, T, is_s)])
            psf_banks[0] = old
            S.barrier()

    def ret_chain(l, si, t0, T, is_s, d, es):
        ntl = T // 128
        pf = "rt%d_" % d
        Sf = sb(es, pf + "S", [64, 4, 128], F32)
        Sb_ = sb(es, pf + "Sb", [64, 4, 128], BF16)
        qTp = Pool_(es, pf + "qT", [64, 4, 128], BF16, 2)
        kTp = Pool_(es, pf + "kT", [64, 4, 128], BF16, 2)
        ktp = Pool_(es, pf + "k", [128, 256], BF16, 2)
        vp = Pool_(es, pf + "v", [128, 512], BF16, 2)
        itp = Pool_(es, pf + "it", [128, 512], BF16, 2)
        qdp = Pool_(es, pf + "qd", [64, 4, 128], BF16, 2)
        kdp = Pool_(es, pf + "kd", [128, 256], BF16, 2)
        op_ = Pool_(es, pf + "o", [128, 512], F32, 2)
        odst, orn = (ofs, "ofs") if d == 0 else (ofb, "ofb")
        if is_s:
            S.dma("sp", Sf[:], st_ret[l, d].rearrange("h k e -> k h e"), W=[Sf])
        else:
            S.op("dve", "memset", W=[Sf], ap=Sf[:], constant=0.0)
        S.op("act", "copy", R=[Sf], W=[Sb_], out=Sb_[:], in_=Sf[:])
        order = range(ntl) if d == 0 else range(ntl - 1, -1, -1)
        for i in order:
            tt = t0 // 128 + i
            c0 = tt * 128
            qT = qTp.next(); kT = kTp.next(); kt = ktp.next(); v = vp.next()
            S.dma("sp", qT[:], bqT[:, :, c0:c0 + 128].rearrange("h p t -> p h t"), R=[RX["bqT"][tt]], W=[qT])
            S.dma("sp", kT[:], bkT[:, :, c0:c0 + 128].rearrange("h p t -> p h t"), R=[RX["bkT"][tt]], W=[kT])
            S.dma("sp", kt[:], bk[c0:c0 + 128, :], R=[RX["bk"][tt]], W=[kt])
            S.dma("sp", v[:], bv[c0:c0 + 128, :], R=[RX["bv"][tt]], W=[v])
            ps = psf()
            for h in range(4):
                S.op("pe", "matmul", R=[kT, qT], W=[ps], out=ps[:, h * 128:(h + 1) * 128], lhsT=kT[:, h, :],
                     rhs=qT[:, h, :], start=True, stop=True)
            it = itp.next()
            S.op("dve", "tensor_tensor", R=[ps, rdmat], W=[it], out=it[:], in0=ps[:],
                 in1=rdmat[:, d, :, :].rearrange("p h i -> p (h i)"), op=ALU.mult)
            qd = qdp.next()
            S.op("pool", "tensor_tensor", R=[qT, rqdec], W=[qd], out=qd[:], in0=qT[:], in1=rqdec[:, d, :, :],
                 op=ALU.mult)
            kd = kdp.next()
            S.op("pool", "tensor_tensor", R=[kt, rkdec], W=[kd], out=kd[:].rearrange("p (h e) -> p h e", h=4),
                 in0=kt[:].rearrange("p (h e) -> p h e", h=4),
                 in1=rkdec[:, d, :].unsqueeze(2).to_broadcast([128, 4, 64]), op=ALU.mult)
            yield
            po = psf()
            for h in range(4):
                S.op("pe", "matmul", R=[it, v], W=[po], out=po[:, h * 128:(h + 1) * 128],
                     lhsT=it[:, h * 128:(h + 1) * 128], rhs=v[:, h * 128:(h + 1) * 128], start=True, stop=False)
                S.op("pe", "matmul", R=[qd, Sb_], W=[po], out=po[:, h * 128:(h + 1) * 128], lhsT=qd[:, h, :],
                     rhs=Sb_[:, h, :], start=False, stop=True)
            pS = psf()
            for h in range(4):
                S.op("pe", "matmul", R=[kd, v], W=[pS], out=pS[0:64, h * 128:(h + 1) * 128],
                     lhsT=kd[:, h * 64:(h + 1) * 64], rhs=v[:, h * 128:(h + 1) * 128], start=True, stop=True)
            S.op("dve", "tensor_tensor", R=[Sf, rcdec], W=[Sf], out=Sf[:], in0=Sf[:],
                 in1=rcdec[0:64, d * 4:d * 4 + 4].unsqueeze(2).to_broadcast([64, 4, 128]), op=ALU.mult)
            S.op("dve", "tensor_tensor", R=[Sf, pS], W=[Sf], out=Sf[:].rearrange("p h e -> p (h e)"),
                 in0=Sf[:].rearrange("p h e -> p (h e)"), in1=pS[0:64, :], op=ALU.add)
            S.op("act", "copy", R=[Sf], W=[Sb_], out=Sb_[:], in_=Sf[:])
            o = op_.next()
            S.op("act", "copy", R=[po], W=[o], out=o[:], in_=po[:])
            S.dma("pool", odst[c0:c0 + 128, :], o[:], R=[o], W=[RX[orn][tt]])
            yield
        if not is_s:
            S.dma("pool", nret_out[si - 1, l, d].rearrange("h k e -> k h e"), Sf[:], R=[Sf])

    def gdn_pre_gen(l, si, t0, T, is_s, es, G, nbuf):
        xin = Pool_(es, "gp_x", [128, 12, G + 4], F32, nbuf)
        acc = Pool_(es, "gp_a", [128, 12, G], F32, nbuf)
        sqp = Pool_(es, "gp_sq", [128, G], F32, nbuf)
        rsp = Pool_(es, "gp_rs", [128, G], F32, nbuf)
        nb = Pool_(es, "gp_nb", [128, 12, G], BF16, nbuf)
        sg = Pool_(es, "gp_sg", [128, 1024], BF16, nbuf)
        for g0 in range(0, T, G):
            x = xin.next()
            a = acc.next()
            lo = 2 if g0 == 0 else 0
            hi = 2 if g0 + G == T else 0
            if lo:
                S.op("pool", "memset", W=[x], ap=x[:, :, 0:2], constant=0.0)
            if hi:
                S.op("pool", "memset", W=[x], ap=x[:, :, G + 2:G + 4], constant=0.0)
            tl = [i for i in range((t0 + g0) // 128 - (0 if lo else 1), (t0 + g0 + G) // 128 + (0 if hi else 1))]
            S.dma("sp", x[:, :, lo:G + 4 - hi],
                  cT[:, t0 + g0 - 2 + lo:t0 + g0 + G + 2 - hi].rearrange("(c p) t -> p c t", p=128),
                  R=[RX["cT"][i] for i in tl], W=[x])
            for c in range(12):
                eng = "dve"
                S.op(eng, "tensor_scalar", R=[x, convw], W=[a], out=a[:, c, :], in0=x[:, c, 0:G],
                     scalar1=convw[:, c, 0:1], scalar2=None, op0=ALU.mult)
                for k in range(1, 5):
                    S.op(eng, "scalar_tensor_tensor", R=[x, convw, a], W=[a], out=a[:, c, :], in0=x[:, c, k:k + G],
                         scalar=convw[:, c, k:k + 1], in1=a[:, c, :], op0=ALU.mult, op1=ALU.add)
                S.op("act", "activation", R=[a], W=[a], out=a[:, c, :], in_=a[:, c, :], func=AF.Silu)
            yield
            n = nb.next()
            for c in range(8):
                sq = sqp.next()
                S.op("act", "activation", R=[a], W=[sq], out=sq[:], in_=a[:, c, :], func=AF.Square)
                ps = psf()
                S.op("pe", "matmul", R=[ones_f, sq], W=[ps], out=ps[:, 0:G], lhsT=ones_f[:], rhs=sq[:], start=True,
                     stop=True)
                rs = rsp.next()
                S.op("dve", "tensor_scalar", R=[ps], W=[rs], out=rs[:], in0=ps[:, 0:G], scalar1=EPS, scalar2=None,
                     op0=ALU.add)
                S.op("act", "activation", R=[rs], W=[rs], out=rs[:], in_=rs[:], func=AF.Sqrt)
                S.op("dve", "reciprocal", R=[rs], W=[rs], out=rs[:], in_=rs[:])
                if c < 4:
                    S.op("dve", "scalar_tensor_tensor", R=[a, rs], W=[n], out=n[:, c, :], in0=a[:, c, :],
                         scalar=float(128 ** -0.5), in1=rs[:], op0=ALU.mult, op1=ALU.mult)
                else:
                    S.op("dve", "tensor_tensor", R=[a, rs], W=[n], out=n[:, c, :], in0=a[:, c, :], in1=rs[:],
                         op=ALU.mult)
            S.op("pool", "tensor_copy", R=[a], W=[n], out=n[:, 8:12, :], in_=a[:, 8:12, :])
            tiles = list(range((t0 + g0) // 128, (t0 + g0 + G) // 128))
            S.dma("pool", gqT[:, :, t0 + g0:t0 + g0 + G].rearrange("h p t -> p h t"), n[:, 0:4, :], R=[n],
                  W=[RX["gqT"][i] for i in tiles])
            S.dma("pool", gkT[:, :, t0 + g0:t0 + g0 + G].rearrange("h p t -> p h t"), n[:, 4:8, :], R=[n],
                  W=[RX["gkT"][i] for i in tiles])
            for ti in range(G // 128):
                tt = (t0 + g0) // 128 + ti
                pb = psb()
                transposes_to(pb, n, [(n[:, 4 + c, ti * 128:(ti + 1) * 128], 128, c * 128) for c in range(8)])
                s = sg.next()
                S.op("act", "copy", R=[pb], W=[s], out=s[:], in_=pb[:])
                S.dma("pool", gk[tt * 128:(tt + 1) * 128, :], s[:, 0:512], R=[s], W=[RX["gk"][tt]])
                S.dma("pool", gv[tt * 128:(tt + 1) * 128, :], s[:, 512:1024], R=[s], W=[RX["gv"][tt]])
            yield

    def gdn_pre(l, si, t0, T, is_s):
        with ExitStack() as es:
            for _ in gdn_pre_gen(l, si, t0, T, is_s, es, min(512, T), 2):
                pass
            S.barrier()

    def pipeline(gens, depth):
        gens = list(gens)
        active = []
        while gens or active:
            while gens and len(active) < depth:
                active.append(gens.pop(0))
            for g in list(active):
                try:
                    next(g)
                except StopIteration:
                    active.remove(g)

    def run_rr(gens):
        gens = list(gens)
        while gens:
            for g in list(gens):
                try:
                    next(g)
                except StopIteration:
                    gens.remove(g)

    def combine(t0, T, w_bc, zi, mi):
        for _ in combine_gen(t0, T, w_bc, zi, mi):
            pass

    def combine_gen(t0, T, w_bc, zi, mi):
        with ExitStack() as es:
            fa = Pool_(es, "cb_a", [128, 512], F32, 2)
            fb = Pool_(es, "cb_b", [128, 512], F32, 2)
            ngts = [ng_tiles(es, "cb%d_" % i) for i in range(2)]
            for i in range(T // 128):
                tt = t0 // 128 + i
                c0 = tt * 128
                a = fa.next(); b = fb.next()
                S.dma("sp", a[:], ofs[c0:c0 + 128, :], R=[RX["ofs"][tt]], W=[a])
                S.dma("sp", b[:], ofb[c0:c0 + 128, :], R=[RX["ofb"][tt]], W=[b])
                S.op("pool", "tensor_tensor", R=[a, b], W=[a], out=a[:], in0=a[:], in1=b[:], op=ALU.add)
                norm_gate_store(ngts[i % 2], a, a[:], w_bc, zi, mi, tt, 128)
                yield
            S.barrier()

    def gdn_scan(l, si, t0, T, is_s):
        with ExitStack() as es:
            run_rr([gdn_chain(l, si, t0, T, is_s, d, es) for d in range(2)])
            S.barrier()
        combine(t0, T, gdnw_bc, 2, 2)

    def gdn_chain(l, si, t0, T, is_s, d, es):
        ntl = T // 128
        pf = "gd%d_" % d
        Sf = sb(es, pf + "S", [128, 4, 128], F32)
        Sb_ = sb(es, pf + "Sb", [128, 4, 128], BF16)
        kTp = Pool_(es, pf + "kT", [128, 4, 128], BF16, 2)
        qTp = Pool_(es, pf + "qT", [128, 4, 128], BF16, 2)
        ktp = Pool_(es, pf + "k", [128, 512], BF16, 2)
        vtp = Pool_(es, pf + "v", [128, 512], BF16, 2)
        bgp = Pool_(es, pf + "bg", [128, 16], F32, 2)
        sm = Pool_(es, pf + "sm", [128, 8, 4], F32, 2)
        X1 = Pool_(es, pf + "X", [128, 4, 128], F32, 1)
        X2 = Pool_(es, pf + "X2", [128, 4, 128], F32, 1)
        Dn = Pool_(es, pf + "Dn", [128, 4, 128], F32, 1)
        Ea = Pool_(es, pf + "Ea", [128, 4, 128], F32, 1)
        Eb = Pool_(es, pf + "Eb", [128, 4, 128], F32, 1)
        Fq = Pool_(es, pf + "Fq", [128, 4, 128], F32, 1)
        Eg = Pool_(es, pf + "Eg", [128, 4, 128], F32, 1)
        Pp = Pool_(es, pf + "P", [128, 4, 128], F32, 2)
        PTp = Pool_(es, pf + "PT", [128, 4, 128], F32, 2)
        TTf = Pool_(es, pf + "TTf", [128, 4, 128], F32, 1)
        qkTp = Pool_(es, pf + "qkT", [128, 4, 128], BF16, 2)
        qgp = Pool_(es, pf + "qg", [128, 4, 128], BF16, 2)
        vbp = Pool_(es, pf + "vb", [128, 4, 128], F32, 1)
        kbp = Pool_(es, pf + "kb", [128, 4, 128], F32, 1)
        kdp = Pool_(es, pf + "kd", [128, 4, 128], BF16, 2)
        Up = Pool_(es, pf + "U", [128, 4, 128], F32, 2)
        WTp = Pool_(es, pf + "WT", [128, 4, 128], BF16, 2)
        vnp = Pool_(es, pf + "vn", [128, 128], BF16, 4)
        op_ = Pool_(es, pf + "o", [64, 512], F32, 2)
        odst, orn = (ofs, "ofs") if d == 0 else (ofb, "ofb")
        Sfr = [Res("gSf%d" % h) for h in range(4)]
        Sbr = [Res("gSb%d" % h) for h in range(4)]
        if is_s:
            S.dma("sp", Sf[:], st_gdn[l, d].rearrange("h k e -> k h e"), W=Sfr)
        else:
            S.op("dve", "memset", W=Sfr, ap=Sf[:], constant=0.0)
        S.op("act", "copy", R=Sfr, W=Sbr, out=Sb_[:], in_=Sf[:])
        idb = ident_f[:].unsqueeze(1).to_broadcast([128, 4, 128])
        v4 = lambda t: t[:].rearrange("p h b -> p (h b)")
        order = range(ntl) if d == 0 else range(ntl - 1, -1, -1)
        for i in order:
            tt = t0 // 128 + i
            c0 = tt * 128
            kT = kTp.next(); qT = qTp.next(); kt = ktp.next(); vt = vtp.next(); bg = bgp.next()
            S.dma("sp", kT[:], gkT[:, :, c0:c0 + 128].rearrange("h p t -> p h t"), R=[RX["gkT"][tt]], W=[kT])
            S.dma("sp", qT[:], gqT[:, :, c0:c0 + 128].rearrange("h p t -> p h t"), R=[RX["gqT"][tt]], W=[qT])
            S.dma("sp", kt[:], gk[c0:c0 + 128, :], R=[RX["gk"][tt]], W=[kt])
            S.dma("sp", vt[:], gv[c0:c0 + 128, :], R=[RX["gv"][tt]], W=[vt])
            S.dma("sp", bg[:], bgs[c0:c0 + 128, :], R=[RX["bgs"][tt]], W=[bg])
            s = sm.next()
            S.op("dve", "tensor_copy", R=[bg], W=[s], out=s[:, 0, :], in_=bg[:, 8 + d * 4:12 + d * 4])
            S.op("dve", "tensor_copy", R=[bg], W=[s], out=s[:, 1, :], in_=bg[:, d * 4:d * 4 + 4])
            pg = psf()
            S.op("pe", "matmul", R=[utri2, s], W=[pg], out=pg[:, 0:4], lhsT=utri2[:, d, :], rhs=s[:, 0, :],
                 start=True, stop=True)
            S.op("pe", "matmul", R=[bd1, s], W=[pg], out=pg[:, 4:8], lhsT=bd1[:], rhs=s[:, 0, :], start=True, stop=True)
            S.op("pe", "matmul", R=[selc, s], W=[pg], out=pg[:, 8:12], lhsT=selc[:, 0, :], rhs=s[:, 0, :],
                 start=True, stop=True)
            S.op("pe", "matmul", R=[selc, s], W=[pg], out=pg[:, 12:16], lhsT=selc[:, 1, :], rhs=s[:, 0, :],
                 start=True, stop=True)
            S.op("dve", "tensor_copy", R=[pg], W=[s], out=s[:, 2, :], in_=pg[:, 0:4])
            S.op("act", "activation", R=[pg], W=[s], out=s[:, 6:8, :].rearrange("p c h -> p (c h)"), in_=pg[:, 8:16],
                 func=AF.Exp)
            S.op("dve", "tensor_tensor", R=[pg], W=[s], out=s[:, 4, :], in0=pg[:, 4:8], in1=s[:, 2, :], op=ALU.subtract)
            S.op("act", "activation", R=[s], W=[s], out=s[:, 4, :], in_=s[:, 4, :], func=AF.Exp)
            S.op("act", "activation", R=[s], W=[s], out=s[:, 5, :], in_=s[:, 2, :], func=AF.Exp)
            S.op("dve", "tensor_tensor", R=[s], W=[s], out=s[:, 5, :], in0=s[:, 5, :], in1=s[:, 1, :], op=ALU.mult)
            ckpt('g0')
            yield
            x1 = X1.next(); x2 = X2.next()
            S.op("dve", "tensor_tensor", R=[ident_f, s], W=[x1], out=x1[:], in0=idb,
                 in1=s[:, 2, :].unsqueeze(2).to_broadcast([128, 4, 128]), op=ALU.mult)
            S.op("pool", "tensor_tensor", R=[ident_f, s], W=[x2], out=x2[:], in0=idb,
                 in1=s[:, 1, :].unsqueeze(2).to_broadcast([128, 4, 128]), op=ALU.mult)
            pR = psf(); pRb = psf()
            S.op("pe", "matmul", R=[ones_f, x1], W=[pR], out=pR[:], lhsT=ones_f[:], rhs=v4(x1), start=True, stop=True)
            S.op("pe", "matmul", R=[ones_f, x2], W=[pRb], out=pRb[:], lhsT=ones_f[:], rhs=v4(x2), start=True, stop=True)
            dn = Dn.next()
            S.op("dve", "tensor_tensor", R=[pR, s], W=[dn], out=dn[:], in0=pR[:].rearrange("p (h b) -> p h b", h=4),
                 in1=s[:, 2, :].unsqueeze(2).to_broadcast([128, 4, 128]), op=ALU.subtract)
            eg = Eg.next()
            S.op("act", "activation", R=[pR], W=[eg], out=v4(eg), in_=pR[:], func=AF.Exp)
            ea = Ea.next(); eb = Eb.next(); fq = Fq.next()
            S.op("dve", "tensor_scalar", R=[dn], W=[ea], out=ea[:], in0=dn[:], scalar1=-1.0, scalar2=0.0,
                 op0=ALU.mult, op1=ALU.min)
            S.op("dve", "tensor_scalar", R=[dn], W=[eb], out=eb[:], in0=dn[:], scalar1=0.0, scalar2=None, op0=ALU.min)
            S.op("act", "activation", R=[ea], W=[ea], out=ea[:], in_=ea[:], func=AF.Exp)
            S.op("act", "activation", R=[eb], W=[eb], out=eb[:], in_=eb[:], func=AF.Exp)
            S.op("dve", "tensor_tensor", R=[ea, gmask2], W=[ea], out=ea[:], in0=ea[:],
                 in1=gmask2[:, d, 0, :].unsqueeze(1).to_broadcast([128, 4, 128]), op=ALU.mult)
            S.op("dve", "tensor_tensor", R=[ea, s], W=[ea], out=ea[:], in0=ea[:],
                 in1=s[:, 1, :].unsqueeze(2).to_broadcast([128, 4, 128]), op=ALU.mult)
            S.op("pool", "tensor_tensor", R=[eb, gmask2], W=[fq], out=fq[:], in0=eb[:],
                 in1=gmask2[:, d, 2, :].unsqueeze(1).to_broadcast([128, 4, 128]), op=ALU.mult)
            S.op("pool", "tensor_tensor", R=[eb, gmask2], W=[eb], out=eb[:], in0=eb[:],
                 in1=gmask2[:, d, 1, :].unsqueeze(1).to_broadcast([128, 4, 128]), op=ALU.mult)
            S.op("dve", "tensor_tensor", R=[eb, pRb], W=[eb], out=eb[:], in0=eb[:],
                 in1=pRb[:].rearrange("p (h b) -> p h b", h=4), op=ALU.mult)
            ckpt('g1')
            yield
            qg = qgp.next()
            S.op("pool", "tensor_tensor", R=[qT, eg], W=[qg], out=qg[:], in0=qT[:], in1=eg[:], op=ALU.mult)
            pK = psf(); pQ = psf()
            for h in range(4):
                S.op("pe", "matmul", R=[kT], W=[pK], out=pK[:, h * 128:(h + 1) * 128], lhsT=kT[:, h, :], rhs=kT[:, h, :],
                     start=True, stop=True)
                S.op("pe", "matmul", R=[kT, qT], W=[pQ], out=pQ[:, h * 128:(h + 1) * 128], lhsT=kT[:, h, :],
                     rhs=qT[:, h, :], start=True, stop=True)
            P = Pp.next(); PT = PTp.next(); ttf = TTf.next(); qkT = qkTp.next()
            pKv = pK[:].rearrange("p (h b) -> p h b", h=4)
            S.op("dve", "tensor_tensor", R=[pK, ea], W=[P], out=P[:], in0=pKv, in1=ea[:], op=ALU.mult)
            S.op("dve", "tensor_tensor", R=[pK, eb], W=[PT], out=PT[:], in0=pKv, in1=eb[:], op=ALU.mult)
            S.op("pool", "tensor_tensor", R=[PT, ident_f], W=[ttf], out=ttf[:], in0=PT[:], in1=idb, op=ALU.add)
            S.op("dve", "tensor_tensor", R=[pQ, fq], W=[qkT], out=qkT[:], in0=pQ[:].rearrange("p (h b) -> p h b", h=4),
                 in1=fq[:], op=ALU.mult)
            ckpt('g2')
            yield
            for lev in range(1, 6):
                p1 = psf()
                for h in range(4):
                    S.op("pe", "matmul", R=[PT, P], W=[p1], out=p1[:, h * 128:(h + 1) * 128], lhsT=PT[:, h, :],
                         rhs=P[:, h, :], start=True, stop=True)
                Pn = Pp.next()
                S.op("dve", "tensor_copy", R=[p1], W=[Pn], out=v4(Pn), in_=p1[:])
                p3 = psf()
                for h in range(4):
                    S.op("pe", "matmul", R=[Pn, ttf], W=[p3], out=p3[:, h * 128:(h + 1) * 128], lhsT=Pn[:, h, :],
                         rhs=ttf[:, h, :], start=True, stop=True)
                if lev < 5:
                    p2 = psf()
                    for h in range(4):
                        if os.environ.get("KNOTR"):
                            S.op("pe", "matmul", R=[PT, P], W=[p2], out=p2[:, h * 128:(h + 1) * 128], lhsT=P[:, h, :],
                                 rhs=PT[:, h, :], start=True, stop=True)
                        else:
                            S.op("pe", "transpose", R=[Pn, ident_f], W=[p2], out=p2[:, h * 128:(h + 1) * 128],
                                 in_=Pn[:, h, :], identity=ident_f[:])
                    PTn = PTp.next()
                    S.op("act", "copy", R=[p2], W=[PTn], out=v4(PTn), in_=p2[:])
                S.op("dve", "tensor_tensor", R=[ttf, p3], W=[ttf], out=v4(ttf), in0=v4(ttf), in1=p3[:], op=ALU.add)
                P = Pn
                if lev < 5:
                    PT = PTn
                ckpt('g3')
                yield
            vb = vbp.next(); kb = kbp.next(); kd = kdp.next()
            S.op("dve", "tensor_tensor", R=[vt, s], W=[vb], out=vb[:], in0=vt[:].rearrange("p (h e) -> p h e", h=4),
                 in1=s[:, 1, :].unsqueeze(2).to_broadcast([128, 4, 128]), op=ALU.mult)
            S.op("pool", "tensor_tensor", R=[kt, s], W=[kb], out=kb[:], in0=kt[:].rearrange("p (h e) -> p h e", h=4),
                 in1=s[:, 5, :].unsqueeze(2).to_broadcast([128, 4, 128]), op=ALU.mult)
            S.op("pool", "tensor_tensor", R=[kt, s], W=[kd], out=kd[:], in0=kt[:].rearrange("p (h e) -> p h e", h=4),
                 in1=s[:, 4, :].unsqueeze(2).to_broadcast([128, 4, 128]), op=ALU.mult)
            U = Up.next(); WT = WTp.next()
            pU = psf(); pW = psf()
            for h in range(4):
                S.op("pe", "matmul", R=[ttf, vb], W=[pU], out=pU[:, h * 128:(h + 1) * 128], lhsT=ttf[:, h, :],
                     rhs=vb[:, h, :], start=True, stop=True)
                S.op("pe", "matmul", R=[kb, ttf], W=[pW], out=pW[:, h * 128:(h + 1) * 128], lhsT=kb[:, h, :],
                     rhs=ttf[:, h, :], start=True, stop=True)
            S.op("dve", "tensor_copy", R=[pU], W=[U], out=v4(U), in_=pU[:])
            S.op("act", "copy", R=[pW], W=[WT], out=v4(WT), in_=pW[:])
            ckpt('g4')
            yield
            for cb in ((0, 1) if d == 0 else (1, 0)):
                rows = slice(cb * 64, cb * 64 + 64)
                cols = slice(cb * 64, cb * 64 + 64)
                po = PSF[4 + d]
                for h in range(4):
                    pa = psf()
                    S.op("pe", "matmul", R=[WT, Sbr[h]], W=[pa], out=pa[:, 0:128], lhsT=WT[:, h, :], rhs=Sb_[:, h, :],
                         start=True, stop=True)
                    vn = vnp.next()
                    S.op("dve", "tensor_tensor", R=[U, pa], W=[vn], out=vn[rows, :], in0=U[rows, h, :],
                         in1=pa[rows, 0:128], op=ALU.subtract)
                    S.op("pe", "matmul", R=[qg, Sbr[h]], W=[po], out=po[0:64, h * 128:(h + 1) * 128], lhsT=qg[:, h, cols],
                         rhs=Sb_[:, h, :], start=True, stop=False)
                    S.op("pe", "matmul", R=[qkT, vn], W=[po], out=po[0:64, h * 128:(h + 1) * 128],
                         lhsT=qkT[rows, h, cols], rhs=vn[rows, :], start=False, stop=True)
                    pS = psf()
                    S.op("pe", "matmul", R=[kd, vn], W=[pS], out=pS[:, 0:128], lhsT=kd[rows, h, :], rhs=vn[rows, :],
                         start=True, stop=True)
                    S.op("dve", "scalar_tensor_tensor", R=[Sfr[h], s, pS], W=[Sfr[h]], out=Sf[:, h, :], in0=Sf[:, h, :],
                         scalar=s[:, 6 + cb, h:h + 1], in1=pS[:, 0:128], op0=ALU.mult, op1=ALU.add)
                    S.op("act", "copy", R=[Sfr[h]], W=[Sbr[h]], out=Sb_[:, h, :], in_=Sf[:, h, :])
                    if h % 2 == 1:
                        ckpt('g5')
                        yield
                r0 = c0 + cb * 64
                o = op_.next()
                S.op("act", "copy", R=[po], W=[o], out=o[:], in_=po[0:64, :])
                S.dma("pool", odst[r0:r0 + 64, :], o[:], R=[o], W=[RX[orn][tt]])
                ckpt('g6')
                yield
        if not is_s:
            S.dma("pool", ngdn_out[si - 1, l, d].rearrange("h k e -> k h e"), Sf[:], R=Sfr)

    def phaseE(l, xsrc, xdst):
        with ExitStack() as es:
            wbr = sb(es, "pe_wbr", [128, 12, D], BF16)
            wo = sb(es, "pe_wo", [128, 8, D], BF16)
            wst = Pool_(es, "pe_wst", [128, 4, D], F32, 2)
            gtp = Pool_(es, "pe_gt", [128, 12, 128], BF16, 2)
            mgp = Pool_(es, "pe_mg", [128, 3 * D], BF16, 2)
            mp = Pool_(es, "pe_m", [128, D], F32, 2)
            mbp = Pool_(es, "pe_mb", [128, D], BF16, 2)
            mTp = Pool_(es, "pe_mT", [128, 8, 128], BF16, 2)
            xp = Pool_(es, "pe_x", [128, D], F32, 2)
            tp = Pool_(es, "pe_t", [128, 512], F32, 2)
            for q in range(3):
                w = wst.next()
                S.dma("sp", w[:], w_branch[l, q].rearrange("(c p) n -> p c n", p=128), W=[w])
                S.op("dve" if q % 2 == 0 else "pool", "tensor_copy", R=[w], W=[wbr], out=wbr[:, q * 4:q * 4 + 4, :],
                     in_=w[:])
            for q in range(2):
                w = wst.next()
                S.dma("sp", w[:], w_out[l].rearrange("(c p) n -> p c n", p=128)[:, q * 4:q * 4 + 4, :], W=[w])
                S.op("dve" if q % 2 == 0 else "pool", "tensor_copy", R=[w], W=[wo], out=wo[:, q * 4:q * 4 + 4, :],
                     in_=w[:])
            for tt in range(NT):
                j = 0 if tt < NTS else 1
                c0 = tt * 128
                gt = gtp.next(); mgt = mgp.next(); m = mp.next(); xt = xp.next()
                S.dma("sp", gt[:], gT[:, :, c0:c0 + 128].rearrange("m (c p) t -> p (m c) t", p=128),
                      R=[RX["gT0"][tt], RX["gT1"][tt], RX["gT2"][tt]], W=[gt])
                S.dma("sp", mgt[:], mg[c0:c0 + 128, :], R=[RX["mg"][tt]], W=[mgt])
                S.dma("sp", xt[:], xsrc[c0:c0 + 128, :], R=[RX["x"][tt]] if l > 0 else [], W=[xt])
                for q in range(3):
                    for nb in range(2):
                        ps = psf()
                        for c in range(4):
                            S.op("pe", "matmul", R=[gt, wbr], W=[ps], out=ps[:], lhsT=gt[:, q * 4 + c, :],
                                 rhs=wbr[:, q * 4 + c, nb * 512:(nb + 1) * 512], start=(c == 0), stop=(c == 3))
                        cols = slice(nb * 512, (nb + 1) * 512)
                        if q == 0:
                            S.op("dve", "tensor_tensor", R=[ps, mgt], W=[m], out=m[:, cols], in0=ps[:],
                                 in1=mgt[:, cols], op=ALU.mult)
                        else:
                            t = tp.next()
                            S.op("dve", "tensor_tensor", R=[ps, mgt], W=[t], out=t[:], in0=ps[:],
                                 in1=mgt[:, q * D + nb * 512:q * D + (nb + 1) * 512], op=ALU.mult)
                            S.op("pool", "tensor_tensor", R=[t, m], W=[m], out=m[:, cols], in0=m[:, cols], in1=t[:],
                                 op=ALU.add)
                mb = mbp.next()
                S.op("act", "copy", R=[m], W=[mb], out=mb[:], in_=m[:])
                pb = psb()
                transposes_to(pb, mb, [(mb[:, c * 128:(c + 1) * 128], 128, c * 128) for c in range(8)])
                mT = mTp.next()
                S.op("act", "copy", R=[pb], W=[mT], out=mT[:].rearrange("p c t -> p (c t)"), in_=pb[:])
                for nb in range(2):
                    ps = psf()
                    for c in range(8):
                        S.op("pe", "matmul", R=[mT, wo], W=[ps], out=ps[:], lhsT=mT[:, c, :],
                             rhs=wo[:, c, nb * 512:(nb + 1) * 512], start=(c == 0), stop=(c == 7))
                    cols = slice(nb * 512, (nb + 1) * 512)
                    t = tp.next()
                    S.op("dve", "tensor_tensor", R=[ps, gate_bc], W=[t], out=t[:], in0=ps[:], in1=gate_bc[:, j, cols],
                         op=ALU.mult)
                    S.op("pool", "tensor_tensor", R=[t, xt], W=[xt], out=xt[:, cols], in0=xt[:, cols], in1=t[:],
                         op=ALU.add)
                S.dma("pool", xdst[c0:c0 + 128, :], xt[:], R=[xt], W=[RX["x"][tt]])
            S.barrier()

    try:
        for l in range(depth):
            layer(l)
    except StopBuild as e:
        print("build stopped at", e)
        root2 = None
    S.barrier()
    S.emit()
    build.stats = (S.n_instr, S.n_wait)
    return nc


_CACHE = {}


def _get_nc(T_S, depth, debug):
    key = (T_S, depth, debug)
    if key not in _CACHE:
        _CACHE[key] = build(T_S, depth, debug)
    return _CACHE[key]


def run_cfg(inp, T_S, depth, debug=False):
    f = lambda a: np.ascontiguousarray(np.asarray(a), dtype=np.float32)
    xs = f(inp["x_sample"])
    xp = f(inp["x_prompt"])
    n_core = 8
    nsb = xs.shape[0]
    consts = make_consts(T_S)
    shared = {
        "norm_w": f(inp["norm_w"])[:depth], "w_ada": f(inp["w_ada"])[:depth], "b_ada": f(inp["b_ada"])[:depth],
        "w_in": f(inp["w_in"])[:depth], "qk_norm_w": f(inp["qk_norm_w"])[:depth],
        "diff_lambda": f(inp["diff_lambda"])[:depth], "subln_w": f(inp["subln_w"])[:depth],
        "ret_decay": f(inp["ret_decay"])[:depth].reshape(depth, 8), "ret_norm_w": f(inp["ret_norm_w"])[:depth],
        "conv_w": f(inp["conv_w"])[:depth], "gdn_a_log": f(inp["gdn_a_log"])[:depth].reshape(depth, 8),
        "gdn_dt_bias": f(inp["gdn_dt_bias"])[:depth].reshape(depth, 8), "gdn_norm_w": f(inp["gdn_norm_w"])[:depth],
        "w_branch": f(inp["w_branch"])[:depth], "w_out": f(inp["w_out"])[:depth],
    }
    for k, v in consts.items():
        shared["c_" + k] = v
    ck = f(inp["cache_attn_k"])
    cv = f(inp["cache_attn_v"])
    sr = f(inp["state_ret"])
    sgd = f(inp["state_gdn"])
    c = f(inp["c"])
    cctx = f(inp["c_ctx"])
    in_maps = []
    for core in range(n_core):
        b = core % nsb
        m = dict(shared)
        m["x_in"] = np.ascontiguousarray(np.concatenate(
            [xs[b, :T_S]] + [xp[core * NPR + i] for i in range(NPR)], axis=0))
        m["cond"] = np.ascontiguousarray(np.stack([c[b], cctx], axis=0))
        m["cache_k"] = np.ascontiguousarray(ck[b, :depth].reshape(depth, PAST, 512))
        m["cache_v"] = np.ascontiguousarray(cv[b, :depth].reshape(depth, PAST, 512))
        m["st_ret"] = np.ascontiguousarray(sr[b, :depth])
        m["st_gdn"] = np.ascontiguousarray(sgd[b, :depth])
        in_maps.append(m)
    nc = _get_nc(T_S, depth, debug)
    res = run_bass_kernel_spmd(nc, in_maps, core_ids=list(range(n_core)))
    R = res.results
    y_s = np.stack([np.asarray(R[b]["y"])[:T_S] for b in range(nsb)], axis=0).astype(np.float32)
    y_p = np.stack([np.asarray(R[core]["y"])[T_S + i * TP:T_S + (i + 1) * TP]
                    for core in range(n_core) for i in range(NPR)], axis=0).astype(np.float32)
    nk = np.concatenate([np.asarray(R[core]["nk"]) for core in range(n_core)], axis=0).astype(np.float32)
    nv = np.concatenate([np.asarray(R[core]["nv"]) for core in range(n_core)], axis=0).astype(np.float32)
    nret = np.concatenate([np.asarray(R[core]["nret"]) for core in range(n_core)], axis=0).astype(np.float32)
    ngdn = np.concatenate([np.asarray(R[core]["ngdn"]) for core in range(n_core)], axis=0).astype(np.float32)
    nk = nk.reshape(n_core * NPR, depth, TP, 4, 2, 64)
    nv = nv.reshape(n_core * NPR, depth, TP, 4, 128)
    outs = (y_p, y_s, nk, nv, nret, ngdn)
    if debug:
        return outs, R
    return outs


def kernel(**inputs):
    return run_cfg(inputs, 4096, DEPTH, False)
```

```python
import math
from contextlib import ExitStack
import numpy as np
import ml_dtypes
import concourse.bass as bass
import concourse.mybir as mybir
from concourse.bass_utils import run_bass_kernel_spmd

F32 = mybir.dt.float32
BF16 = mybir.dt.bfloat16
AF = mybir.ActivationFunctionType
ALU = mybir.AluOpType
AX = mybir.AxisListType

D = 1024
DEPTH = 4
TP = 256
NPR = 2
PAST = 256
D_IN = 8720
EPS = 1e-6
CH = 64


import os


class StopBuild(Exception):
    pass


def ckpt(name):
    if os.environ.get("KSTOP", "") == name:
        raise StopBuild(name)


class Res:
    __slots__ = ("name", "w", "r", "excl")

    def __init__(self, name=""):
        self.name = name
        self.w = None
        self.r = {}
        self.excl = False


class Tile:
    def __init__(self, h, name, psum=False):
        self.h = h
        self.res = Res(name)
        self.res.excl = psum

    def __getitem__(self, k):
        return self.h[k]


def _res(x):
    out = []
    for t in x:
        if isinstance(t, (list, tuple)):
            out.extend(_res(t))
        elif isinstance(t, Res):
            out.append(t)
        else:
            out.append(t.res)
    return out


class Sched:
    ENG = ("pe", "act", "dve", "pool", "sp")

    def __init__(self, nc, n_dma_slots=8):
        self.nc = nc
        self.streams = {e: [] for e in self.ENG}
        self.sems = {}
        self.cnt = {}
        for e in ("pe", "act", "dve", "pool"):
            self.sems[e] = nc.alloc_semaphore("s_" + e)
            self.cnt[e] = 0
        self.nslots = n_dma_slots
        self.dq = {}
        for q in ("sp", "pool", "act"):
            slots = []
            for i in range(n_dma_slots):
                k = "d_%s%d" % (q, i)
                self.sems[k] = nc.alloc_semaphore(k)
                slots.append([k, 0])
            self.dq[q] = [slots, 0]
        self.seen = {e: {} for e in self.ENG}
        self.n_instr = 0
        self.n_wait = 0

    def _wait(self, eng, key, val):
        if val is None or val <= 0:
            return
        if eng == "pe" and key == "pe":
            return
        s = self.seen[eng]
        if s.get(key, 0) >= val:
            return
        s[key] = val
        sem = self.sems[key]
        self.streams[eng].append(lambda e, sem=sem, val=val: e.wait_ge(sem, val))
        self.n_wait += 1

    def _deps(self, eng, reads, writes, is_dma=False):
        for r in reads:
            if r.w is not None:
                self._wait(eng, r.w[0], r.w[1])
        for w in writes:
            if w.w is not None:
                if is_dma or not (w.w[0] == eng):
                    self._wait(eng, w.w[0], w.w[1])
            for k, v in w.r.items():
                if (not is_dma) and k == eng:
                    continue
                self._wait(eng, k, v)

    def _mark(self, key, val, reads, writes):
        for r in reads:
            if r.r.get(key, 0) < val:
                r.r[key] = val
        for w in writes:
            w.w = (key, val)
            w.r = {}

    def op(self, eng, method, R=(), W=(), **kw):
        reads = _res(R)
        writes = _res(W)
        ex = [r for r in reads if r.excl]
        if ex:
            reads = [r for r in reads if not r.excl]
            writes = writes + [r for r in ex if r not in writes]
        self._deps(eng, reads, writes)
        self.cnt[eng] += 1
        val = self.cnt[eng]
        sem = self.sems[eng]
        import traceback
        org = traceback.extract_stack(limit=3)[0]
        org = "%s:%d" % (org.name, org.lineno)

        def _f(e, m=method, kw=kw, sem=sem, org=org):
            try:
                return getattr(e, m)(**kw).then_inc(sem, 1)
            except Exception as ex:
                raise RuntimeError("emit failed at %s (%s): %s" % (org, m, ex)) from ex
        self.streams[eng].append(_f)
        self._mark(eng, val, reads, writes)
        self.n_instr += 1

    def dma(self, q, out, in_, R=(), W=(), **kw):
        reads = _res(R)
        writes = _res(W)
        slots, idx = self.dq[q]
        slot = slots[idx % self.nslots]
        self.dq[q][1] = idx + 1
        key = slot[0]
        if slot[1] > 0:
            self._wait(q, key, slot[1])
        self._deps(q, reads, writes, is_dma=True)
        slot[1] += 16
        val = slot[1]
        sem = self.sems[key]
        import traceback
        org = traceback.extract_stack(limit=3)[0]
        org = "%s:%d" % (org.name, org.lineno)

        def _f(e, out=out, in_=in_, sem=sem, kw=kw, org=org):
            try:
                return e.dma_start(out=out, in_=in_, **kw).then_inc(sem, 16)
            except Exception as ex:
                raise RuntimeError("dma emit failed at %s: %s" % (org, ex)) from ex
        self.streams[q].append(_f)
        self._mark(key, val, reads, writes)
        self.n_instr += 1

    def barrier(self):
        for e in self.ENG:
            for k in ("pe", "act", "dve", "pool"):
                if k != e:
                    self._wait(e, k, self.cnt[k])
            for q in self.dq:
                for slot in self.dq[q][0]:
                    if slot[1] > 0:
                        self._wait(e, slot[0], slot[1])

    def emit(self):
        nc = self.nc
        st = self.streams
        with nc.Block() as block:
            @block.sync
            def _(e):
                for f in st["sp"]:
                    f(e)

            @block.tensor
            def _(e):
                for f in st["pe"]:
                    f(e)

            @block.scalar
            def _(e):
                for f in st["act"]:
                    f(e)

            @block.vector
            def _(e):
                for f in st["dve"]:
                    f(e)

            @block.gpsimd
            def _(e):
                for f in st["pool"]:
                    f(e)


def make_consts(T_S):
    c = {}
    c["ident"] = np.eye(128, dtype=np.float32)
    n_rows = T_S // 64
    row = np.repeat(np.arange(n_rows, dtype=np.float32), 64)
    col = np.tile(np.arange(64, dtype=np.float32), n_rows)
    inv = (1.0 / (10000.0 ** (np.arange(16, dtype=np.float32) / 16))).astype(np.float32)
    ar = row[:, None] * inv
    ac = col[:, None] * inv
    ang = np.concatenate([ar, ar, ac, ac], axis=-1).astype(np.float32)
    cos = np.cos(ang).astype(np.float32)
    sin = np.sin(ang).astype(np.float32)
    sgn = np.tile(np.concatenate([-np.ones(16), np.ones(16)]), 2).astype(np.float32)
    c["ropec"] = cos
    c["ropes"] = (sin * sgn).astype(np.float32)
    a = np.arange(64)[:, None]
    b = np.arange(64)[None, :]
    low = (a > b).astype(np.float32)
    up = (b > a).astype(np.float32)
    upi = (b >= a).astype(np.float32)
    lowi = (a >= b).astype(np.float32)
    gm = np.zeros((64, 2, 3, 64), np.float32)
    gm[:, 0, 0] = -low
    gm[:, 0, 1] = -up
    gm[:, 0, 2] = upi
    gm[:, 1, 0] = -up
    gm[:, 1, 1] = -low
    gm[:, 1, 2] = lowi
    c["gmask"] = gm
    a2 = np.arange(128)[:, None]
    b2 = np.arange(128)[None, :]
    same = (a2 // 64 == b2 // 64)
    gm2 = np.zeros((128, 2, 3, 128), np.float32)
    gm2[:, 0, 0] = -1.0 * ((a2 > b2) & same)
    gm2[:, 0, 1] = -1.0 * ((b2 > a2) & same)
    gm2[:, 0, 2] = ((b2 >= a2) & same)
    gm2[:, 1, 0] = -1.0 * ((b2 > a2) & same)
    gm2[:, 1, 1] = -1.0 * ((a2 > b2) & same)
    gm2[:, 1, 2] = ((a2 >= b2) & same)
    c["gmask2"] = gm2
    ut2 = np.zeros((128, 2, 128), np.float32)
    ut2[:, 0] = ((a2 <= b2) & same)
    ut2[:, 1] = ((a2 >= b2) & same)
    c["utri2"] = ut2
    c["bd1"] = same.astype(np.float32)
    sel = np.zeros((128, 2, 128), np.float32)
    sel[0:64, 0, :] = 1.0
    sel[64:128, 1, :] = 1.0
    c["selc"] = sel
    ut = np.zeros((64, 2, 64), np.float32)
    ut[:, 0] = (a <= b).astype(np.float32)
    ut[:, 1] = (a >= b).astype(np.float32)
    c["utri"] = ut
    j = np.arange(128)[:, None].astype(np.float32)
    i = np.arange(128)[None, :].astype(np.float32)
    rr = np.zeros((128, 2, 128), np.float32)
    rm = np.zeros((128, 2, 128), np.float32)
    rr[:, 0] = np.maximum(i - j, 0)
    rm[:, 0] = (i >= j)
    rr[:, 1] = np.maximum(j - i, 0)
    rm[:, 1] = (j >= i)
    c["rrel"] = rr
    c["rmask"] = rm
    rq = np.zeros((64, 2, 128), np.float32)
    rq[:, 0] = (np.arange(128) + 1.0)[None, :]
    rq[:, 1] = (128.0 - np.arange(128))[None, :]
    c["rqexp"] = rq
    rk = np.zeros((128, 2), np.float32)
    rk[:, 0] = 127.0 - np.arange(128)
    rk[:, 1] = np.arange(128)
    c["rkexp"] = rk
    return c


CONST_SHAPES = lambda T_S: {k: v.shape for k, v in make_consts(T_S).items()}


def build(T_S=4096, depth=DEPTH, debug=False):
    nc = bass.Bass("TRN2", target_bir_lowering=False)
    S = Sched(nc)
    TT = T_S + NPR * TP
    NT = TT // 128
    NTS = T_S // 128
    seqs = [(0, T_S, True)] + [(T_S + i * TP, TP, False) for i in range(NPR)]
    lam_inits = [0.8 - 0.6 * math.exp(-0.3 * l) for l in range(depth)]

    def din(name, shape, dt=F32):
        return nc.dram_tensor(name, list(shape), dt, kind="ExternalInput").ap()

    def dout(name, shape, dt=F32):
        return nc.dram_tensor(name, list(shape), dt, kind="ExternalOutput").ap()

    def scr(name, shape, dt):
        if debug:
            return nc.dram_tensor(name, list(shape), dt, kind="ExternalOutput").ap()
        return nc.dram_tensor(name, list(shape), dt).ap()

    x_in = din("x_in", [TT, D])
    cond = din("cond", [2, D])
    cache_k = din("cache_k", [depth, PAST, 512])
    cache_v = din("cache_v", [depth, PAST, 512])
    st_ret = din("st_ret", [depth, 2, 4, 64, 128])
    st_gdn = din("st_gdn", [depth, 2, 4, 128, 128])
    norm_w = din("norm_w", [depth, D])
    w_ada = din("w_ada", [depth, D, 3 * D])
    b_ada = din("b_ada", [depth, 3 * D])
    w_in = din("w_in", [depth, D, D_IN])
    qk_norm_w = din("qk_norm_w", [depth, 2, 64])
    diff_lambda = din("diff_lambda", [depth, 4, 64])
    subln_w = din("subln_w", [depth, 128])
    ret_decay = din("ret_decay", [depth, 8])
    ret_norm_w = din("ret_norm_w", [depth, 128])
    conv_w = din("conv_w", [depth, 5, 1536])
    gdn_a_log = din("gdn_a_log", [depth, 8])
    gdn_dt_bias = din("gdn_dt_bias", [depth, 8])
    gdn_norm_w = din("gdn_norm_w", [depth, 128])
    w_branch = din("w_branch", [depth, 3, 512, D])
    w_out = din("w_out", [depth, D, D])
    cst = {k: din("c_" + k, shp) for k, shp in CONST_SHAPES(T_S).items()}
    y_out = dout("y", [TT, D])
    nk_out = dout("nk", [NPR, depth, TP, 512])
    nv_out = dout("nv", [NPR, depth, TP, 512])
    nret_out = dout("nret", [NPR, depth, 2, 4, 64, 128])
    ngdn_out = dout("ngdn", [NPR, depth, 2, 4, 128, 128])
    xcur = scr("xcur", [TT, D], F32)
    aqT = scr("aqT", [4, 128, TT], BF16)
    akT = scr("akT", [4, 128, TT], BF16)
    av = scr("av", [TT, 512], BF16)
    zg = scr("zg", [3, TT, 512], BF16)
    bqT = scr("bqT", [4, 64, TT], BF16)
    bkT = scr("bkT", [4, 64, TT], BF16)
    bk = scr("bk", [TT, 256], BF16)
    bv = scr("bv", [TT, 512], BF16)
    cT = scr("cT", [1536, TT], F32)
    gqT = scr("gqT", [4, 128, TT], BF16)
    gkT = scr("gkT", [4, 128, TT], BF16)
    gk = scr("gk", [TT, 512], BF16)
    gv = scr("gv", [TT, 512], BF16)
    bgs = scr("bgs", [TT, 16], F32)
    mg = scr("mg", [TT, 3 * D], BF16)
    ofs = scr("ofs", [TT, 512], F32)
    ofb = scr("ofb", [TT, 512], F32)
    gT = scr("gT", [3, 512, TT], BF16)

    def tres(n):
        return [Res("%s%d" % (n, i)) for i in range(NT)]
    RX = {n: tres(n) for n in ("x", "aqT", "akT", "av", "zg0", "zg1", "zg2", "bqT", "bkT", "bk", "bv", "cT",
                               "gqT", "gkT", "gk", "gv", "bgs", "mg", "ofs", "ofb", "gT0", "gT1", "gT2")}
    R_out = Res("outs")

    uid = [0]

    def sb(es, name, shape, dt):
        uid[0] += 1
        nm = "%s_u%d" % (name, uid[0])
        return Tile(es.enter_context(nc.sbuf_tensor(nm, list(shape), dt)), nm)

    class Pool_:
        def __init__(self, es, name, shape, dt, n):
            self.t = [sb(es, "%s_%d" % (name, i), shape, dt) for i in range(n)]
            self.i = 0

        def next(self):
            t = self.t[self.i % len(self.t)]
            self.i += 1
            return t

    root = ExitStack()
    PSF = [Tile(nc.alloc_psum_tensor("psf%d" % i, [128, 512], F32), "psf%d" % i, True) for i in range(6)]
    PSB = [Tile(nc.alloc_psum_tensor("psb%d" % i, [128, 1024], BF16), "psb%d" % i, True) for i in range(2)]
    psc = [0, 0]

    psf_banks = [[0, 1, 2, 3]]

    def psf():
        psc[0] += 1
        bk = psf_banks[0]
        return PSF[bk[psc[0] % len(bk)]]

    def psb():
        psc[1] += 1
        return PSB[psc[1] % 2]

    ident_f = sb(root, "ident_f", [128, 128], F32)
    ident_b = sb(root, "ident_b", [128, 128], BF16)
    ones_f = sb(root, "ones_f", [128, 128], F32)
    ropec = sb(root, "ropec", [128, NTS, 64], F32)
    ropes = sb(root, "ropes", [128, NTS, 64], F32)
    gmask = sb(root, "gmask", [64, 2, 3, 64], F32)
    utri = sb(root, "utri", [64, 2, 64], F32)
    gmask2 = sb(root, "gmask2", [128, 2, 3, 128], F32)
    utri2 = sb(root, "utri2", [128, 2, 128], F32)
    bd1 = sb(root, "bd1", [128, 128], F32)
    selc = sb(root, "selc", [128, 2, 128], F32)
    rrel = sb(root, "rrel", [128, 2, 128], F32)
    rmask = sb(root, "rmask", [128, 2, 128], F32)
    rqexp = sb(root, "rqexp", [64, 2, 128], F32)
    rkexp = sb(root, "rkexp", [128, 2], F32)
    gate_bc = sb(root, "gate_bc", [128, 2, D], F32)
    wq_bc = sb(root, "wq_bc", [128, 2, 64], F32)
    subw_bc = sb(root, "subw_bc", [128, 128], F32)
    retw_bc = sb(root, "retw_bc", [128, 128], F32)
    gdnw_bc = sb(root, "gdnw_bc", [128, 128], F32)
    lamt = sb(root, "lamt", [128, 8], F32)
    dl_bc = sb(root, "dl_bc", [128, 4, 64], F32)
    lg_bc = sb(root, "lg_bc", [128, 8], F32)
    rdmat = sb(root, "rdmat", [128, 2, 4, 128], F32)
    rqdec = sb(root, "rqdec", [64, 2, 4, 128], F32)
    rkdec = sb(root, "rkdec", [128, 2, 4], F32)
    rcdec = sb(root, "rcdec", [128, 8], F32)
    negA_bc = sb(root, "negA_bc", [128, 8], F32)
    dtb_bc = sb(root, "dtb_bc", [128, 8], F32)
    convw = sb(root, "convw", [128, 12, 5], F32)
    tmp8 = sb(root, "tmp8", [128, 8], F32)

    S.dma("sp", ident_f[:], cst["ident"], W=[ident_f])
    S.op("dve", "tensor_copy", R=[ident_f], W=[ident_b], out=ident_b[:], in_=ident_f[:])
    S.op("dve", "memset", W=[ones_f], ap=ones_f[:], constant=1.0)
    S.dma("sp", ropec[:], cst["ropec"].rearrange("(n p) d -> p n d", p=128), W=[ropec])
    S.dma("sp", ropes[:], cst["ropes"].rearrange("(n p) d -> p n d", p=128), W=[ropes])
    S.dma("sp", gmask[:], cst["gmask"], W=[gmask])
    S.dma("sp", utri[:], cst["utri"], W=[utri])
    S.dma("sp", gmask2[:], cst["gmask2"], W=[gmask2])
    S.dma("sp", utri2[:], cst["utri2"], W=[utri2])
    S.dma("sp", bd1[:], cst["bd1"], W=[bd1])
    S.dma("sp", selc[:], cst["selc"], W=[selc])
    S.dma("sp", rrel[:], cst["rrel"], W=[rrel])
    S.dma("sp", rmask[:], cst["rmask"], W=[rmask])
    S.dma("sp", rqexp[:], cst["rqexp"], W=[rqexp])
    S.dma("sp", rkexp[:], cst["rkexp"], W=[rkexp])

    def rsqrt_inplace(t, ap, scale, eps):
        S.op("dve", "tensor_scalar", R=[t], W=[t], out=ap, in0=ap, scalar1=scale, scalar2=eps,
             op0=ALU.mult, op1=ALU.add)
        S.op("act", "activation", R=[t], W=[t], out=ap, in_=ap, func=AF.Sqrt)
        S.op("dve", "reciprocal", R=[t], W=[t], out=ap, in_=ap)

    def transposes_to(ps_t, src_t, blocks, rows=128):
        for (sap, w, off) in blocks:
            S.op("pe", "transpose", R=[src_t, ident_b], W=[ps_t], out=ps_t[0:w, off:off + rows], in_=sap,
                 identity=ident_b[0:rows, 0:rows])

    def layer(l):
        last = (l == depth - 1)
        xsrc = x_in if l == 0 else xcur
        xdst = y_out if last else xcur
        esH = ExitStack()
        hT = sb(esH, "hT", [128, 8, TT], BF16)
        esA = ExitStack()
        A_bc = sb(esA, "A_bc", [128, 2, D], F32)
        sh_bc = sb(esA, "sh_bc", [128, 2, D], F32)
        with ExitStack() as es:
            nw_bc = sb(es, "nw_bc", [128, D], F32)
            cs = sb(es, "cs", [128, 2, 8], F32)
            rep = sb(es, "rep", [128, 16, 128], F32)
            bada = sb(es, "bada", [1, 3 * D], F32)
            wa = Pool_(es, "wa", [128, 8, 512], F32, 2)
            S.dma("sp", nw_bc[:], norm_w[l].partition_broadcast(128), W=[nw_bc])
            S.dma("sp", cs[:], cond.rearrange("j (p c) -> p j c", c=8), W=[cs])
            S.dma("sp", bada[:], b_ada[l:l + 1, :], W=[bada])
            S.dma("sp", wq_bc[:],
                  qk_norm_w[l].rearrange("a d -> (a d)").partition_broadcast(128).rearrange("p (a d) -> p a d", a=2),
                  W=[wq_bc])
            S.dma("sp", subw_bc[:], subln_w[l].partition_broadcast(128), W=[subw_bc])
            S.dma("sp", retw_bc[:], ret_norm_w[l].partition_broadcast(128), W=[retw_bc])
            S.dma("sp", gdnw_bc[:], gdn_norm_w[l].partition_broadcast(128), W=[gdnw_bc])
            S.dma("sp", dl_bc[:], diff_lambda[l].rearrange("a d -> (a d)").partition_broadcast(128)
                  .rearrange("p (a d) -> p a d", a=4), W=[dl_bc])
            S.dma("sp", lg_bc[:], ret_decay[l].partition_broadcast(128), W=[lg_bc])
            S.dma("sp", negA_bc[:], gdn_a_log[l].partition_broadcast(128), W=[negA_bc])
            S.dma("sp", dtb_bc[:], gdn_dt_bias[l].partition_broadcast(128), W=[dtb_bc])
            for k in range(5):
                S.dma("sp", convw[:, :, k:k + 1], conv_w[l, k].rearrange("(c p o) -> p c o", p=128, o=1), W=[convw],
                      allow_slow_non_contiguous=True)
            S.op("dve", "tensor_scalar", R=[wq_bc], W=[wq_bc], out=wq_bc[:, 0, :], in0=wq_bc[:, 0, :],
                 scalar1=0.125, scalar2=None, op0=ALU.mult)
            S.op("dve", "tensor_scalar", R=[subw_bc], W=[subw_bc], out=subw_bc[:], in0=subw_bc[:],
                 scalar1=float(1.0 - lam_inits[l]), scalar2=None, op0=ALU.mult)
            S.op("dve", "tensor_tensor", R=[dl_bc], W=[dl_bc], out=dl_bc[:, 0, :], in0=dl_bc[:, 0, :],
                 in1=dl_bc[:, 1, :], op=ALU.mult)
            S.op("dve", "tensor_tensor", R=[dl_bc], W=[dl_bc], out=dl_bc[:, 2, :], in0=dl_bc[:, 2, :],
                 in1=dl_bc[:, 3, :], op=ALU.mult)
            S.op("dve", "tensor_reduce", R=[dl_bc], W=[lamt], out=lamt[:, 0:4], in_=dl_bc[:], axis=AX.X, op=ALU.add)
            S.op("act", "activation", R=[lamt], W=[lamt], out=lamt[:, 4:8], in_=lamt[:, 0:4], func=AF.Exp)
            S.op("dve", "tensor_tensor", R=[lamt], W=[lamt], out=lamt[:, 1:2], in0=lamt[:, 6:7], in1=lamt[:, 4:5],
                 op=ALU.subtract)
            S.op("dve", "tensor_scalar", R=[lamt], W=[lamt], out=lamt[:, 0:1], in0=lamt[:, 1:2],
                 scalar1=float(-lam_inits[l]), scalar2=None, op0=ALU.add)
            S.op("act", "activation", R=[lg_bc], W=[lg_bc], out=lg_bc[:], in_=lg_bc[:], func=AF.Exp, scale=-1.0)
            S.op("act", "activation", R=[lg_bc], W=[lg_bc], out=lg_bc[:], in_=lg_bc[:], func=AF.Ln, bias=1.0)
            S.op("dve", "tensor_scalar", R=[lg_bc], W=[lg_bc], out=lg_bc[:], in0=lg_bc[:], scalar1=-1.0,
                 scalar2=None, op0=ALU.mult)
            for d in range(2):
                for h in range(4):
                    u = d * 4 + h
                    S.op("act", "activation", R=[rrel, lg_bc], W=[rdmat], out=rdmat[:, d, h, :], in_=rrel[:, d, :],
                         func=AF.Exp, scale=lg_bc[:, u:u + 1])
                    S.op("dve", "tensor_tensor", R=[rdmat, rmask], W=[rdmat], out=rdmat[:, d, h, :],
                         in0=rdmat[:, d, h, :], in1=rmask[:, d, :], op=ALU.mult)
                    S.op("act", "activation", R=[rqexp, lg_bc], W=[rqdec], out=rqdec[:, d, h, :], in_=rqexp[:, d, :],
                         func=AF.Exp, scale=lg_bc[0:64, u:u + 1])
                    S.op("act", "activation", R=[rkexp, lg_bc], W=[rkdec], out=rkdec[:, d, h:h + 1],
                         in_=rkexp[:, d:d + 1], func=AF.Exp, scale=lg_bc[:, u:u + 1])
            S.op("act", "activation", R=[lg_bc], W=[rcdec], out=rcdec[:], in_=lg_bc[:], func=AF.Exp, scale=128.0)
            S.op("act", "activation", R=[negA_bc], W=[negA_bc], out=negA_bc[:], in_=negA_bc[:], func=AF.Exp)
            S.op("dve", "tensor_scalar", R=[negA_bc], W=[negA_bc], out=negA_bc[:], in0=negA_bc[:], scalar1=-1.0,
                 scalar2=None, op0=ALU.mult)
            S.op("act", "activation", R=[cs], W=[cs], out=cs[:], in_=cs[:], func=AF.Silu)
            S.op("dve", "tensor_copy", R=[cs], W=[rep], out=rep[:],
                 in_=cs[:].rearrange("p j c -> p (j c)").unsqueeze(2).to_broadcast([128, 16, 128]))
            for nb in range(6):
                w = wa.next()
                S.dma("sp", w[:], w_ada[l].rearrange("(p c) n -> p c n", c=8)[:, :, nb * 512:(nb + 1) * 512], W=[w])
                for j in range(2):
                    ps = psf()
                    for c in range(8):
                        S.op("pe", "matmul", R=[rep, w], W=[ps], out=ps[:], lhsT=rep[:, j * 8 + c, :], rhs=w[:, c, :],
                             start=(c == 0), stop=False)
                    S.op("pe", "matmul", R=[ones_f, bada], W=[ps], out=ps[:], lhsT=ones_f[0:1, :],
                         rhs=bada[0:1, nb * 512:(nb + 1) * 512], start=False, stop=True)
                    cols = slice((nb % 2) * 512, (nb % 2) * 512 + 512)
                    if nb < 2:
                        S.op("act", "copy", R=[ps], W=[sh_bc], out=sh_bc[:, j, cols], in_=ps[:])
                    elif nb < 4:
                        S.op("dve", "scalar_tensor_tensor", R=[ps, nw_bc], W=[A_bc], out=A_bc[:, j, cols], in0=ps[:],
                             scalar=1.0, in1=nw_bc[:, cols], op0=ALU.add, op1=ALU.mult)
                    else:
                        S.op("act", "copy", R=[ps], W=[gate_bc], out=gate_bc[:, j, cols], in_=ps[:])
            S.barrier()
        ckpt("A")
        with ExitStack() as es:
            xp = Pool_(es, "xB", [128, D], F32, 4)
            hp = Pool_(es, "hB", [128, D], F32, 4)
            hbp = Pool_(es, "hbB", [128, D], BF16, 4)
            junk = sb(es, "junkB", [128, D], F32)
            ssp = Pool_(es, "ssB", [128, 1], F32, 4)
            def tileB(tt):
                j = 0 if tt < NTS else 1
                xt = xp.next()
                ht = hp.next()
                hb = hbp.next()
                ss = ssp.next()
                S.dma("sp", xt[:], xsrc[tt * 128:(tt + 1) * 128, :], R=[RX["x"][tt]] if l > 0 else [], W=[xt])
                S.op("act", "activation", R=[xt], W=[junk, ss], out=junk[:], in_=xt[:], func=AF.Square, accum_out=ss[:])
                yield
                rsqrt_inplace(ss, ss[:], 1.0 / D, EPS)
                S.op("dve", "scalar_tensor_tensor", R=[xt, ss, A_bc], W=[ht], out=ht[:], in0=xt[:], scalar=ss[:, 0:1],
                     in1=A_bc[:, j, :], op0=ALU.mult, op1=ALU.mult)
                S.op("dve", "tensor_tensor", R=[ht, sh_bc], W=[hb], out=hb[:], in0=ht[:], in1=sh_bc[:, j, :], op=ALU.add)
                yield
                pb = psb()
                transposes_to(pb, hb, [(hb[:, c * 128:(c + 1) * 128], 128, c * 128) for c in range(8)])
                S.op("act", "copy", R=[pb], W=[hT], out=hT[:, :, tt * 128:(tt + 1) * 128],
                     in_=pb[:].rearrange("p (c t) -> p c t", c=8))
            pipeline([tileB(tt) for tt in range(NT)], 3)
            S.barrier()
        esA.close()
        ckpt("B")
        with ExitStack() as es:
            wf = Pool_(es, "wfC", [128, 4, 512], F32, 2)
            wbp = Pool_(es, "wbC", [128, 8, 512], BF16, 2)
            f1 = Pool_(es, "f1C", [128, 512], F32, 3)
            f2 = Pool_(es, "f2C", [128, 512], F32, 2)
            b1 = Pool_(es, "b1C", [128, 512], BF16, 3)
            st8 = Pool_(es, "st8C", [128, 16], F32, 2)
            stg = Pool_(es, "stgC", [128, 1024], BF16, 2)
            cfp = Pool_(es, "cfC", [128, 512], F32, 3)

            def load_wblock(c0, ncols):
                wb = wbp.next()
                for half in range(2):
                    w = wf.next()
                    S.dma("sp", w[:, :, 0:ncols],
                          w_in[l].rearrange("(c p) n -> p c n", p=128)[:, half * 4:half * 4 + 4, c0:c0 + ncols], W=[w])
                    S.op("dve" if half == 0 else "pool", "tensor_copy", R=[w], W=[wb],
                         out=wb[:, half * 4:half * 4 + 4, 0:ncols], in_=w[:, :, 0:ncols])
                return (wb, (c0, ncols))

            wblocks = [(0, 512), (512, 512), (1024, 512), (1536, 512), (2560, 512), (3072, 512), (5120, 512),
                       (2048, 512), (3584, 512), (4096, 512), (4608, 512), (5632, 16)] + \
                      [(5648 + i * 512, 512) for i in range(6)]
            wq = []

            def next_wb(c0, ncols):
                if not wq:
                    wq.append(load_wblock(*wblocks.pop(0)))
                wb = wq.pop(0)
                assert wb[1] == (c0, ncols), (wb[1], c0, ncols)
                if wblocks:
                    wq.append(load_wblock(*wblocks.pop(0)))
                return wb[0]

            def mm_tok(wb, tt, ncols):
                ps = psf()
                for c in range(8):
                    S.op("pe", "matmul", R=[hT, wb], W=[ps], out=ps[:, 0:ncols], lhsT=hT[:, c, tt * 128:(tt + 1) * 128],
                         rhs=wb[:, c, 0:ncols], start=(c == 0), stop=(c == 7))
                return ps

            def rope(src, tt, ngrp, dst_pool):
                t1 = dst_pool.next()
                t2 = dst_pool.next()
                s4 = src[:, 0:ngrp * 64].rearrange("p (g a h f) -> p g a h f", a=2, h=2, f=16)
                S.op("dve", "tensor_tensor", R=[src, ropec], W=[t1],
                     out=t1[:, 0:ngrp * 64].rearrange("p (g d) -> p g d", d=64),
                     in0=src[:, 0:ngrp * 64].rearrange("p (g d) -> p g d", d=64),
                     in1=ropec[:, tt:tt + 1, :].to_broadcast([128, ngrp, 64]), op=ALU.mult)
                t24 = t2[:, 0:ngrp * 64].rearrange("p (g a h f) -> p g a h f", a=2, h=2, f=16)
                sn4 = ropes[:, tt, :].rearrange("p (a h f) -> p a h f", a=2, h=2)
                for hh in range(2):
                    S.op("pool", "tensor_tensor", R=[src, ropes], W=[t2], out=t24[:, :, :, hh, :],
                         in0=s4[:, :, :, 1 - hh, :],
                         in1=sn4[:, :, hh, :].unsqueeze(1).to_broadcast([128, ngrp, 2, 16]), op=ALU.mult)
                S.op("dve", "tensor_tensor", R=[t1, t2], W=[t1], out=t1[:, 0:ngrp * 64], in0=t1[:, 0:ngrp * 64],
                     in1=t2[:, 0:ngrp * 64], op=ALU.add)
                return t1

            for blk in range(2):
                wb = next_wb(blk * 512, 512)
                for tt in range(NT):
                    is_s = tt < NTS
                    ps = mm_tok(wb, tt, 512)
                    sq = f2.next()
                    s8 = st8.next()
                    qn = f1.next()
                    S.op("act", "activation", R=[ps], W=[sq], out=sq[:], in_=ps[:], func=AF.Square)
                    S.op("dve", "tensor_reduce", R=[sq], W=[s8], out=s8[:, 0:8],
                         in_=sq[:].rearrange("p (g d) -> p g d", d=64), axis=AX.X, op=ALU.add)
                    rsqrt_inplace(s8, s8[:, 0:8], 1.0 / 64, EPS)
                    S.op("dve", "tensor_tensor", R=[ps, s8], W=[qn], out=qn[:].rearrange("p (g d) -> p g d", d=64),
                         in0=ps[:].rearrange("p (g d) -> p g d", d=64),
                         in1=s8[:, 0:8].unsqueeze(2).to_broadcast([128, 8, 64]), op=ALU.mult)
                    S.op("pool", "tensor_tensor", R=[qn, wq_bc], W=[qn], out=qn[:].rearrange("p (g d) -> p g d", d=64),
                         in0=qn[:].rearrange("p (g d) -> p g d", d=64),
                         in1=wq_bc[:, blk:blk + 1, :].to_broadcast([128, 8, 64]), op=ALU.mult)
                    if blk == 1 and not is_s:
                        pi = (tt - NTS) // 2
                        r0 = ((tt - NTS) % 2) * 128
                        S.dma("pool", nk_out[pi, l, r0:r0 + 128, :], qn[:], R=[qn])
                    if is_s:
                        qn = rope(qn, tt, 8, f1)
                    qb = b1.next()
                    S.op("act", "copy", R=[qn], W=[qb], out=qb[:], in_=qn[:])
                    pb = psb()
                    transposes_to(pb, qb, [(qb[:, h * 128:(h + 1) * 128], 128, h * 128) for h in range(4)])
                    sg = stg.next()
                    S.op("dve", "tensor_copy", R=[pb], W=[sg], out=sg[:, 0:512], in_=pb[:, 0:512])
                    dst = aqT if blk == 0 else akT
                    S.dma("pool", dst[:, :, tt * 128:(tt + 1) * 128].rearrange("h p t -> p h t"),
                          sg[:, 0:512].rearrange("p (h t) -> p h t", h=4), R=[sg],
                          W=[RX["aqT" if blk == 0 else "akT"][tt]])
            ckpt("C0")
            for (c0, kind) in ((1024, "av"), (1536, "z0"), (2560, "bv"), (3072, "z1"), (5120, "z2")):
                wb = next_wb(c0, 512)
                for tt in range(NT):
                    is_s = tt < NTS
                    ps = mm_tok(wb, tt, 512)
                    ob = b1.next()
                    if kind in ("av", "bv"):
                        S.op("act", "copy", R=[ps], W=[ob], out=ob[:], in_=ps[:])
                        dst, rn = (av, "av") if kind == "av" else (bv, "bv")
                        S.dma("pool", dst[tt * 128:(tt + 1) * 128, :], ob[:], R=[ob], W=[RX[rn][tt]])
                        if kind == "av" and not is_s:
                            of_ = f1.next()
                            S.op("dve", "tensor_copy", R=[ps], W=[of_], out=of_[:], in_=ps[:])
                            pi = (tt - NTS) // 2
                            r0 = ((tt - NTS) % 2) * 128
                            S.dma("pool", nv_out[pi, l, r0:r0 + 128, :], of_[:], R=[of_])
                    else:
                        zi = int(kind[1])
                        S.op("act", "activation", R=[ps], W=[ob], out=ob[:], in_=ps[:], func=AF.Silu)
                        S.dma("pool", zg[zi, tt * 128:(tt + 1) * 128, :], ob[:], R=[ob], W=[RX["zg%d" % zi][tt]])
                ckpt("C1_" + kind)
            ckpt("C1")
            wb = next_wb(2048, 512)
            for tt in range(NT):
                is_s = tt < NTS
                ps = mm_tok(wb, tt, 512)
                qk = f1.next()
                S.op("act", "copy", R=[ps], W=[qk], out=qk[:, 0:256], in_=ps[:, 0:256])
                S.op("act", "mul", R=[ps], W=[qk], out=qk[:, 256:512], in_=ps[:, 256:512], mul=0.125)
                if is_s:
                    qk = rope(qk, tt, 8, f1)
                qb = b1.next()
                S.op("act", "copy", R=[qk], W=[qb], out=qb[:], in_=qk[:])
                S.dma("pool", bk[tt * 128:(tt + 1) * 128, :], qb[:, 256:512], R=[qb], W=[RX["bk"][tt]])
                pb = psb()
                transposes_to(pb, qb, [(qb[:, g * 64:(g + 1) * 64], 64, g * 128) for g in range(8)])
                sg = stg.next()
                S.op("dve", "tensor_copy", R=[pb], W=[sg], out=sg[0:64, :], in_=pb[0:64, :])
                S.dma("pool", bqT[:, :, tt * 128:(tt + 1) * 128].rearrange("h p t -> p h t"),
                      sg[0:64, 0:512].rearrange("p (h t) -> p h t", h=4), R=[sg], W=[RX["bqT"][tt]])
                S.dma("pool", bkT[:, :, tt * 128:(tt + 1) * 128].rearrange("h p t -> p h t"),
                      sg[0:64, 512:1024].rearrange("p (h t) -> p h t", h=4), R=[sg], W=[RX["bkT"][tt]])
            ckpt("C2")
            for blk in range(3):
                wb = next_wb(3584 + blk * 512, 512)
                for cc in range(4):
                    for (t0, T, _) in seqs:
                        for g0 in range(0, T, 512):
                            n = min(512, T - g0)
                            ps = psf()
                            for c in range(8):
                                S.op("pe", "matmul", R=[hT, wb], W=[ps], out=ps[:, 0:n],
                                     lhsT=wb[:, c, cc * 128:(cc + 1) * 128], rhs=hT[:, c, t0 + g0:t0 + g0 + n],
                                     start=(c == 0), stop=(c == 7))
                            cf = cfp.next()
                            S.op("act", "copy", R=[ps], W=[cf], out=cf[:, 0:n], in_=ps[:, 0:n])
                            ch0 = (blk * 4 + cc) * 128
                            tiles = range((t0 + g0) // 128, (t0 + g0 + n) // 128)
                            S.dma("pool", cT[ch0:ch0 + 128, t0 + g0:t0 + g0 + n], cf[:, 0:n], R=[cf],
                                  W=[RX["cT"][i] for i in tiles])
            ckpt("C3")
            wb = next_wb(5632, 16)
            for tt in range(NT):
                ps = mm_tok(wb, tt, 16)
                o16 = st8.next()
                S.op("act", "activation", R=[ps], W=[o16], out=o16[:, 0:8], in_=ps[:, 0:8], func=AF.Sigmoid)
                S.op("dve", "tensor_tensor", R=[ps, dtb_bc], W=[o16], out=o16[:, 8:16], in0=ps[:, 8:16], in1=dtb_bc[:],
                     op=ALU.add)
                S.op("act", "activation", R=[o16], W=[o16], out=o16[:, 8:16], in_=o16[:, 8:16], func=AF.Exp)
                S.op("act", "activation", R=[o16], W=[o16], out=o16[:, 8:16], in_=o16[:, 8:16], func=AF.Ln, bias=1.0)
                S.op("dve", "tensor_tensor", R=[o16, negA_bc], W=[o16], out=o16[:, 8:16], in0=o16[:, 8:16],
                     in1=negA_bc[:], op=ALU.mult)
                S.dma("pool", bgs[tt * 128:(tt + 1) * 128, :], o16[:], R=[o16], W=[RX["bgs"][tt]])
            ckpt("C4")
            for blk in range(6):
                wb = next_wb(5648 + blk * 512, 512)
                for tt in range(NT):
                    ps = mm_tok(wb, tt, 512)
                    ob = b1.next()
                    S.op("act", "activation", R=[ps], W=[ob], out=ob[:], in_=ps[:], func=AF.Sigmoid)
                    S.dma("pool", mg[tt * 128:(tt + 1) * 128, blk * 512:(blk + 1) * 512], ob[:], R=[ob],
                          W=[RX["mg"][tt]] if blk == 5 else [])
            S.barrier()
        esH.close()
        ckpt("C")
        for si, (t0, T, is_s) in enumerate(seqs):
            if is_s and not os.environ.get("KNOOVL"):
                sample_mixers_overlapped(l, si, t0, T, is_s)
                ckpt("gpre%d" % si)
            else:
                attention(l, si, t0, T, is_s)
                ckpt("att%d" % si)
                retention(l, si, t0, T, is_s)
                ckpt("ret%d" % si)
                gdn_pre(l, si, t0, T, is_s)
                ckpt("gpre%d" % si)
            gdn_scan(l, si, t0, T, is_s)
            ckpt("gscan%d" % si)
        phaseE(l, xsrc, xdst)

    def norm_gate_store(es_tiles, o_t, o_ap, w_bc, zi, mi, tt, rows, col_lo=None):
        sq, s4, zt, gb, sg = es_tiles
        tok0 = tt * 128 + (col_lo or 0)
        S.op("act", "activation", R=[o_t], W=[sq], out=sq[0:rows, :], in_=o_ap, func=AF.Square)
        S.op("dve", "tensor_reduce", R=[sq], W=[s4], out=s4[0:rows, 0:4],
             in_=sq[0:rows, :].rearrange("p (g d) -> p g d", d=128), axis=AX.X, op=ALU.add)
        rsqrt_inplace(s4, s4[0:rows, 0:4], 1.0 / 128, EPS)
        S.dma("sp", zt[0:rows, :], zg[zi, tok0:tok0 + rows, :], R=[RX["zg%d" % zi][tt]], W=[zt])
        S.op("dve", "tensor_tensor", R=[o_t, s4], W=[o_t], out=o_ap.rearrange("p (g d) -> p g d", d=128),
             in0=o_ap.rearrange("p (g d) -> p g d", d=128),
             in1=s4[0:rows, 0:4].unsqueeze(2).to_broadcast([rows, 4, 128]), op=ALU.mult)
        S.op("pool", "tensor_tensor", R=[o_t, w_bc], W=[o_t], out=o_ap.rearrange("p (g d) -> p g d", d=128),
             in0=o_ap.rearrange("p (g d) -> p g d", d=128),
             in1=w_bc[0:rows, :].unsqueeze(1).to_broadcast([rows, 4, 128]), op=ALU.mult)
        S.op("dve", "tensor_tensor", R=[o_t, zt], W=[gb], out=gb[0:rows, :], in0=o_ap, in1=zt[0:rows, :], op=ALU.mult)
        pb = psb()
        for g in range(4):
            S.op("pe", "transpose", R=[gb, ident_b], W=[pb], out=pb[:, g * 128:g * 128 + rows],
                 in_=gb[0:rows, g * 128:(g + 1) * 128], identity=ident_b[0:rows, 0:rows])
        pv = pb[:, 0:512].rearrange("p (g t) -> p g t", g=4)[:, :, 0:rows]
        S.op("act", "copy", R=[pb], W=[sg], out=sg[:, :, 0:rows], in_=pv)
        S.dma("pool", gT[mi, :, tok0:tok0 + rows].rearrange("(g p) t -> p g t", p=128), sg[:, :, 0:rows], R=[sg],
              W=[RX["gT%d" % mi][tt]])

    def ng_tiles(es, pfx):
        return (sb(es, pfx + "sq", [128, 512], F32), sb(es, pfx + "s4", [128, 4], F32),
                sb(es, pfx + "zt", [128, 512], BF16), sb(es, pfx + "gb", [128, 512], BF16),
                sb(es, pfx + "sg", [128, 4, 128], BF16))

    def attention_gen(l, si, t0, T, is_s, es, acc_sets):
        Sk = T + (PAST if is_s else 0)
        nst = Sk // 128
        ntl = T // 128
        QB = min(512, T)
        kT = sb(es, "at_kT", [128, 4, Sk], BF16)
        V1 = sb(es, "at_V1", [128, nst, 4, 130], BF16)
        qTp = Pool_(es, "at_qT", [128, 4, QB], BF16, 2)
        ex = Pool_(es, "at_ex", [128, QB], BF16, 3)
        osb = [sb(es, "at_os%d" % qs, [128, 512], F32) for qs in range(QB // 128)]
        rc = Pool_(es, "at_rc", [128, 2], F32, 4)
        ngt = ng_tiles(es, "at_")
        grp = [0]
        scn = [0]
        S.dma("sp", kT[:, :, 0:T], akT[:, :, t0:t0 + T].rearrange("h p t -> p h t"),
              R=[RX["akT"][i] for i in range(t0 // 128, (t0 + T) // 128)], W=[kT])
        S.op("pool", "memset", W=[V1], ap=V1[:, :, :, 128:130], constant=1.0)
        for i in range(ntl):
            S.dma("sp", V1[:, i, :, 0:128], av[t0 + i * 128:t0 + (i + 1) * 128, :].rearrange("p (h e) -> p h e", h=4),
                  R=[RX["av"][t0 // 128 + i]], W=[V1])
        if is_s:
            with ExitStack() as es2:
                ck = sb(es2, "at_ck", [128, 2, 512], F32)
                cv = sb(es2, "at_cv", [128, 2, 512], F32)
                ckb = sb(es2, "at_ckb", [128, 2, 512], BF16)
                S.dma("sp", ck[:], cache_k[l].rearrange("(n p) f -> p n f", p=128), W=[ck])
                S.dma("sp", cv[:], cache_v[l].rearrange("(n p) f -> p n f", p=128), W=[cv])
                S.op("dve", "tensor_copy", R=[ck], W=[ckb], out=ckb[:], in_=ck[:])
                for n in range(2):
                    S.op("pool", "tensor_copy", R=[cv], W=[V1], out=V1[:, ntl + n, :, 0:128],
                         in_=cv[:, n, :].rearrange("p (h e) -> p h e", h=4))
                    pb = psb()
                    transposes_to(pb, ckb, [(ckb[:, n, h * 128:(h + 1) * 128], 128, h * 128) for h in range(4)])
                    S.op("act", "copy", R=[pb], W=[kT], out=kT[:, :, T + n * 128:T + (n + 1) * 128],
                         in_=pb[:, 0:512].rearrange("p (h t) -> p h t", h=4))
                S.barrier()
        for qb0 in range(0, T, QB):
            qT = qTp.next()
            S.dma("sp", qT[:], aqT[:, :, t0 + qb0:t0 + qb0 + QB].rearrange("h p t -> p h t"),
                  R=[RX["aqT"][i] for i in range((t0 + qb0) // 128, (t0 + qb0 + QB) // 128)], W=[qT])
            nqs = QB // 128
            for h in range(4):
                for m in range(2):
                    grp[0] += 1
                    acc = acc_sets[grp[0] % len(acc_sets)]

                    def pv(st, e, acc=acc, h=h):
                        for qs in range(nqs):
                            a = acc[qs // 2]
                            S.op("pe", "matmul", R=[e, V1], W=[a], out=a[:, (qs % 2) * 256:(qs % 2) * 256 + 129],
                                 lhsT=e[:, qs * 128:(qs + 1) * 128], rhs=V1[:, st, h, 0:129],
                                 start=(st == 0 and qs % 2 == 0), stop=(st == nst - 1), skip_group_check=True)
                    pend = None
                    for st in range(nst):
                        scn[0] += 1
                        ps = PSF[scn[0] % 2]
                        S.op("pe", "matmul", R=[kT, qT], W=[ps], out=ps[:, 0:QB],
                             lhsT=kT[m * 64:(m + 1) * 64, h, st * 128:(st + 1) * 128],
                             rhs=qT[m * 64:(m + 1) * 64, h, :], start=True, stop=True)
                        e = ex.next()
                        S.op("act", "activation", R=[ps], W=[e], out=e[:, 0:QB], in_=ps[:, 0:QB], func=AF.Exp)
                        if pend is not None:
                            pv(*pend)
                        pend = (st, e)
                        yield
                    pv(*pend)
                    for qs in range(nqs):
                        a = acc[qs // 2]
                        c0 = (qs % 2) * 256
                        r = rc.next()
                        S.op("dve", "reciprocal", R=[a], W=[r], out=r[:, 0:1], in_=a[:, c0 + 128:c0 + 129])
                        if m == 0:
                            S.op("dve", "tensor_scalar", R=[a, r], W=[osb[qs]], out=osb[qs][:, h * 128:(h + 1) * 128],
                                 in0=a[:, c0:c0 + 128], scalar1=r[:, 0:1], scalar2=None, op0=ALU.mult)
                        else:
                            S.op("dve", "tensor_tensor", R=[r, lamt], W=[r], out=r[:, 1:2], in0=r[:, 0:1],
                                 in1=lamt[:, 0:1], op=ALU.mult)
                            S.op("dve", "scalar_tensor_tensor", R=[a, r, osb[qs]], W=[osb[qs]],
                                 out=osb[qs][:, h * 128:(h + 1) * 128], in0=a[:, c0:c0 + 128], scalar=r[:, 1:2],
                                 in1=osb[qs][:, h * 128:(h + 1) * 128], op0=ALU.mult, op1=ALU.add)
            for qs in range(nqs):
                tt = (t0 + qb0) // 128 + qs
                norm_gate_store(ngt, osb[qs], osb[qs][:], subw_bc, 0, 0, tt, 128)

    def attention(l, si, t0, T, is_s):
        with ExitStack() as es:
            for _ in attention_gen(l, si, t0, T, is_s, es, [[PSF[2], PSF[3]], [PSF[4], PSF[5]]]):
                pass
            S.barrier()

    def retention(l, si, t0, T, is_s):
        with ExitStack() as es:
            run_rr([ret_chain(l, si, t0, T, is_s, d, es) for d in range(2)])
            S.barrier()
        combine(t0, T, retw_bc, 1, 1)

    def rr_gen(gens):
        gens = list(gens)
        while gens:
            for g in list(gens):
                try:
                    next(g)
                    yield
                except StopIteration:
                    gens.remove(g)

    def side_gen(l, si, t0, T, is_s):
        with ExitStack() as es2:
            yield from gdn_pre_gen(l, si, t0, T, is_s, es2, 256, 1)
            S.barrier()
        with ExitStack() as es3:
            yield from rr_gen([ret_chain(l, si, t0, T, is_s, d, es3) for d in range(2)])
            S.barrier()
        yield from combine_gen(t0, T, retw_bc, 1, 1)

    def sample_mixers_overlapped(l, si, t0, T, is_s):
        with ExitStack() as es:
            att = attention_gen(l, si, t0, T, is_s, es, [[PSF[2], PSF[3]]])
            next(att)
            old = psf_banks[0]
            psf_banks[0] = [4, 5]
            n_att = (T // min(512, T)) * 8 * ((T + PAST) // 128)
            n_side = (T // 256) * 22 + (T // 128) * 5
            run_weighted(att, side_gen(l, si, t0, T, is_s), max(1, int(0.9 * n_att / n_side)))
            psf_banks[0] = old
            S.barrier()

    def ret_chain(l, si, t0, T, is_s, d, es):
        ntl = T // 128
        pf = "rt%d_" % d
        Sf = sb(es, pf + "S", [64, 4, 128], F32)
        Sb_ = sb(es, pf + "Sb", [64, 4, 128], BF16)
        qTp = Pool_(es, pf + "qT", [64, 4, 128], BF16, 2)
        kTp = Pool_(es, pf + "kT", [64, 4, 128], BF16, 2)
        ktp = Pool_(es, pf + "k", [128, 256], BF16, 2)
        vp = Pool_(es, pf + "v", [128, 512], BF16, 2)
        itp = Pool_(es, pf + "it", [128, 512], BF16, 2)
        qdp = Pool_(es, pf + "qd", [64, 4, 128], BF16, 2)
        kdp = Pool_(es, pf + "kd", [128, 256], BF16, 2)
        op_ = Pool_(es, pf + "o", [128, 512], F32, 2)
        odst, orn = (ofs, "ofs") if d == 0 else (ofb, "ofb")
        if is_s:
            S.dma("sp", Sf[:], st_ret[l, d].rearrange("h k e -> k h e"), W=[Sf])
        else:
            S.op("dve", "memset", W=[Sf], ap=Sf[:], constant=0.0)
        S.op("act", "copy", R=[Sf], W=[Sb_], out=Sb_[:], in_=Sf[:])
        order = range(ntl) if d == 0 else range(ntl - 1, -1, -1)
        for i in order:
            tt = t0 // 128 + i
            c0 = tt * 128
            qT = qTp.next(); kT = kTp.next(); kt = ktp.next(); v = vp.next()
            S.dma("sp", qT[:], bqT[:, :, c0:c0 + 128].rearrange("h p t -> p h t"), R=[RX["bqT"][tt]], W=[qT])
            S.dma("sp", kT[:], bkT[:, :, c0:c0 + 128].rearrange("h p t -> p h t"), R=[RX["bkT"][tt]], W=[kT])
            S.dma("sp", kt[:], bk[c0:c0 + 128, :], R=[RX["bk"][tt]], W=[kt])
            S.dma("sp", v[:], bv[c0:c0 + 128, :], R=[RX["bv"][tt]], W=[v])
            ps = psf()
            for h in range(4):
                S.op("pe", "matmul", R=[kT, qT], W=[ps], out=ps[:, h * 128:(h + 1) * 128], lhsT=kT[:, h, :],
                     rhs=qT[:, h, :], start=True, stop=True)
            it = itp.next()
            S.op("dve", "tensor_tensor", R=[ps, rdmat], W=[it], out=it[:], in0=ps[:],
                 in1=rdmat[:, d, :, :].rearrange("p h i -> p (h i)"), op=ALU.mult)
            qd = qdp.next()
            S.op("pool", "tensor_tensor", R=[qT, rqdec], W=[qd], out=qd[:], in0=qT[:], in1=rqdec[:, d, :, :],
                 op=ALU.mult)
            kd = kdp.next()
            S.op("pool", "tensor_tensor", R=[kt, rkdec], W=[kd], out=kd[:].rearrange("p (h e) -> p h e", h=4),
                 in0=kt[:].rearrange("p (h e) -> p h e", h=4),
                 in1=rkdec[:, d, :].unsqueeze(2).to_broadcast([128, 4, 64]), op=ALU.mult)
            yield
            po = psf()
            for h in range(4):
                S.op("pe", "matmul", R=[it, v], W=[po], out=po[:, h * 128:(h + 1) * 128],
                     lhsT=it[:, h * 128:(h + 1) * 128], rhs=v[:, h * 128:(h + 1) * 128], start=True, stop=False)
                S.op("pe", "matmul", R=[qd, Sb_], W=[po], out=po[:, h * 128:(h + 1) * 128], lhsT=qd[:, h, :],
                     rhs=Sb_[:, h, :], start=False, stop=True)
            pS = psf()
            for h in range(4):
                S.op("pe", "matmul", R=[kd, v], W=[pS], out=pS[0:64, h * 128:(h + 1) * 128],
                     lhsT=kd[:, h * 64:(h + 1) * 64], rhs=v[:, h * 128:(h + 1) * 128], start=True, stop=True)
            S.op("dve", "tensor_tensor", R=[Sf, rcdec], W=[Sf], out=Sf[:], in0=Sf[:],
                 in1=rcdec[0:64, d * 4:d * 4 + 4].unsqueeze(2).to_broadcast([64, 4, 128]), op=ALU.mult)
            S.op("dve", "tensor_tensor", R=[Sf, pS], W=[Sf], out=Sf[:].rearrange("p h e -> p (h e)"),
                 in0=Sf[:].rearrange("p h e -> p (h e)"), in1=pS[0:64, :], op=ALU.add)
            S.op("act", "copy", R=[Sf], W=[Sb_], out=Sb_[:], in_=Sf[:])
            o = op_.next()
            S.op("act", "copy", R=[po], W=[o], out=o[:], in_=po[:])
            S.dma("pool", odst[c0:c0 + 128, :], o[:], R=[o], W=[RX[orn][tt]])
            yield
        if not is_s:
            S.dma("pool", nret_out[si - 1, l, d].rearrange("h k e -> k h e"), Sf[:], R=[Sf])

    def gdn_pre_gen(l, si, t0, T, is_s, es, G, nbuf):
        xin = Pool_(es, "gp_x", [128, 12, G + 4], F32, nbuf)
        acc = Pool_(es, "gp_a", [128, 12, G], F32, nbuf)
        sqp = Pool_(es, "gp_sq", [128, G], F32, nbuf)
        rsp = Pool_(es, "gp_rs", [128, G], F32, nbuf)
        nb = Pool_(es, "gp_nb", [128, 12, G], BF16, nbuf)
        sg = Pool_(es, "gp_sg", [128, 1024], BF16, nbuf)
        for g0 in range(0, T, G):
            x = xin.next()
            a = acc.next()
            lo = 2 if g0 == 0 else 0
            hi = 2 if g0 + G == T else 0
            if lo:
                S.op("pool", "memset", W=[x], ap=x[:, :, 0:2], constant=0.0)
            if hi:
                S.op("pool", "memset", W=[x], ap=x[:, :, G + 2:G + 4], constant=0.0)
            tl = [i for i in range((t0 + g0) // 128 - (0 if lo else 1), (t0 + g0 + G) // 128 + (0 if hi else 1))]
            S.dma("sp", x[:, :, lo:G + 4 - hi],
                  cT[:, t0 + g0 - 2 + lo:t0 + g0 + G + 2 - hi].rearrange("(c p) t -> p c t", p=128),
                  R=[RX["cT"][i] for i in tl], W=[x])
            for c in range(12):
                eng = "dve"
                S.op(eng, "tensor_scalar", R=[x, convw], W=[a], out=a[:, c, :], in0=x[:, c, 0:G],
                     scalar1=convw[:, c, 0:1], scalar2=None, op0=ALU.mult)
                for k in range(1, 5):
                    S.op(eng, "scalar_tensor_tensor", R=[x, convw, a], W=[a], out=a[:, c, :], in0=x[:, c, k:k + G],
                         scalar=convw[:, c, k:k + 1], in1=a[:, c, :], op0=ALU.mult, op1=ALU.add)
                S.op("act", "activation", R=[a], W=[a], out=a[:, c, :], in_=a[:, c, :], func=AF.Silu)
            yield
            n = nb.next()
            for c in range(8):
                sq = sqp.next()
                S.op("act", "activation", R=[a], W=[sq], out=sq[:], in_=a[:, c, :], func=AF.Square)
                ps = psf()
                S.op("pe", "matmul", R=[ones_f, sq], W=[ps], out=ps[:, 0:G], lhsT=ones_f[:], rhs=sq[:], start=True,
                     stop=True)
                rs = rsp.next()
                S.op("dve", "tensor_scalar", R=[ps], W=[rs], out=rs[:], in0=ps[:, 0:G], scalar1=EPS, scalar2=None,
                     op0=ALU.add)
                S.op("act", "activation", R=[rs], W=[rs], out=rs[:], in_=rs[:], func=AF.Sqrt)
                S.op("dve", "reciprocal", R=[rs], W=[rs], out=rs[:], in_=rs[:])
                if c < 4:
                    S.op("dve", "scalar_tensor_tensor", R=[a, rs], W=[n], out=n[:, c, :], in0=a[:, c, :],
                         scalar=float(128 ** -0.5), in1=rs[:], op0=ALU.mult, op1=ALU.mult)
                else:
                    S.op("dve", "tensor_tensor", R=[a, rs], W=[n], out=n[:, c, :], in0=a[:, c, :], in1=rs[:],
                         op=ALU.mult)
            S.op("pool", "tensor_copy", R=[a], W=[n], out=n[:, 8:12, :], in_=a[:, 8:12, :])
            tiles = list(range((t0 + g0) // 128, (t0 + g0 + G) // 128))
            S.dma("pool", gqT[:, :, t0 + g0:t0 + g0 + G].rearrange("h p t -> p h t"), n[:, 0:4, :], R=[n],
                  W=[RX["gqT"][i] for i in tiles])
            S.dma("pool", gkT[:, :, t0 + g0:t0 + g0 + G].rearrange("h p t -> p h t"), n[:, 4:8, :], R=[n],
                  W=[RX["gkT"][i] for i in tiles])
            for ti in range(G // 128):
                tt = (t0 + g0) // 128 + ti
                pb = psb()
                transposes_to(pb, n, [(n[:, 4 + c, ti * 128:(ti + 1) * 128], 128, c * 128) for c in range(8)])
                s = sg.next()
                S.op("act", "copy", R=[pb], W=[s], out=s[:], in_=pb[:])
                S.dma("pool", gk[tt * 128:(tt + 1) * 128, :], s[:, 0:512], R=[s], W=[RX["gk"][tt]])
                S.dma("pool", gv[tt * 128:(tt + 1) * 128, :], s[:, 512:1024], R=[s], W=[RX["gv"][tt]])
            yield

    def gdn_pre(l, si, t0, T, is_s):
        with ExitStack() as es:
            for _ in gdn_pre_gen(l, si, t0, T, is_s, es, min(512, T), 2):
                pass
            S.barrier()

    def pipeline(gens, depth):
        gens = list(gens)
        active = []
        while gens or active:
            while gens and len(active) < depth:
                active.append(gens.pop(0))
            for g in list(active):
                try:
                    next(g)
                except StopIteration:
                    active.remove(g)

    def run_weighted(main, side, ratio):
        main_alive = side_alive = True
        while main_alive or side_alive:
            if main_alive:
                for _ in range(ratio):
                    try:
                        next(main)
                    except StopIteration:
                        main_alive = False
                        break
            if side_alive:
                try:
                    next(side)
                except StopIteration:
                    side_alive = False

    def run_rr(gens):
        gens = list(gens)
        while gens:
            for g in list(gens):
                try:
                    next(g)
                except StopIteration:
                    gens.remove(g)

    def combine(t0, T, w_bc, zi, mi):
        for _ in combine_gen(t0, T, w_bc, zi, mi):
            pass

    def combine_gen(t0, T, w_bc, zi, mi):
        with ExitStack() as es:
            fa = Pool_(es, "cb_a", [128, 512], F32, 2)
            fb = Pool_(es, "cb_b", [128, 512], F32, 2)
            ngts = [ng_tiles(es, "cb%d_" % i) for i in range(2)]
            for i in range(T // 128):
                tt = t0 // 128 + i
                c0 = tt * 128
                a = fa.next(); b = fb.next()
                S.dma("sp", a[:], ofs[c0:c0 + 128, :], R=[RX["ofs"][tt]], W=[a])
                S.dma("sp", b[:], ofb[c0:c0 + 128, :], R=[RX["ofb"][tt]], W=[b])
                S.op("pool", "tensor_tensor", R=[a, b], W=[a], out=a[:], in0=a[:], in1=b[:], op=ALU.add)
                norm_gate_store(ngts[i % 2], a, a[:], w_bc, zi, mi, tt, 128)
                yield
            S.barrier()

    def gdn_scan(l, si, t0, T, is_s):
        with ExitStack() as es:
            run_rr([gdn_chain(l, si, t0, T, is_s, d, es) for d in range(2)])
            S.barrier()
        combine(t0, T, gdnw_bc, 2, 2)

    def gdn_chain(l, si, t0, T, is_s, d, es):
        ntl = T // 128
        pf = "gd%d_" % d
        Sf = sb(es, pf + "S", [128, 4, 128], F32)
        Sb_ = sb(es, pf + "Sb", [128, 4, 128], BF16)
        kTp = Pool_(es, pf + "kT", [128, 4, 128], BF16, 2)
        qTp = Pool_(es, pf + "qT", [128, 4, 128], BF16, 2)
        ktp = Pool_(es, pf + "k", [128, 512], BF16, 2)
        vtp = Pool_(es, pf + "v", [128, 512], BF16, 2)
        bgp = Pool_(es, pf + "bg", [128, 16], F32, 2)
        sm = Pool_(es, pf + "sm", [128, 8, 4], F32, 2)
        X1 = Pool_(es, pf + "X", [128, 4, 128], F32, 1)
        X2 = Pool_(es, pf + "X2", [128, 4, 128], F32, 1)
        Dn = Pool_(es, pf + "Dn", [128, 4, 128], F32, 1)
        Ea = Pool_(es, pf + "Ea", [128, 4, 128], F32, 1)
        Eb = Pool_(es, pf + "Eb", [128, 4, 128], F32, 1)
        Fq = Pool_(es, pf + "Fq", [128, 4, 128], F32, 1)
        Eg = Pool_(es, pf + "Eg", [128, 4, 128], F32, 1)
        Pp = Pool_(es, pf + "P", [128, 4, 128], F32, 2)
        PTp = Pool_(es, pf + "PT", [128, 4, 128], F32, 2)
        TTf = Pool_(es, pf + "TTf", [128, 4, 128], F32, 1)
        qkTp = Pool_(es, pf + "qkT", [128, 4, 128], BF16, 2)
        qgp = Pool_(es, pf + "qg", [128, 4, 128], BF16, 2)
        vbp = Pool_(es, pf + "vb", [128, 4, 128], F32, 1)
        kbp = Pool_(es, pf + "kb", [128, 4, 128], F32, 1)
        kdp = Pool_(es, pf + "kd", [128, 4, 128], BF16, 2)
        Up = Pool_(es, pf + "U", [128, 4, 128], F32, 2)
        WTp = Pool_(es, pf + "WT", [128, 4, 128], BF16, 2)
        vnp = Pool_(es, pf + "vn", [128, 128], BF16, 4)
        op_ = Pool_(es, pf + "o", [64, 512], F32, 2)
        odst, orn = (ofs, "ofs") if d == 0 else (ofb, "ofb")
        Sfr = [Res("gSf%d" % h) for h in range(4)]
        Sbr = [Res("gSb%d" % h) for h in range(4)]
        if is_s:
            S.dma("sp", Sf[:], st_gdn[l, d].rearrange("h k e -> k h e"), W=Sfr)
        else:
            S.op("dve", "memset", W=Sfr, ap=Sf[:], constant=0.0)
        S.op("act", "copy", R=Sfr, W=Sbr, out=Sb_[:], in_=Sf[:])
        idb = ident_f[:].unsqueeze(1).to_broadcast([128, 4, 128])
        v4 = lambda t: t[:].rearrange("p h b -> p (h b)")
        order = range(ntl) if d == 0 else range(ntl - 1, -1, -1)
        for i in order:
            tt = t0 // 128 + i
            c0 = tt * 128
            kT = kTp.next(); qT = qTp.next(); kt = ktp.next(); vt = vtp.next(); bg = bgp.next()
            S.dma("sp", kT[:], gkT[:, :, c0:c0 + 128].rearrange("h p t -> p h t"), R=[RX["gkT"][tt]], W=[kT])
            S.dma("sp", qT[:], gqT[:, :, c0:c0 + 128].rearrange("h p t -> p h t"), R=[RX["gqT"][tt]], W=[qT])
            S.dma("sp", kt[:], gk[c0:c0 + 128, :], R=[RX["gk"][tt]], W=[kt])
            S.dma("sp", vt[:], gv[c0:c0 + 128, :], R=[RX["gv"][tt]], W=[vt])
            S.dma("sp", bg[:], bgs[c0:c0 + 128, :], R=[RX["bgs"][tt]], W=[bg])
            s = sm.next()
            S.op("dve", "tensor_copy", R=[bg], W=[s], out=s[:, 0, :], in_=bg[:, 8 + d * 4:12 + d * 4])
            S.op("dve", "tensor_copy", R=[bg], W=[s], out=s[:, 1, :], in_=bg[:, d * 4:d * 4 + 4])
            pg = psf()
            S.op("pe", "matmul", R=[utri2, s], W=[pg], out=pg[:, 0:4], lhsT=utri2[:, d, :], rhs=s[:, 0, :],
                 start=True, stop=True)
            S.op("pe", "matmul", R=[bd1, s], W=[pg], out=pg[:, 4:8], lhsT=bd1[:], rhs=s[:, 0, :], start=True, stop=True)
            S.op("pe", "matmul", R=[selc, s], W=[pg], out=pg[:, 8:12], lhsT=selc[:, 0, :], rhs=s[:, 0, :],
                 start=True, stop=True)
            S.op("pe", "matmul", R=[selc, s], W=[pg], out=pg[:, 12:16], lhsT=selc[:, 1, :], rhs=s[:, 0, :],
                 start=True, stop=True)
            S.op("dve", "tensor_copy", R=[pg], W=[s], out=s[:, 2, :], in_=pg[:, 0:4])
            S.op("act", "activation", R=[pg], W=[s], out=s[:, 6:8, :].rearrange("p c h -> p (c h)"), in_=pg[:, 8:16],
                 func=AF.Exp)
            S.op("dve", "tensor_tensor", R=[pg], W=[s], out=s[:, 4, :], in0=pg[:, 4:8], in1=s[:, 2, :], op=ALU.subtract)
            S.op("act", "activation", R=[s], W=[s], out=s[:, 4, :], in_=s[:, 4, :], func=AF.Exp)
            S.op("act", "activation", R=[s], W=[s], out=s[:, 5, :], in_=s[:, 2, :], func=AF.Exp)
            S.op("dve", "tensor_tensor", R=[s], W=[s], out=s[:, 5, :], in0=s[:, 5, :], in1=s[:, 1, :], op=ALU.mult)
            ckpt('g0')
            yield
            x1 = X1.next(); x2 = X2.next()
            S.op("dve", "tensor_tensor", R=[ident_f, s], W=[x1], out=x1[:], in0=idb,
                 in1=s[:, 2, :].unsqueeze(2).to_broadcast([128, 4, 128]), op=ALU.mult)
            S.op("pool", "tensor_tensor", R=[ident_f, s], W=[x2], out=x2[:], in0=idb,
                 in1=s[:, 1, :].unsqueeze(2).to_broadcast([128, 4, 128]), op=ALU.mult)
            pR = psf(); pRb = psf()
            S.op("pe", "matmul", R=[ones_f, x1], W=[pR], out=pR[:], lhsT=ones_f[:], rhs=v4(x1), start=True, stop=True)
            S.op("pe", "matmul", R=[ones_f, x2], W=[pRb], out=pRb[:], lhsT=ones_f[:], rhs=v4(x2), start=True, stop=True)
            dn = Dn.next()
            S.op("dve", "tensor_tensor", R=[pR, s], W=[dn], out=dn[:], in0=pR[:].rearrange("p (h b) -> p h b", h=4),
                 in1=s[:, 2, :].unsqueeze(2).to_broadcast([128, 4, 128]), op=ALU.subtract)
            eg = Eg.next()
            S.op("act", "activation", R=[pR], W=[eg], out=v4(eg), in_=pR[:], func=AF.Exp)
            ea = Ea.next(); eb = Eb.next(); fq = Fq.next()
            S.op("dve", "tensor_scalar", R=[dn], W=[ea], out=ea[:], in0=dn[:], scalar1=-1.0, scalar2=0.0,
                 op0=ALU.mult, op1=ALU.min)
            S.op("dve", "tensor_scalar", R=[dn], W=[eb], out=eb[:], in0=dn[:], scalar1=0.0, scalar2=None, op0=ALU.min)
            S.op("act", "activation", R=[ea], W=[ea], out=ea[:], in_=ea[:], func=AF.Exp)
            S.op("act", "activation", R=[eb], W=[eb], out=eb[:], in_=eb[:], func=AF.Exp)
            S.op("dve", "tensor_tensor", R=[ea, gmask2], W=[ea], out=ea[:], in0=ea[:],
                 in1=gmask2[:, d, 0, :].unsqueeze(1).to_broadcast([128, 4, 128]), op=ALU.mult)
            S.op("dve", "tensor_tensor", R=[ea, s], W=[ea], out=ea[:], in0=ea[:],
                 in1=s[:, 1, :].unsqueeze(2).to_broadcast([128, 4, 128]), op=ALU.mult)
            S.op("pool", "tensor_tensor", R=[eb, gmask2], W=[fq], out=fq[:], in0=eb[:],
                 in1=gmask2[:, d, 2, :].unsqueeze(1).to_broadcast([128, 4, 128]), op=ALU.mult)
            S.op("pool", "tensor_tensor", R=[eb, gmask2], W=[eb], out=eb[:], in0=eb[:],
                 in1=gmask2[:, d, 1, :].unsqueeze(1).to_broadcast([128, 4, 128]), op=ALU.mult)
            S.op("dve", "tensor_tensor", R=[eb, pRb], W=[eb], out=eb[:], in0=eb[:],
                 in1=pRb[:].rearrange("p (h b) -> p h b", h=4), op=ALU.mult)
            ckpt('g1')
            yield
            qg = qgp.next()
            S.op("pool", "tensor_tensor", R=[qT, eg], W=[qg], out=qg[:], in0=qT[:], in1=eg[:], op=ALU.mult)
            pK = psf(); pQ = psf()
            for h in range(4):
                S.op("pe", "matmul", R=[kT], W=[pK], out=pK[:, h * 128:(h + 1) * 128], lhsT=kT[:, h, :], rhs=kT[:, h, :],
                     start=True, stop=True)
                S.op("pe", "matmul", R=[kT, qT], W=[pQ], out=pQ[:, h * 128:(h + 1) * 128], lhsT=kT[:, h, :],
                     rhs=qT[:, h, :], start=True, stop=True)
            P = Pp.next(); PT = PTp.next(); ttf = TTf.next(); qkT = qkTp.next()
            pKv = pK[:].rearrange("p (h b) -> p h b", h=4)
            S.op("dve", "tensor_tensor", R=[pK, ea], W=[P], out=P[:], in0=pKv, in1=ea[:], op=ALU.mult)
            S.op("dve", "tensor_tensor", R=[pK, eb], W=[PT], out=PT[:], in0=pKv, in1=eb[:], op=ALU.mult)
            S.op("pool", "tensor_tensor", R=[PT, ident_f], W=[ttf], out=ttf[:], in0=PT[:], in1=idb, op=ALU.add)
            S.op("dve", "tensor_tensor", R=[pQ, fq], W=[qkT], out=qkT[:], in0=pQ[:].rearrange("p (h b) -> p h b", h=4),
                 in1=fq[:], op=ALU.mult)
            ckpt('g2')
            yield
            for lev in range(1, 6):
                p1 = psf()
                for h in range(4):
                    S.op("pe", "matmul", R=[PT, P], W=[p1], out=p1[:, h * 128:(h + 1) * 128], lhsT=PT[:, h, :],
                         rhs=P[:, h, :], start=True, stop=True)
                Pn = Pp.next()
                S.op("dve", "tensor_copy", R=[p1], W=[Pn], out=v4(Pn), in_=p1[:])
                p3 = psf()
                for h in range(4):
                    S.op("pe", "matmul", R=[Pn, ttf], W=[p3], out=p3[:, h * 128:(h + 1) * 128], lhsT=Pn[:, h, :],
                         rhs=ttf[:, h, :], start=True, stop=True)
                if lev < 5:
                    p2 = psf()
                    for h in range(4):
                        if os.environ.get("KNOTR"):
                            S.op("pe", "matmul", R=[PT, P], W=[p2], out=p2[:, h * 128:(h + 1) * 128], lhsT=P[:, h, :],
                                 rhs=PT[:, h, :], start=True, stop=True)
                        else:
                            S.op("pe", "transpose", R=[Pn, ident_f], W=[p2], out=p2[:, h * 128:(h + 1) * 128],
                                 in_=Pn[:, h, :], identity=ident_f[:])
                    PTn = PTp.next()
                    S.op("act", "copy", R=[p2], W=[PTn], out=v4(PTn), in_=p2[:])
                S.op("dve", "tensor_tensor", R=[ttf, p3], W=[ttf], out=v4(ttf), in0=v4(ttf), in1=p3[:], op=ALU.add)
                P = Pn
                if lev < 5:
                    PT = PTn
                ckpt('g3')
                yield
            vb = vbp.next(); kb = kbp.next(); kd = kdp.next()
            S.op("dve", "tensor_tensor", R=[vt, s], W=[vb], out=vb[:], in0=vt[:].rearrange("p (h e) -> p h e", h=4),
                 in1=s[:, 1, :].unsqueeze(2).to_broadcast([128, 4, 128]), op=ALU.mult)
            S.op("pool", "tensor_tensor", R=[kt, s], W=[kb], out=kb[:], in0=kt[:].rearrange("p (h e) -> p h e", h=4),
                 in1=s[:, 5, :].unsqueeze(2).to_broadcast([128, 4, 128]), op=ALU.mult)
            S.op("pool", "tensor_tensor", R=[kt, s], W=[kd], out=kd[:], in0=kt[:].rearrange("p (h e) -> p h e", h=4),
                 in1=s[:, 4, :].unsqueeze(2).to_broadcast([128, 4, 128]), op=ALU.mult)
            U = Up.next(); WT = WTp.next()
            pU = psf(); pW = psf()
            for h in range(4):
                S.op("pe", "matmul", R=[ttf, vb], W=[pU], out=pU[:, h * 128:(h + 1) * 128], lhsT=ttf[:, h, :],
                     rhs=vb[:, h, :], start=True, stop=True)
                S.op("pe", "matmul", R=[kb, ttf], W=[pW], out=pW[:, h * 128:(h + 1) * 128], lhsT=kb[:, h, :],
                     rhs=ttf[:, h, :], start=True, stop=True)
            S.op("dve", "tensor_copy", R=[pU], W=[U], out=v4(U), in_=pU[:])
            S.op("act", "copy", R=[pW], W=[WT], out=v4(WT), in_=pW[:])
            ckpt('g4')
            yield
            for cb in ((0, 1) if d == 0 else (1, 0)):
                rows = slice(cb * 64, cb * 64 + 64)
                cols = slice(cb * 64, cb * 64 + 64)
                po = PSF[4 + d]
                for h in range(4):
                    pa = psf()
                    S.op("pe", "matmul", R=[WT, Sbr[h]], W=[pa], out=pa[:, 0:128], lhsT=WT[:, h, :], rhs=Sb_[:, h, :],
                         start=True, stop=True)
                    vn = vnp.next()
                    S.op("dve", "tensor_tensor", R=[U, pa], W=[vn], out=vn[rows, :], in0=U[rows, h, :],
                         in1=pa[rows, 0:128], op=ALU.subtract)
                    S.op("pe", "matmul", R=[qg, Sbr[h]], W=[po], out=po[0:64, h * 128:(h + 1) * 128], lhsT=qg[:, h, cols],
                         rhs=Sb_[:, h, :], start=True, stop=False)
                    S.op("pe", "matmul", R=[qkT, vn], W=[po], out=po[0:64, h * 128:(h + 1) * 128],
                         lhsT=qkT[rows, h, cols], rhs=vn[rows, :], start=False, stop=True)
                    pS = psf()
                    S.op("pe", "matmul", R=[kd, vn], W=[pS], out=pS[:, 0:128], lhsT=kd[rows, h, :], rhs=vn[rows, :],
                         start=True, stop=True)
                    S.op("dve", "scalar_tensor_tensor", R=[Sfr[h], s, pS], W=[Sfr[h]], out=Sf[:, h, :], in0=Sf[:, h, :],
                         scalar=s[:, 6 + cb, h:h + 1], in1=pS[:, 0:128], op0=ALU.mult, op1=ALU.add)
                    S.op("act", "copy", R=[Sfr[h]], W=[Sbr[h]], out=Sb_[:, h, :], in_=Sf[:, h, :])
                    if h % 2 == 1:
                        ckpt('g5')
                        yield
                r0 = c0 + cb * 64
                o = op_.next()
                S.op("act", "copy", R=[po], W=[o], out=o[:], in_=po[0:64, :])
                S.dma("pool", odst[r0:r0 + 64, :], o[:], R=[o], W=[RX[orn][tt]])
                ckpt('g6')
                yield
        if not is_s:
            S.dma("pool", ngdn_out[si - 1, l, d].rearrange("h k e -> k h e"), Sf[:], R=Sfr)

    def phaseE(l, xsrc, xdst):
        with ExitStack() as es:
            wbr = sb(es, "pe_wbr", [128, 12, D], BF16)
            wo = sb(es, "pe_wo", [128, 8, D], BF16)
            wst = Pool_(es, "pe_wst", [128, 4, D], F32, 2)
            gtp = Pool_(es, "pe_gt", [128, 12, 128], BF16, 2)
            mgp = Pool_(es, "pe_mg", [128, 3 * D], BF16, 2)
            mp = Pool_(es, "pe_m", [128, D], F32, 2)
            mbp = Pool_(es, "pe_mb", [128, D], BF16, 2)
            mTp = Pool_(es, "pe_mT", [128, 8, 128], BF16, 2)
            xp = Pool_(es, "pe_x", [128, D], F32, 2)
            tp = Pool_(es, "pe_t", [128, 512], F32, 2)
            for q in range(3):
                w = wst.next()
                S.dma("sp", w[:], w_branch[l, q].rearrange("(c p) n -> p c n", p=128), W=[w])
                S.op("dve" if q % 2 == 0 else "pool", "tensor_copy", R=[w], W=[wbr], out=wbr[:, q * 4:q * 4 + 4, :],
                     in_=w[:])
            for q in range(2):
                w = wst.next()
                S.dma("sp", w[:], w_out[l].rearrange("(c p) n -> p c n", p=128)[:, q * 4:q * 4 + 4, :], W=[w])
                S.op("dve" if q % 2 == 0 else "pool", "tensor_copy", R=[w], W=[wo], out=wo[:, q * 4:q * 4 + 4, :],
                     in_=w[:])
            for tt in range(NT):
                j = 0 if tt < NTS else 1
                c0 = tt * 128
                gt = gtp.next(); mgt = mgp.next(); m = mp.next(); xt = xp.next()
                S.dma("sp", gt[:], gT[:, :, c0:c0 + 128].rearrange("m (c p) t -> p (m c) t", p=128),
                      R=[RX["gT0"][tt], RX["gT1"][tt], RX["gT2"][tt]], W=[gt])
                S.dma("sp", mgt[:], mg[c0:c0 + 128, :], R=[RX["mg"][tt]], W=[mgt])
                S.dma("sp", xt[:], xsrc[c0:c0 + 128, :], R=[RX["x"][tt]] if l > 0 else [], W=[xt])
                for q in range(3):
                    for nb in range(2):
                        ps = psf()
                        for c in range(4):
                            S.op("pe", "matmul", R=[gt, wbr], W=[ps], out=ps[:], lhsT=gt[:, q * 4 + c, :],
                                 rhs=wbr[:, q * 4 + c, nb * 512:(nb + 1) * 512], start=(c == 0), stop=(c == 3))
                        cols = slice(nb * 512, (nb + 1) * 512)
                        if q == 0:
                            S.op("dve", "tensor_tensor", R=[ps, mgt], W=[m], out=m[:, cols], in0=ps[:],
                                 in1=mgt[:, cols], op=ALU.mult)
                        else:
                            t = tp.next()
                            S.op("dve", "tensor_tensor", R=[ps, mgt], W=[t], out=t[:], in0=ps[:],
                                 in1=mgt[:, q * D + nb * 512:q * D + (nb + 1) * 512], op=ALU.mult)
                            S.op("pool", "tensor_tensor", R=[t, m], W=[m], out=m[:, cols], in0=m[:, cols], in1=t[:],
                                 op=ALU.add)
                mb = mbp.next()
                S.op("act", "copy", R=[m], W=[mb], out=mb[:], in_=m[:])
                pb = psb()
                transposes_to(pb, mb, [(mb[:, c * 128:(c + 1) * 128], 128, c * 128) for c in range(8)])
                mT = mTp.next()
                S.op("act", "copy", R=[pb], W=[mT], out=mT[:].rearrange("p c t -> p (c t)"), in_=pb[:])
                for nb in range(2):
                    ps = psf()
                    for c in range(8):
                        S.op("pe", "matmul", R=[mT, wo], W=[ps], out=ps[:], lhsT=mT[:, c, :],
                             rhs=wo[:, c, nb * 512:(nb + 1) * 512], start=(c == 0), stop=(c == 7))
                    cols = slice(nb * 512, (nb + 1) * 512)
                    t = tp.next()
                    S.op("dve", "tensor_tensor", R=[ps, gate_bc], W=[t], out=t[:], in0=ps[:], in1=gate_bc[:, j, cols],
                         op=ALU.mult)
                    S.op("pool", "tensor_tensor", R=[t, xt], W=[xt], out=xt[:, cols], in0=xt[:, cols], in1=t[:],
                         op=ALU.add)
                S.dma("pool", xdst[c0:c0 + 128, :], xt[:], R=[xt], W=[RX["x"][tt]])
            S.barrier()

    try:
        for l in range(depth):
            layer(l)
    except StopBuild as e:
        print("build stopped at", e)
        root2 = None
    S.barrier()
    S.emit()
    build.stats = (S.n_instr, S.n_wait)
    return nc


_CACHE = {}


def _get_nc(T_S, depth, debug):
    key = (T_S, depth, debug)
    if key not in _CACHE:
        _CACHE[key] = build(T_S, depth, debug)
    return _CACHE[key]


def run_cfg(inp, T_S, depth, debug=False):
    f = lambda a: np.ascontiguousarray(np.asarray(a), dtype=np.float32)
    xs = f(inp["x_sample"])
    xp = f(inp["x_prompt"])
    n_core = 8
    nsb = xs.shape[0]
    consts = make_consts(T_S)
    shared = {
        "norm_w": f(inp["norm_w"])[:depth], "w_ada": f(inp["w_ada"])[:depth], "b_ada": f(inp["b_ada"])[:depth],
        "w_in": f(inp["w_in"])[:depth], "qk_norm_w": f(inp["qk_norm_w"])[:depth],
        "diff_lambda": f(inp["diff_lambda"])[:depth], "subln_w": f(inp["subln_w"])[:depth],
        "ret_decay": f(inp["ret_decay"])[:depth].reshape(depth, 8), "ret_norm_w": f(inp["ret_norm_w"])[:depth],
        "conv_w": f(inp["conv_w"])[:depth], "gdn_a_log": f(inp["gdn_a_log"])[:depth].reshape(depth, 8),
        "gdn_dt_bias": f(inp["gdn_dt_bias"])[:depth].reshape(depth, 8), "gdn_norm_w": f(inp["gdn_norm_w"])[:depth],
        "w_branch": f(inp["w_branch"])[:depth], "w_out": f(inp["w_out"])[:depth],
    }
    for k, v in consts.items():
        shared["c_" + k] = v
    ck = f(inp["cache_attn_k"])
    cv = f(inp["cache_attn_v"])
    sr = f(inp["state_ret"])
    sgd = f(inp["state_gdn"])
    c = f(inp["c"])
    cctx = f(inp["c_ctx"])
    in_maps = []
    for core in range(n_core):
        b = core % nsb
        m = dict(shared)
        m["x_in"] = np.ascontiguousarray(np.concatenate(
            [xs[b, :T_S]] + [xp[core * NPR + i] for i in range(NPR)], axis=0))
        m["cond"] = np.ascontiguousarray(np.stack([c[b], cctx], axis=0))
        m["cache_k"] = np.ascontiguousarray(ck[b, :depth].reshape(depth, PAST, 512))
        m["cache_v"] = np.ascontiguousarray(cv[b, :depth].reshape(depth, PAST, 512))
        m["st_ret"] = np.ascontiguousarray(sr[b, :depth])
        m["st_gdn"] = np.ascontiguousarray(sgd[b, :depth])
        in_maps.append(m)
    nc = _get_nc(T_S, depth, debug)
    res = run_bass_kernel_spmd(nc, in_maps, core_ids=list(range(n_core)))
    R = res.results
    y_s = np.stack([np.asarray(R[b]["y"])[:T_S] for b in range(nsb)], axis=0).astype(np.float32)
    y_p = np.stack([np.asarray(R[core]["y"])[T_S + i * TP:T_S + (i + 1) * TP]
                    for core in range(n_core) for i in range(NPR)], axis=0).astype(np.float32)
    nk = np.concatenate([np.asarray(R[core]["nk"]) for core in range(n_core)], axis=0).astype(np.float32)
    nv = np.concatenate([np.asarray(R[core]["nv"]) for core in range(n_core)], axis=0).astype(np.float32)
    nret = np.concatenate([np.asarray(R[core]["nret"]) for core in range(n_core)], axis=0).astype(np.float32)
    ngdn = np.concatenate([np.asarray(R[core]["ngdn"]) for core in range(n_core)], axis=0).astype(np.float32)
    nk = nk.reshape(n_core * NPR, depth, TP, 4, 2, 64)
    nv = nv.reshape(n_core * NPR, depth, TP, 4, 128)
    outs = (y_p, y_s, nk, nv, nret, ngdn)
    if debug:
        return outs, R
    return outs


def kernel(**inputs):
    return run_cfg(inputs, 4096, DEPTH, False)
```

```python
import math
from contextlib import ExitStack
import numpy as np
import ml_dtypes
import concourse.bass as bass
import concourse.mybir as mybir
from concourse.bass_utils import run_bass_kernel_spmd

F32 = mybir.dt.float32
BF16 = mybir.dt.bfloat16
AF = mybir.ActivationFunctionType
ALU = mybir.AluOpType
AX = mybir.AxisListType

D = 1024
DEPTH = 4
TP = 256
NPR = 2
PAST = 256
D_IN = 8720
EPS = 1e-6
CH = 64


import os


class StopBuild(Exception):
    pass


def ckpt(name):
    if os.environ.get("KSTOP", "") == name:
        raise StopBuild(name)


class Res:
    __slots__ = ("name", "w", "r", "excl")

    def __init__(self, name=""):
        self.name = name
        self.w = None
        self.r = {}
        self.excl = False


class Tile:
    def __init__(self, h, name, psum=False):
        self.h = h
        self.res = Res(name)
        self.res.excl = psum

    def __getitem__(self, k):
        return self.h[k]


def _res(x):
    out = []
    for t in x:
        if isinstance(t, (list, tuple)):
            out.extend(_res(t))
        elif isinstance(t, Res):
            out.append(t)
        else:
            out.append(t.res)
    return out


class Sched:
    ENG = ("pe", "act", "dve", "pool", "sp")

    def __init__(self, nc, n_dma_slots=8):
        self.nc = nc
        self.streams = {e: [] for e in self.ENG}
        self.sems = {}
        self.cnt = {}
        for e in ("pe", "act", "dve", "pool"):
            self.sems[e] = nc.alloc_semaphore("s_" + e)
            self.cnt[e] = 0
        self.nslots = n_dma_slots
        self.dq = {}
        for q in ("sp", "pool", "act"):
            slots = []
            for i in range(n_dma_slots):
                k = "d_%s%d" % (q, i)
                self.sems[k] = nc.alloc_semaphore(k)
                slots.append([k, 0])
            self.dq[q] = [slots, 0]
        self.seen = {e: {} for e in self.ENG}
        self.n_instr = 0
        self.n_wait = 0

    def _wait(self, eng, key, val):
        if val is None or val <= 0:
            return
        if eng == "pe" and key == "pe":
            return
        s = self.seen[eng]
        if s.get(key, 0) >= val:
            return
        s[key] = val
        sem = self.sems[key]
        self.streams[eng].append(lambda e, sem=sem, val=val: e.wait_ge(sem, val))
        self.n_wait += 1

    def _deps(self, eng, reads, writes, is_dma=False):
        for r in reads:
            if r.w is not None:
                self._wait(eng, r.w[0], r.w[1])
        for w in writes:
            if w.w is not None:
                if is_dma or not (w.w[0] == eng):
                    self._wait(eng, w.w[0], w.w[1])
            for k, v in w.r.items():
                if (not is_dma) and k == eng:
                    continue
                self._wait(eng, k, v)

    def _mark(self, key, val, reads, writes):
        for r in reads:
            if r.r.get(key, 0) < val:
                r.r[key] = val
        for w in writes:
            w.w = (key, val)
            w.r = {}

    def op(self, eng, method, R=(), W=(), **kw):
        reads = _res(R)
        writes = _res(W)
        ex = [r for r in reads if r.excl]
        if ex:
            reads = [r for r in reads if not r.excl]
            writes = writes + [r for r in ex if r not in writes]
        self._deps(eng, reads, writes)
        self.cnt[eng] += 1
        val = self.cnt[eng]
        sem = self.sems[eng]
        import traceback
        org = traceback.extract_stack(limit=3)[0]
        org = "%s:%d" % (org.name, org.lineno)

        def _f(e, m=method, kw=kw, sem=sem, org=org):
            try:
                return getattr(e, m)(**kw).then_inc(sem, 1)
            except Exception as ex:
                raise RuntimeError("emit failed at %s (%s): %s" % (org, m, ex)) from ex
        self.streams[eng].append(_f)
        self._mark(eng, val, reads, writes)
        self.n_instr += 1

    def dma(self, q, out, in_, R=(), W=(), **kw):
        reads = _res(R)
        writes = _res(W)
        slots, idx = self.dq[q]
        slot = slots[idx % self.nslots]
        self.dq[q][1] = idx + 1
        key = slot[0]
        if slot[1] > 0:
            self._wait(q, key, slot[1])
        self._deps(q, reads, writes, is_dma=True)
        slot[1] += 16
        val = slot[1]
        sem = self.sems[key]
        import traceback
        org = traceback.extract_stack(limit=3)[0]
        org = "%s:%d" % (org.name, org.lineno)

        def _f(e, out=out, in_=in_, sem=sem, kw=kw, org=org):
            try:
                return e.dma_start(out=out, in_=in_, **kw).then_inc(sem, 16)
            except Exception as ex:
                raise RuntimeError("dma emit failed at %s: %s" % (org, ex)) from ex
        self.streams[q].append(_f)
        self._mark(key, val, reads, writes)
        self.n_instr += 1

    def barrier(self):
        for e in self.ENG:
            for k in ("pe", "act", "dve", "pool"):
                if k != e:
                    self._wait(e, k, self.cnt[k])
            for q in self.dq:
                for slot in self.dq[q][0]:
                    if slot[1] > 0:
                        self._wait(e, slot[0], slot[1])

    def emit(self):
        nc = self.nc
        st = self.streams
        with nc.Block() as block:
            @block.sync
            def _(e):
                for f in st["sp"]:
                    f(e)

            @block.tensor
            def _(e):
                for f in st["pe"]:
                    f(e)

            @block.scalar
            def _(e):
                for f in st["act"]:
                    f(e)

            @block.vector
            def _(e):
                for f in st["dve"]:
                    f(e)

            @block.gpsimd
            def _(e):
                for f in st["pool"]:
                    f(e)


def make_consts(T_S):
    c = {}
    c["ident"] = np.eye(128, dtype=np.float32)
    n_rows = T_S // 64
    row = np.repeat(np.arange(n_rows, dtype=np.float32), 64)
    col = np.tile(np.arange(64, dtype=np.float32), n_rows)
    inv = (1.0 / (10000.0 ** (np.arange(16, dtype=np.float32) / 16))).astype(np.float32)
    ar = row[:, None] * inv
    ac = col[:, None] * inv
    ang = np.concatenate([ar, ar, ac, ac], axis=-1).astype(np.float32)
    cos = np.cos(ang).astype(np.float32)
    sin = np.sin(ang).astype(np.float32)
    sgn = np.tile(np.concatenate([-np.ones(16), np.ones(16)]), 2).astype(np.float32)
    c["ropec"] = cos
    c["ropes"] = (sin * sgn).astype(np.float32)
    a = np.arange(64)[:, None]
    b = np.arange(64)[None, :]
    low = (a > b).astype(np.float32)
    up = (b > a).astype(np.float32)
    upi = (b >= a).astype(np.float32)
    lowi = (a >= b).astype(np.float32)
    gm = np.zeros((64, 2, 3, 64), np.float32)
    gm[:, 0, 0] = -low
    gm[:, 0, 1] = -up
    gm[:, 0, 2] = upi
    gm[:, 1, 0] = -up
    gm[:, 1, 1] = -low
    gm[:, 1, 2] = lowi
    c["gmask"] = gm
    a2 = np.arange(128)[:, None]
    b2 = np.arange(128)[None, :]
    same = (a2 // 64 == b2 // 64)
    gm2 = np.zeros((128, 2, 3, 128), np.float32)
    gm2[:, 0, 0] = -1.0 * ((a2 > b2) & same)
    gm2[:, 0, 1] = -1.0 * ((b2 > a2) & same)
    gm2[:, 0, 2] = ((b2 >= a2) & same)
    gm2[:, 1, 0] = -1.0 * ((b2 > a2) & same)
    gm2[:, 1, 1] = -1.0 * ((a2 > b2) & same)
    gm2[:, 1, 2] = ((a2 >= b2) & same)
    c["gmask2"] = gm2
    ut2 = np.zeros((128, 2, 128), np.float32)
    ut2[:, 0] = ((a2 <= b2) & same)
    ut2[:, 1] = ((a2 >= b2) & same)
    c["utri2"] = ut2
    c["bd1"] = same.astype(np.float32)
    sel = np.zeros((128, 2, 128), np.float32)
    sel[0:64, 0, :] = 1.0
    sel[64:128, 1, :] = 1.0
    c["selc"] = sel
    ut = np.zeros((64, 2, 64), np.float32)
    ut[:, 0] = (a <= b).astype(np.float32)
    ut[:, 1] = (a >= b).astype(np.float32)
    c["utri"] = ut
    j = np.arange(128)[:, None].astype(np.float32)
    i = np.arange(128)[None, :].astype(np.float32)
    rr = np.zeros((128, 2, 128), np.float32)
    rm = np.zeros((128, 2, 128), np.float32)
    rr[:, 0] = np.maximum(i - j, 0)
    rm[:, 0] = (i >= j)
    rr[:, 1] = np.maximum(j - i, 0)
    rm[:, 1] = (j >= i)
    c["rrel"] = rr
    c["rmask"] = rm
    rq = np.zeros((64, 2, 128), np.float32)
    rq[:, 0] = (np.arange(128) + 1.0)[None, :]
    rq[:, 1] = (128.0 - np.arange(128))[None, :]
    c["rqexp"] = rq
    rk = np.zeros((128, 2), np.float32)
    rk[:, 0] = 127.0 - np.arange(128)
    rk[:, 1] = np.arange(128)
    c["rkexp"] = rk
    return c


CONST_SHAPES = lambda T_S: {k: v.shape for k, v in make_consts(T_S).items()}


def build(T_S=4096, depth=DEPTH, debug=False):
    nc = bass.Bass("TRN2", target_bir_lowering=False)
    S = Sched(nc)
    TT = T_S + NPR * TP
    NT = TT // 128
    NTS = T_S // 128
    seqs = [(0, T_S, True)] + [(T_S + i * TP, TP, False) for i in range(NPR)]
    lam_inits = [0.8 - 0.6 * math.exp(-0.3 * l) for l in range(depth)]

    def din(name, shape, dt=F32):
        return nc.dram_tensor(name, list(shape), dt, kind="ExternalInput").ap()

    def dout(name, shape, dt=F32):
        return nc.dram_tensor(name, list(shape), dt, kind="ExternalOutput").ap()

    def scr(name, shape, dt):
        if debug:
            return nc.dram_tensor(name, list(shape), dt, kind="ExternalOutput").ap()
        return nc.dram_tensor(name, list(shape), dt).ap()

    x_in = din("x_in", [TT, D])
    cond = din("cond", [2, D])
    cache_k = din("cache_k", [depth, PAST, 512])
    cache_v = din("cache_v", [depth, PAST, 512])
    st_ret = din("st_ret", [depth, 2, 4, 64, 128])
    st_gdn = din("st_gdn", [depth, 2, 4, 128, 128])
    norm_w = din("norm_w", [depth, D])
    w_ada = din("w_ada", [depth, D, 3 * D])
    b_ada = din("b_ada", [depth, 3 * D])
    w_in = din("w_in", [depth, D, D_IN])
    qk_norm_w = din("qk_norm_w", [depth, 2, 64])
    diff_lambda = din("diff_lambda", [depth, 4, 64])
    subln_w = din("subln_w", [depth, 128])
    ret_decay = din("ret_decay", [depth, 8])
    ret_norm_w = din("ret_norm_w", [depth, 128])
    conv_w = din("conv_w", [depth, 5, 1536])
    gdn_a_log = din("gdn_a_log", [depth, 8])
    gdn_dt_bias = din("gdn_dt_bias", [depth, 8])
    gdn_norm_w = din("gdn_norm_w", [depth, 128])
    w_branch = din("w_branch", [depth, 3, 512, D])
    w_out = din("w_out", [depth, D, D])
    cst = {k: din("c_" + k, shp) for k, shp in CONST_SHAPES(T_S).items()}
    y_out = dout("y", [TT, D])
    nk_out = dout("nk", [NPR, depth, TP, 512])
    nv_out = dout("nv", [NPR, depth, TP, 512])
    nret_out = dout("nret", [NPR, depth, 2, 4, 64, 128])
    ngdn_out = dout("ngdn", [NPR, depth, 2, 4, 128, 128])
    xcur = scr("xcur", [TT, D], F32)
    aqT = scr("aqT", [4, 128, TT], BF16)
    akT = scr("akT", [4, 128, TT], BF16)
    av = scr("av", [TT, 512], BF16)
    zg = scr("zg", [3, TT, 512], BF16)
    bqT = scr("bqT", [4, 64, TT], BF16)
    bkT = scr("bkT", [4, 64, TT], BF16)
    bk = scr("bk", [TT, 256], BF16)
    bv = scr("bv", [TT, 512], BF16)
    cT = scr("cT", [1536, TT], F32)
    gqT = scr("gqT", [4, 128, TT], BF16)
    gkT = scr("gkT", [4, 128, TT], BF16)
    gk = scr("gk", [TT, 512], BF16)
    gv = scr("gv", [TT, 512], BF16)
    bgs = scr("bgs", [TT, 16], F32)
    mg = scr("mg", [TT, 3 * D], BF16)
    ofs = scr("ofs", [TT, 512], F32)
    ofb = scr("ofb", [TT, 512], F32)
    gT = scr("gT", [3, 512, TT], BF16)

    def tres(n):
        return [Res("%s%d" % (n, i)) for i in range(NT)]
    RX = {n: tres(n) for n in ("x", "aqT", "akT", "av", "zg0", "zg1", "zg2", "bqT", "bkT", "bk", "bv", "cT",
                               "gqT", "gkT", "gk", "gv", "bgs", "mg", "ofs", "ofb", "gT0", "gT1", "gT2")}
    R_out = Res("outs")

    uid = [0]

    def sb(es, name, shape, dt):
        uid[0] += 1
        nm = "%s_u%d" % (name, uid[0])
        return Tile(es.enter_context(nc.sbuf_tensor(nm, list(shape), dt)), nm)

    class Pool_:
        def __init__(self, es, name, shape, dt, n):
            self.t = [sb(es, "%s_%d" % (name, i), shape, dt) for i in range(n)]
            self.i = 0

        def next(self):
            t = self.t[self.i % len(self.t)]
            self.i += 1
            return t

    root = ExitStack()
    PSF = [Tile(nc.alloc_psum_tensor("psf%d" % i, [128, 512], F32), "psf%d" % i, True) for i in range(6)]
    PSB = [Tile(nc.alloc_psum_tensor("psb%d" % i, [128, 1024], BF16), "psb%d" % i, True) for i in range(2)]
    psc = [0, 0]

    psf_banks = [[0, 1, 2, 3]]

    def psf():
        psc[0] += 1
        bk = psf_banks[0]
        return PSF[bk[psc[0] % len(bk)]]

    def psb():
        psc[1] += 1
        return PSB[psc[1] % 2]

    ident_f = sb(root, "ident_f", [128, 128], F32)
    ident_b = sb(root, "ident_b", [128, 128], BF16)
    ones_f = sb(root, "ones_f", [128, 128], F32)
    ropec = sb(root, "ropec", [128, NTS, 64], F32)
    ropes = sb(root, "ropes", [128, NTS, 64], F32)
    gmask = sb(root, "gmask", [64, 2, 3, 64], F32)
    utri = sb(root, "utri", [64, 2, 64], F32)
    gmask2 = sb(root, "gmask2", [128, 2, 3, 128], F32)
    utri2 = sb(root, "utri2", [128, 2, 128], F32)
    bd1 = sb(root, "bd1", [128, 128], F32)
    selc = sb(root, "selc", [128, 2, 128], F32)
    rrel = sb(root, "rrel", [128, 2, 128], F32)
    rmask = sb(root, "rmask", [128, 2, 128], F32)
    rqexp = sb(root, "rqexp", [64, 2, 128], F32)
    rkexp = sb(root, "rkexp", [128, 2], F32)
    gate_bc = sb(root, "gate_bc", [128, 2, D], F32)
    wq_bc = sb(root, "wq_bc", [128, 2, 64], F32)
    subw_bc = sb(root, "subw_bc", [128, 128], F32)
    retw_bc = sb(root, "retw_bc", [128, 128], F32)
    gdnw_bc = sb(root, "gdnw_bc", [128, 128], F32)
    lamt = sb(root, "lamt", [128, 8], F32)
    dl_bc = sb(root, "dl_bc", [128, 4, 64], F32)
    lg_bc = sb(root, "lg_bc", [128, 8], F32)
    rdmat = sb(root, "rdmat", [128, 2, 4, 128], F32)
    rqdec = sb(root, "rqdec", [64, 2, 4, 128], F32)
    rkdec = sb(root, "rkdec", [128, 2, 4], F32)
    rcdec = sb(root, "rcdec", [128, 8], F32)
    negA_bc = sb(root, "negA_bc", [128, 8], F32)
    dtb_bc = sb(root, "dtb_bc", [128, 8], F32)
    convw = sb(root, "convw", [128, 12, 5], F32)
    tmp8 = sb(root, "tmp8", [128, 8], F32)

    S.dma("sp", ident_f[:], cst["ident"], W=[ident_f])
    S.op("dve", "tensor_copy", R=[ident_f], W=[ident_b], out=ident_b[:], in_=ident_f[:])
    S.op("dve", "memset", W=[ones_f], ap=ones_f[:], constant=1.0)
    S.dma("sp", ropec[:], cst["ropec"].rearrange("(n p) d -> p n d", p=128), W=[ropec])
    S.dma("sp", ropes[:], cst["ropes"].rearrange("(n p) d -> p n d", p=128), W=[ropes])
    S.dma("sp", gmask[:], cst["gmask"], W=[gmask])
    S.dma("sp", utri[:], cst["utri"], W=[utri])
    S.dma("sp", gmask2[:], cst["gmask2"], W=[gmask2])
    S.dma("sp", utri2[:], cst["utri2"], W=[utri2])
    S.dma("sp", bd1[:], cst["bd1"], W=[bd1])
    S.dma("sp", selc[:], cst["selc"], W=[selc])
    S.dma("sp", rrel[:], cst["rrel"], W=[rrel])
    S.dma("sp", rmask[:], cst["rmask"], W=[rmask])
    S.dma("sp", rqexp[:], cst["rqexp"], W=[rqexp])
    S.dma("sp", rkexp[:], cst["rkexp"], W=[rkexp])

    def rsqrt_inplace(t, ap, scale, eps):
        S.op("dve", "tensor_scalar", R=[t], W=[t], out=ap, in0=ap, scalar1=scale, scalar2=eps,
             op0=ALU.mult, op1=ALU.add)
        S.op("act", "activation", R=[t], W=[t], out=ap, in_=ap, func=AF.Sqrt)
        S.op("dve", "reciprocal", R=[t], W=[t], out=ap, in_=ap)

    def transposes_to(ps_t, src_t, blocks, rows=128):
        for (sap, w, off) in blocks:
            S.op("pe", "transpose", R=[src_t, ident_b], W=[ps_t], out=ps_t[0:w, off:off + rows], in_=sap,
                 identity=ident_b[0:rows, 0:rows])

    def layer(l):
        last = (l == depth - 1)
        xsrc = x_in if l == 0 else xcur
        xdst = y_out if last else xcur
        esH = ExitStack()
        hT = sb(esH, "hT", [128, 8, TT], BF16)
        esA = ExitStack()
        A_bc = sb(esA, "A_bc", [128, 2, D], F32)
        sh_bc = sb(esA, "sh_bc", [128, 2, D], F32)
        with ExitStack() as es:
            nw_bc = sb(es, "nw_bc", [128, D], F32)
            cs = sb(es, "cs", [128, 2, 8], F32)
            rep = sb(es, "rep", [128, 16, 128], F32)
            bada = sb(es, "bada", [1, 3 * D], F32)
            wa = Pool_(es, "wa", [128, 8, 512], F32, 2)
            S.dma("sp", nw_bc[:], norm_w[l].partition_broadcast(128), W=[nw_bc])
            S.dma("sp", cs[:], cond.rearrange("j (p c) -> p j c", c=8), W=[cs])
            S.dma("sp", bada[:], b_ada[l:l + 1, :], W=[bada])
            S.dma("sp", wq_bc[:],
                  qk_norm_w[l].rearrange("a d -> (a d)").partition_broadcast(128).rearrange("p (a d) -> p a d", a=2),
                  W=[wq_bc])
            S.dma("sp", subw_bc[:], subln_w[l].partition_broadcast(128), W=[subw_bc])
            S.dma("sp", retw_bc[:], ret_norm_w[l].partition_broadcast(128), W=[retw_bc])
            S.dma("sp", gdnw_bc[:], gdn_norm_w[l].partition_broadcast(128), W=[gdnw_bc])
            S.dma("sp", dl_bc[:], diff_lambda[l].rearrange("a d -> (a d)").partition_broadcast(128)
                  .rearrange("p (a d) -> p a d", a=4), W=[dl_bc])
            S.dma("sp", lg_bc[:], ret_decay[l].partition_broadcast(128), W=[lg_bc])
            S.dma("sp", negA_bc[:], gdn_a_log[l].partition_broadcast(128), W=[negA_bc])
            S.dma("sp", dtb_bc[:], gdn_dt_bias[l].partition_broadcast(128), W=[dtb_bc])
            for k in range(5):
                S.dma("sp", convw[:, :, k:k + 1], conv_w[l, k].rearrange("(c p o) -> p c o", p=128, o=1), W=[convw],
                      allow_slow_non_contiguous=True)
            S.op("dve", "tensor_scalar", R=[wq_bc], W=[wq_bc], out=wq_bc[:, 0, :], in0=wq_bc[:, 0, :],
                 scalar1=0.125, scalar2=None, op0=ALU.mult)
            S.op("dve", "tensor_scalar", R=[subw_bc], W=[subw_bc], out=subw_bc[:], in0=subw_bc[:],
                 scalar1=float(1.0 - lam_inits[l]), scalar2=None, op0=ALU.mult)
            S.op("dve", "tensor_tensor", R=[dl_bc], W=[dl_bc], out=dl_bc[:, 0, :], in0=dl_bc[:, 0, :],
                 in1=dl_bc[:, 1, :], op=ALU.mult)
            S.op("dve", "tensor_tensor", R=[dl_bc], W=[dl_bc], out=dl_bc[:, 2, :], in0=dl_bc[:, 2, :],
                 in1=dl_bc[:, 3, :], op=ALU.mult)
            S.op("dve", "tensor_reduce", R=[dl_bc], W=[lamt], out=lamt[:, 0:4], in_=dl_bc[:], axis=AX.X, op=ALU.add)
            S.op("act", "activation", R=[lamt], W=[lamt], out=lamt[:, 4:8], in_=lamt[:, 0:4], func=AF.Exp)
            S.op("dve", "tensor_tensor", R=[lamt], W=[lamt], out=lamt[:, 1:2], in0=lamt[:, 6:7], in1=lamt[:, 4:5],
                 op=ALU.subtract)
            S.op("dve", "tensor_scalar", R=[lamt], W=[lamt], out=lamt[:, 0:1], in0=lamt[:, 1:2],
                 scalar1=float(-lam_inits[l]), scalar2=None, op0=ALU.add)
            S.op("act", "activation", R=[lg_bc], W=[lg_bc], out=lg_bc[:], in_=lg_bc[:], func=AF.Exp, scale=-1.0)
            S.op("act", "activation", R=[lg_bc], W=[lg_bc], out=lg_bc[:], in_=lg_bc[:], func=AF.Ln, bias=1.0)
            S.op("dve", "tensor_scalar", R=[lg_bc], W=[lg_bc], out=lg_bc[:], in0=lg_bc[:], scalar1=-1.0,
                 scalar2=None, op0=ALU.mult)
            for d in range(2):
                for h in range(4):
                    u = d * 4 + h
                    S.op("act", "activation", R=[rrel, lg_bc], W=[rdmat], out=rdmat[:, d, h, :], in_=rrel[:, d, :],
                         func=AF.Exp, scale=lg_bc[:, u:u + 1])
                    S.op("dve", "tensor_tensor", R=[rdmat, rmask], W=[rdmat], out=rdmat[:, d, h, :],
                         in0=rdmat[:, d, h, :], in1=rmask[:, d, :], op=ALU.mult)
                    S.op("act", "activation", R=[rqexp, lg_bc], W=[rqdec], out=rqdec[:, d, h, :], in_=rqexp[:, d, :],
                         func=AF.Exp, scale=lg_bc[0:64, u:u + 1])
                    S.op("act", "activation", R=[rkexp, lg_bc], W=[rkdec], out=rkdec[:, d, h:h + 1],
                         in_=rkexp[:, d:d + 1], func=AF.Exp, scale=lg_bc[:, u:u + 1])
            S.op("act", "activation", R=[lg_bc], W=[rcdec], out=rcdec[:], in_=lg_bc[:], func=AF.Exp, scale=128.0)
            S.op("act", "activation", R=[negA_bc], W=[negA_bc], out=negA_bc[:], in_=negA_bc[:], func=AF.Exp)
            S.op("dve", "tensor_scalar", R=[negA_bc], W=[negA_bc], out=negA_bc[:], in0=negA_bc[:], scalar1=-1.0,
                 scalar2=None, op0=ALU.mult)
            S.op("act", "activation", R=[cs], W=[cs], out=cs[:], in_=cs[:], func=AF.Silu)
            S.op("dve", "tensor_copy", R=[cs], W=[rep], out=rep[:],
                 in_=cs[:].rearrange("p j c -> p (j c)").unsqueeze(2).to_broadcast([128, 16, 128]))
            for nb in range(6):
                w = wa.next()
                S.dma("sp", w[:], w_ada[l].rearrange("(p c) n -> p c n", c=8)[:, :, nb * 512:(nb + 1) * 512], W=[w])
                for j in range(2):
                    ps = psf()
                    for c in range(8):
                        S.op("pe", "matmul", R=[rep, w], W=[ps], out=ps[:], lhsT=rep[:, j * 8 + c, :], rhs=w[:, c, :],
                             start=(c == 0), stop=False)
                    S.op("pe", "matmul", R=[ones_f, bada], W=[ps], out=ps[:], lhsT=ones_f[0:1, :],
                         rhs=bada[0:1, nb * 512:(nb + 1) * 512], start=False, stop=True)
                    cols = slice((nb % 2) * 512, (nb % 2) * 512 + 512)
                    if nb < 2:
                        S.op("act", "copy", R=[ps], W=[sh_bc], out=sh_bc[:, j, cols], in_=ps[:])
                    elif nb < 4:
                        S.op("dve", "scalar_tensor_tensor", R=[ps, nw_bc], W=[A_bc], out=A_bc[:, j, cols], in0=ps[:],
                             scalar=1.0, in1=nw_bc[:, cols], op0=ALU.add, op1=ALU.mult)
                    else:
                        S.op("act", "copy", R=[ps], W=[gate_bc], out=gate_bc[:, j, cols], in_=ps[:])
            S.barrier()
        ckpt("A")
        with ExitStack() as es:
            xp = Pool_(es, "xB", [128, D], F32, 4)
            hp = Pool_(es, "hB", [128, D], F32, 4)
            hbp = Pool_(es, "hbB", [128, D], BF16, 4)
            junk = sb(es, "junkB", [128, D], F32)
            ssp = Pool_(es, "ssB", [128, 1], F32, 4)
            def tileB(tt):
                j = 0 if tt < NTS else 1
                xt = xp.next()
                ht = hp.next()
                hb = hbp.next()
                ss = ssp.next()
                S.dma("sp", xt[:], xsrc[tt * 128:(tt + 1) * 128, :], R=[RX["x"][tt]] if l > 0 else [], W=[xt])
                S.op("act", "activation", R=[xt], W=[junk, ss], out=junk[:], in_=xt[:], func=AF.Square, accum_out=ss[:])
                yield
                rsqrt_inplace(ss, ss[:], 1.0 / D, EPS)
                S.op("dve", "scalar_tensor_tensor", R=[xt, ss, A_bc], W=[ht], out=ht[:], in0=xt[:], scalar=ss[:, 0:1],
                     in1=A_bc[:, j, :], op0=ALU.mult, op1=ALU.mult)
                S.op("dve", "tensor_tensor", R=[ht, sh_bc], W=[hb], out=hb[:], in0=ht[:], in1=sh_bc[:, j, :], op=ALU.add)
                yield
                pb = psb()
                transposes_to(pb, hb, [(hb[:, c * 128:(c + 1) * 128], 128, c * 128) for c in range(8)])
                S.op("act", "copy", R=[pb], W=[hT], out=hT[:, :, tt * 128:(tt + 1) * 128],
                     in_=pb[:].rearrange("p (c t) -> p c t", c=8))
            pipeline([tileB(tt) for tt in range(NT)], 3)
            S.barrier()
        esA.close()
        ckpt("B")
        with ExitStack() as es:
            wf = Pool_(es, "wfC", [128, 4, 512], F32, 2)
            wbp = Pool_(es, "wbC", [128, 8, 512], BF16, 2)
            f1 = Pool_(es, "f1C", [128, 512], F32, 9)
            f2 = Pool_(es, "f2C", [128, 512], F32, 3)
            b1 = Pool_(es, "b1C", [128, 512], BF16, 4)
            st8 = Pool_(es, "st8C", [128, 16], F32, 4)
            stg = Pool_(es, "stgC", [128, 1024], BF16, 3)
            cfp = Pool_(es, "cfC", [128, 512], F32, 3)

            def load_wblock(c0, ncols):
                wb = wbp.next()
                for half in range(2):
                    w = wf.next()
                    S.dma("sp", w[:, :, 0:ncols],
                          w_in[l].rearrange("(c p) n -> p c n", p=128)[:, half * 4:half * 4 + 4, c0:c0 + ncols], W=[w])
                    S.op("dve" if half == 0 else "pool", "tensor_copy", R=[w], W=[wb],
                         out=wb[:, half * 4:half * 4 + 4, 0:ncols], in_=w[:, :, 0:ncols])
                return (wb, (c0, ncols))

            wblocks = [(0, 512), (512, 512), (1024, 512), (1536, 512), (2560, 512), (3072, 512), (5120, 512),
                       (2048, 512), (3584, 512), (4096, 512), (4608, 512), (5632, 16)] + \
                      [(5648 + i * 512, 512) for i in range(6)]
            wq = []

            def next_wb(c0, ncols):
                if not wq:
                    wq.append(load_wblock(*wblocks.pop(0)))
                wb = wq.pop(0)
                assert wb[1] == (c0, ncols), (wb[1], c0, ncols)
                if wblocks:
                    wq.append(load_wblock(*wblocks.pop(0)))
                return wb[0]

            def mm_tok(wb, tt, ncols):
                ps = psf()
                for c in range(8):
                    S.op("pe", "matmul", R=[hT, wb], W=[ps], out=ps[:, 0:ncols], lhsT=hT[:, c, tt * 128:(tt + 1) * 128],
                         rhs=wb[:, c, 0:ncols], start=(c == 0), stop=(c == 7))
                return ps

            def rope(src, tt, ngrp, dst_pool):
                t1 = dst_pool.next()
                t2 = dst_pool.next()
                s4 = src[:, 0:ngrp * 64].rearrange("p (g a h f) -> p g a h f", a=2, h=2, f=16)
                S.op("dve", "tensor_tensor", R=[src, ropec], W=[t1],
                     out=t1[:, 0:ngrp * 64].rearrange("p (g d) -> p g d", d=64),
                     in0=src[:, 0:ngrp * 64].rearrange("p (g d) -> p g d", d=64),
                     in1=ropec[:, tt:tt + 1, :].to_broadcast([128, ngrp, 64]), op=ALU.mult)
                t24 = t2[:, 0:ngrp * 64].rearrange("p (g a h f) -> p g a h f", a=2, h=2, f=16)
                sn4 = ropes[:, tt, :].rearrange("p (a h f) -> p a h f", a=2, h=2)
                for hh in range(2):
                    S.op("pool", "tensor_tensor", R=[src, ropes], W=[t2], out=t24[:, :, :, hh, :],
                         in0=s4[:, :, :, 1 - hh, :],
                         in1=sn4[:, :, hh, :].unsqueeze(1).to_broadcast([128, ngrp, 2, 16]), op=ALU.mult)
                S.op("dve", "tensor_tensor", R=[t1, t2], W=[t1], out=t1[:, 0:ngrp * 64], in0=t1[:, 0:ngrp * 64],
                     in1=t2[:, 0:ngrp * 64], op=ALU.add)
                return t1

            def tileQK(blk, wb, tt):
                is_s = tt < NTS
                ps = mm_tok(wb, tt, 512)
                sq = f2.next()
                s8 = st8.next()
                qn = f1.next()
                S.op("act", "activation", R=[ps], W=[sq], out=sq[:], in_=ps[:], func=AF.Square)
                S.op("dve", "tensor_reduce", R=[sq], W=[s8], out=s8[:, 0:8],
                     in_=sq[:].rearrange("p (g d) -> p g d", d=64), axis=AX.X, op=ALU.add)
                yield
                rsqrt_inplace(s8, s8[:, 0:8], 1.0 / 64, EPS)
                S.op("dve", "tensor_tensor", R=[ps, s8], W=[qn], out=qn[:].rearrange("p (g d) -> p g d", d=64),
                     in0=ps[:].rearrange("p (g d) -> p g d", d=64),
                     in1=s8[:, 0:8].unsqueeze(2).to_broadcast([128, 8, 64]), op=ALU.mult)
                S.op("pool", "tensor_tensor", R=[qn, wq_bc], W=[qn], out=qn[:].rearrange("p (g d) -> p g d", d=64),
                     in0=qn[:].rearrange("p (g d) -> p g d", d=64),
                     in1=wq_bc[:, blk:blk + 1, :].to_broadcast([128, 8, 64]), op=ALU.mult)
                if blk == 1 and not is_s:
                    pi = (tt - NTS) // 2
                    r0 = ((tt - NTS) % 2) * 128
                    S.dma("pool", nk_out[pi, l, r0:r0 + 128, :], qn[:], R=[qn])
                yield
                if is_s:
                    qn = rope(qn, tt, 8, f1)
                qb = b1.next()
                S.op("act", "copy", R=[qn], W=[qb], out=qb[:], in_=qn[:])
                yield
                pb = psb()
                transposes_to(pb, qb, [(qb[:, h * 128:(h + 1) * 128], 128, h * 128) for h in range(4)])
                sg = stg.next()
                S.op("dve", "tensor_copy", R=[pb], W=[sg], out=sg[:, 0:512], in_=pb[:, 0:512])
                dst = aqT if blk == 0 else akT
                S.dma("pool", dst[:, :, tt * 128:(tt + 1) * 128].rearrange("h p t -> p h t"),
                      sg[:, 0:512].rearrange("p (h t) -> p h t", h=4), R=[sg],
                      W=[RX["aqT" if blk == 0 else "akT"][tt]])

            for blk in range(2):
                wb = next_wb(blk * 512, 512)
                pipeline([tileQK(blk, wb, tt) for tt in range(NT)], 3)
            ckpt("C0")
            for (c0, kind) in ((1024, "av"), (1536, "z0"), (2560, "bv"), (3072, "z1"), (5120, "z2")):
                wb = next_wb(c0, 512)
                for tt in range(NT):
                    is_s = tt < NTS
                    ps = mm_tok(wb, tt, 512)
                    ob = b1.next()
                    if kind in ("av", "bv"):
                        S.op("act", "copy", R=[ps], W=[ob], out=ob[:], in_=ps[:])
                        dst, rn = (av, "av") if kind == "av" else (bv, "bv")
                        S.dma("pool", dst[tt * 128:(tt + 1) * 128, :], ob[:], R=[ob], W=[RX[rn][tt]])
                        if kind == "av" and not is_s:
                            of_ = f1.next()
                            S.op("dve", "tensor_copy", R=[ps], W=[of_], out=of_[:], in_=ps[:])
                            pi = (tt - NTS) // 2
                            r0 = ((tt - NTS) % 2) * 128
                            S.dma("pool", nv_out[pi, l, r0:r0 + 128, :], of_[:], R=[of_])
                    else:
                        zi = int(kind[1])
                        S.op("act", "activation", R=[ps], W=[ob], out=ob[:], in_=ps[:], func=AF.Silu)
                        S.dma("pool", zg[zi, tt * 128:(tt + 1) * 128, :], ob[:], R=[ob], W=[RX["zg%d" % zi][tt]])
                ckpt("C1_" + kind)
            ckpt("C1")
            wb = next_wb(2048, 512)
            for tt in range(NT):
                is_s = tt < NTS
                ps = mm_tok(wb, tt, 512)
                qk = f1.next()
                S.op("act", "copy", R=[ps], W=[qk], out=qk[:, 0:256], in_=ps[:, 0:256])
                S.op("act", "mul", R=[ps], W=[qk], out=qk[:, 256:512], in_=ps[:, 256:512], mul=0.125)
                if is_s:
                    qk = rope(qk, tt, 8, f1)
                qb = b1.next()
                S.op("act", "copy", R=[qk], W=[qb], out=qb[:], in_=qk[:])
                S.dma("pool", bk[tt * 128:(tt + 1) * 128, :], qb[:, 256:512], R=[qb], W=[RX["bk"][tt]])
                pb = psb()
                transposes_to(pb, qb, [(qb[:, g * 64:(g + 1) * 64], 64, g * 128) for g in range(8)])
                sg = stg.next()
                S.op("dve", "tensor_copy", R=[pb], W=[sg], out=sg[0:64, :], in_=pb[0:64, :])
                S.dma("pool", bqT[:, :, tt * 128:(tt + 1) * 128].rearrange("h p t -> p h t"),
                      sg[0:64, 0:512].rearrange("p (h t) -> p h t", h=4), R=[sg], W=[RX["bqT"][tt]])
                S.dma("pool", bkT[:, :, tt * 128:(tt + 1) * 128].rearrange("h p t -> p h t"),
                      sg[0:64, 512:1024].rearrange("p (h t) -> p h t", h=4), R=[sg], W=[RX["bkT"][tt]])
            ckpt("C2")
            for blk in range(3):
                wb = next_wb(3584 + blk * 512, 512)
                for cc in range(4):
                    for (t0, T, _) in seqs:
                        for g0 in range(0, T, 512):
                            n = min(512, T - g0)
                            ps = psf()
                            for c in range(8):
                                S.op("pe", "matmul", R=[hT, wb], W=[ps], out=ps[:, 0:n],
                                     lhsT=wb[:, c, cc * 128:(cc + 1) * 128], rhs=hT[:, c, t0 + g0:t0 + g0 + n],
                                     start=(c == 0), stop=(c == 7))
                            cf = cfp.next()
                            S.op("act", "copy", R=[ps], W=[cf], out=cf[:, 0:n], in_=ps[:, 0:n])
                            ch0 = (blk * 4 + cc) * 128
                            tiles = range((t0 + g0) // 128, (t0 + g0 + n) // 128)
                            S.dma("pool", cT[ch0:ch0 + 128, t0 + g0:t0 + g0 + n], cf[:, 0:n], R=[cf],
                                  W=[RX["cT"][i] for i in tiles])
            ckpt("C3")
            wb = next_wb(5632, 16)
            for tt in range(NT):
                ps = mm_tok(wb, tt, 16)
                o16 = st8.next()
                S.op("act", "activation", R=[ps], W=[o16], out=o16[:, 0:8], in_=ps[:, 0:8], func=AF.Sigmoid)
                S.op("dve", "tensor_tensor", R=[ps, dtb_bc], W=[o16], out=o16[:, 8:16], in0=ps[:, 8:16], in1=dtb_bc[:],
                     op=ALU.add)
                S.op("act", "activation", R=[o16], W=[o16], out=o16[:, 8:16], in_=o16[:, 8:16], func=AF.Exp)
                S.op("act", "activation", R=[o16], W=[o16], out=o16[:, 8:16], in_=o16[:, 8:16], func=AF.Ln, bias=1.0)
                S.op("dve", "tensor_tensor", R=[o16, negA_bc], W=[o16], out=o16[:, 8:16], in0=o16[:, 8:16],
                     in1=negA_bc[:], op=ALU.mult)
                S.dma("pool", bgs[tt * 128:(tt + 1) * 128, :], o16[:], R=[o16], W=[RX["bgs"][tt]])
            ckpt("C4")
            for blk in range(6):
                wb = next_wb(5648 + blk * 512, 512)
                for tt in range(NT):
                    ps = mm_tok(wb, tt, 512)
                    ob = b1.next()
                    S.op("act", "activation", R=[ps], W=[ob], out=ob[:], in_=ps[:], func=AF.Sigmoid)
                    S.dma("pool", mg[tt * 128:(tt + 1) * 128, blk * 512:(blk + 1) * 512], ob[:], R=[ob],
                          W=[RX["mg"][tt]] if blk == 5 else [])
            S.barrier()
        esH.close()
        ckpt("C")
        for si, (t0, T, is_s) in enumerate(seqs):
            if is_s and not os.environ.get("KNOOVL"):
                sample_mixers_overlapped(l, si, t0, T, is_s)
                ckpt("gpre%d" % si)
            else:
                attention(l, si, t0, T, is_s)
                ckpt("att%d" % si)
                retention(l, si, t0, T, is_s)
                ckpt("ret%d" % si)
                gdn_pre(l, si, t0, T, is_s)
                ckpt("gpre%d" % si)
            gdn_scan(l, si, t0, T, is_s)
            ckpt("gscan%d" % si)
        phaseE(l, xsrc, xdst)

    def norm_gate_store(es_tiles, o_t, o_ap, w_bc, zi, mi, tt, rows, col_lo=None):
        sq, s4, zt, gb, sg = es_tiles
        tok0 = tt * 128 + (col_lo or 0)
        S.op("act", "activation", R=[o_t], W=[sq], out=sq[0:rows, :], in_=o_ap, func=AF.Square)
        S.op("dve", "tensor_reduce", R=[sq], W=[s4], out=s4[0:rows, 0:4],
             in_=sq[0:rows, :].rearrange("p (g d) -> p g d", d=128), axis=AX.X, op=ALU.add)
        rsqrt_inplace(s4, s4[0:rows, 0:4], 1.0 / 128, EPS)
        S.dma("sp", zt[0:rows, :], zg[zi, tok0:tok0 + rows, :], R=[RX["zg%d" % zi][tt]], W=[zt])
        S.op("dve", "tensor_tensor", R=[o_t, s4], W=[o_t], out=o_ap.rearrange("p (g d) -> p g d", d=128),
             in0=o_ap.rearrange("p (g d) -> p g d", d=128),
             in1=s4[0:rows, 0:4].unsqueeze(2).to_broadcast([rows, 4, 128]), op=ALU.mult)
        S.op("pool", "tensor_tensor", R=[o_t, w_bc], W=[o_t], out=o_ap.rearrange("p (g d) -> p g d", d=128),
             in0=o_ap.rearrange("p (g d) -> p g d", d=128),
             in1=w_bc[0:rows, :].unsqueeze(1).to_broadcast([rows, 4, 128]), op=ALU.mult)
        S.op("dve", "tensor_tensor", R=[o_t, zt], W=[gb], out=gb[0:rows, :], in0=o_ap, in1=zt[0:rows, :], op=ALU.mult)
        pb = psb()
        for g in range(4):
            S.op("pe", "transpose", R=[gb, ident_b], W=[pb], out=pb[:, g * 128:g * 128 + rows],
                 in_=gb[0:rows, g * 128:(g + 1) * 128], identity=ident_b[0:rows, 0:rows])
        pv = pb[:, 0:512].rearrange("p (g t) -> p g t", g=4)[:, :, 0:rows]
        S.op("act", "copy", R=[pb], W=[sg], out=sg[:, :, 0:rows], in_=pv)
        S.dma("pool", gT[mi, :, tok0:tok0 + rows].rearrange("(g p) t -> p g t", p=128), sg[:, :, 0:rows], R=[sg],
              W=[RX["gT%d" % mi][tt]])

    def ng_tiles(es, pfx):
        return (sb(es, pfx + "sq", [128, 512], F32), sb(es, pfx + "s4", [128, 4], F32),
                sb(es, pfx + "zt", [128, 512], BF16), sb(es, pfx + "gb", [128, 512], BF16),
                sb(es, pfx + "sg", [128, 4, 128], BF16))

    def attention_gen(l, si, t0, T, is_s, es, acc_sets):
        Sk = T + (PAST if is_s else 0)
        nst = Sk // 128
        ntl = T // 128
        QB = min(512, T)
        kT = sb(es, "at_kT", [128, 4, Sk], BF16)
        V1 = sb(es, "at_V1", [128, nst, 4, 130], BF16)
        qTp = Pool_(es, "at_qT", [128, 4, QB], BF16, 2)
        ex = Pool_(es, "at_ex", [128, QB], BF16, 3)
        osb = [sb(es, "at_os%d" % qs, [128, 512], F32) for qs in range(QB // 128)]
        rc = Pool_(es, "at_rc", [128, 2], F32, 4)
        ngt = ng_tiles(es, "at_")
        grp = [0]
        scn = [0]
        S.dma("sp", kT[:, :, 0:T], akT[:, :, t0:t0 + T].rearrange("h p t -> p h t"),
              R=[RX["akT"][i] for i in range(t0 // 128, (t0 + T) // 128)], W=[kT])
        S.op("pool", "memset", W=[V1], ap=V1[:, :, :, 128:130], constant=1.0)
        for i in range(ntl):
            S.dma("sp", V1[:, i, :, 0:128], av[t0 + i * 128:t0 + (i + 1) * 128, :].rearrange("p (h e) -> p h e", h=4),
                  R=[RX["av"][t0 // 128 + i]], W=[V1])
        if is_s:
            with ExitStack() as es2:
                ck = sb(es2, "at_ck", [128, 2, 512], F32)
                cv = sb(es2, "at_cv", [128, 2, 512], F32)
                ckb = sb(es2, "at_ckb", [128, 2, 512], BF16)
                S.dma("sp", ck[:], cache_k[l].rearrange("(n p) f -> p n f", p=128), W=[ck])
                S.dma("sp", cv[:], cache_v[l].rearrange("(n p) f -> p n f", p=128), W=[cv])
                S.op("dve", "tensor_copy", R=[ck], W=[ckb], out=ckb[:], in_=ck[:])
                for n in range(2):
                    S.op("pool", "tensor_copy", R=[cv], W=[V1], out=V1[:, ntl + n, :, 0:128],
                         in_=cv[:, n, :].rearrange("p (h e) -> p h e", h=4))
                    pb = psb()
                    transposes_to(pb, ckb, [(ckb[:, n, h * 128:(h + 1) * 128], 128, h * 128) for h in range(4)])
                    S.op("act", "copy", R=[pb], W=[kT], out=kT[:, :, T + n * 128:T + (n + 1) * 128],
                         in_=pb[:, 0:512].rearrange("p (h t) -> p h t", h=4))
                S.barrier()
        for qb0 in range(0, T, QB):
            qT = qTp.next()
            S.dma("sp", qT[:], aqT[:, :, t0 + qb0:t0 + qb0 + QB].rearrange("h p t -> p h t"),
                  R=[RX["aqT"][i] for i in range((t0 + qb0) // 128, (t0 + qb0 + QB) // 128)], W=[qT])
            nqs = QB // 128
            for h in range(4):
                for m in range(2):
                    grp[0] += 1
                    acc = acc_sets[grp[0] % len(acc_sets)]

                    def pv(st, e, acc=acc, h=h):
                        for qs in range(nqs):
                            a = acc[qs // 2]
                            S.op("pe", "matmul", R=[e, V1], W=[a], out=a[:, (qs % 2) * 256:(qs % 2) * 256 + 129],
                                 lhsT=e[:, qs * 128:(qs + 1) * 128], rhs=V1[:, st, h, 0:129],
                                 start=(st == 0 and qs % 2 == 0), stop=(st == nst - 1), skip_group_check=True)
                    pend = None
                    for st in range(nst):
                        scn[0] += 1
                        ps = PSF[scn[0] % 2]
                        S.op("pe", "matmul", R=[kT, qT], W=[ps], out=ps[:, 0:QB],
                             lhsT=kT[m * 64:(m + 1) * 64, h, st * 128:(st + 1) * 128],
                             rhs=qT[m * 64:(m + 1) * 64, h, :], start=True, stop=True)
                        e = ex.next()
                        S.op("act", "activation", R=[ps], W=[e], out=e[:, 0:QB], in_=ps[:, 0:QB], func=AF.Exp)
                        if pend is not None:
                            pv(*pend)
                        pend = (st, e)
                        yield
                    pv(*pend)
                    for qs in range(nqs):
                        a = acc[qs // 2]
                        c0 = (qs % 2) * 256
                        r = rc.next()
                        S.op("dve", "reciprocal", R=[a], W=[r], out=r[:, 0:1], in_=a[:, c0 + 128:c0 + 129])
                        if m == 0:
                            S.op("dve", "tensor_scalar", R=[a, r], W=[osb[qs]], out=osb[qs][:, h * 128:(h + 1) * 128],
                                 in0=a[:, c0:c0 + 128], scalar1=r[:, 0:1], scalar2=None, op0=ALU.mult)
                        else:
                            S.op("dve", "tensor_tensor", R=[r, lamt], W=[r], out=r[:, 1:2], in0=r[:, 0:1],
                                 in1=lamt[:, 0:1], op=ALU.mult)
                            S.op("dve", "scalar_tensor_tensor", R=[a, r, osb[qs]], W=[osb[qs]],
                                 out=osb[qs][:, h * 128:(h + 1) * 128], in0=a[:, c0:c0 + 128], scalar=r[:, 1:2],
                                 in1=osb[qs][:, h * 128:(h + 1) * 128], op0=ALU.mult, op1=ALU.add)
            for qs in range(nqs):
                tt = (t0 + qb0) // 128 + qs
                norm_gate_store(ngt, osb[qs], osb[qs][:], subw_bc, 0, 0, tt, 128)

    def attention(l, si, t0, T, is_s):
        with ExitStack() as es:
            for _ in attention_gen(l, si, t0, T, is_s, es, [[PSF[2], PSF[3]], [PSF[4], PSF[5]]]):
                pass
            S.barrier()

    def retention(l, si, t0, T, is_s):
        with ExitStack() as es:
            run_rr([ret_chain(l, si, t0, T, is_s, d, es) for d in range(2)])
            S.barrier()
        combine(t0, T, retw_bc, 1, 1)

    def rr_gen(gens):
        gens = list(gens)
        while gens:
            for g in list(gens):
                try:
                    next(g)
                    yield
                except StopIteration:
                    gens.remove(g)

    def side_gen(l, si, t0, T, is_s):
        with ExitStack() as es2:
            yield from gdn_pre_gen(l, si, t0, T, is_s, es2, 256, 1)
            S.barrier()
        with ExitStack() as es3:
            yield from rr_gen([ret_chain(l, si, t0, T, is_s, d, es3) for d in range(2)])
            S.barrier()
        yield from combine_gen(t0, T, retw_bc, 1, 1)

    def sample_mixers_overlapped(l, si, t0, T, is_s):
        with ExitStack() as es:
            att = attention_gen(l, si, t0, T, is_s, es, [[PSF[2], PSF[3]]])
            next(att)
            old = psf_banks[0]
            psf_banks[0] = [4, 5]
            n_att = (T // min(512, T)) * 8 * ((T + PAST) // 128)
            n_side = (T // 256) * 36 + (T // 128) * 5
            run_weighted(att, side_gen(l, si, t0, T, is_s), max(1, int(0.9 * n_att / n_side)))
            psf_banks[0] = old
            S.barrier()

    def ret_chain(l, si, t0, T, is_s, d, es):
        ntl = T // 128
        pf = "rt%d_" % d
        Sf = sb(es, pf + "S", [64, 4, 128], F32)
        Sb_ = sb(es, pf + "Sb", [64, 4, 128], BF16)
        qTp = Pool_(es, pf + "qT", [64, 4, 128], BF16, 2)
        kTp = Pool_(es, pf + "kT", [64, 4, 128], BF16, 2)
        ktp = Pool_(es, pf + "k", [128, 256], BF16, 2)
        vp = Pool_(es, pf + "v", [128, 512], BF16, 2)
        itp = Pool_(es, pf + "it", [128, 512], BF16, 2)
        qdp = Pool_(es, pf + "qd", [64, 4, 128], BF16, 2)
        kdp = Pool_(es, pf + "kd", [128, 256], BF16, 2)
        op_ = Pool_(es, pf + "o", [128, 512], F32, 2)
        odst, orn = (ofs, "ofs") if d == 0 else (ofb, "ofb")
        if is_s:
            S.dma("sp", Sf[:], st_ret[l, d].rearrange("h k e -> k h e"), W=[Sf])
        else:
            S.op("dve", "memset", W=[Sf], ap=Sf[:], constant=0.0)
        S.op("act", "copy", R=[Sf], W=[Sb_], out=Sb_[:], in_=Sf[:])
        order = range(ntl) if d == 0 else range(ntl - 1, -1, -1)
        for i in order:
            tt = t0 // 128 + i
            c0 = tt * 128
            qT = qTp.next(); kT = kTp.next(); kt = ktp.next(); v = vp.next()
            S.dma("sp", qT[:], bqT[:, :, c0:c0 + 128].rearrange("h p t -> p h t"), R=[RX["bqT"][tt]], W=[qT])
            S.dma("sp", kT[:], bkT[:, :, c0:c0 + 128].rearrange("h p t -> p h t"), R=[RX["bkT"][tt]], W=[kT])
            S.dma("sp", kt[:], bk[c0:c0 + 128, :], R=[RX["bk"][tt]], W=[kt])
            S.dma("sp", v[:], bv[c0:c0 + 128, :], R=[RX["bv"][tt]], W=[v])
            ps = psf()
            for h in range(4):
                S.op("pe", "matmul", R=[kT, qT], W=[ps], out=ps[:, h * 128:(h + 1) * 128], lhsT=kT[:, h, :],
                     rhs=qT[:, h, :], start=True, stop=True)
            it = itp.next()
            S.op("dve", "tensor_tensor", R=[ps, rdmat], W=[it], out=it[:], in0=ps[:],
                 in1=rdmat[:, d, :, :].rearrange("p h i -> p (h i)"), op=ALU.mult)
            qd = qdp.next()
            S.op("pool", "tensor_tensor", R=[qT, rqdec], W=[qd], out=qd[:], in0=qT[:], in1=rqdec[:, d, :, :],
                 op=ALU.mult)
            kd = kdp.next()
            S.op("pool", "tensor_tensor", R=[kt, rkdec], W=[kd], out=kd[:].rearrange("p (h e) -> p h e", h=4),
                 in0=kt[:].rearrange("p (h e) -> p h e", h=4),
                 in1=rkdec[:, d, :].unsqueeze(2).to_broadcast([128, 4, 64]), op=ALU.mult)
            yield
            po = psf()
            for h in range(4):
                S.op("pe", "matmul", R=[it, v], W=[po], out=po[:, h * 128:(h + 1) * 128],
                     lhsT=it[:, h * 128:(h + 1) * 128], rhs=v[:, h * 128:(h + 1) * 128], start=True, stop=False)
                S.op("pe", "matmul", R=[qd, Sb_], W=[po], out=po[:, h * 128:(h + 1) * 128], lhsT=qd[:, h, :],
                     rhs=Sb_[:, h, :], start=False, stop=True)
            pS = psf()
            for h in range(4):
                S.op("pe", "matmul", R=[kd, v], W=[pS], out=pS[0:64, h * 128:(h + 1) * 128],
                     lhsT=kd[:, h * 64:(h + 1) * 64], rhs=v[:, h * 128:(h + 1) * 128], start=True, stop=True)
            S.op("dve", "tensor_tensor", R=[Sf, rcdec], W=[Sf], out=Sf[:], in0=Sf[:],
                 in1=rcdec[0:64, d * 4:d * 4 + 4].unsqueeze(2).to_broadcast([64, 4, 128]), op=ALU.mult)
            S.op("dve", "tensor_tensor", R=[Sf, pS], W=[Sf], out=Sf[:].rearrange("p h e -> p (h e)"),
                 in0=Sf[:].rearrange("p h e -> p (h e)"), in1=pS[0:64, :], op=ALU.add)
            S.op("act", "copy", R=[Sf], W=[Sb_], out=Sb_[:], in_=Sf[:])
            o = op_.next()
            S.op("act", "copy", R=[po], W=[o], out=o[:], in_=po[:])
            S.dma("pool", odst[c0:c0 + 128, :], o[:], R=[o], W=[RX[orn][tt]])
            yield
        if not is_s:
            S.dma("pool", nret_out[si - 1, l, d].rearrange("h k e -> k h e"), Sf[:], R=[Sf])

    def pipeline_gen(gens, depth):
        gens = list(gens)
        active = []
        while gens or active:
            while gens and len(active) < depth:
                active.append(gens.pop(0))
            for g in list(active):
                try:
                    next(g)
                except StopIteration:
                    active.remove(g)
            yield

    def gdn_pre_gen(l, si, t0, T, is_s, es, G, nbuf):
        xin = Pool_(es, "gp_x", [128, 12, G + 4], F32, nbuf)
        acc = Pool_(es, "gp_a", [128, 12, G], F32, nbuf)
        sqp = Pool_(es, "gp_sq", [128, G], F32, 5)
        rsp = Pool_(es, "gp_rs", [128, G], F32, 5)
        nb = Pool_(es, "gp_nb", [128, 12, G], BF16, nbuf)
        sg = Pool_(es, "gp_sg", [128, 1024], BF16, 2)
        chunk_res = {}
        for g0 in range(0, T, G):
            x = xin.next()
            a = acc.next()
            lo = 2 if g0 == 0 else 0
            hi = 2 if g0 + G == T else 0
            if lo:
                S.op("pool", "memset", W=[x], ap=x[:, :, 0:2], constant=0.0)
            if hi:
                S.op("pool", "memset", W=[x], ap=x[:, :, G + 2:G + 4], constant=0.0)
            tl = [i for i in range((t0 + g0) // 128 - (0 if lo else 1), (t0 + g0 + G) // 128 + (0 if hi else 1))]
            S.dma("sp", x[:, :, lo:G + 4 - hi],
                  cT[:, t0 + g0 - 2 + lo:t0 + g0 + G + 2 - hi].rearrange("(c p) t -> p c t", p=128),
                  R=[RX["cT"][i] for i in tl], W=[x])
            yield
            yield
            ar = chunk_res.setdefault(("a", id(a)), [Res("gpa%d" % c) for c in range(12)])

            def convc(c):
                S.op("dve", "tensor_scalar", R=[x, convw], W=[ar[c]], out=a[:, c, :], in0=x[:, c, 0:G],
                     scalar1=convw[:, c, 0:1], scalar2=None, op0=ALU.mult)
                for k in range(1, 5):
                    S.op("dve", "scalar_tensor_tensor", R=[x, convw, ar[c]], W=[ar[c]], out=a[:, c, :],
                         in0=x[:, c, k:k + G], scalar=convw[:, c, k:k + 1], in1=a[:, c, :], op0=ALU.mult, op1=ALU.add)
                yield
                yield
                S.op("act", "activation", R=[ar[c]], W=[ar[c]], out=a[:, c, :], in_=a[:, c, :], func=AF.Silu)
            yield from pipeline_gen([convc(c) for c in range(12)], 3)
            n = nb.next()
            nr = chunk_res.setdefault(("n", id(n)), [Res("gpn%d" % c) for c in range(12)])

            def l2c(c):
                sq = sqp.next()
                S.op("act", "activation", R=[ar[c]], W=[sq], out=sq[:], in_=a[:, c, :], func=AF.Square)
                yield
                yield
                ps = psf()
                S.op("pe", "matmul", R=[ones_f, sq], W=[ps], out=ps[:, 0:G], lhsT=ones_f[:], rhs=sq[:], start=True,
                     stop=True)
                rs = rsp.next()
                S.op("dve", "tensor_scalar", R=[ps], W=[rs], out=rs[:], in0=ps[:, 0:G], scalar1=EPS, scalar2=None,
                     op0=ALU.add)
                yield
                yield
                S.op("act", "activation", R=[rs], W=[rs], out=rs[:], in_=rs[:], func=AF.Sqrt)
                yield
                yield
                S.op("dve", "reciprocal", R=[rs], W=[rs], out=rs[:], in_=rs[:])
                if c < 4:
                    S.op("dve", "scalar_tensor_tensor", R=[ar[c], rs], W=[nr[c]], out=n[:, c, :], in0=a[:, c, :],
                         scalar=float(128 ** -0.5), in1=rs[:], op0=ALU.mult, op1=ALU.mult)
                else:
                    S.op("dve", "tensor_tensor", R=[ar[c], rs], W=[nr[c]], out=n[:, c, :], in0=a[:, c, :], in1=rs[:],
                         op=ALU.mult)
            yield from pipeline_gen([l2c(c) for c in range(8)], 4)
            S.op("pool", "tensor_copy", R=ar[8:12], W=nr[8:12], out=n[:, 8:12, :], in_=a[:, 8:12, :])
            tiles = list(range((t0 + g0) // 128, (t0 + g0 + G) // 128))
            S.dma("pool", gqT[:, :, t0 + g0:t0 + g0 + G].rearrange("h p t -> p h t"), n[:, 0:4, :], R=nr[0:4],
                  W=[RX["gqT"][i] for i in tiles])
            S.dma("pool", gkT[:, :, t0 + g0:t0 + g0 + G].rearrange("h p t -> p h t"), n[:, 4:8, :], R=nr[4:8],
                  W=[RX["gkT"][i] for i in tiles])
            yield
            yield
            for ti in range(G // 128):
                tt = (t0 + g0) // 128 + ti
                pb = psb()
                transposes_to(pb, nr[4:12], [(n[:, 4 + c, ti * 128:(ti + 1) * 128], 128, c * 128) for c in range(8)])
                s = sg.next()
                S.op("act", "copy", R=[pb], W=[s], out=s[:], in_=pb[:])
                S.dma("pool", gk[tt * 128:(tt + 1) * 128, :], s[:, 0:512], R=[s], W=[RX["gk"][tt]])
                S.dma("pool", gv[tt * 128:(tt + 1) * 128, :], s[:, 512:1024], R=[s], W=[RX["gv"][tt]])
                yield

    def gdn_pre(l, si, t0, T, is_s):
        with ExitStack() as es:
            for _ in gdn_pre_gen(l, si, t0, T, is_s, es, min(512, T), 2):
                pass
            S.barrier()

    def pipeline(gens, depth):
        gens = list(gens)
        active = []
        while gens or active:
            while gens and len(active) < depth:
                active.append(gens.pop(0))
            for g in list(active):
                try:
                    next(g)
                except StopIteration:
                    active.remove(g)

    def run_weighted(main, side, ratio):
        main_alive = side_alive = True
        while main_alive or side_alive:
            if main_alive:
                for _ in range(ratio):
                    try:
                        next(main)
                    except StopIteration:
                        main_alive = False
                        break
            if side_alive:
                try:
                    next(side)
                except StopIteration:
                    side_alive = False

    def run_rr(gens):
        gens = list(gens)
        while gens:
            for g in list(gens):
                try:
                    next(g)
                except StopIteration:
                    gens.remove(g)

    def combine(t0, T, w_bc, zi, mi):
        for _ in combine_gen(t0, T, w_bc, zi, mi):
            pass

    def combine_gen(t0, T, w_bc, zi, mi):
        with ExitStack() as es:
            fa = Pool_(es, "cb_a", [128, 512], F32, 2)
            fb = Pool_(es, "cb_b", [128, 512], F32, 2)
            ngts = [ng_tiles(es, "cb%d_" % i) for i in range(2)]
            for i in range(T // 128):
                tt = t0 // 128 + i
                c0 = tt * 128
                a = fa.next(); b = fb.next()
                S.dma("sp", a[:], ofs[c0:c0 + 128, :], R=[RX["ofs"][tt]], W=[a])
                S.dma("sp", b[:], ofb[c0:c0 + 128, :], R=[RX["ofb"][tt]], W=[b])
                S.op("pool", "tensor_tensor", R=[a, b], W=[a], out=a[:], in0=a[:], in1=b[:], op=ALU.add)
                norm_gate_store(ngts[i % 2], a, a[:], w_bc, zi, mi, tt, 128)
                yield
            S.barrier()

    def gdn_scan(l, si, t0, T, is_s):
        with ExitStack() as es:
            run_rr([gdn_chain(l, si, t0, T, is_s, d, es) for d in range(2)])
            S.barrier()
        combine(t0, T, gdnw_bc, 2, 2)

    def gdn_chain(l, si, t0, T, is_s, d, es):
        ntl = T // 128
        pf = "gd%d_" % d
        Sf = sb(es, pf + "S", [128, 4, 128], F32)
        Sb_ = sb(es, pf + "Sb", [128, 4, 128], BF16)
        kTp = Pool_(es, pf + "kT", [128, 4, 128], BF16, 2)
        qTp = Pool_(es, pf + "qT", [128, 4, 128], BF16, 2)
        ktp = Pool_(es, pf + "k", [128, 512], BF16, 2)
        vtp = Pool_(es, pf + "v", [128, 512], BF16, 2)
        bgp = Pool_(es, pf + "bg", [128, 16], F32, 2)
        sm = Pool_(es, pf + "sm", [128, 8, 4], F32, 2)
        X1 = Pool_(es, pf + "X", [128, 4, 128], F32, 1)
        X2 = Pool_(es, pf + "X2", [128, 4, 128], F32, 1)
        Dn = Pool_(es, pf + "Dn", [128, 4, 128], F32, 1)
        Ea = Pool_(es, pf + "Ea", [128, 4, 128], F32, 1)
        Eb = Pool_(es, pf + "Eb", [128, 4, 128], F32, 1)
        Fq = Pool_(es, pf + "Fq", [128, 4, 128], F32, 1)
        Eg = Pool_(es, pf + "Eg", [128, 4, 128], F32, 1)
        Pp = Pool_(es, pf + "P", [128, 4, 128], F32, 2)
        PTp = Pool_(es, pf + "PT", [128, 4, 128], F32, 2)
        TTf = Pool_(es, pf + "TTf", [128, 4, 128], F32, 1)
        qkTp = Pool_(es, pf + "qkT", [128, 4, 128], BF16, 2)
        qgp = Pool_(es, pf + "qg", [128, 4, 128], BF16, 2)
        vbp = Pool_(es, pf + "vb", [128, 4, 128], F32, 1)
        kbp = Pool_(es, pf + "kb", [128, 4, 128], F32, 1)
        kdp = Pool_(es, pf + "kd", [128, 4, 128], BF16, 2)
        Up = Pool_(es, pf + "U", [128, 4, 128], F32, 2)
        WTp = Pool_(es, pf + "WT", [128, 4, 128], BF16, 2)
        vnp = Pool_(es, pf + "vn", [128, 128], BF16, 4)
        op_ = Pool_(es, pf + "o", [64, 512], F32, 2)
        odst, orn = (ofs, "ofs") if d == 0 else (ofb, "ofb")
        Sfr = [Res("gSf%d" % h) for h in range(4)]
        Sbr = [Res("gSb%d" % h) for h in range(4)]
        if is_s:
            S.dma("sp", Sf[:], st_gdn[l, d].rearrange("h k e -> k h e"), W=Sfr)
        else:
            S.op("dve", "memset", W=Sfr, ap=Sf[:], constant=0.0)
        S.op("act", "copy", R=Sfr, W=Sbr, out=Sb_[:], in_=Sf[:])
        idb = ident_f[:].unsqueeze(1).to_broadcast([128, 4, 128])
        v4 = lambda t: t[:].rearrange("p h b -> p (h b)")
        order = range(ntl) if d == 0 else range(ntl - 1, -1, -1)
        for i in order:
            tt = t0 // 128 + i
            c0 = tt * 128
            kT = kTp.next(); qT = qTp.next(); kt = ktp.next(); vt = vtp.next(); bg = bgp.next()
            S.dma("sp", kT[:], gkT[:, :, c0:c0 + 128].rearrange("h p t -> p h t"), R=[RX["gkT"][tt]], W=[kT])
            S.dma("sp", qT[:], gqT[:, :, c0:c0 + 128].rearrange("h p t -> p h t"), R=[RX["gqT"][tt]], W=[qT])
            S.dma("sp", kt[:], gk[c0:c0 + 128, :], R=[RX["gk"][tt]], W=[kt])
            S.dma("sp", vt[:], gv[c0:c0 + 128, :], R=[RX["gv"][tt]], W=[vt])
            S.dma("sp", bg[:], bgs[c0:c0 + 128, :], R=[RX["bgs"][tt]], W=[bg])
            s = sm.next()
            S.op("dve", "tensor_copy", R=[bg], W=[s], out=s[:, 0, :], in_=bg[:, 8 + d * 4:12 + d * 4])
            S.op("dve", "tensor_copy", R=[bg], W=[s], out=s[:, 1, :], in_=bg[:, d * 4:d * 4 + 4])
            pg = psf()
            S.op("pe", "matmul", R=[utri2, s], W=[pg], out=pg[:, 0:4], lhsT=utri2[:, d, :], rhs=s[:, 0, :],
                 start=True, stop=True)
            S.op("pe", "matmul", R=[bd1, s], W=[pg], out=pg[:, 4:8], lhsT=bd1[:], rhs=s[:, 0, :], start=True, stop=True)
            S.op("pe", "matmul", R=[selc, s], W=[pg], out=pg[:, 8:12], lhsT=selc[:, 0, :], rhs=s[:, 0, :],
                 start=True, stop=True)
            S.op("pe", "matmul", R=[selc, s], W=[pg], out=pg[:, 12:16], lhsT=selc[:, 1, :], rhs=s[:, 0, :],
                 start=True, stop=True)
            S.op("dve", "tensor_copy", R=[pg], W=[s], out=s[:, 2, :], in_=pg[:, 0:4])
            S.op("act", "activation", R=[pg], W=[s], out=s[:, 6:8, :].rearrange("p c h -> p (c h)"), in_=pg[:, 8:16],
                 func=AF.Exp)
            S.op("dve", "tensor_tensor", R=[pg], W=[s], out=s[:, 4, :], in0=pg[:, 4:8], in1=s[:, 2, :], op=ALU.subtract)
            S.op("act", "activation", R=[s], W=[s], out=s[:, 4, :], in_=s[:, 4, :], func=AF.Exp)
            S.op("act", "activation", R=[s], W=[s], out=s[:, 5, :], in_=s[:, 2, :], func=AF.Exp)
            S.op("dve", "tensor_tensor", R=[s], W=[s], out=s[:, 5, :], in0=s[:, 5, :], in1=s[:, 1, :], op=ALU.mult)
            ckpt('g0')
            yield
            x1 = X1.next(); x2 = X2.next()
            S.op("dve", "tensor_tensor", R=[ident_f, s], W=[x1], out=x1[:], in0=idb,
                 in1=s[:, 2, :].unsqueeze(2).to_broadcast([128, 4, 128]), op=ALU.mult)
            S.op("pool", "tensor_tensor", R=[ident_f, s], W=[x2], out=x2[:], in0=idb,
                 in1=s[:, 1, :].unsqueeze(2).to_broadcast([128, 4, 128]), op=ALU.mult)
            pR = psf(); pRb = psf()
            S.op("pe", "matmul", R=[ones_f, x1], W=[pR], out=pR[:], lhsT=ones_f[:], rhs=v4(x1), start=True, stop=True)
            S.op("pe", "matmul", R=[ones_f, x2], W=[pRb], out=pRb[:], lhsT=ones_f[:], rhs=v4(x2), start=True, stop=True)
            dn = Dn.next()
            S.op("dve", "tensor_tensor", R=[pR, s], W=[dn], out=dn[:], in0=pR[:].rearrange("p (h b) -> p h b", h=4),
                 in1=s[:, 2, :].unsqueeze(2).to_broadcast([128, 4, 128]), op=ALU.subtract)
            eg = Eg.next()
            S.op("act", "activation", R=[pR], W=[eg], out=v4(eg), in_=pR[:], func=AF.Exp)
            ea = Ea.next(); eb = Eb.next(); fq = Fq.next()
            S.op("dve", "tensor_scalar", R=[dn], W=[ea], out=ea[:], in0=dn[:], scalar1=-1.0, scalar2=0.0,
                 op0=ALU.mult, op1=ALU.min)
            S.op("dve", "tensor_scalar", R=[dn], W=[eb], out=eb[:], in0=dn[:], scalar1=0.0, scalar2=None, op0=ALU.min)
            S.op("act", "activation", R=[ea], W=[ea], out=ea[:], in_=ea[:], func=AF.Exp)
            S.op("act", "activation", R=[eb], W=[eb], out=eb[:], in_=eb[:], func=AF.Exp)
            S.op("dve", "tensor_tensor", R=[ea, gmask2], W=[ea], out=ea[:], in0=ea[:],
                 in1=gmask2[:, d, 0, :].unsqueeze(1).to_broadcast([128, 4, 128]), op=ALU.mult)
            S.op("dve", "tensor_tensor", R=[ea, s], W=[ea], out=ea[:], in0=ea[:],
                 in1=s[:, 1, :].unsqueeze(2).to_broadcast([128, 4, 128]), op=ALU.mult)
            S.op("pool", "tensor_tensor", R=[eb, gmask2], W=[fq], out=fq[:], in0=eb[:],
                 in1=gmask2[:, d, 2, :].unsqueeze(1).to_broadcast([128, 4, 128]), op=ALU.mult)
            S.op("pool", "tensor_tensor", R=[eb, gmask2], W=[eb], out=eb[:], in0=eb[:],
                 in1=gmask2[:, d, 1, :].unsqueeze(1).to_broadcast([128, 4, 128]), op=ALU.mult)
            S.op("dve", "tensor_tensor", R=[eb, pRb], W=[eb], out=eb[:], in0=eb[:],
                 in1=pRb[:].rearrange("p (h b) -> p h b", h=4), op=ALU.mult)
            ckpt('g1')
            yield
            qg = qgp.next()
            S.op("pool", "tensor_tensor", R=[qT, eg], W=[qg], out=qg[:], in0=qT[:], in1=eg[:], op=ALU.mult)
            pK = psf(); pQ = psf()
            for h in range(4):
                S.op("pe", "matmul", R=[kT], W=[pK], out=pK[:, h * 128:(h + 1) * 128], lhsT=kT[:, h, :], rhs=kT[:, h, :],
                     start=True, stop=True)
                S.op("pe", "matmul", R=[kT, qT], W=[pQ], out=pQ[:, h * 128:(h + 1) * 128], lhsT=kT[:, h, :],
                     rhs=qT[:, h, :], start=True, stop=True)
            P = Pp.next(); PT = PTp.next(); ttf = TTf.next(); qkT = qkTp.next()
            pKv = pK[:].rearrange("p (h b) -> p h b", h=4)
            S.op("dve", "tensor_tensor", R=[pK, ea], W=[P], out=P[:], in0=pKv, in1=ea[:], op=ALU.mult)
            S.op("dve", "tensor_tensor", R=[pK, eb], W=[PT], out=PT[:], in0=pKv, in1=eb[:], op=ALU.mult)
            S.op("pool", "tensor_tensor", R=[PT, ident_f], W=[ttf], out=ttf[:], in0=PT[:], in1=idb, op=ALU.add)
            S.op("dve", "tensor_tensor", R=[pQ, fq], W=[qkT], out=qkT[:], in0=pQ[:].rearrange("p (h b) -> p h b", h=4),
                 in1=fq[:], op=ALU.mult)
            ckpt('g2')
            yield
            for lev in range(1, 6):
                p1 = psf()
                for h in range(4):
                    S.op("pe", "matmul", R=[PT, P], W=[p1], out=p1[:, h * 128:(h + 1) * 128], lhsT=PT[:, h, :],
                         rhs=P[:, h, :], start=True, stop=True)
                Pn = Pp.next()
                S.op("dve", "tensor_copy", R=[p1], W=[Pn], out=v4(Pn), in_=p1[:])
                p3 = psf()
                for h in range(4):
                    S.op("pe", "matmul", R=[Pn, ttf], W=[p3], out=p3[:, h * 128:(h + 1) * 128], lhsT=Pn[:, h, :],
                         rhs=ttf[:, h, :], start=True, stop=True)
                if lev < 5:
                    p2 = psf()
                    for h in range(4):
                        if os.environ.get("KNOTR"):
                            S.op("pe", "matmul", R=[PT, P], W=[p2], out=p2[:, h * 128:(h + 1) * 128], lhsT=P[:, h, :],
                                 rhs=PT[:, h, :], start=True, stop=True)
                        else:
                            S.op("pe", "transpose", R=[Pn, ident_f], W=[p2], out=p2[:, h * 128:(h + 1) * 128],
                                 in_=Pn[:, h, :], identity=ident_f[:])
                    PTn = PTp.next()
                    S.op("act", "copy", R=[p2], W=[PTn], out=v4(PTn), in_=p2[:])
                S.op("dve", "tensor_tensor", R=[ttf, p3], W=[ttf], out=v4(ttf), in0=v4(ttf), in1=p3[:], op=ALU.add)
                P = Pn
                if lev < 5:
                    PT = PTn
                ckpt('g3')
                yield
            vb = vbp.next(); kb = kbp.next(); kd = kdp.next()
            S.op("dve", "tensor_tensor", R=[vt, s], W=[vb], out=vb[:], in0=vt[:].rearrange("p (h e) -> p h e", h=4),
                 in1=s[:, 1, :].unsqueeze(2).to_broadcast([128, 4, 128]), op=ALU.mult)
            S.op("pool", "tensor_tensor", R=[kt, s], W=[kb], out=kb[:], in0=kt[:].rearrange("p (h e) -> p h e", h=4),
                 in1=s[:, 5, :].unsqueeze(2).to_broadcast([128, 4, 128]), op=ALU.mult)
            S.op("pool", "tensor_tensor", R=[kt, s], W=[kd], out=kd[:], in0=kt[:].rearrange("p (h e) -> p h e", h=4),
                 in1=s[:, 4, :].unsqueeze(2).to_broadcast([128, 4, 128]), op=ALU.mult)
            U = Up.next(); WT = WTp.next()
            pU = psf(); pW = psf()
            for h in range(4):
                S.op("pe", "matmul", R=[ttf, vb], W=[pU], out=pU[:, h * 128:(h + 1) * 128], lhsT=ttf[:, h, :],
                     rhs=vb[:, h, :], start=True, stop=True)
                S.op("pe", "matmul", R=[kb, ttf], W=[pW], out=pW[:, h * 128:(h + 1) * 128], lhsT=kb[:, h, :],
                     rhs=ttf[:, h, :], start=True, stop=True)
            S.op("dve", "tensor_copy", R=[pU], W=[U], out=v4(U), in_=pU[:])
            S.op("act", "copy", R=[pW], W=[WT], out=v4(WT), in_=pW[:])
            ckpt('g4')
            yield
            for cb in ((0, 1) if d == 0 else (1, 0)):
                rows = slice(cb * 64, cb * 64 + 64)
                cols = slice(cb * 64, cb * 64 + 64)
                po = PSF[4 + d]
                for h in range(4):
                    pa = psf()
                    S.op("pe", "matmul", R=[WT, Sbr[h]], W=[pa], out=pa[:, 0:128], lhsT=WT[:, h, :], rhs=Sb_[:, h, :],
                         start=True, stop=True)
                    vn = vnp.next()
                    S.op("dve", "tensor_tensor", R=[U, pa], W=[vn], out=vn[rows, :], in0=U[rows, h, :],
                         in1=pa[rows, 0:128], op=ALU.subtract)
                    S.op("pe", "matmul", R=[qg, Sbr[h]], W=[po], out=po[0:64, h * 128:(h + 1) * 128], lhsT=qg[:, h, cols],
                         rhs=Sb_[:, h, :], start=True, stop=False)
                    S.op("pe", "matmul", R=[qkT, vn], W=[po], out=po[0:64, h * 128:(h + 1) * 128],
                         lhsT=qkT[rows, h, cols], rhs=vn[rows, :], start=False, stop=True)
                    pS = psf()
                    S.op("pe", "matmul", R=[kd, vn], W=[pS], out=pS[:, 0:128], lhsT=kd[rows, h, :], rhs=vn[rows, :],
                         start=True, stop=True)
                    S.op("dve", "scalar_tensor_tensor", R=[Sfr[h], s, pS], W=[Sfr[h]], out=Sf[:, h, :], in0=Sf[:, h, :],
                         scalar=s[:, 6 + cb, h:h + 1], in1=pS[:, 0:128], op0=ALU.mult, op1=ALU.add)
                    S.op("act", "copy", R=[Sfr[h]], W=[Sbr[h]], out=Sb_[:, h, :], in_=Sf[:, h, :])
                    if h % 2 == 1:
                        ckpt('g5')
                        yield
                r0 = c0 + cb * 64
                o = op_.next()
                S.op("act", "copy", R=[po], W=[o], out=o[:], in_=po[0:64, :])
                S.dma("pool", odst[r0:r0 + 64, :], o[:], R=[o], W=[RX[orn][tt]])
                ckpt('g6')
                yield
        if not is_s:
            S.dma("pool", ngdn_out[si - 1, l, d].rearrange("h k e -> k h e"), Sf[:], R=Sfr)

    def phaseE(l, xsrc, xdst):
        with ExitStack() as es:
            wbr = sb(es, "pe_wbr", [128, 12, D], BF16)
            wo = sb(es, "pe_wo", [128, 8, D], BF16)
            wst = Pool_(es, "pe_wst", [128, 4, D], F32, 2)
            gtp = Pool_(es, "pe_gt", [128, 12, 128], BF16, 2)
            mgp = Pool_(es, "pe_mg", [128, 3 * D], BF16, 2)
            mp = Pool_(es, "pe_m", [128, D], F32, 2)
            mbp = Pool_(es, "pe_mb", [128, D], BF16, 2)
            mTp = Pool_(es, "pe_mT", [128, 8, 128], BF16, 2)
            xp = Pool_(es, "pe_x", [128, D], F32, 2)
            tp = Pool_(es, "pe_t", [128, 512], F32, 2)
            for q in range(3):
                w = wst.next()
                S.dma("sp", w[:], w_branch[l, q].rearrange("(c p) n -> p c n", p=128), W=[w])
                S.op("dve" if q % 2 == 0 else "pool", "tensor_copy", R=[w], W=[wbr], out=wbr[:, q * 4:q * 4 + 4, :],
                     in_=w[:])
            for q in range(2):
                w = wst.next()
                S.dma("sp", w[:], w_out[l].rearrange("(c p) n -> p c n", p=128)[:, q * 4:q * 4 + 4, :], W=[w])
                S.op("dve" if q % 2 == 0 else "pool", "tensor_copy", R=[w], W=[wo], out=wo[:, q * 4:q * 4 + 4, :],
                     in_=w[:])
            for tt in range(NT):
                j = 0 if tt < NTS else 1
                c0 = tt * 128
                gt = gtp.next(); mgt = mgp.next(); m = mp.next(); xt = xp.next()
                S.dma("sp", gt[:], gT[:, :, c0:c0 + 128].rearrange("m (c p) t -> p (m c) t", p=128),
                      R=[RX["gT0"][tt], RX["gT1"][tt], RX["gT2"][tt]], W=[gt])
                S.dma("sp", mgt[:], mg[c0:c0 + 128, :], R=[RX["mg"][tt]], W=[mgt])
                S.dma("sp", xt[:], xsrc[c0:c0 + 128, :], R=[RX["x"][tt]] if l > 0 else [], W=[xt])
                for q in range(3):
                    for nb in range(2):
                        ps = psf()
                        for c in range(4):
                            S.op("pe", "matmul", R=[gt, wbr], W=[ps], out=ps[:], lhsT=gt[:, q * 4 + c, :],
                                 rhs=wbr[:, q * 4 + c, nb * 512:(nb + 1) * 512], start=(c == 0), stop=(c == 3))
                        cols = slice(nb * 512, (nb + 1) * 512)
                        if q == 0:
                            S.op("dve", "tensor_tensor", R=[ps, mgt], W=[m], out=m[:, cols], in0=ps[:],
                                 in1=mgt[:, cols], op=ALU.mult)
                        else:
                            t = tp.next()
                            S.op("dve", "tensor_tensor", R=[ps, mgt], W=[t], out=t[:], in0=ps[:],
                                 in1=mgt[:, q * D + nb * 512:q * D + (nb + 1) * 512], op=ALU.mult)
                            S.op("pool", "tensor_tensor", R=[t, m], W=[m], out=m[:, cols], in0=m[:, cols], in1=t[:],
                                 op=ALU.add)
                mb = mbp.next()
                S.op("act", "copy", R=[m], W=[mb], out=mb[:], in_=m[:])
                pb = psb()
                transposes_to(pb, mb, [(mb[:, c * 128:(c + 1) * 128], 128, c * 128) for c in range(8)])
                mT = mTp.next()
                S.op("act", "copy", R=[pb], W=[mT], out=mT[:].rearrange("p c t -> p (c t)"), in_=pb[:])
                for nb in range(2):
                    ps = psf()
                    for c in range(8):
                        S.op("pe", "matmul", R=[mT, wo], W=[ps], out=ps[:], lhsT=mT[:, c, :],
                             rhs=wo[:, c, nb * 512:(nb + 1) * 512], start=(c == 0), stop=(c == 7))
                    cols = slice(nb * 512, (nb + 1) * 512)
                    t = tp.next()
                    S.op("dve", "tensor_tensor", R=[ps, gate_bc], W=[t], out=t[:], in0=ps[:], in1=gate_bc[:, j, cols],
                         op=ALU.mult)
                    S.op("pool", "tensor_tensor", R=[t, xt], W=[xt], out=xt[:, cols], in0=xt[:, cols], in1=t[:],
                         op=ALU.add)
                S.dma("pool", xdst[c0:c0 + 128, :], xt[:], R=[xt], W=[RX["x"][tt]])
            S.barrier()

    try:
        for l in range(depth):
            layer(l)
    except StopBuild as e:
        print("build stopped at", e)
        root2 = None
    S.barrier()
    S.emit()
    build.stats = (S.n_instr, S.n_wait)
    return nc


_CACHE = {}


def _get_nc(T_S, depth, debug):
    key = (T_S, depth, debug)
    if key not in _CACHE:
        _CACHE[key] = build(T_S, depth, debug)
    return _CACHE[key]


def run_cfg(inp, T_S, depth, debug=False):
    f = lambda a: np.ascontiguousarray(np.asarray(a), dtype=np.float32)
    xs = f(inp["x_sample"])
    xp = f(inp["x_prompt"])
    n_core = 8
    nsb = xs.shape[0]
    consts = make_consts(T_S)
    shared = {
        "norm_w": f(inp["norm_w"])[:depth], "w_ada": f(inp["w_ada"])[:depth], "b_ada": f(inp["b_ada"])[:depth],
        "w_in": f(inp["w_in"])[:depth], "qk_norm_w": f(inp["qk_norm_w"])[:depth],
        "diff_lambda": f(inp["diff_lambda"])[:depth], "subln_w": f(inp["subln_w"])[:depth],
        "ret_decay": f(inp["ret_decay"])[:depth].reshape(depth, 8), "ret_norm_w": f(inp["ret_norm_w"])[:depth],
        "conv_w": f(inp["conv_w"])[:depth], "gdn_a_log": f(inp["gdn_a_log"])[:depth].reshape(depth, 8),
        "gdn_dt_bias": f(inp["gdn_dt_bias"])[:depth].reshape(depth, 8), "gdn_norm_w": f(inp["gdn_norm_w"])[:depth],
        "w_branch": f(inp["w_branch"])[:depth], "w_out": f(inp["w_out"])[:depth],
    }
    for k, v in consts.items():
        shared["c_" + k] = v
    ck = f(inp["cache_attn_k"])
    cv = f(inp["cache_attn_v"])
    sr = f(inp["state_ret"])
    sgd = f(inp["state_gdn"])
    c = f(inp["c"])
    cctx = f(inp["c_ctx"])
    in_maps = []
    for core in range(n_core):
        b = core % nsb
        m = dict(shared)
        m["x_in"] = np.ascontiguousarray(np.concatenate(
            [xs[b, :T_S]] + [xp[core * NPR + i] for i in range(NPR)], axis=0))
        m["cond"] = np.ascontiguousarray(np.stack([c[b], cctx], axis=0))
        m["cache_k"] = np.ascontiguousarray(ck[b, :depth].reshape(depth, PAST, 512))
        m["cache_v"] = np.ascontiguousarray(cv[b, :depth].reshape(depth, PAST, 512))
        m["st_ret"] = np.ascontiguousarray(sr[b, :depth])
        m["st_gdn"] = np.ascontiguousarray(sgd[b, :depth])
        in_maps.append(m)
    nc = _get_nc(T_S, depth, debug)
    res = run_bass_kernel_spmd(nc, in_maps, core_ids=list(range(n_core)))
    R = res.results
    y_s = np.stack([np.asarray(R[b]["y"])[:T_S] for b in range(nsb)], axis=0).astype(np.float32)
    y_p = np.stack([np.asarray(R[core]["y"])[T_S + i * TP:T_S + (i + 1) * TP]
                    for core in range(n_core) for i in range(NPR)], axis=0).astype(np.float32)
    nk = np.concatenate([np.asarray(R[core]["nk"]) for core in range(n_core)], axis=0).astype(np.float32)
    nv = np.concatenate([np.asarray(R[core]["nv"]) for core in range(n_core)], axis=0).astype(np.float32)
    nret = np.concatenate([np.asarray(R[core]["nret"]) for core in range(n_core)], axis=0).astype(np.float32)
    ngdn = np.concatenate([np.asarray(R[core]["ngdn"]) for core in range(n_core)], axis=0).astype(np.float32)
    nk = nk.reshape(n_core * NPR, depth, TP, 4, 2, 64)
    nv = nv.reshape(n_core * NPR, depth, TP, 4, 128)
    outs = (y_p, y_s, nk, nv, nret, ngdn)
    if debug:
        return outs, R
    return outs


def kernel(**inputs):
    return run_cfg(inputs, 4096, DEPTH, False)
```

```python
import math
from contextlib import ExitStack
import numpy as np
import ml_dtypes
import concourse.bass as bass
import concourse.mybir as mybir
from concourse.bass_utils import run_bass_kernel_spmd

F32 = mybir.dt.float32
BF16 = mybir.dt.bfloat16
AF = mybir.ActivationFunctionType
ALU = mybir.AluOpType
AX = mybir.AxisListType

D = 1024
DEPTH = 4
TP = 256
NPR = 2
PAST = 256
D_IN = 8720
EPS = 1e-6
CH = 64


import os


class StopBuild(Exception):
    pass


def ckpt(name):
    if os.environ.get("KSTOP", "") == name:
        raise StopBuild(name)


class Res:
    __slots__ = ("name", "w", "r", "excl")

    def __init__(self, name=""):
        self.name = name
        self.w = None
        self.r = {}
        self.excl = False


class Tile:
    def __init__(self, h, name, psum=False):
        self.h = h
        self.res = Res(name)
        self.res.excl = psum

    def __getitem__(self, k):
        return self.h[k]


def _res(x):
    out = []
    for t in x:
        if isinstance(t, (list, tuple)):
            out.extend(_res(t))
        elif isinstance(t, Res):
            out.append(t)
        else:
            out.append(t.res)
    return out


class Sched:
    ENG = ("pe", "act", "dve", "pool", "sp")

    def __init__(self, nc, n_dma_slots=8):
        self.nc = nc
        self.streams = {e: [] for e in self.ENG}
        self.sems = {}
        self.cnt = {}
        for e in ("pe", "act", "dve", "pool"):
            self.sems[e] = nc.alloc_semaphore("s_" + e)
            self.cnt[e] = 0
        self.nslots = n_dma_slots
        self.dq = {}
        for q in ("sp", "pool", "act"):
            slots = []
            for i in range(n_dma_slots):
                k = "d_%s%d" % (q, i)
                self.sems[k] = nc.alloc_semaphore(k)
                slots.append([k, 0])
            self.dq[q] = [slots, 0]
        self.seen = {e: {} for e in self.ENG}
        self.n_instr = 0
        self.n_wait = 0

    def _wait(self, eng, key, val):
        if val is None or val <= 0:
            return
        if eng == "pe" and key == "pe":
            return
        s = self.seen[eng]
        if s.get(key, 0) >= val:
            return
        s[key] = val
        sem = self.sems[key]
        self.streams[eng].append(lambda e, sem=sem, val=val: e.wait_ge(sem, val))
        self.n_wait += 1

    def _deps(self, eng, reads, writes, is_dma=False):
        for r in reads:
            if r.w is not None:
                self._wait(eng, r.w[0], r.w[1])
        for w in writes:
            if w.w is not None:
                if is_dma or not (w.w[0] == eng):
                    self._wait(eng, w.w[0], w.w[1])
            for k, v in w.r.items():
                if (not is_dma) and k == eng:
                    continue
                self._wait(eng, k, v)

    def _mark(self, key, val, reads, writes):
        for r in reads:
            if r.r.get(key, 0) < val:
                r.r[key] = val
        for w in writes:
            w.w = (key, val)
            w.r = {}

    def op(self, eng, method, R=(), W=(), **kw):
        reads = _res(R)
        writes = _res(W)
        ex = [r for r in reads if r.excl]
        if ex:
            reads = [r for r in reads if not r.excl]
            writes = writes + [r for r in ex if r not in writes]
        self._deps(eng, reads, writes)
        self.cnt[eng] += 1
        val = self.cnt[eng]
        sem = self.sems[eng]
        import traceback
        org = traceback.extract_stack(limit=3)[0]
        org = "%s:%d" % (org.name, org.lineno)

        def _f(e, m=method, kw=kw, sem=sem, org=org):
            try:
                return getattr(e, m)(**kw).then_inc(sem, 1)
            except Exception as ex:
                raise RuntimeError("emit failed at %s (%s): %s" % (org, m, ex)) from ex
        self.streams[eng].append(_f)
        self._mark(eng, val, reads, writes)
        self.n_instr += 1

    def dma(self, q, out, in_, R=(), W=(), **kw):
        reads = _res(R)
        writes = _res(W)
        slots, idx = self.dq[q]
        slot = slots[idx % self.nslots]
        self.dq[q][1] = idx + 1
        key = slot[0]
        if slot[1] > 0:
            self._wait(q, key, slot[1])
        self._deps(q, reads, writes, is_dma=True)
        slot[1] += 16
        val = slot[1]
        sem = self.sems[key]
        import traceback
        org = traceback.extract_stack(limit=3)[0]
        org = "%s:%d" % (org.name, org.lineno)

        def _f(e, out=out, in_=in_, sem=sem, kw=kw, org=org):
            try:
                return e.dma_start(out=out, in_=in_, **kw).then_inc(sem, 16)
            except Exception as ex:
                raise RuntimeError("dma emit failed at %s: %s" % (org, ex)) from ex
        self.streams[q].append(_f)
        self._mark(key, val, reads, writes)
        self.n_instr += 1

    def barrier(self):
        for e in self.ENG:
            for k in ("pe", "act", "dve", "pool"):
                if k != e:
                    self._wait(e, k, self.cnt[k])
            for q in self.dq:
                for slot in self.dq[q][0]:
                    if slot[1] > 0:
                        self._wait(e, slot[0], slot[1])

    def emit(self):
        nc = self.nc
        st = self.streams
        with nc.Block() as block:
            @block.sync
            def _(e):
                for f in st["sp"]:
                    f(e)

            @block.tensor
            def _(e):
                for f in st["pe"]:
                    f(e)

            @block.scalar
            def _(e):
                for f in st["act"]:
                    f(e)

            @block.vector
            def _(e):
                for f in st["dve"]:
                    f(e)

            @block.gpsimd
            def _(e):
                for f in st["pool"]:
                    f(e)


def make_consts(T_S):
    c = {}
    c["ident"] = np.eye(128, dtype=np.float32)
    n_rows = T_S // 64
    row = np.repeat(np.arange(n_rows, dtype=np.float32), 64)
    col = np.tile(np.arange(64, dtype=np.float32), n_rows)
    inv = (1.0 / (10000.0 ** (np.arange(16, dtype=np.float32) / 16))).astype(np.float32)
    ar = row[:, None] * inv
    ac = col[:, None] * inv
    ang = np.concatenate([ar, ar, ac, ac], axis=-1).astype(np.float32)
    cos = np.cos(ang).astype(np.float32)
    sin = np.sin(ang).astype(np.float32)
    sgn = np.tile(np.concatenate([-np.ones(16), np.ones(16)]), 2).astype(np.float32)
    c["ropec"] = cos
    c["ropes"] = (sin * sgn).astype(np.float32)
    a = np.arange(64)[:, None]
    b = np.arange(64)[None, :]
    low = (a > b).astype(np.float32)
    up = (b > a).astype(np.float32)
    upi = (b >= a).astype(np.float32)
    lowi = (a >= b).astype(np.float32)
    gm = np.zeros((64, 2, 3, 64), np.float32)
    gm[:, 0, 0] = -low
    gm[:, 0, 1] = -up
    gm[:, 0, 2] = upi
    gm[:, 1, 0] = -up
    gm[:, 1, 1] = -low
    gm[:, 1, 2] = lowi
    c["gmask"] = gm
    a2 = np.arange(128)[:, None]
    b2 = np.arange(128)[None, :]
    same = (a2 // 64 == b2 // 64)
    gm2 = np.zeros((128, 2, 3, 128), np.float32)
    gm2[:, 0, 0] = -1.0 * ((a2 > b2) & same)
    gm2[:, 0, 1] = -1.0 * ((b2 > a2) & same)
    gm2[:, 0, 2] = ((b2 >= a2) & same)
    gm2[:, 1, 0] = -1.0 * ((b2 > a2) & same)
    gm2[:, 1, 1] = -1.0 * ((a2 > b2) & same)
    gm2[:, 1, 2] = ((a2 >= b2) & same)
    c["gmask2"] = gm2
    ut2 = np.zeros((128, 2, 128), np.float32)
    ut2[:, 0] = ((a2 <= b2) & same)
    ut2[:, 1] = ((a2 >= b2) & same)
    c["utri2"] = ut2
    c["bd1"] = same.astype(np.float32)
    sel = np.zeros((128, 2, 128), np.float32)
    sel[0:64, 0, :] = 1.0
    sel[64:128, 1, :] = 1.0
    c["selc"] = sel
    ut = np.zeros((64, 2, 64), np.float32)
    ut[:, 0] = (a <= b).astype(np.float32)
    ut[:, 1] = (a >= b).astype(np.float32)
    c["utri"] = ut
    j = np.arange(128)[:, None].astype(np.float32)
    i = np.arange(128)[None, :].astype(np.float32)
    rr = np.zeros((128, 2, 128), np.float32)
    rm = np.zeros((128, 2, 128), np.float32)
    rr[:, 0] = np.maximum(i - j, 0)
    rm[:, 0] = (i >= j)
    rr[:, 1] = np.maximum(j - i, 0)
    rm[:, 1] = (j >= i)
    c["rrel"] = rr
    c["rmask"] = rm
    rq = np.zeros((64, 2, 128), np.float32)
    rq[:, 0] = (np.arange(128) + 1.0)[None, :]
    rq[:, 1] = (128.0 - np.arange(128))[None, :]
    c["rqexp"] = rq
    rk = np.zeros((128, 2), np.float32)
    rk[:, 0] = 127.0 - np.arange(128)
    rk[:, 1] = np.arange(128)
    c["rkexp"] = rk
    return c


CONST_SHAPES = lambda T_S: {k: v.shape for k, v in make_consts(T_S).items()}


def build(T_S=4096, depth=DEPTH, debug=False):
    nc = bass.Bass("TRN2", target_bir_lowering=False)
    S = Sched(nc)
    TT = T_S + NPR * TP
    NT = TT // 128
    NTS = T_S // 128
    seqs = [(0, T_S, True)] + [(T_S + i * TP, TP, False) for i in range(NPR)]
    lam_inits = [0.8 - 0.6 * math.exp(-0.3 * l) for l in range(depth)]

    def din(name, shape, dt=F32):
        return nc.dram_tensor(name, list(shape), dt, kind="ExternalInput").ap()

    def dout(name, shape, dt=F32):
        return nc.dram_tensor(name, list(shape), dt, kind="ExternalOutput").ap()

    def scr(name, shape, dt):
        if debug:
            return nc.dram_tensor(name, list(shape), dt, kind="ExternalOutput").ap()
        return nc.dram_tensor(name, list(shape), dt).ap()

    x_in = din("x_in", [TT, D])
    cond = din("cond", [2, D])
    cache_k = din("cache_k", [depth, PAST, 512])
    cache_v = din("cache_v", [depth, PAST, 512])
    st_ret = din("st_ret", [depth, 2, 4, 64, 128])
    st_gdn = din("st_gdn", [depth, 2, 4, 128, 128])
    norm_w = din("norm_w", [depth, D])
    w_ada = din("w_ada", [depth, D, 3 * D])
    b_ada = din("b_ada", [depth, 3 * D])
    w_in = din("w_in", [depth, D, D_IN])
    qk_norm_w = din("qk_norm_w", [depth, 2, 64])
    diff_lambda = din("diff_lambda", [depth, 4, 64])
    subln_w = din("subln_w", [depth, 128])
    ret_decay = din("ret_decay", [depth, 8])
    ret_norm_w = din("ret_norm_w", [depth, 128])
    conv_w = din("conv_w", [depth, 5, 1536])
    gdn_a_log = din("gdn_a_log", [depth, 8])
    gdn_dt_bias = din("gdn_dt_bias", [depth, 8])
    gdn_norm_w = din("gdn_norm_w", [depth, 128])
    w_branch = din("w_branch", [depth, 3, 512, D])
    w_out = din("w_out", [depth, D, D])
    cst = {k: din("c_" + k, shp) for k, shp in CONST_SHAPES(T_S).items()}
    y_out = dout("y", [TT, D])
    nk_out = dout("nk", [NPR, depth, TP, 512])
    nv_out = dout("nv", [NPR, depth, TP, 512])
    nret_out = dout("nret", [NPR, depth, 2, 4, 64, 128])
    ngdn_out = dout("ngdn", [NPR, depth, 2, 4, 128, 128])
    xcur = scr("xcur", [TT, D], F32)
    aqT = scr("aqT", [4, 128, TT], BF16)
    akT = scr("akT", [4, 128, TT], BF16)
    av = scr("av", [TT, 512], BF16)
    zg = scr("zg", [3, TT, 512], BF16)
    bqT = scr("bqT", [4, 64, TT], BF16)
    bkT = scr("bkT", [4, 64, TT], BF16)
    bk = scr("bk", [TT, 256], BF16)
    bv = scr("bv", [TT, 512], BF16)
    cT = scr("cT", [1536, TT], F32)
    gqT = scr("gqT", [4, 128, TT], BF16)
    gkT = scr("gkT", [4, 128, TT], BF16)
    gk = scr("gk", [TT, 512], BF16)
    gv = scr("gv", [TT, 512], BF16)
    bgs = scr("bgs", [TT, 16], F32)
    mg = scr("mg", [TT, 3 * D], BF16)
    ofs = scr("ofs", [TT, 512], F32)
    ofb = scr("ofb", [TT, 512], F32)
    gT = scr("gT", [3, 512, TT], BF16)

    def tres(n):
        return [Res("%s%d" % (n, i)) for i in range(NT)]
    RX = {n: tres(n) for n in ("x", "aqT", "akT", "av", "zg0", "zg1", "zg2", "bqT", "bkT", "bk", "bv", "cT",
                               "gqT", "gkT", "gk", "gv", "bgs", "mg", "ofs", "ofb", "gT0", "gT1", "gT2")}
    R_out = Res("outs")

    uid = [0]

    def sb(es, name, shape, dt):
        uid[0] += 1
        nm = "%s_u%d" % (name, uid[0])
        return Tile(es.enter_context(nc.sbuf_tensor(nm, list(shape), dt)), nm)

    class Pool_:
        def __init__(self, es, name, shape, dt, n):
            self.t = [sb(es, "%s_%d" % (name, i), shape, dt) for i in range(n)]
            self.i = 0

        def next(self):
            t = self.t[self.i % len(self.t)]
            self.i += 1
            return t

    root = ExitStack()
    PSF = [Tile(nc.alloc_psum_tensor("psf%d" % i, [128, 512], F32), "psf%d" % i, True) for i in range(6)]
    PSB = [Tile(nc.alloc_psum_tensor("psb%d" % i, [128, 1024], BF16), "psb%d" % i, True) for i in range(2)]
    psc = [0, 0]

    psf_banks = [[0, 1, 2, 3]]

    def psf():
        psc[0] += 1
        bk = psf_banks[0]
        return PSF[bk[psc[0] % len(bk)]]

    def psb():
        psc[1] += 1
        return PSB[psc[1] % 2]

    ident_f = sb(root, "ident_f", [128, 128], F32)
    ident_b = sb(root, "ident_b", [128, 128], BF16)
    ones_f = sb(root, "ones_f", [128, 128], F32)
    ropec = sb(root, "ropec", [128, NTS, 64], F32)
    ropes = sb(root, "ropes", [128, NTS, 64], F32)
    gmask = sb(root, "gmask", [64, 2, 3, 64], F32)
    utri = sb(root, "utri", [64, 2, 64], F32)
    gmask2 = sb(root, "gmask2", [128, 2, 3, 128], F32)
    utri2 = sb(root, "utri2", [128, 2, 128], F32)
    bd1 = sb(root, "bd1", [128, 128], F32)
    selc = sb(root, "selc", [128, 2, 128], F32)
    rrel = sb(root, "rrel", [128, 2, 128], F32)
    rmask = sb(root, "rmask", [128, 2, 128], F32)
    rqexp = sb(root, "rqexp", [64, 2, 128], F32)
    rkexp = sb(root, "rkexp", [128, 2], F32)
    gate_bc = sb(root, "gate_bc", [128, 2, D], F32)
    wq_bc = sb(root, "wq_bc", [128, 2, 64], F32)
    subw_bc = sb(root, "subw_bc", [128, 128], F32)
    retw_bc = sb(root, "retw_bc", [128, 128], F32)
    gdnw_bc = sb(root, "gdnw_bc", [128, 128], F32)
    lamt = sb(root, "lamt", [128, 8], F32)
    dl_bc = sb(root, "dl_bc", [128, 4, 64], F32)
    lg_bc = sb(root, "lg_bc", [128, 8], F32)
    rdmat = sb(root, "rdmat", [128, 2, 4, 128], F32)
    rqdec = sb(root, "rqdec", [64, 2, 4, 128], F32)
    rkdec = sb(root, "rkdec", [128, 2, 4], F32)
    rcdec = sb(root, "rcdec", [128, 8], F32)
    negA_bc = sb(root, "negA_bc", [128, 8], F32)
    dtb_bc = sb(root, "dtb_bc", [128, 8], F32)
    convw = sb(root, "convw", [128, 12, 5], F32)
    tmp8 = sb(root, "tmp8", [128, 8], F32)

    S.dma("sp", ident_f[:], cst["ident"], W=[ident_f])
    S.op("dve", "tensor_copy", R=[ident_f], W=[ident_b], out=ident_b[:], in_=ident_f[:])
    S.op("dve", "memset", W=[ones_f], ap=ones_f[:], constant=1.0)
    S.dma("sp", ropec[:], cst["ropec"].rearrange("(n p) d -> p n d", p=128), W=[ropec])
    S.dma("sp", ropes[:], cst["ropes"].rearrange("(n p) d -> p n d", p=128), W=[ropes])
    S.dma("sp", gmask[:], cst["gmask"], W=[gmask])
    S.dma("sp", utri[:], cst["utri"], W=[utri])
    S.dma("sp", gmask2[:], cst["gmask2"], W=[gmask2])
    S.dma("sp", utri2[:], cst["utri2"], W=[utri2])
    S.dma("sp", bd1[:], cst["bd1"], W=[bd1])
    S.dma("sp", selc[:], cst["selc"], W=[selc])
    S.dma("sp", rrel[:], cst["rrel"], W=[rrel])
    S.dma("sp", rmask[:], cst["rmask"], W=[rmask])
    S.dma("sp", rqexp[:], cst["rqexp"], W=[rqexp])
    S.dma("sp", rkexp[:], cst["rkexp"], W=[rkexp])

    def rsqrt_inplace(t, ap, scale, eps):
        S.op("dve", "tensor_scalar", R=[t], W=[t], out=ap, in0=ap, scalar1=scale, scalar2=eps,
             op0=ALU.mult, op1=ALU.add)
        S.op("act", "activation", R=[t], W=[t], out=ap, in_=ap, func=AF.Sqrt)
        S.op("dve", "reciprocal", R=[t], W=[t], out=ap, in_=ap)

    def transposes_to(ps_t, src_t, blocks, rows=128):
        for (sap, w, off) in blocks:
            S.op("pe", "transpose", R=[src_t, ident_b], W=[ps_t], out=ps_t[0:w, off:off + rows], in_=sap,
                 identity=ident_b[0:rows, 0:rows])

    def layer(l):
        last = (l == depth - 1)
        xsrc = x_in if l == 0 else xcur
        xdst = y_out if last else xcur
        esH = ExitStack()
        hT = sb(esH, "hT", [128, 8, TT], BF16)
        esA = ExitStack()
        A_bc = sb(esA, "A_bc", [128, 2, D], F32)
        sh_bc = sb(esA, "sh_bc", [128, 2, D], F32)
        with ExitStack() as es:
            nw_bc = sb(es, "nw_bc", [128, D], F32)
            cs = sb(es, "cs", [128, 2, 8], F32)
            rep = sb(es, "rep", [128, 16, 128], F32)
            bada = sb(es, "bada", [1, 3 * D], F32)
            wa = Pool_(es, "wa", [128, 8, 512], F32, 2)
            S.dma("sp", nw_bc[:], norm_w[l].partition_broadcast(128), W=[nw_bc])
            S.dma("sp", cs[:], cond.rearrange("j (p c) -> p j c", c=8), W=[cs])
            S.dma("sp", bada[:], b_ada[l:l + 1, :], W=[bada])
            S.dma("sp", wq_bc[:],
                  qk_norm_w[l].rearrange("a d -> (a d)").partition_broadcast(128).rearrange("p (a d) -> p a d", a=2),
                  W=[wq_bc])
            S.dma("sp", subw_bc[:], subln_w[l].partition_broadcast(128), W=[subw_bc])
            S.dma("sp", retw_bc[:], ret_norm_w[l].partition_broadcast(128), W=[retw_bc])
            S.dma("sp", gdnw_bc[:], gdn_norm_w[l].partition_broadcast(128), W=[gdnw_bc])
            S.dma("sp", dl_bc[:], diff_lambda[l].rearrange("a d -> (a d)").partition_broadcast(128)
                  .rearrange("p (a d) -> p a d", a=4), W=[dl_bc])
            S.dma("sp", lg_bc[:], ret_decay[l].partition_broadcast(128), W=[lg_bc])
            S.dma("sp", negA_bc[:], gdn_a_log[l].partition_broadcast(128), W=[negA_bc])
            S.dma("sp", dtb_bc[:], gdn_dt_bias[l].partition_broadcast(128), W=[dtb_bc])
            for k in range(5):
                S.dma("sp", convw[:, :, k:k + 1], conv_w[l, k].rearrange("(c p o) -> p c o", p=128, o=1), W=[convw],
                      allow_slow_non_contiguous=True)
            S.op("dve", "tensor_scalar", R=[wq_bc], W=[wq_bc], out=wq_bc[:, 0, :], in0=wq_bc[:, 0, :],
                 scalar1=0.125, scalar2=None, op0=ALU.mult)
            S.op("dve", "tensor_scalar", R=[subw_bc], W=[subw_bc], out=subw_bc[:], in0=subw_bc[:],
                 scalar1=float(1.0 - lam_inits[l]), scalar2=None, op0=ALU.mult)
            S.op("dve", "tensor_tensor", R=[dl_bc], W=[dl_bc], out=dl_bc[:, 0, :], in0=dl_bc[:, 0, :],
                 in1=dl_bc[:, 1, :], op=ALU.mult)
            S.op("dve", "tensor_tensor", R=[dl_bc], W=[dl_bc], out=dl_bc[:, 2, :], in0=dl_bc[:, 2, :],
                 in1=dl_bc[:, 3, :], op=ALU.mult)
            S.op("dve", "tensor_reduce", R=[dl_bc], W=[lamt], out=lamt[:, 0:4], in_=dl_bc[:], axis=AX.X, op=ALU.add)
            S.op("act", "activation", R=[lamt], W=[lamt], out=lamt[:, 4:8], in_=lamt[:, 0:4], func=AF.Exp)
            S.op("dve", "tensor_tensor", R=[lamt], W=[lamt], out=lamt[:, 1:2], in0=lamt[:, 6:7], in1=lamt[:, 4:5],
                 op=ALU.subtract)
            S.op("dve", "tensor_scalar", R=[lamt], W=[lamt], out=lamt[:, 0:1], in0=lamt[:, 1:2],
                 scalar1=float(-lam_inits[l]), scalar2=None, op0=ALU.add)
            S.op("act", "activation", R=[lg_bc], W=[lg_bc], out=lg_bc[:], in_=lg_bc[:], func=AF.Exp, scale=-1.0)
            S.op("act", "activation", R=[lg_bc], W=[lg_bc], out=lg_bc[:], in_=lg_bc[:], func=AF.Ln, bias=1.0)
            S.op("dve", "tensor_scalar", R=[lg_bc], W=[lg_bc], out=lg_bc[:], in0=lg_bc[:], scalar1=-1.0,
                 scalar2=None, op0=ALU.mult)
            for d in range(2):
                for h in range(4):
                    u = d * 4 + h
                    S.op("act", "activation", R=[rrel, lg_bc], W=[rdmat], out=rdmat[:, d, h, :], in_=rrel[:, d, :],
                         func=AF.Exp, scale=lg_bc[:, u:u + 1])
                    S.op("dve", "tensor_tensor", R=[rdmat, rmask], W=[rdmat], out=rdmat[:, d, h, :],
                         in0=rdmat[:, d, h, :], in1=rmask[:, d, :], op=ALU.mult)
                    S.op("act", "activation", R=[rqexp, lg_bc], W=[rqdec], out=rqdec[:, d, h, :], in_=rqexp[:, d, :],
                         func=AF.Exp, scale=lg_bc[0:64, u:u + 1])
                    S.op("act", "activation", R=[rkexp, lg_bc], W=[rkdec], out=rkdec[:, d, h:h + 1],
                         in_=rkexp[:, d:d + 1], func=AF.Exp, scale=lg_bc[:, u:u + 1])
            S.op("act", "activation", R=[lg_bc], W=[rcdec], out=rcdec[:], in_=lg_bc[:], func=AF.Exp, scale=128.0)
            S.op("act", "activation", R=[negA_bc], W=[negA_bc], out=negA_bc[:], in_=negA_bc[:], func=AF.Exp)
            S.op("dve", "tensor_scalar", R=[negA_bc], W=[negA_bc], out=negA_bc[:], in0=negA_bc[:], scalar1=-1.0,
                 scalar2=None, op0=ALU.mult)
            S.op("act", "activation", R=[cs], W=[cs], out=cs[:], in_=cs[:], func=AF.Silu)
            S.op("dve", "tensor_copy", R=[cs], W=[rep], out=rep[:],
                 in_=cs[:].rearrange("p j c -> p (j c)").unsqueeze(2).to_broadcast([128, 16, 128]))
            for nb in range(6):
                w = wa.next()
                S.dma("sp", w[:], w_ada[l].rearrange("(p c) n -> p c n", c=8)[:, :, nb * 512:(nb + 1) * 512], W=[w])
                for j in range(2):
                    ps = psf()
                    for c in range(8):
                        S.op("pe", "matmul", R=[rep, w], W=[ps], out=ps[:], lhsT=rep[:, j * 8 + c, :], rhs=w[:, c, :],
                             start=(c == 0), stop=False)
                    S.op("pe", "matmul", R=[ones_f, bada], W=[ps], out=ps[:], lhsT=ones_f[0:1, :],
                         rhs=bada[0:1, nb * 512:(nb + 1) * 512], start=False, stop=True)
                    cols = slice((nb % 2) * 512, (nb % 2) * 512 + 512)
                    if nb < 2:
                        S.op("act", "copy", R=[ps], W=[sh_bc], out=sh_bc[:, j, cols], in_=ps[:])
                    elif nb < 4:
                        S.op("dve", "scalar_tensor_tensor", R=[ps, nw_bc], W=[A_bc], out=A_bc[:, j, cols], in0=ps[:],
                             scalar=1.0, in1=nw_bc[:, cols], op0=ALU.add, op1=ALU.mult)
                    else:
                        S.op("act", "copy", R=[ps], W=[gate_bc], out=gate_bc[:, j, cols], in_=ps[:])
            S.barrier()
        ckpt("A")
        with ExitStack() as es:
            xp = Pool_(es, "xB", [128, D], F32, 4)
            hp = Pool_(es, "hB", [128, D], F32, 4)
            hbp = Pool_(es, "hbB", [128, D], BF16, 4)
            junk = sb(es, "junkB", [128, D], F32)
            ssp = Pool_(es, "ssB", [128, 1], F32, 4)
            def tileB(tt):
                j = 0 if tt < NTS else 1
                xt = xp.next()
                ht = hp.next()
                hb = hbp.next()
                ss = ssp.next()
                S.dma("sp", xt[:], xsrc[tt * 128:(tt + 1) * 128, :], R=[RX["x"][tt]] if l > 0 else [], W=[xt])
                S.op("act", "activation", R=[xt], W=[junk, ss], out=junk[:], in_=xt[:], func=AF.Square, accum_out=ss[:])
                yield
                rsqrt_inplace(ss, ss[:], 1.0 / D, EPS)
                S.op("dve", "scalar_tensor_tensor", R=[xt, ss, A_bc], W=[ht], out=ht[:], in0=xt[:], scalar=ss[:, 0:1],
                     in1=A_bc[:, j, :], op0=ALU.mult, op1=ALU.mult)
                S.op("dve", "tensor_tensor", R=[ht, sh_bc], W=[hb], out=hb[:], in0=ht[:], in1=sh_bc[:, j, :], op=ALU.add)
                yield
                pb = psb()
                transposes_to(pb, hb, [(hb[:, c * 128:(c + 1) * 128], 128, c * 128) for c in range(8)])
                S.op("act", "copy", R=[pb], W=[hT], out=hT[:, :, tt * 128:(tt + 1) * 128],
                     in_=pb[:].rearrange("p (c t) -> p c t", c=8))
            pipeline([tileB(tt) for tt in range(NT)], 3)
            S.barrier()
        esA.close()
        ckpt("B")
        with ExitStack() as es:
            wf = Pool_(es, "wfC", [128, 4, 512], F32, 2)
            wbp = Pool_(es, "wbC", [128, 8, 512], BF16, 2)
            f1 = Pool_(es, "f1C", [128, 512], F32, 9)
            f2 = Pool_(es, "f2C", [128, 512], F32, 3)
            b1 = Pool_(es, "b1C", [128, 512], BF16, 4)
            st8 = Pool_(es, "st8C", [128, 16], F32, 4)
            stg = Pool_(es, "stgC", [128, 1024], BF16, 3)
            cfp = Pool_(es, "cfC", [128, 512], F32, 3)

            def load_wblock(c0, ncols):
                wb = wbp.next()
                for half in range(2):
                    w = wf.next()
                    S.dma("sp", w[:, :, 0:ncols],
                          w_in[l].rearrange("(c p) n -> p c n", p=128)[:, half * 4:half * 4 + 4, c0:c0 + ncols], W=[w])
                    S.op("dve" if half == 0 else "pool", "tensor_copy", R=[w], W=[wb],
                         out=wb[:, half * 4:half * 4 + 4, 0:ncols], in_=w[:, :, 0:ncols])
                return (wb, (c0, ncols))

            wblocks = [(0, 512), (512, 512), (1024, 512), (1536, 512), (2560, 512), (3072, 512), (5120, 512),
                       (2048, 512), (3584, 512), (4096, 512), (4608, 512), (5632, 16)] + \
                      [(5648 + i * 512, 512) for i in range(6)]
            wq = []

            def next_wb(c0, ncols):
                if not wq:
                    wq.append(load_wblock(*wblocks.pop(0)))
                wb = wq.pop(0)
                assert wb[1] == (c0, ncols), (wb[1], c0, ncols)
                if wblocks:
                    wq.append(load_wblock(*wblocks.pop(0)))
                return wb[0]

            def mm_tok(wb, tt, ncols):
                ps = psf()
                for c in range(8):
                    S.op("pe", "matmul", R=[hT, wb], W=[ps], out=ps[:, 0:ncols], lhsT=hT[:, c, tt * 128:(tt + 1) * 128],
                         rhs=wb[:, c, 0:ncols], start=(c == 0), stop=(c == 7))
                return ps

            def rope(src, tt, ngrp, dst_pool):
                t1 = dst_pool.next()
                t2 = dst_pool.next()
                s4 = src[:, 0:ngrp * 64].rearrange("p (g a h f) -> p g a h f", a=2, h=2, f=16)
                S.op("dve", "tensor_tensor", R=[src, ropec], W=[t1],
                     out=t1[:, 0:ngrp * 64].rearrange("p (g d) -> p g d", d=64),
                     in0=src[:, 0:ngrp * 64].rearrange("p (g d) -> p g d", d=64),
                     in1=ropec[:, tt:tt + 1, :].to_broadcast([128, ngrp, 64]), op=ALU.mult)
                t24 = t2[:, 0:ngrp * 64].rearrange("p (g a h f) -> p g a h f", a=2, h=2, f=16)
                sn4 = ropes[:, tt, :].rearrange("p (a h f) -> p a h f", a=2, h=2)
                for hh in range(2):
                    S.op("pool", "tensor_tensor", R=[src, ropes], W=[t2], out=t24[:, :, :, hh, :],
                         in0=s4[:, :, :, 1 - hh, :],
                         in1=sn4[:, :, hh, :].unsqueeze(1).to_broadcast([128, ngrp, 2, 16]), op=ALU.mult)
                S.op("dve", "tensor_tensor", R=[t1, t2], W=[t1], out=t1[:, 0:ngrp * 64], in0=t1[:, 0:ngrp * 64],
                     in1=t2[:, 0:ngrp * 64], op=ALU.add)
                return t1

            def tileQK(blk, wb, tt):
                is_s = tt < NTS
                ps = mm_tok(wb, tt, 512)
                sq = f2.next()
                s8 = st8.next()
                qn = f1.next()
                S.op("act", "activation", R=[ps], W=[sq], out=sq[:], in_=ps[:], func=AF.Square)
                S.op("dve", "tensor_reduce", R=[sq], W=[s8], out=s8[:, 0:8],
                     in_=sq[:].rearrange("p (g d) -> p g d", d=64), axis=AX.X, op=ALU.add)
                yield
                rsqrt_inplace(s8, s8[:, 0:8], 1.0 / 64, EPS)
                S.op("dve", "tensor_tensor", R=[ps, s8], W=[qn], out=qn[:].rearrange("p (g d) -> p g d", d=64),
                     in0=ps[:].rearrange("p (g d) -> p g d", d=64),
                     in1=s8[:, 0:8].unsqueeze(2).to_broadcast([128, 8, 64]), op=ALU.mult)
                S.op("pool", "tensor_tensor", R=[qn, wq_bc], W=[qn], out=qn[:].rearrange("p (g d) -> p g d", d=64),
                     in0=qn[:].rearrange("p (g d) -> p g d", d=64),
                     in1=wq_bc[:, blk:blk + 1, :].to_broadcast([128, 8, 64]), op=ALU.mult)
                if blk == 1 and not is_s:
                    pi = (tt - NTS) // 2
                    r0 = ((tt - NTS) % 2) * 128
                    S.dma("pool", nk_out[pi, l, r0:r0 + 128, :], qn[:], R=[qn])
                yield
                if is_s:
                    qn = rope(qn, tt, 8, f1)
                qb = b1.next()
                S.op("act", "copy", R=[qn], W=[qb], out=qb[:], in_=qn[:])
                yield
                pb = psb()
                transposes_to(pb, qb, [(qb[:, h * 128:(h + 1) * 128], 128, h * 128) for h in range(4)])
                sg = stg.next()
                S.op("dve", "tensor_copy", R=[pb], W=[sg], out=sg[:, 0:512], in_=pb[:, 0:512])
                dst = aqT if blk == 0 else akT
                S.dma("pool", dst[:, :, tt * 128:(tt + 1) * 128].rearrange("h p t -> p h t"),
                      sg[:, 0:512].rearrange("p (h t) -> p h t", h=4), R=[sg],
                      W=[RX["aqT" if blk == 0 else "akT"][tt]])

            for blk in range(2):
                wb = next_wb(blk * 512, 512)
                pipeline([tileQK(blk, wb, tt) for tt in range(NT)], 3)
            ckpt("C0")
            for (c0, kind) in ((1024, "av"), (1536, "z0"), (2560, "bv"), (3072, "z1"), (5120, "z2")):
                wb = next_wb(c0, 512)
                for tt in range(NT):
                    is_s = tt < NTS
                    ps = mm_tok(wb, tt, 512)
                    ob = b1.next()
                    if kind in ("av", "bv"):
                        S.op("act", "copy", R=[ps], W=[ob], out=ob[:], in_=ps[:])
                        dst, rn = (av, "av") if kind == "av" else (bv, "bv")
                        S.dma("pool", dst[tt * 128:(tt + 1) * 128, :], ob[:], R=[ob], W=[RX[rn][tt]])
                        if kind == "av" and not is_s:
                            of_ = f1.next()
                            S.op("dve", "tensor_copy", R=[ps], W=[of_], out=of_[:], in_=ps[:])
                            pi = (tt - NTS) // 2
                            r0 = ((tt - NTS) % 2) * 128
                            S.dma("pool", nv_out[pi, l, r0:r0 + 128, :], of_[:], R=[of_])
                    else:
                        zi = int(kind[1])
                        S.op("act", "activation", R=[ps], W=[ob], out=ob[:], in_=ps[:], func=AF.Silu)
                        S.dma("pool", zg[zi, tt * 128:(tt + 1) * 128, :], ob[:], R=[ob], W=[RX["zg%d" % zi][tt]])
                ckpt("C1_" + kind)
            ckpt("C1")
            wb = next_wb(2048, 512)
            for tt in range(NT):
                is_s = tt < NTS
                ps = mm_tok(wb, tt, 512)
                qk = f1.next()
                S.op("act", "copy", R=[ps], W=[qk], out=qk[:, 0:256], in_=ps[:, 0:256])
                S.op("act", "mul", R=[ps], W=[qk], out=qk[:, 256:512], in_=ps[:, 256:512], mul=0.125)
                if is_s:
                    qk = rope(qk, tt, 8, f1)
                qb = b1.next()
                S.op("act", "copy", R=[qk], W=[qb], out=qb[:], in_=qk[:])
                S.dma("pool", bk[tt * 128:(tt + 1) * 128, :], qb[:, 256:512], R=[qb], W=[RX["bk"][tt]])
                pb = psb()
                transposes_to(pb, qb, [(qb[:, g * 64:(g + 1) * 64], 64, g * 128) for g in range(8)])
                sg = stg.next()
                S.op("dve", "tensor_copy", R=[pb], W=[sg], out=sg[0:64, :], in_=pb[0:64, :])
                S.dma("pool", bqT[:, :, tt * 128:(tt + 1) * 128].rearrange("h p t -> p h t"),
                      sg[0:64, 0:512].rearrange("p (h t) -> p h t", h=4), R=[sg], W=[RX["bqT"][tt]])
                S.dma("pool", bkT[:, :, tt * 128:(tt + 1) * 128].rearrange("h p t -> p h t"),
                      sg[0:64, 512:1024].rearrange("p (h t) -> p h t", h=4), R=[sg], W=[RX["bkT"][tt]])
            ckpt("C2")
            for blk in range(3):
                wb = next_wb(3584 + blk * 512, 512)
                for cc in range(4):
                    for (t0, T, _) in seqs:
                        for g0 in range(0, T, 512):
                            n = min(512, T - g0)
                            ps = psf()
                            for c in range(8):
                                S.op("pe", "matmul", R=[hT, wb], W=[ps], out=ps[:, 0:n],
                                     lhsT=wb[:, c, cc * 128:(cc + 1) * 128], rhs=hT[:, c, t0 + g0:t0 + g0 + n],
                                     start=(c == 0), stop=(c == 7))
                            cf = cfp.next()
                            S.op("act", "copy", R=[ps], W=[cf], out=cf[:, 0:n], in_=ps[:, 0:n])
                            ch0 = (blk * 4 + cc) * 128
                            tiles = range((t0 + g0) // 128, (t0 + g0 + n) // 128)
                            S.dma("pool", cT[ch0:ch0 + 128, t0 + g0:t0 + g0 + n], cf[:, 0:n], R=[cf],
                                  W=[RX["cT"][i] for i in tiles])
            ckpt("C3")
            wb = next_wb(5632, 16)
            for tt in range(NT):
                ps = mm_tok(wb, tt, 16)
                o16 = st8.next()
                S.op("act", "activation", R=[ps], W=[o16], out=o16[:, 0:8], in_=ps[:, 0:8], func=AF.Sigmoid)
                S.op("dve", "tensor_tensor", R=[ps, dtb_bc], W=[o16], out=o16[:, 8:16], in0=ps[:, 8:16], in1=dtb_bc[:],
                     op=ALU.add)
                S.op("act", "activation", R=[o16], W=[o16], out=o16[:, 8:16], in_=o16[:, 8:16], func=AF.Exp)
                S.op("act", "activation", R=[o16], W=[o16], out=o16[:, 8:16], in_=o16[:, 8:16], func=AF.Ln, bias=1.0)
                S.op("dve", "tensor_tensor", R=[o16, negA_bc], W=[o16], out=o16[:, 8:16], in0=o16[:, 8:16],
                     in1=negA_bc[:], op=ALU.mult)
                S.dma("pool", bgs[tt * 128:(tt + 1) * 128, :], o16[:], R=[o16], W=[RX["bgs"][tt]])
            ckpt("C4")
            for blk in range(6):
                wb = next_wb(5648 + blk * 512, 512)
                for tt in range(NT):
                    ps = mm_tok(wb, tt, 512)
                    ob = b1.next()
                    S.op("act", "activation", R=[ps], W=[ob], out=ob[:], in_=ps[:], func=AF.Sigmoid)
                    S.dma("pool", mg[tt * 128:(tt + 1) * 128, blk * 512:(blk + 1) * 512], ob[:], R=[ob],
                          W=[RX["mg"][tt]] if blk == 5 else [])
            S.barrier()
        esH.close()
        ckpt("C")
        for si, (t0, T, is_s) in enumerate(seqs):
            if is_s and not os.environ.get("KNOOVL"):
                sample_mixers_overlapped(l, si, t0, T, is_s)
                ckpt("gpre%d" % si)
            elif os.environ.get("KNOOVL"):
                attention(l, si, t0, T, is_s)
                ckpt("att%d" % si)
                retention(l, si, t0, T, is_s)
                ckpt("ret%d" % si)
                gdn_pre(l, si, t0, T, is_s)
                ckpt("gpre%d" % si)
            gdn_scan(l, si, t0, T, is_s)
            ckpt("gscan%d" % si)
        phaseE(l, xsrc, xdst)

    def norm_gate_store(es_tiles, o_t, o_ap, w_bc, zi, mi, tt, rows, col_lo=None):
        sq, s4, zt, gb, sg = es_tiles
        tok0 = tt * 128 + (col_lo or 0)
        S.op("act", "activation", R=[o_t], W=[sq], out=sq[0:rows, :], in_=o_ap, func=AF.Square)
        S.op("dve", "tensor_reduce", R=[sq], W=[s4], out=s4[0:rows, 0:4],
             in_=sq[0:rows, :].rearrange("p (g d) -> p g d", d=128), axis=AX.X, op=ALU.add)
        rsqrt_inplace(s4, s4[0:rows, 0:4], 1.0 / 128, EPS)
        S.dma("sp", zt[0:rows, :], zg[zi, tok0:tok0 + rows, :], R=[RX["zg%d" % zi][tt]], W=[zt])
        S.op("dve", "tensor_tensor", R=[o_t, s4], W=[o_t], out=o_ap.rearrange("p (g d) -> p g d", d=128),
             in0=o_ap.rearrange("p (g d) -> p g d", d=128),
             in1=s4[0:rows, 0:4].unsqueeze(2).to_broadcast([rows, 4, 128]), op=ALU.mult)
        S.op("pool", "tensor_tensor", R=[o_t, w_bc], W=[o_t], out=o_ap.rearrange("p (g d) -> p g d", d=128),
             in0=o_ap.rearrange("p (g d) -> p g d", d=128),
             in1=w_bc[0:rows, :].unsqueeze(1).to_broadcast([rows, 4, 128]), op=ALU.mult)
        S.op("dve", "tensor_tensor", R=[o_t, zt], W=[gb], out=gb[0:rows, :], in0=o_ap, in1=zt[0:rows, :], op=ALU.mult)
        pb = psb()
        for g in range(4):
            S.op("pe", "transpose", R=[gb, ident_b], W=[pb], out=pb[:, g * 128:g * 128 + rows],
                 in_=gb[0:rows, g * 128:(g + 1) * 128], identity=ident_b[0:rows, 0:rows])
        pv = pb[:, 0:512].rearrange("p (g t) -> p g t", g=4)[:, :, 0:rows]
        S.op("act", "copy", R=[pb], W=[sg], out=sg[:, :, 0:rows], in_=pv)
        S.dma("pool", gT[mi, :, tok0:tok0 + rows].rearrange("(g p) t -> p g t", p=128), sg[:, :, 0:rows], R=[sg],
              W=[RX["gT%d" % mi][tt]])

    def ng_tiles(es, pfx):
        return (sb(es, pfx + "sq", [128, 512], F32), sb(es, pfx + "s4", [128, 4], F32),
                sb(es, pfx + "zt", [128, 512], BF16), sb(es, pfx + "gb", [128, 512], BF16),
                sb(es, pfx + "sg", [128, 4, 128], BF16))

    def attention_gen(l, si, t0, T, is_s, es, acc_sets, sc_banks=None):
        Sk = T + (PAST if is_s else 0)
        nst = Sk // 128
        ntl = T // 128
        QB = min(512, T)
        kT = sb(es, "at_kT", [128, 4, Sk], BF16)
        V1 = sb(es, "at_V1", [128, nst, 4, 130], BF16)
        qTp = Pool_(es, "at_qT", [128, 4, QB], BF16, 2)
        ex = Pool_(es, "at_ex", [128, QB], BF16, 3)
        osb = [sb(es, "at_os%d" % qs, [128, 512], F32) for qs in range(QB // 128)]
        rc = Pool_(es, "at_rc", [128, 2], F32, 4)
        ngt = ng_tiles(es, "at_")
        grp = [0]
        scn = [0]
        S.dma("sp", kT[:, :, 0:T], akT[:, :, t0:t0 + T].rearrange("h p t -> p h t"),
              R=[RX["akT"][i] for i in range(t0 // 128, (t0 + T) // 128)], W=[kT])
        S.op("pool", "memset", W=[V1], ap=V1[:, :, :, 128:130], constant=1.0)
        for i in range(ntl):
            S.dma("sp", V1[:, i, :, 0:128], av[t0 + i * 128:t0 + (i + 1) * 128, :].rearrange("p (h e) -> p h e", h=4),
                  R=[RX["av"][t0 // 128 + i]], W=[V1])
        if is_s:
            with ExitStack() as es2:
                ck = sb(es2, "at_ck", [128, 2, 512], F32)
                cv = sb(es2, "at_cv", [128, 2, 512], F32)
                ckb = sb(es2, "at_ckb", [128, 2, 512], BF16)
                S.dma("sp", ck[:], cache_k[l].rearrange("(n p) f -> p n f", p=128), W=[ck])
                S.dma("sp", cv[:], cache_v[l].rearrange("(n p) f -> p n f", p=128), W=[cv])
                S.op("dve", "tensor_copy", R=[ck], W=[ckb], out=ckb[:], in_=ck[:])
                for n in range(2):
                    S.op("pool", "tensor_copy", R=[cv], W=[V1], out=V1[:, ntl + n, :, 0:128],
                         in_=cv[:, n, :].rearrange("p (h e) -> p h e", h=4))
                    pb = psb()
                    transposes_to(pb, ckb, [(ckb[:, n, h * 128:(h + 1) * 128], 128, h * 128) for h in range(4)])
                    S.op("act", "copy", R=[pb], W=[kT], out=kT[:, :, T + n * 128:T + (n + 1) * 128],
                         in_=pb[:, 0:512].rearrange("p (h t) -> p h t", h=4))
                S.barrier()
        for qb0 in range(0, T, QB):
            qT = qTp.next()
            S.dma("sp", qT[:], aqT[:, :, t0 + qb0:t0 + qb0 + QB].rearrange("h p t -> p h t"),
                  R=[RX["aqT"][i] for i in range((t0 + qb0) // 128, (t0 + qb0 + QB) // 128)], W=[qT])
            nqs = QB // 128
            for h in range(4):
                for m in range(2):
                    grp[0] += 1
                    acc = acc_sets[grp[0] % len(acc_sets)]

                    def pv(st, e, acc=acc, h=h):
                        for qs in range(nqs):
                            a = acc[qs // 2]
                            S.op("pe", "matmul", R=[e, V1], W=[a], out=a[:, (qs % 2) * 256:(qs % 2) * 256 + 129],
                                 lhsT=e[:, qs * 128:(qs + 1) * 128], rhs=V1[:, st, h, 0:129],
                                 start=(st == 0 and qs % 2 == 0), stop=(st == nst - 1), skip_group_check=True)
                    pend = None
                    for st in range(nst):
                        scn[0] += 1
                        _sb = sc_banks or [PSF[0], PSF[1]]
                        ps = _sb[scn[0] % len(_sb)]
                        S.op("pe", "matmul", R=[kT, qT], W=[ps], out=ps[:, 0:QB],
                             lhsT=kT[m * 64:(m + 1) * 64, h, st * 128:(st + 1) * 128],
                             rhs=qT[m * 64:(m + 1) * 64, h, :], start=True, stop=True)
                        e = ex.next()
                        S.op("act", "activation", R=[ps], W=[e], out=e[:, 0:QB], in_=ps[:, 0:QB], func=AF.Exp)
                        if pend is not None:
                            pv(*pend)
                        pend = (st, e)
                        yield
                    pv(*pend)
                    for qs in range(nqs):
                        a = acc[qs // 2]
                        c0 = (qs % 2) * 256
                        r = rc.next()
                        S.op("dve", "reciprocal", R=[a], W=[r], out=r[:, 0:1], in_=a[:, c0 + 128:c0 + 129])
                        if m == 0:
                            S.op("dve", "tensor_scalar", R=[a, r], W=[osb[qs]], out=osb[qs][:, h * 128:(h + 1) * 128],
                                 in0=a[:, c0:c0 + 128], scalar1=r[:, 0:1], scalar2=None, op0=ALU.mult)
                        else:
                            S.op("dve", "tensor_tensor", R=[r, lamt], W=[r], out=r[:, 1:2], in0=r[:, 0:1],
                                 in1=lamt[:, 0:1], op=ALU.mult)
                            S.op("dve", "scalar_tensor_tensor", R=[a, r, osb[qs]], W=[osb[qs]],
                                 out=osb[qs][:, h * 128:(h + 1) * 128], in0=a[:, c0:c0 + 128], scalar=r[:, 1:2],
                                 in1=osb[qs][:, h * 128:(h + 1) * 128], op0=ALU.mult, op1=ALU.add)
            for qs in range(nqs):
                tt = (t0 + qb0) // 128 + qs
                norm_gate_store(ngt, osb[qs], osb[qs][:], subw_bc, 0, 0, tt, 128)

    def attention(l, si, t0, T, is_s):
        with ExitStack() as es:
            for _ in attention_gen(l, si, t0, T, is_s, es, [[PSF[2], PSF[3]], [PSF[4], PSF[5]]]):
                pass
            S.barrier()

    def retention(l, si, t0, T, is_s):
        with ExitStack() as es:
            run_rr([ret_chain(l, si, t0, T, is_s, d, es) for d in range(2)])
            S.barrier()
        combine(t0, T, retw_bc, 1, 1)

    def rr_gen(gens):
        gens = list(gens)
        while gens:
            for g in list(gens):
                try:
                    next(g)
                    yield
                except StopIteration:
                    gens.remove(g)

    def side_gen(l, si, t0, T, is_s):
        with ExitStack() as es2:
            yield from gdn_pre_gen(l, si, t0, T, is_s, es2, 256, 1)
            S.barrier()
        with ExitStack() as es3:
            yield from rr_gen([ret_chain(l, si, t0, T, is_s, d, es3) for d in range(2)])
            S.barrier()
        yield from combine_gen(t0, T, retw_bc, 1, 1)
        for pi in range(1, len(seqs)):
            (tp0, Tp, _) = seqs[pi]
            with ExitStack() as esp:
                yield from attention_gen(l, pi, tp0, Tp, False, esp, [[PSF[5]]], sc_banks=[PSF[4]])
                S.barrier()
            with ExitStack() as esp:
                yield from gdn_pre_gen(l, pi, tp0, Tp, False, esp, 256, 1)
                S.barrier()
            with ExitStack() as esp:
                yield from rr_gen([ret_chain(l, pi, tp0, Tp, False, d, esp) for d in range(2)])
                S.barrier()
            yield from combine_gen(tp0, Tp, retw_bc, 1, 1)

    def sample_mixers_overlapped(l, si, t0, T, is_s):
        with ExitStack() as es:
            att = attention_gen(l, si, t0, T, is_s, es, [[PSF[2], PSF[3]]])
            next(att)
            old = psf_banks[0]
            psf_banks[0] = [4, 5]
            n_att = (T // min(512, T)) * 8 * ((T + PAST) // 128)
            n_side = (T // 256) * 36 + (T // 128) * 5 + (len(seqs) - 1) * 70
            run_weighted(att, side_gen(l, si, t0, T, is_s), max(1, int(0.9 * n_att / n_side)))
            psf_banks[0] = old
            S.barrier()

    def ret_chain(l, si, t0, T, is_s, d, es):
        ntl = T // 128
        pf = "rt%d_" % d
        Sf = sb(es, pf + "S", [64, 4, 128], F32)
        Sb_ = sb(es, pf + "Sb", [64, 4, 128], BF16)
        qTp = Pool_(es, pf + "qT", [64, 4, 128], BF16, 2)
        kTp = Pool_(es, pf + "kT", [64, 4, 128], BF16, 2)
        ktp = Pool_(es, pf + "k", [128, 256], BF16, 2)
        vp = Pool_(es, pf + "v", [128, 512], BF16, 2)
        itp = Pool_(es, pf + "it", [128, 512], BF16, 2)
        qdp = Pool_(es, pf + "qd", [64, 4, 128], BF16, 2)
        kdp = Pool_(es, pf + "kd", [128, 256], BF16, 2)
        op_ = Pool_(es, pf + "o", [128, 512], F32, 2)
        odst, orn = (ofs, "ofs") if d == 0 else (ofb, "ofb")
        if is_s:
            S.dma("sp", Sf[:], st_ret[l, d].rearrange("h k e -> k h e"), W=[Sf])
        else:
            S.op("dve", "memset", W=[Sf], ap=Sf[:], constant=0.0)
        S.op("act", "copy", R=[Sf], W=[Sb_], out=Sb_[:], in_=Sf[:])
        order = range(ntl) if d == 0 else range(ntl - 1, -1, -1)
        for i in order:
            tt = t0 // 128 + i
            c0 = tt * 128
            qT = qTp.next(); kT = kTp.next(); kt = ktp.next(); v = vp.next()
            S.dma("sp", qT[:], bqT[:, :, c0:c0 + 128].rearrange("h p t -> p h t"), R=[RX["bqT"][tt]], W=[qT])
            S.dma("sp", kT[:], bkT[:, :, c0:c0 + 128].rearrange("h p t -> p h t"), R=[RX["bkT"][tt]], W=[kT])
            S.dma("sp", kt[:], bk[c0:c0 + 128, :], R=[RX["bk"][tt]], W=[kt])
            S.dma("sp", v[:], bv[c0:c0 + 128, :], R=[RX["bv"][tt]], W=[v])
            ps = psf()
            for h in range(4):
                S.op("pe", "matmul", R=[kT, qT], W=[ps], out=ps[:, h * 128:(h + 1) * 128], lhsT=kT[:, h, :],
                     rhs=qT[:, h, :], start=True, stop=True)
            it = itp.next()
            S.op("dve", "tensor_tensor", R=[ps, rdmat], W=[it], out=it[:], in0=ps[:],
                 in1=rdmat[:, d, :, :].rearrange("p h i -> p (h i)"), op=ALU.mult)
            qd = qdp.next()
            S.op("pool", "tensor_tensor", R=[qT, rqdec], W=[qd], out=qd[:], in0=qT[:], in1=rqdec[:, d, :, :],
                 op=ALU.mult)
            kd = kdp.next()
            S.op("pool", "tensor_tensor", R=[kt, rkdec], W=[kd], out=kd[:].rearrange("p (h e) -> p h e", h=4),
                 in0=kt[:].rearrange("p (h e) -> p h e", h=4),
                 in1=rkdec[:, d, :].unsqueeze(2).to_broadcast([128, 4, 64]), op=ALU.mult)
            yield
            po = psf()
            for h in range(4):
                S.op("pe", "matmul", R=[it, v], W=[po], out=po[:, h * 128:(h + 1) * 128],
                     lhsT=it[:, h * 128:(h + 1) * 128], rhs=v[:, h * 128:(h + 1) * 128], start=True, stop=False)
                S.op("pe", "matmul", R=[qd, Sb_], W=[po], out=po[:, h * 128:(h + 1) * 128], lhsT=qd[:, h, :],
                     rhs=Sb_[:, h, :], start=False, stop=True)
            pS = psf()
            for h in range(4):
                S.op("pe", "matmul", R=[kd, v], W=[pS], out=pS[0:64, h * 128:(h + 1) * 128],
                     lhsT=kd[:, h * 64:(h + 1) * 64], rhs=v[:, h * 128:(h + 1) * 128], start=True, stop=True)
            S.op("dve", "tensor_tensor", R=[Sf, rcdec], W=[Sf], out=Sf[:], in0=Sf[:],
                 in1=rcdec[0:64, d * 4:d * 4 + 4].unsqueeze(2).to_broadcast([64, 4, 128]), op=ALU.mult)
            S.op("dve", "tensor_tensor", R=[Sf, pS], W=[Sf], out=Sf[:].rearrange("p h e -> p (h e)"),
                 in0=Sf[:].rearrange("p h e -> p (h e)"), in1=pS[0:64, :], op=ALU.add)
            S.op("act", "copy", R=[Sf], W=[Sb_], out=Sb_[:], in_=Sf[:])
            o = op_.next()
            S.op("act", "copy", R=[po], W=[o], out=o[:], in_=po[:])
            S.dma("pool", odst[c0:c0 + 128, :], o[:], R=[o], W=[RX[orn][tt]])
            yield
        if not is_s:
            S.dma("pool", nret_out[si - 1, l, d].rearrange("h k e -> k h e"), Sf[:], R=[Sf])

    def pipeline_gen(gens, depth):
        gens = list(gens)
        active = []
        while gens or active:
            while gens and len(active) < depth:
                active.append(gens.pop(0))
            for g in list(active):
                try:
                    next(g)
                except StopIteration:
                    active.remove(g)
            yield

    def gdn_pre_gen(l, si, t0, T, is_s, es, G, nbuf):
        xin = Pool_(es, "gp_x", [128, 12, G + 4], F32, nbuf)
        acc = Pool_(es, "gp_a", [128, 12, G], F32, nbuf)
        sqp = Pool_(es, "gp_sq", [128, G], F32, 5)
        rsp = Pool_(es, "gp_rs", [128, G], F32, 5)
        nb = Pool_(es, "gp_nb", [128, 12, G], BF16, nbuf)
        sg = Pool_(es, "gp_sg", [128, 1024], BF16, 2)
        chunk_res = {}
        for g0 in range(0, T, G):
            x = xin.next()
            a = acc.next()
            lo = 2 if g0 == 0 else 0
            hi = 2 if g0 + G == T else 0
            if lo:
                S.op("pool", "memset", W=[x], ap=x[:, :, 0:2], constant=0.0)
            if hi:
                S.op("pool", "memset", W=[x], ap=x[:, :, G + 2:G + 4], constant=0.0)
            tl = [i for i in range((t0 + g0) // 128 - (0 if lo else 1), (t0 + g0 + G) // 128 + (0 if hi else 1))]
            S.dma("sp", x[:, :, lo:G + 4 - hi],
                  cT[:, t0 + g0 - 2 + lo:t0 + g0 + G + 2 - hi].rearrange("(c p) t -> p c t", p=128),
                  R=[RX["cT"][i] for i in tl], W=[x])
            yield
            yield
            ar = chunk_res.setdefault(("a", id(a)), [Res("gpa%d" % c) for c in range(12)])

            def convc(c):
                S.op("dve", "tensor_scalar", R=[x, convw], W=[ar[c]], out=a[:, c, :], in0=x[:, c, 0:G],
                     scalar1=convw[:, c, 0:1], scalar2=None, op0=ALU.mult)
                for k in range(1, 5):
                    S.op("dve", "scalar_tensor_tensor", R=[x, convw, ar[c]], W=[ar[c]], out=a[:, c, :],
                         in0=x[:, c, k:k + G], scalar=convw[:, c, k:k + 1], in1=a[:, c, :], op0=ALU.mult, op1=ALU.add)
                yield
                yield
                S.op("act", "activation", R=[ar[c]], W=[ar[c]], out=a[:, c, :], in_=a[:, c, :], func=AF.Silu)
            yield from pipeline_gen([convc(c) for c in range(12)], 3)
            n = nb.next()
            nr = chunk_res.setdefault(("n", id(n)), [Res("gpn%d" % c) for c in range(12)])

            def l2c(c):
                sq = sqp.next()
                S.op("act", "activation", R=[ar[c]], W=[sq], out=sq[:], in_=a[:, c, :], func=AF.Square)
                yield
                yield
                ps = psf()
                S.op("pe", "matmul", R=[ones_f, sq], W=[ps], out=ps[:, 0:G], lhsT=ones_f[:], rhs=sq[:], start=True,
                     stop=True)
                rs = rsp.next()
                S.op("dve", "tensor_scalar", R=[ps], W=[rs], out=rs[:], in0=ps[:, 0:G], scalar1=EPS, scalar2=None,
                     op0=ALU.add)
                yield
                yield
                S.op("act", "activation", R=[rs], W=[rs], out=rs[:], in_=rs[:], func=AF.Sqrt)
                yield
                yield
                S.op("dve", "reciprocal", R=[rs], W=[rs], out=rs[:], in_=rs[:])
                if c < 4:
                    S.op("dve", "scalar_tensor_tensor", R=[ar[c], rs], W=[nr[c]], out=n[:, c, :], in0=a[:, c, :],
                         scalar=float(128 ** -0.5), in1=rs[:], op0=ALU.mult, op1=ALU.mult)
                else:
                    S.op("dve", "tensor_tensor", R=[ar[c], rs], W=[nr[c]], out=n[:, c, :], in0=a[:, c, :], in1=rs[:],
                         op=ALU.mult)
            yield from pipeline_gen([l2c(c) for c in range(8)], 4)
            S.op("pool", "tensor_copy", R=ar[8:12], W=nr[8:12], out=n[:, 8:12, :], in_=a[:, 8:12, :])
            tiles = list(range((t0 + g0) // 128, (t0 + g0 + G) // 128))
            S.dma("pool", gqT[:, :, t0 + g0:t0 + g0 + G].rearrange("h p t -> p h t"), n[:, 0:4, :], R=nr[0:4],
                  W=[RX["gqT"][i] for i in tiles])
            S.dma("pool", gkT[:, :, t0 + g0:t0 + g0 + G].rearrange("h p t -> p h t"), n[:, 4:8, :], R=nr[4:8],
                  W=[RX["gkT"][i] for i in tiles])
            yield
            yield
            for ti in range(G // 128):
                tt = (t0 + g0) // 128 + ti
                pb = psb()
                transposes_to(pb, nr[4:12], [(n[:, 4 + c, ti * 128:(ti + 1) * 128], 128, c * 128) for c in range(8)])
                s = sg.next()
                S.op("act", "copy", R=[pb], W=[s], out=s[:], in_=pb[:])
                S.dma("pool", gk[tt * 128:(tt + 1) * 128, :], s[:, 0:512], R=[s], W=[RX["gk"][tt]])
                S.dma("pool", gv[tt * 128:(tt + 1) * 128, :], s[:, 512:1024], R=[s], W=[RX["gv"][tt]])
                yield

    def gdn_pre(l, si, t0, T, is_s):
        with ExitStack() as es:
            for _ in gdn_pre_gen(l, si, t0, T, is_s, es, min(512, T), 2):
                pass
            S.barrier()

    def pipeline(gens, depth):
        gens = list(gens)
        active = []
        while gens or active:
            while gens and len(active) < depth:
                active.append(gens.pop(0))
            for g in list(active):
                try:
                    next(g)
                except StopIteration:
                    active.remove(g)

    def run_weighted(main, side, ratio):
        main_alive = side_alive = True
        while main_alive or side_alive:
            if main_alive:
                for _ in range(ratio):
                    try:
                        next(main)
                    except StopIteration:
                        main_alive = False
                        break
            if side_alive:
                try:
                    next(side)
                except StopIteration:
                    side_alive = False

    def run_rr(gens):
        gens = list(gens)
        while gens:
            for g in list(gens):
                try:
                    next(g)
                except StopIteration:
                    gens.remove(g)

    def combine(t0, T, w_bc, zi, mi):
        for _ in combine_gen(t0, T, w_bc, zi, mi):
            pass

    def combine_gen(t0, T, w_bc, zi, mi):
        with ExitStack() as es:
            fa = Pool_(es, "cb_a", [128, 512], F32, 2)
            fb = Pool_(es, "cb_b", [128, 512], F32, 2)
            ngts = [ng_tiles(es, "cb%d_" % i) for i in range(2)]
            for i in range(T // 128):
                tt = t0 // 128 + i
                c0 = tt * 128
                a = fa.next(); b = fb.next()
                S.dma("sp", a[:], ofs[c0:c0 + 128, :], R=[RX["ofs"][tt]], W=[a])
                S.dma("sp", b[:], ofb[c0:c0 + 128, :], R=[RX["ofb"][tt]], W=[b])
                S.op("pool", "tensor_tensor", R=[a, b], W=[a], out=a[:], in0=a[:], in1=b[:], op=ALU.add)
                norm_gate_store(ngts[i % 2], a, a[:], w_bc, zi, mi, tt, 128)
                yield
            S.barrier()

    def gdn_scan(l, si, t0, T, is_s):
        with ExitStack() as es:
            run_rr([gdn_chain(l, si, t0, T, is_s, d, es) for d in range(2)])
            S.barrier()
        combine(t0, T, gdnw_bc, 2, 2)

    def gdn_chain(l, si, t0, T, is_s, d, es):
        ntl = T // 128
        pf = "gd%d_" % d
        Sf = sb(es, pf + "S", [128, 4, 128], F32)
        Sb_ = sb(es, pf + "Sb", [128, 4, 128], BF16)
        kTp = Pool_(es, pf + "kT", [128, 4, 128], BF16, 2)
        qTp = Pool_(es, pf + "qT", [128, 4, 128], BF16, 2)
        ktp = Pool_(es, pf + "k", [128, 512], BF16, 2)
        vtp = Pool_(es, pf + "v", [128, 512], BF16, 2)
        bgp = Pool_(es, pf + "bg", [128, 16], F32, 2)
        sm = Pool_(es, pf + "sm", [128, 8, 4], F32, 2)
        X1 = Pool_(es, pf + "X", [128, 4, 128], F32, 1)
        X2 = Pool_(es, pf + "X2", [128, 4, 128], F32, 1)
        Dn = Pool_(es, pf + "Dn", [128, 4, 128], F32, 1)
        Ea = Pool_(es, pf + "Ea", [128, 4, 128], F32, 1)
        Eb = Pool_(es, pf + "Eb", [128, 4, 128], F32, 1)
        Fq = Pool_(es, pf + "Fq", [128, 4, 128], F32, 1)
        Eg = Pool_(es, pf + "Eg", [128, 4, 128], F32, 1)
        Pp = Pool_(es, pf + "P", [128, 4, 128], F32, 2)
        PTp = Pool_(es, pf + "PT", [128, 4, 128], F32, 2)
        TTf = Pool_(es, pf + "TTf", [128, 4, 128], F32, 1)
        qkTp = Pool_(es, pf + "qkT", [128, 4, 128], BF16, 2)
        qgp = Pool_(es, pf + "qg", [128, 4, 128], BF16, 2)
        vbp = Pool_(es, pf + "vb", [128, 4, 128], F32, 1)
        kbp = Pool_(es, pf + "kb", [128, 4, 128], F32, 1)
        kdp = Pool_(es, pf + "kd", [128, 4, 128], BF16, 2)
        Up = Pool_(es, pf + "U", [128, 4, 128], F32, 2)
        WTp = Pool_(es, pf + "WT", [128, 4, 128], BF16, 2)
        vnp = Pool_(es, pf + "vn", [128, 128], BF16, 4)
        op_ = Pool_(es, pf + "o", [64, 512], F32, 2)
        odst, orn = (ofs, "ofs") if d == 0 else (ofb, "ofb")
        Sfr = [Res("gSf%d" % h) for h in range(4)]
        Sbr = [Res("gSb%d" % h) for h in range(4)]
        if is_s:
            S.dma("sp", Sf[:], st_gdn[l, d].rearrange("h k e -> k h e"), W=Sfr)
        else:
            S.op("dve", "memset", W=Sfr, ap=Sf[:], constant=0.0)
        S.op("act", "copy", R=Sfr, W=Sbr, out=Sb_[:], in_=Sf[:])
        idb = ident_f[:].unsqueeze(1).to_broadcast([128, 4, 128])
        v4 = lambda t: t[:].rearrange("p h b -> p (h b)")
        order = range(ntl) if d == 0 else range(ntl - 1, -1, -1)
        for i in order:
            tt = t0 // 128 + i
            c0 = tt * 128
            kT = kTp.next(); qT = qTp.next(); kt = ktp.next(); vt = vtp.next(); bg = bgp.next()
            S.dma("sp", kT[:], gkT[:, :, c0:c0 + 128].rearrange("h p t -> p h t"), R=[RX["gkT"][tt]], W=[kT])
            S.dma("sp", qT[:], gqT[:, :, c0:c0 + 128].rearrange("h p t -> p h t"), R=[RX["gqT"][tt]], W=[qT])
            S.dma("sp", kt[:], gk[c0:c0 + 128, :], R=[RX["gk"][tt]], W=[kt])
            S.dma("sp", vt[:], gv[c0:c0 + 128, :], R=[RX["gv"][tt]], W=[vt])
            S.dma("sp", bg[:], bgs[c0:c0 + 128, :], R=[RX["bgs"][tt]], W=[bg])
            s = sm.next()
            S.op("dve", "tensor_copy", R=[bg], W=[s], out=s[:, 0, :], in_=bg[:, 8 + d * 4:12 + d * 4])
            S.op("dve", "tensor_copy", R=[bg], W=[s], out=s[:, 1, :], in_=bg[:, d * 4:d * 4 + 4])
            pg = psf()
            S.op("pe", "matmul", R=[utri2, s], W=[pg], out=pg[:, 0:4], lhsT=utri2[:, d, :], rhs=s[:, 0, :],
                 start=True, stop=True)
            S.op("pe", "matmul", R=[bd1, s], W=[pg], out=pg[:, 4:8], lhsT=bd1[:], rhs=s[:, 0, :], start=True, stop=True)
            S.op("pe", "matmul", R=[selc, s], W=[pg], out=pg[:, 8:12], lhsT=selc[:, 0, :], rhs=s[:, 0, :],
                 start=True, stop=True)
            S.op("pe", "matmul", R=[selc, s], W=[pg], out=pg[:, 12:16], lhsT=selc[:, 1, :], rhs=s[:, 0, :],
                 start=True, stop=True)
            S.op("dve", "tensor_copy", R=[pg], W=[s], out=s[:, 2, :], in_=pg[:, 0:4])
            S.op("act", "activation", R=[pg], W=[s], out=s[:, 6:8, :].rearrange("p c h -> p (c h)"), in_=pg[:, 8:16],
                 func=AF.Exp)
            S.op("dve", "tensor_tensor", R=[pg], W=[s], out=s[:, 4, :], in0=pg[:, 4:8], in1=s[:, 2, :], op=ALU.subtract)
            S.op("act", "activation", R=[s], W=[s], out=s[:, 4, :], in_=s[:, 4, :], func=AF.Exp)
            S.op("act", "activation", R=[s], W=[s], out=s[:, 5, :], in_=s[:, 2, :], func=AF.Exp)
            S.op("dve", "tensor_tensor", R=[s], W=[s], out=s[:, 5, :], in0=s[:, 5, :], in1=s[:, 1, :], op=ALU.mult)
            ckpt('g0')
            yield
            x1 = X1.next(); x2 = X2.next()
            S.op("dve", "tensor_tensor", R=[ident_f, s], W=[x1], out=x1[:], in0=idb,
                 in1=s[:, 2, :].unsqueeze(2).to_broadcast([128, 4, 128]), op=ALU.mult)
            S.op("pool", "tensor_tensor", R=[ident_f, s], W=[x2], out=x2[:], in0=idb,
                 in1=s[:, 1, :].unsqueeze(2).to_broadcast([128, 4, 128]), op=ALU.mult)
            pR = psf(); pRb = psf()
            S.op("pe", "matmul", R=[ones_f, x1], W=[pR], out=pR[:], lhsT=ones_f[:], rhs=v4(x1), start=True, stop=True)
            S.op("pe", "matmul", R=[ones_f, x2], W=[pRb], out=pRb[:], lhsT=ones_f[:], rhs=v4(x2), start=True, stop=True)
            dn = Dn.next()
            S.op("dve", "tensor_tensor", R=[pR, s], W=[dn], out=dn[:], in0=pR[:].rearrange("p (h b) -> p h b", h=4),
                 in1=s[:, 2, :].unsqueeze(2).to_broadcast([128, 4, 128]), op=ALU.subtract)
            eg = Eg.next()
            S.op("act", "activation", R=[pR], W=[eg], out=v4(eg), in_=pR[:], func=AF.Exp)
            ea = Ea.next(); eb = Eb.next(); fq = Fq.next()
            S.op("dve", "tensor_scalar", R=[dn], W=[ea], out=ea[:], in0=dn[:], scalar1=-1.0, scalar2=0.0,
                 op0=ALU.mult, op1=ALU.min)
            S.op("dve", "tensor_scalar", R=[dn], W=[eb], out=eb[:], in0=dn[:], scalar1=0.0, scalar2=None, op0=ALU.min)
            S.op("act", "activation", R=[ea], W=[ea], out=ea[:], in_=ea[:], func=AF.Exp)
            S.op("act", "activation", R=[eb], W=[eb], out=eb[:], in_=eb[:], func=AF.Exp)
            S.op("dve", "tensor_tensor", R=[ea, gmask2], W=[ea], out=ea[:], in0=ea[:],
                 in1=gmask2[:, d, 0, :].unsqueeze(1).to_broadcast([128, 4, 128]), op=ALU.mult)
            S.op("dve", "tensor_tensor", R=[ea, s], W=[ea], out=ea[:], in0=ea[:],
                 in1=s[:, 1, :].unsqueeze(2).to_broadcast([128, 4, 128]), op=ALU.mult)
            S.op("pool", "tensor_tensor", R=[eb, gmask2], W=[fq], out=fq[:], in0=eb[:],
                 in1=gmask2[:, d, 2, :].unsqueeze(1).to_broadcast([128, 4, 128]), op=ALU.mult)
            S.op("pool", "tensor_tensor", R=[eb, gmask2], W=[eb], out=eb[:], in0=eb[:],
                 in1=gmask2[:, d, 1, :].unsqueeze(1).to_broadcast([128, 4, 128]), op=ALU.mult)
            S.op("dve", "tensor_tensor", R=[eb, pRb], W=[eb], out=eb[:], in0=eb[:],
                 in1=pRb[:].rearrange("p (h b) -> p h b", h=4), op=ALU.mult)
            ckpt('g1')
            yield
            qg = qgp.next()
            S.op("pool", "tensor_tensor", R=[qT, eg], W=[qg], out=qg[:], in0=qT[:], in1=eg[:], op=ALU.mult)
            pK = psf(); pQ = psf()
            for h in range(4):
                S.op("pe", "matmul", R=[kT], W=[pK], out=pK[:, h * 128:(h + 1) * 128], lhsT=kT[:, h, :], rhs=kT[:, h, :],
                     start=True, stop=True)
                S.op("pe", "matmul", R=[kT, qT], W=[pQ], out=pQ[:, h * 128:(h + 1) * 128], lhsT=kT[:, h, :],
                     rhs=qT[:, h, :], start=True, stop=True)
            P = Pp.next(); PT = PTp.next(); ttf = TTf.next(); qkT = qkTp.next()
            pKv = pK[:].rearrange("p (h b) -> p h b", h=4)
            S.op("dve", "tensor_tensor", R=[pK, ea], W=[P], out=P[:], in0=pKv, in1=ea[:], op=ALU.mult)
            S.op("dve", "tensor_tensor", R=[pK, eb], W=[PT], out=PT[:], in0=pKv, in1=eb[:], op=ALU.mult)
            S.op("pool", "tensor_tensor", R=[PT, ident_f], W=[ttf], out=ttf[:], in0=PT[:], in1=idb, op=ALU.add)
            S.op("dve", "tensor_tensor", R=[pQ, fq], W=[qkT], out=qkT[:], in0=pQ[:].rearrange("p (h b) -> p h b", h=4),
                 in1=fq[:], op=ALU.mult)
            ckpt('g2')
            yield
            for lev in range(1, 6):
                p1 = psf()
                for h in range(4):
                    S.op("pe", "matmul", R=[PT, P], W=[p1], out=p1[:, h * 128:(h + 1) * 128], lhsT=PT[:, h, :],
                         rhs=P[:, h, :], start=True, stop=True)
                Pn = Pp.next()
                S.op("dve", "tensor_copy", R=[p1], W=[Pn], out=v4(Pn), in_=p1[:])
                p3 = psf()
                for h in range(4):
                    S.op("pe", "matmul", R=[Pn, ttf], W=[p3], out=p3[:, h * 128:(h + 1) * 128], lhsT=Pn[:, h, :],
                         rhs=ttf[:, h, :], start=True, stop=True)
                if lev < 5:
                    p2 = psf()
                    for h in range(4):
                        if os.environ.get("KNOTR"):
                            S.op("pe", "matmul", R=[PT, P], W=[p2], out=p2[:, h * 128:(h + 1) * 128], lhsT=P[:, h, :],
                                 rhs=PT[:, h, :], start=True, stop=True)
                        else:
                            S.op("pe", "transpose", R=[Pn, ident_f], W=[p2], out=p2[:, h * 128:(h + 1) * 128],
                                 in_=Pn[:, h, :], identity=ident_f[:])
                    PTn = PTp.next()
                    S.op("act", "copy", R=[p2], W=[PTn], out=v4(PTn), in_=p2[:])
                S.op("dve", "tensor_tensor", R=[ttf, p3], W=[ttf], out=v4(ttf), in0=v4(ttf), in1=p3[:], op=ALU.add)
                P = Pn
                if lev < 5:
                    PT = PTn
                ckpt('g3')
                yield
            vb = vbp.next(); kb = kbp.next(); kd = kdp.next()
            S.op("dve", "tensor_tensor", R=[vt, s], W=[vb], out=vb[:], in0=vt[:].rearrange("p (h e) -> p h e", h=4),
                 in1=s[:, 1, :].unsqueeze(2).to_broadcast([128, 4, 128]), op=ALU.mult)
            S.op("pool", "tensor_tensor", R=[kt, s], W=[kb], out=kb[:], in0=kt[:].rearrange("p (h e) -> p h e", h=4),
                 in1=s[:, 5, :].unsqueeze(2).to_broadcast([128, 4, 128]), op=ALU.mult)
            S.op("pool", "tensor_tensor", R=[kt, s], W=[kd], out=kd[:], in0=kt[:].rearrange("p (h e) -> p h e", h=4),
                 in1=s[:, 4, :].unsqueeze(2).to_broadcast([128, 4, 128]), op=ALU.mult)
            U = Up.next(); WT = WTp.next()
            pU = psf(); pW = psf()
            for h in range(4):
                S.op("pe", "matmul", R=[ttf, vb], W=[pU], out=pU[:, h * 128:(h + 1) * 128], lhsT=ttf[:, h, :],
                     rhs=vb[:, h, :], start=True, stop=True)
                S.op("pe", "matmul", R=[kb, ttf], W=[pW], out=pW[:, h * 128:(h + 1) * 128], lhsT=kb[:, h, :],
                     rhs=ttf[:, h, :], start=True, stop=True)
            S.op("dve", "tensor_copy", R=[pU], W=[U], out=v4(U), in_=pU[:])
            S.op("act", "copy", R=[pW], W=[WT], out=v4(WT), in_=pW[:])
            ckpt('g4')
            yield
            for cb in ((0, 1) if d == 0 else (1, 0)):
                rows = slice(cb * 64, cb * 64 + 64)
                cols = slice(cb * 64, cb * 64 + 64)
                po = PSF[4 + d]
                for h in range(4):
                    pa = psf()
                    S.op("pe", "matmul", R=[WT, Sbr[h]], W=[pa], out=pa[:, 0:128], lhsT=WT[:, h, :], rhs=Sb_[:, h, :],
                         start=True, stop=True)
                    vn = vnp.next()
                    S.op("dve", "tensor_tensor", R=[U, pa], W=[vn], out=vn[rows, :], in0=U[rows, h, :],
                         in1=pa[rows, 0:128], op=ALU.subtract)
                    S.op("pe", "matmul", R=[qg, Sbr[h]], W=[po], out=po[0:64, h * 128:(h + 1) * 128], lhsT=qg[:, h, cols],
                         rhs=Sb_[:, h, :], start=True, stop=False)
                    S.op("pe", "matmul", R=[qkT, vn], W=[po], out=po[0:64, h * 128:(h + 1) * 128],
                         lhsT=qkT[rows, h, cols], rhs=vn[rows, :], start=False, stop=True)
                    pS = psf()
                    S.op("pe", "matmul", R=[kd, vn], W=[pS], out=pS[:, 0:128], lhsT=kd[rows, h, :], rhs=vn[rows, :],
                         start=True, stop=True)
                    S.op("dve", "scalar_tensor_tensor", R=[Sfr[h], s, pS], W=[Sfr[h]], out=Sf[:, h, :], in0=Sf[:, h, :],
                         scalar=s[:, 6 + cb, h:h + 1], in1=pS[:, 0:128], op0=ALU.mult, op1=ALU.add)
                    S.op("act", "copy", R=[Sfr[h]], W=[Sbr[h]], out=Sb_[:, h, :], in_=Sf[:, h, :])
                    if h % 2 == 1:
                        ckpt('g5')
                        yield
                r0 = c0 + cb * 64
                o = op_.next()
                S.op("act", "copy", R=[po], W=[o], out=o[:], in_=po[0:64, :])
                S.dma("pool", odst[r0:r0 + 64, :], o[:], R=[o], W=[RX[orn][tt]])
                ckpt('g6')
                yield
        if not is_s:
            S.dma("pool", ngdn_out[si - 1, l, d].rearrange("h k e -> k h e"), Sf[:], R=Sfr)

    def phaseE(l, xsrc, xdst):
        with ExitStack() as es:
            wbr = sb(es, "pe_wbr", [128, 12, D], BF16)
            wo = sb(es, "pe_wo", [128, 8, D], BF16)
            wst = Pool_(es, "pe_wst", [128, 4, D], F32, 2)
            gtp = Pool_(es, "pe_gt", [128, 12, 128], BF16, 2)
            mgp = Pool_(es, "pe_mg", [128, 3 * D], BF16, 2)
            mp = Pool_(es, "pe_m", [128, D], F32, 2)
            mbp = Pool_(es, "pe_mb", [128, D], BF16, 2)
            mTp = Pool_(es, "pe_mT", [128, 8, 128], BF16, 2)
            xp = Pool_(es, "pe_x", [128, D], F32, 2)
            tp = Pool_(es, "pe_t", [128, 512], F32, 2)
            for q in range(3):
                w = wst.next()
                S.dma("sp", w[:], w_branch[l, q].rearrange("(c p) n -> p c n", p=128), W=[w])
                S.op("dve" if q % 2 == 0 else "pool", "tensor_copy", R=[w], W=[wbr], out=wbr[:, q * 4:q * 4 + 4, :],
                     in_=w[:])
            for q in range(2):
                w = wst.next()
                S.dma("sp", w[:], w_out[l].rearrange("(c p) n -> p c n", p=128)[:, q * 4:q * 4 + 4, :], W=[w])
                S.op("dve" if q % 2 == 0 else "pool", "tensor_copy", R=[w], W=[wo], out=wo[:, q * 4:q * 4 + 4, :],
                     in_=w[:])
            for tt in range(NT):
                j = 0 if tt < NTS else 1
                c0 = tt * 128
                gt = gtp.next(); mgt = mgp.next(); m = mp.next(); xt = xp.next()
                S.dma("sp", gt[:], gT[:, :, c0:c0 + 128].rearrange("m (c p) t -> p (m c) t", p=128),
                      R=[RX["gT0"][tt], RX["gT1"][tt], RX["gT2"][tt]], W=[gt])
                S.dma("sp", mgt[:], mg[c0:c0 + 128, :], R=[RX["mg"][tt]], W=[mgt])
                S.dma("sp", xt[:], xsrc[c0:c0 + 128, :], R=[RX["x"][tt]] if l > 0 else [], W=[xt])
                for q in range(3):
                    for nb in range(2):
                        ps = psf()
                        for c in range(4):
                            S.op("pe", "matmul", R=[gt, wbr], W=[ps], out=ps[:], lhsT=gt[:, q * 4 + c, :],
                                 rhs=wbr[:, q * 4 + c, nb * 512:(nb + 1) * 512], start=(c == 0), stop=(c == 3))
                        cols = slice(nb * 512, (nb + 1) * 512)
                        if q == 0:
                            S.op("dve", "tensor_tensor", R=[ps, mgt], W=[m], out=m[:, cols], in0=ps[:],
                                 in1=mgt[:, cols], op=ALU.mult)
                        else:
                            t = tp.next()
                            S.op("dve", "tensor_tensor", R=[ps, mgt], W=[t], out=t[:], in0=ps[:],
                                 in1=mgt[:, q * D + nb * 512:q * D + (nb + 1) * 512], op=ALU.mult)
                            S.op("pool", "tensor_tensor", R=[t, m], W=[m], out=m[:, cols], in0=m[:, cols], in1=t[:],
                                 op=ALU.add)
                mb = mbp.next()
                S.op("act", "copy", R=[m], W=[mb], out=mb[:], in_=m[:])
                pb = psb()
                transposes_to(pb, mb, [(mb[:, c * 128:(c + 1) * 128], 128, c * 128) for c in range(8)])
                mT = mTp.next()
                S.op("act", "copy", R=[pb], W=[mT], out=mT[:].rearrange("p c t -> p (c t)"), in_=pb[:])
                for nb in range(2):
                    ps = psf()
                    for c in range(8):
                        S.op("pe", "matmul", R=[mT, wo], W=[ps], out=ps[:], lhsT=mT[:, c, :],
                             rhs=wo[:, c, nb * 512:(nb + 1) * 512], start=(c == 0), stop=(c == 7))
                    cols = slice(nb * 512, (nb + 1) * 512)
                    t = tp.next()
                    S.op("dve", "tensor_tensor", R=[ps, gate_bc], W=[t], out=t[:], in0=ps[:], in1=gate_bc[:, j, cols],
                         op=ALU.mult)
                    S.op("pool", "tensor_tensor", R=[t, xt], W=[xt], out=xt[:, cols], in0=xt[:, cols], in1=t[:],
                         op=ALU.add)
                S.dma("pool", xdst[c0:c0 + 128, :], xt[:], R=[xt], W=[RX["x"][tt]])
            S.barrier()

    try:
        for l in range(depth):
            layer(l)
    except StopBuild as e:
        print("build stopped at", e)
        root2 = None
    S.barrier()
    S.emit()
    build.stats = (S.n_instr, S.n_wait)
    return nc


_CACHE = {}


def _get_nc(T_S, depth, debug):
    key = (T_S, depth, debug)
    if key not in _CACHE:
        _CACHE[key] = build(T_S, depth, debug)
    return _CACHE[key]


def run_cfg(inp, T_S, depth, debug=False):
    f = lambda a: np.ascontiguousarray(np.asarray(a), dtype=np.float32)
    xs = f(inp["x_sample"])
    xp = f(inp["x_prompt"])
    n_core = 8
    nsb = xs.shape[0]
    consts = make_consts(T_S)
    shared = {
        "norm_w": f(inp["norm_w"])[:depth], "w_ada": f(inp["w_ada"])[:depth], "b_ada": f(inp["b_ada"])[:depth],
        "w_in": f(inp["w_in"])[:depth], "qk_norm_w": f(inp["qk_norm_w"])[:depth],
        "diff_lambda": f(inp["diff_lambda"])[:depth], "subln_w": f(inp["subln_w"])[:depth],
        "ret_decay": f(inp["ret_decay"])[:depth].reshape(depth, 8), "ret_norm_w": f(inp["ret_norm_w"])[:depth],
        "conv_w": f(inp["conv_w"])[:depth], "gdn_a_log": f(inp["gdn_a_log"])[:depth].reshape(depth, 8),
        "gdn_dt_bias": f(inp["gdn_dt_bias"])[:depth].reshape(depth, 8), "gdn_norm_w": f(inp["gdn_norm_w"])[:depth],
        "w_branch": f(inp["w_branch"])[:depth], "w_out": f(inp["w_out"])[:depth],
    }
    for k, v in consts.items():
        shared["c_" + k] = v
    ck = f(inp["cache_attn_k"])
    cv = f(inp["cache_attn_v"])
    sr = f(inp["state_ret"])
    sgd = f(inp["state_gdn"])
    c = f(inp["c"])
    cctx = f(inp["c_ctx"])
    in_maps = []
    for core in range(n_core):
        b = core % nsb
        m = dict(shared)
        m["x_in"] = np.ascontiguousarray(np.concatenate(
            [xs[b, :T_S]] + [xp[core * NPR + i] for i in range(NPR)], axis=0))
        m["cond"] = np.ascontiguousarray(np.stack([c[b], cctx], axis=0))
        m["cache_k"] = np.ascontiguousarray(ck[b, :depth].reshape(depth, PAST, 512))
        m["cache_v"] = np.ascontiguousarray(cv[b, :depth].reshape(depth, PAST, 512))
        m["st_ret"] = np.ascontiguousarray(sr[b, :depth])
        m["st_gdn"] = np.ascontiguousarray(sgd[b, :depth])
        in_maps.append(m)
    nc = _get_nc(T_S, depth, debug)
    res = run_bass_kernel_spmd(nc, in_maps, core_ids=list(range(n_core)))
    R = res.results
    y_s = np.stack([np.asarray(R[b]["y"])[:T_S] for b in range(nsb)], axis=0).astype(np.float32)
    y_p = np.stack([np.asarray(R[core]["y"])[T_S + i * TP:T_S + (i + 1) * TP]
                    for core in range(n_core) for i in range(NPR)], axis=0).astype(np.float32)
    nk = np.concatenate([np.asarray(R[core]["nk"]) for core in range(n_core)], axis=0).astype(np.float32)
    nv = np.concatenate([np.asarray(R[core]["nv"]) for core in range(n_core)], axis=0).astype(np.float32)
    nret = np.concatenate([np.asarray(R[core]["nret"]) for core in range(n_core)], axis=0).astype(np.float32)
    ngdn = np.concatenate([np.asarray(R[core]["ngdn"]) for core in range(n_core)], axis=0).astype(np.float32)
    nk = nk.reshape(n_core * NPR, depth, TP, 4, 2, 64)
    nv = nv.reshape(n_core * NPR, depth, TP, 4, 128)
    outs = (y_p, y_s, nk, nv, nret, ngdn)
    if debug:
        return outs, R
    return outs


def kernel(**inputs):
    return run_cfg(inputs, 4096, DEPTH, False)
```

```python
import math
from contextlib import ExitStack
import numpy as np
import ml_dtypes
import concourse.bass as bass
import concourse.mybir as mybir
from concourse.bass_utils import run_bass_kernel_spmd

F32 = mybir.dt.float32
BF16 = mybir.dt.bfloat16
AF = mybir.ActivationFunctionType
ALU = mybir.AluOpType
AX = mybir.AxisListType

D = 1024
DEPTH = 4
TP = 256
NPR = 2
PAST = 256
D_IN = 8720
EPS = 1e-6
CH = 64


import os


class StopBuild(Exception):
    pass


def ckpt(name):
    if os.environ.get("KSTOP", "") == name:
        raise StopBuild(name)


class Res:
    __slots__ = ("name", "w", "r", "excl")

    def __init__(self, name=""):
        self.name = name
        self.w = None
        self.r = {}
        self.excl = False


class Tile:
    def __init__(self, h, name, psum=False):
        self.h = h
        self.res = Res(name)
        self.res.excl = psum

    def __getitem__(self, k):
        return self.h[k]


def _res(x):
    out = []
    for t in x:
        if isinstance(t, (list, tuple)):
            out.extend(_res(t))
        elif isinstance(t, Res):
            out.append(t)
        else:
            out.append(t.res)
    return out


class Sched:
    ENG = ("pe", "act", "dve", "pool", "sp")

    def __init__(self, nc, n_dma_slots=8):
        self.nc = nc
        self.streams = {e: [] for e in self.ENG}
        self.sems = {}
        self.cnt = {}
        for e in ("pe", "act", "dve", "pool"):
            self.sems[e] = nc.alloc_semaphore("s_" + e)
            self.cnt[e] = 0
        self.nslots = n_dma_slots
        self.dq = {}
        for q in ("sp", "pool", "act"):
            slots = []
            for i in range(n_dma_slots):
                k = "d_%s%d" % (q, i)
                self.sems[k] = nc.alloc_semaphore(k)
                slots.append([k, 0])
            self.dq[q] = [slots, 0]
        self.seen = {e: {} for e in self.ENG}
        self.n_instr = 0
        self.n_wait = 0

    def _wait(self, eng, key, val):
        if val is None or val <= 0:
            return
        if eng == "pe" and key == "pe":
            return
        s = self.seen[eng]
        if s.get(key, 0) >= val:
            return
        s[key] = val
        sem = self.sems[key]
        self.streams[eng].append(lambda e, sem=sem, val=val: e.wait_ge(sem, val))
        self.n_wait += 1

    def _deps(self, eng, reads, writes, is_dma=False):
        for r in reads:
            if r.w is not None:
                self._wait(eng, r.w[0], r.w[1])
        for w in writes:
            if w.w is not None:
                if is_dma or not (w.w[0] == eng):
                    self._wait(eng, w.w[0], w.w[1])
            for k, v in w.r.items():
                if (not is_dma) and k == eng:
                    continue
                self._wait(eng, k, v)

    def _mark(self, key, val, reads, writes):
        for r in reads:
            if r.r.get(key, 0) < val:
                r.r[key] = val
        for w in writes:
            w.w = (key, val)
            w.r = {}

    def op(self, eng, method, R=(), W=(), **kw):
        reads = _res(R)
        writes = _res(W)
        ex = [r for r in reads if r.excl]
        if ex:
            reads = [r for r in reads if not r.excl]
            writes = writes + [r for r in ex if r not in writes]
        self._deps(eng, reads, writes)
        self.cnt[eng] += 1
        val = self.cnt[eng]
        sem = self.sems[eng]
        import traceback
        org = traceback.extract_stack(limit=3)[0]
        org = "%s:%d" % (org.name, org.lineno)

        def _f(e, m=method, kw=kw, sem=sem, org=org):
            try:
                return getattr(e, m)(**kw).then_inc(sem, 1)
            except Exception as ex:
                raise RuntimeError("emit failed at %s (%s): %s" % (org, m, ex)) from ex
        self.streams[eng].append(_f)
        self._mark(eng, val, reads, writes)
        self.n_instr += 1

    def dma(self, q, out, in_, R=(), W=(), **kw):
        reads = _res(R)
        writes = _res(W)
        slots, idx = self.dq[q]
        slot = slots[idx % self.nslots]
        self.dq[q][1] = idx + 1
        key = slot[0]
        if slot[1] > 0:
            self._wait(q, key, slot[1])
        self._deps(q, reads, writes, is_dma=True)
        slot[1] += 16
        val = slot[1]
        sem = self.sems[key]
        import traceback
        org = traceback.extract_stack(limit=3)[0]
        org = "%s:%d" % (org.name, org.lineno)

        def _f(e, out=out, in_=in_, sem=sem, kw=kw, org=org):
            try:
                return e.dma_start(out=out, in_=in_, **kw).then_inc(sem, 16)
            except Exception as ex:
                raise RuntimeError("dma emit failed at %s: %s" % (org, ex)) from ex
        self.streams[q].append(_f)
        self._mark(key, val, reads, writes)
        self.n_instr += 1

    def barrier(self):
        for e in self.ENG:
            for k in ("pe", "act", "dve", "pool"):
                if k != e:
                    self._wait(e, k, self.cnt[k])
            for q in self.dq:
                for slot in self.dq[q][0]:
                    if slot[1] > 0:
                        self._wait(e, slot[0], slot[1])

    def emit(self):
        nc = self.nc
        st = self.streams
        with nc.Block() as block:
            @block.sync
            def _(e):
                for f in st["sp"]:
                    f(e)

            @block.tensor
            def _(e):
                for f in st["pe"]:
                    f(e)

            @block.scalar
            def _(e):
                for f in st["act"]:
                    f(e)

            @block.vector
            def _(e):
                for f in st["dve"]:
                    f(e)

            @block.gpsimd
            def _(e):
                for f in st["pool"]:
                    f(e)


def make_consts(T_S):
    c = {}
    c["ident"] = np.eye(128, dtype=np.float32)
    n_rows = T_S // 64
    row = np.repeat(np.arange(n_rows, dtype=np.float32), 64)
    col = np.tile(np.arange(64, dtype=np.float32), n_rows)
    inv = (1.0 / (10000.0 ** (np.arange(16, dtype=np.float32) / 16))).astype(np.float32)
    ar = row[:, None] * inv
    ac = col[:, None] * inv
    ang = np.concatenate([ar, ar, ac, ac], axis=-1).astype(np.float32)
    cos = np.cos(ang).astype(np.float32)
    sin = np.sin(ang).astype(np.float32)
    sgn = np.tile(np.concatenate([-np.ones(16), np.ones(16)]), 2).astype(np.float32)
    c["ropec"] = cos
    c["ropes"] = (sin * sgn).astype(np.float32)
    a = np.arange(64)[:, None]
    b = np.arange(64)[None, :]
    low = (a > b).astype(np.float32)
    up = (b > a).astype(np.float32)
    upi = (b >= a).astype(np.float32)
    lowi = (a >= b).astype(np.float32)
    gm = np.zeros((64, 2, 3, 64), np.float32)
    gm[:, 0, 0] = -low
    gm[:, 0, 1] = -up
    gm[:, 0, 2] = upi
    gm[:, 1, 0] = -up
    gm[:, 1, 1] = -low
    gm[:, 1, 2] = lowi
    c["gmask"] = gm
    a2 = np.arange(128)[:, None]
    b2 = np.arange(128)[None, :]
    same = (a2 // 64 == b2 // 64)
    gm2 = np.zeros((128, 2, 3, 128), np.float32)
    gm2[:, 0, 0] = -1.0 * ((a2 > b2) & same)
    gm2[:, 0, 1] = -1.0 * ((b2 > a2) & same)
    gm2[:, 0, 2] = ((b2 >= a2) & same)
    gm2[:, 1, 0] = -1.0 * ((b2 > a2) & same)
    gm2[:, 1, 1] = -1.0 * ((a2 > b2) & same)
    gm2[:, 1, 2] = ((a2 >= b2) & same)
    c["gmask2"] = gm2
    ut2 = np.zeros((128, 2, 128), np.float32)
    ut2[:, 0] = ((a2 <= b2) & same)
    ut2[:, 1] = ((a2 >= b2) & same)
    c["utri2"] = ut2
    c["bd1"] = same.astype(np.float32)
    sel = np.zeros((128, 2, 128), np.float32)
    sel[0:64, 0, :] = 1.0
    sel[64:128, 1, :] = 1.0
    c["selc"] = sel
    ut = np.zeros((64, 2, 64), np.float32)
    ut[:, 0] = (a <= b).astype(np.float32)
    ut[:, 1] = (a >= b).astype(np.float32)
    c["utri"] = ut
    j = np.arange(128)[:, None].astype(np.float32)
    i = np.arange(128)[None, :].astype(np.float32)
    rr = np.zeros((128, 2, 128), np.float32)
    rm = np.zeros((128, 2, 128), np.float32)
    rr[:, 0] = np.maximum(i - j, 0)
    rm[:, 0] = (i >= j)
    rr[:, 1] = np.maximum(j - i, 0)
    rm[:, 1] = (j >= i)
    c["rrel"] = rr
    c["rmask"] = rm
    rq = np.zeros((64, 2, 128), np.float32)
    rq[:, 0] = (np.arange(128) + 1.0)[None, :]
    rq[:, 1] = (128.0 - np.arange(128))[None, :]
    c["rqexp"] = rq
    rk = np.zeros((128, 2), np.float32)
    rk[:, 0] = 127.0 - np.arange(128)
    rk[:, 1] = np.arange(128)
    c["rkexp"] = rk
    return c


CONST_SHAPES = lambda T_S: {k: v.shape for k, v in make_consts(T_S).items()}


def build(T_S=4096, depth=DEPTH, debug=False):
    nc = bass.Bass("TRN2", target_bir_lowering=False)
    S = Sched(nc)
    TT = T_S + NPR * TP
    NT = TT // 128
    NTS = T_S // 128
    seqs = [(0, T_S, True)] + [(T_S + i * TP, TP, False) for i in range(NPR)]
    lam_inits = [0.8 - 0.6 * math.exp(-0.3 * l) for l in range(depth)]

    def din(name, shape, dt=F32):
        return nc.dram_tensor(name, list(shape), dt, kind="ExternalInput").ap()

    def dout(name, shape, dt=F32):
        return nc.dram_tensor(name, list(shape), dt, kind="ExternalOutput").ap()

    def scr(name, shape, dt):
        if debug:
            return nc.dram_tensor(name, list(shape), dt, kind="ExternalOutput").ap()
        return nc.dram_tensor(name, list(shape), dt).ap()

    x_in = din("x_in", [TT, D])
    cond = din("cond", [2, D])
    cache_k = din("cache_k", [depth, PAST, 512])
    cache_v = din("cache_v", [depth, PAST, 512])
    st_ret = din("st_ret", [depth, 2, 4, 64, 128])
    st_gdn = din("st_gdn", [depth, 2, 4, 128, 128])
    norm_w = din("norm_w", [depth, D])
    w_ada = din("w_ada", [depth, D, 3 * D])
    b_ada = din("b_ada", [depth, 3 * D])
    w_in = din("w_in", [depth, D, D_IN])
    qk_norm_w = din("qk_norm_w", [depth, 2, 64])
    diff_lambda = din("diff_lambda", [depth, 4, 64])
    subln_w = din("subln_w", [depth, 128])
    ret_decay = din("ret_decay", [depth, 8])
    ret_norm_w = din("ret_norm_w", [depth, 128])
    conv_w = din("conv_w", [depth, 5, 1536])
    gdn_a_log = din("gdn_a_log", [depth, 8])
    gdn_dt_bias = din("gdn_dt_bias", [depth, 8])
    gdn_norm_w = din("gdn_norm_w", [depth, 128])
    w_branch = din("w_branch", [depth, 3, 512, D])
    w_out = din("w_out", [depth, D, D])
    cst = {k: din("c_" + k, shp) for k, shp in CONST_SHAPES(T_S).items()}
    y_out = dout("y", [TT, D])
    nk_out = dout("nk", [NPR, depth, TP, 512])
    nv_out = dout("nv", [NPR, depth, TP, 512])
    nret_out = dout("nret", [NPR, depth, 2, 4, 64, 128])
    ngdn_out = dout("ngdn", [NPR, depth, 2, 4, 128, 128])
    xcur = scr("xcur", [TT, D], F32)
    aqT = scr("aqT", [4, 128, TT], BF16)
    akT = scr("akT", [4, 128, TT], BF16)
    av = scr("av", [TT, 512], BF16)
    zg = scr("zg", [3, TT, 512], BF16)
    bqT = scr("bqT", [4, 64, TT], BF16)
    bkT = scr("bkT", [4, 64, TT], BF16)
    bk = scr("bk", [TT, 256], BF16)
    bv = scr("bv", [TT, 512], BF16)
    cT = scr("cT", [1536, TT], F32)
    gqT = scr("gqT", [4, 128, TT], BF16)
    gkT = scr("gkT", [4, 128, TT], BF16)
    gk = scr("gk", [TT, 512], BF16)
    gv = scr("gv", [TT, 512], BF16)
    bgs = scr("bgs", [TT, 16], F32)
    mg = scr("mg", [TT, 3 * D], BF16)
    ofs = scr("ofs", [TT, 512], F32)
    ofb = scr("ofb", [TT, 512], F32)
    gT = scr("gT", [3, 512, TT], BF16)

    def tres(n):
        return [Res("%s%d" % (n, i)) for i in range(NT)]
    RX = {n: tres(n) for n in ("x", "aqT", "akT", "av", "zg0", "zg1", "zg2", "bqT", "bkT", "bk", "bv", "cT",
                               "gqT", "gkT", "gk", "gv", "bgs", "mg", "ofs", "ofb", "gT0", "gT1", "gT2")}
    R_out = Res("outs")

    uid = [0]

    def sb(es, name, shape, dt):
        uid[0] += 1
        nm = "%s_u%d" % (name, uid[0])
        return Tile(es.enter_context(nc.sbuf_tensor(nm, list(shape), dt)), nm)

    class Pool_:
        def __init__(self, es, name, shape, dt, n):
            self.t = [sb(es, "%s_%d" % (name, i), shape, dt) for i in range(n)]
            self.i = 0

        def next(self):
            t = self.t[self.i % len(self.t)]
            self.i += 1
            return t

    root = ExitStack()
    PSF = [Tile(nc.alloc_psum_tensor("psf%d" % i, [128, 512], F32), "psf%d" % i, True) for i in range(6)]
    PSB = [Tile(nc.alloc_psum_tensor("psb%d" % i, [128, 1024], BF16), "psb%d" % i, True) for i in range(2)]
    psc = [0, 0]

    psf_banks = [[0, 1, 2, 3]]

    def psf():
        psc[0] += 1
        bk = psf_banks[0]
        return PSF[bk[psc[0] % len(bk)]]

    def psb():
        psc[1] += 1
        return PSB[psc[1] % 2]

    ident_f = sb(root, "ident_f", [128, 128], F32)
    ident_b = sb(root, "ident_b", [128, 128], BF16)
    ones_f = sb(root, "ones_f", [128, 128], F32)
    ropec = sb(root, "ropec", [128, NTS, 64], F32)
    ropes = sb(root, "ropes", [128, NTS, 64], F32)
    gmask = sb(root, "gmask", [64, 2, 3, 64], F32)
    utri = sb(root, "utri", [64, 2, 64], F32)
    gmask2 = sb(root, "gmask2", [128, 2, 3, 128], F32)
    utri2 = sb(root, "utri2", [128, 2, 128], F32)
    bd1 = sb(root, "bd1", [128, 128], F32)
    selc = sb(root, "selc", [128, 2, 128], F32)
    rrel = sb(root, "rrel", [128, 2, 128], F32)
    rmask = sb(root, "rmask", [128, 2, 128], F32)
    rqexp = sb(root, "rqexp", [64, 2, 128], F32)
    rkexp = sb(root, "rkexp", [128, 2], F32)
    gate_bc = sb(root, "gate_bc", [128, 2, D], F32)
    wq_bc = sb(root, "wq_bc", [128, 2, 64], F32)
    subw_bc = sb(root, "subw_bc", [128, 128], F32)
    retw_bc = sb(root, "retw_bc", [128, 128], F32)
    gdnw_bc = sb(root, "gdnw_bc", [128, 128], F32)
    lamt = sb(root, "lamt", [128, 8], F32)
    dl_bc = sb(root, "dl_bc", [128, 4, 64], F32)
    lg_bc = sb(root, "lg_bc", [128, 8], F32)
    rdmat = sb(root, "rdmat", [128, 2, 4, 128], F32)
    rqdec = sb(root, "rqdec", [64, 2, 4, 128], F32)
    rkdec = sb(root, "rkdec", [128, 2, 4], F32)
    rcdec = sb(root, "rcdec", [128, 8], F32)
    negA_bc = sb(root, "negA_bc", [128, 8], F32)
    dtb_bc = sb(root, "dtb_bc", [128, 8], F32)
    convw = sb(root, "convw", [128, 12, 5], F32)
    tmp8 = sb(root, "tmp8", [128, 8], F32)

    S.dma("sp", ident_f[:], cst["ident"], W=[ident_f])
    S.op("dve", "tensor_copy", R=[ident_f], W=[ident_b], out=ident_b[:], in_=ident_f[:])
    S.op("dve", "memset", W=[ones_f], ap=ones_f[:], constant=1.0)
    S.dma("sp", ropec[:], cst["ropec"].rearrange("(n p) d -> p n d", p=128), W=[ropec])
    S.dma("sp", ropes[:], cst["ropes"].rearrange("(n p) d -> p n d", p=128), W=[ropes])
    S.dma("sp", gmask[:], cst["gmask"], W=[gmask])
    S.dma("sp", utri[:], cst["utri"], W=[utri])
    S.dma("sp", gmask2[:], cst["gmask2"], W=[gmask2])
    S.dma("sp", utri2[:], cst["utri2"], W=[utri2])
    S.dma("sp", bd1[:], cst["bd1"], W=[bd1])
    S.dma("sp", selc[:], cst["selc"], W=[selc])
    S.dma("sp", rrel[:], cst["rrel"], W=[rrel])
    S.dma("sp", rmask[:], cst["rmask"], W=[rmask])
    S.dma("sp", rqexp[:], cst["rqexp"], W=[rqexp])
    S.dma("sp", rkexp[:], cst["rkexp"], W=[rkexp])

    def rsqrt_inplace(t, ap, scale, eps):
        S.op("dve", "tensor_scalar", R=[t], W=[t], out=ap, in0=ap, scalar1=scale, scalar2=eps,
             op0=ALU.mult, op1=ALU.add)
        S.op("act", "activation", R=[t], W=[t], out=ap, in_=ap, func=AF.Sqrt)
        S.op("dve", "reciprocal", R=[t], W=[t], out=ap, in_=ap)

    def transposes_to(ps_t, src_t, blocks, rows=128):
        for (sap, w, off) in blocks:
            S.op("pe", "transpose", R=[src_t, ident_b], W=[ps_t], out=ps_t[0:w, off:off + rows], in_=sap,
                 identity=ident_b[0:rows, 0:rows])

    def layer(l):
        last = (l == depth - 1)
        xsrc = x_in if l == 0 else xcur
        xdst = y_out if last else xcur
        esH = ExitStack()
        hT = sb(esH, "hT", [128, 8, TT], BF16)
        esA = ExitStack()
        A_bc = sb(esA, "A_bc", [128, 2, D], F32)
        sh_bc = sb(esA, "sh_bc", [128, 2, D], F32)
        with ExitStack() as es:
            nw_bc = sb(es, "nw_bc", [128, D], F32)
            cs = sb(es, "cs", [128, 2, 8], F32)
            rep = sb(es, "rep", [128, 16, 128], F32)
            bada = sb(es, "bada", [1, 3 * D], F32)
            wa = Pool_(es, "wa", [128, 8, 512], F32, 2)
            S.dma("sp", nw_bc[:], norm_w[l].partition_broadcast(128), W=[nw_bc])
            S.dma("sp", cs[:], cond.rearrange("j (p c) -> p j c", c=8), W=[cs])
            S.dma("sp", bada[:], b_ada[l:l + 1, :], W=[bada])
            S.dma("sp", wq_bc[:],
                  qk_norm_w[l].rearrange("a d -> (a d)").partition_broadcast(128).rearrange("p (a d) -> p a d", a=2),
                  W=[wq_bc])
            S.dma("sp", subw_bc[:], subln_w[l].partition_broadcast(128), W=[subw_bc])
            S.dma("sp", retw_bc[:], ret_norm_w[l].partition_broadcast(128), W=[retw_bc])
            S.dma("sp", gdnw_bc[:], gdn_norm_w[l].partition_broadcast(128), W=[gdnw_bc])
            S.dma("sp", dl_bc[:], diff_lambda[l].rearrange("a d -> (a d)").partition_broadcast(128)
                  .rearrange("p (a d) -> p a d", a=4), W=[dl_bc])
            S.dma("sp", lg_bc[:], ret_decay[l].partition_broadcast(128), W=[lg_bc])
            S.dma("sp", negA_bc[:], gdn_a_log[l].partition_broadcast(128), W=[negA_bc])
            S.dma("sp", dtb_bc[:], gdn_dt_bias[l].partition_broadcast(128), W=[dtb_bc])
            for k in range(5):
                S.dma("sp", convw[:, :, k:k + 1], conv_w[l, k].rearrange("(c p o) -> p c o", p=128, o=1), W=[convw],
                      allow_slow_non_contiguous=True)
            S.op("dve", "tensor_scalar", R=[wq_bc], W=[wq_bc], out=wq_bc[:, 0, :], in0=wq_bc[:, 0, :],
                 scalar1=0.125, scalar2=None, op0=ALU.mult)
            S.op("dve", "tensor_scalar", R=[subw_bc], W=[subw_bc], out=subw_bc[:], in0=subw_bc[:],
                 scalar1=float(1.0 - lam_inits[l]), scalar2=None, op0=ALU.mult)
            S.op("dve", "tensor_tensor", R=[dl_bc], W=[dl_bc], out=dl_bc[:, 0, :], in0=dl_bc[:, 0, :],
                 in1=dl_bc[:, 1, :], op=ALU.mult)
            S.op("dve", "tensor_tensor", R=[dl_bc], W=[dl_bc], out=dl_bc[:, 2, :], in0=dl_bc[:, 2, :],
                 in1=dl_bc[:, 3, :], op=ALU.mult)
            S.op("dve", "tensor_reduce", R=[dl_bc], W=[lamt], out=lamt[:, 0:4], in_=dl_bc[:], axis=AX.X, op=ALU.add)
            S.op("act", "activation", R=[lamt], W=[lamt], out=lamt[:, 4:8], in_=lamt[:, 0:4], func=AF.Exp)
            S.op("dve", "tensor_tensor", R=[lamt], W=[lamt], out=lamt[:, 1:2], in0=lamt[:, 6:7], in1=lamt[:, 4:5],
                 op=ALU.subtract)
            S.op("dve", "tensor_scalar", R=[lamt], W=[lamt], out=lamt[:, 0:1], in0=lamt[:, 1:2],
                 scalar1=float(-lam_inits[l]), scalar2=None, op0=ALU.add)
            S.op("act", "activation", R=[lg_bc], W=[lg_bc], out=lg_bc[:], in_=lg_bc[:], func=AF.Exp, scale=-1.0)
            S.op("act", "activation", R=[lg_bc], W=[lg_bc], out=lg_bc[:], in_=lg_bc[:], func=AF.Ln, bias=1.0)
            S.op("dve", "tensor_scalar", R=[lg_bc], W=[lg_bc], out=lg_bc[:], in0=lg_bc[:], scalar1=-1.0,
                 scalar2=None, op0=ALU.mult)
            for d in range(2):
                for h in range(4):
                    u = d * 4 + h
                    S.op("act", "activation", R=[rrel, lg_bc], W=[rdmat], out=rdmat[:, d, h, :], in_=rrel[:, d, :],
                         func=AF.Exp, scale=lg_bc[:, u:u + 1])
                    S.op("dve", "tensor_tensor", R=[rdmat, rmask], W=[rdmat], out=rdmat[:, d, h, :],
                         in0=rdmat[:, d, h, :], in1=rmask[:, d, :], op=ALU.mult)
                    S.op("act", "activation", R=[rqexp, lg_bc], W=[rqdec], out=rqdec[:, d, h, :], in_=rqexp[:, d, :],
                         func=AF.Exp, scale=lg_bc[0:64, u:u + 1])
                    S.op("act", "activation", R=[rkexp, lg_bc], W=[rkdec], out=rkdec[:, d, h:h + 1],
                         in_=rkexp[:, d:d + 1], func=AF.Exp, scale=lg_bc[:, u:u + 1])
            S.op("act", "activation", R=[lg_bc], W=[rcdec], out=rcdec[:], in_=lg_bc[:], func=AF.Exp, scale=128.0)
            S.op("act", "activation", R=[negA_bc], W=[negA_bc], out=negA_bc[:], in_=negA_bc[:], func=AF.Exp)
            S.op("dve", "tensor_scalar", R=[negA_bc], W=[negA_bc], out=negA_bc[:], in0=negA_bc[:], scalar1=-1.0,
                 scalar2=None, op0=ALU.mult)
            S.op("act", "activation", R=[cs], W=[cs], out=cs[:], in_=cs[:], func=AF.Silu)
            S.op("dve", "tensor_copy", R=[cs], W=[rep], out=rep[:],
                 in_=cs[:].rearrange("p j c -> p (j c)").unsqueeze(2).to_broadcast([128, 16, 128]))
            for nb in range(6):
                w = wa.next()
                S.dma("sp", w[:], w_ada[l].rearrange("(p c) n -> p c n", c=8)[:, :, nb * 512:(nb + 1) * 512], W=[w])
                for j in range(2):
                    ps = psf()
                    for c in range(8):
                        S.op("pe", "matmul", R=[rep, w], W=[ps], out=ps[:], lhsT=rep[:, j * 8 + c, :], rhs=w[:, c, :],
                             start=(c == 0), stop=False)
                    S.op("pe", "matmul", R=[ones_f, bada], W=[ps], out=ps[:], lhsT=ones_f[0:1, :],
                         rhs=bada[0:1, nb * 512:(nb + 1) * 512], start=False, stop=True)
                    cols = slice((nb % 2) * 512, (nb % 2) * 512 + 512)
                    if nb < 2:
                        S.op("act", "copy", R=[ps], W=[sh_bc], out=sh_bc[:, j, cols], in_=ps[:])
                    elif nb < 4:
                        S.op("dve", "scalar_tensor_tensor", R=[ps, nw_bc], W=[A_bc], out=A_bc[:, j, cols], in0=ps[:],
                             scalar=1.0, in1=nw_bc[:, cols], op0=ALU.add, op1=ALU.mult)
                    else:
                        S.op("act", "copy", R=[ps], W=[gate_bc], out=gate_bc[:, j, cols], in_=ps[:])
            S.barrier()
        ckpt("A")
        with ExitStack() as es:
            xp = Pool_(es, "xB", [128, D], F32, 4)
            hp = Pool_(es, "hB", [128, D], F32, 4)
            hbp = Pool_(es, "hbB", [128, D], BF16, 4)
            junk = sb(es, "junkB", [128, D], F32)
            ssp = Pool_(es, "ssB", [128, 1], F32, 4)
            def tileB(tt):
                j = 0 if tt < NTS else 1
                xt = xp.next()
                ht = hp.next()
                hb = hbp.next()
                ss = ssp.next()
                S.dma("sp", xt[:], xsrc[tt * 128:(tt + 1) * 128, :], R=[RX["x"][tt]] if l > 0 else [], W=[xt])
                S.op("act", "activation", R=[xt], W=[junk, ss], out=junk[:], in_=xt[:], func=AF.Square, accum_out=ss[:])
                yield
                rsqrt_inplace(ss, ss[:], 1.0 / D, EPS)
                S.op("dve", "scalar_tensor_tensor", R=[xt, ss, A_bc], W=[ht], out=ht[:], in0=xt[:], scalar=ss[:, 0:1],
                     in1=A_bc[:, j, :], op0=ALU.mult, op1=ALU.mult)
                S.op("dve", "tensor_tensor", R=[ht, sh_bc], W=[hb], out=hb[:], in0=ht[:], in1=sh_bc[:, j, :], op=ALU.add)
                yield
                pb = psb()
                transposes_to(pb, hb, [(hb[:, c * 128:(c + 1) * 128], 128, c * 128) for c in range(8)])
                S.op("act", "copy", R=[pb], W=[hT], out=hT[:, :, tt * 128:(tt + 1) * 128],
                     in_=pb[:].rearrange("p (c t) -> p c t", c=8))
            pipeline([tileB(tt) for tt in range(NT)], 3)
            S.barrier()
        esA.close()
        ckpt("B")
        with ExitStack() as es:
            wf = Pool_(es, "wfC", [128, 4, 512], F32, 2)
            wbp = Pool_(es, "wbC", [128, 8, 512], BF16, 2)
            f1 = Pool_(es, "f1C", [128, 512], F32, 9)
            f2 = Pool_(es, "f2C", [128, 512], F32, 3)
            b1 = Pool_(es, "b1C", [128, 512], BF16, 4)
            st8 = Pool_(es, "st8C", [128, 16], F32, 4)
            stg = Pool_(es, "stgC", [128, 1024], BF16, 3)
            cfp = Pool_(es, "cfC", [128, 512], F32, 3)

            def load_wblock(c0, ncols):
                wb = wbp.next()
                for half in range(2):
                    w = wf.next()
                    S.dma("sp", w[:, :, 0:ncols],
                          w_in[l].rearrange("(c p) n -> p c n", p=128)[:, half * 4:half * 4 + 4, c0:c0 + ncols], W=[w])
                    S.op("dve" if half == 0 else "pool", "tensor_copy", R=[w], W=[wb],
                         out=wb[:, half * 4:half * 4 + 4, 0:ncols], in_=w[:, :, 0:ncols])
                return (wb, (c0, ncols))

            wblocks = [(0, 512), (512, 512), (1024, 512), (1536, 512), (2560, 512), (3072, 512), (5120, 512),
                       (2048, 512), (3584, 512), (4096, 512), (4608, 512), (5632, 16)] + \
                      [(5648 + i * 512, 512) for i in range(6)]
            wq = []

            def next_wb(c0, ncols):
                if not wq:
                    wq.append(load_wblock(*wblocks.pop(0)))
                wb = wq.pop(0)
                assert wb[1] == (c0, ncols), (wb[1], c0, ncols)
                if wblocks:
                    wq.append(load_wblock(*wblocks.pop(0)))
                return wb[0]

            def mm_tok(wb, tt, ncols):
                ps = psf()
                for c in range(8):
                    S.op("pe", "matmul", R=[hT, wb], W=[ps], out=ps[:, 0:ncols], lhsT=hT[:, c, tt * 128:(tt + 1) * 128],
                         rhs=wb[:, c, 0:ncols], start=(c == 0), stop=(c == 7))
                return ps

            def rope(src, tt, ngrp, dst_pool):
                t1 = dst_pool.next()
                t2 = dst_pool.next()
                s4 = src[:, 0:ngrp * 64].rearrange("p (g a h f) -> p g a h f", a=2, h=2, f=16)
                S.op("dve", "tensor_tensor", R=[src, ropec], W=[t1],
                     out=t1[:, 0:ngrp * 64].rearrange("p (g d) -> p g d", d=64),
                     in0=src[:, 0:ngrp * 64].rearrange("p (g d) -> p g d", d=64),
                     in1=ropec[:, tt:tt + 1, :].to_broadcast([128, ngrp, 64]), op=ALU.mult)
                t24 = t2[:, 0:ngrp * 64].rearrange("p (g a h f) -> p g a h f", a=2, h=2, f=16)
                sn4 = ropes[:, tt, :].rearrange("p (a h f) -> p a h f", a=2, h=2)
                for hh in range(2):
                    S.op("pool", "tensor_tensor", R=[src, ropes], W=[t2], out=t24[:, :, :, hh, :],
                         in0=s4[:, :, :, 1 - hh, :],
                         in1=sn4[:, :, hh, :].unsqueeze(1).to_broadcast([128, ngrp, 2, 16]), op=ALU.mult)
                S.op("dve", "tensor_tensor", R=[t1, t2], W=[t1], out=t1[:, 0:ngrp * 64], in0=t1[:, 0:ngrp * 64],
                     in1=t2[:, 0:ngrp * 64], op=ALU.add)
                return t1

            def tileQK(blk, wb, tt):
                is_s = tt < NTS
                ps = mm_tok(wb, tt, 512)
                sq = f2.next()
                s8 = st8.next()
                qn = f1.next()
                S.op("act", "activation", R=[ps], W=[sq], out=sq[:], in_=ps[:], func=AF.Square)
                S.op("dve", "tensor_reduce", R=[sq], W=[s8], out=s8[:, 0:8],
                     in_=sq[:].rearrange("p (g d) -> p g d", d=64), axis=AX.X, op=ALU.add)
                yield
                rsqrt_inplace(s8, s8[:, 0:8], 1.0 / 64, EPS)
                S.op("dve", "tensor_tensor", R=[ps, s8], W=[qn], out=qn[:].rearrange("p (g d) -> p g d", d=64),
                     in0=ps[:].rearrange("p (g d) -> p g d", d=64),
                     in1=s8[:, 0:8].unsqueeze(2).to_broadcast([128, 8, 64]), op=ALU.mult)
                S.op("pool", "tensor_tensor", R=[qn, wq_bc], W=[qn], out=qn[:].rearrange("p (g d) -> p g d", d=64),
                     in0=qn[:].rearrange("p (g d) -> p g d", d=64),
                     in1=wq_bc[:, blk:blk + 1, :].to_broadcast([128, 8, 64]), op=ALU.mult)
                if blk == 1 and not is_s:
                    pi = (tt - NTS) // 2
                    r0 = ((tt - NTS) % 2) * 128
                    S.dma("pool", nk_out[pi, l, r0:r0 + 128, :], qn[:], R=[qn])
                yield
                if is_s:
                    qn = rope(qn, tt, 8, f1)
                qb = b1.next()
                S.op("act", "copy", R=[qn], W=[qb], out=qb[:], in_=qn[:])
                yield
                pb = psb()
                transposes_to(pb, qb, [(qb[:, h * 128:(h + 1) * 128], 128, h * 128) for h in range(4)])
                sg = stg.next()
                S.op("dve", "tensor_copy", R=[pb], W=[sg], out=sg[:, 0:512], in_=pb[:, 0:512])
                dst = aqT if blk == 0 else akT
                S.dma("pool", dst[:, :, tt * 128:(tt + 1) * 128].rearrange("h p t -> p h t"),
                      sg[:, 0:512].rearrange("p (h t) -> p h t", h=4), R=[sg],
                      W=[RX["aqT" if blk == 0 else "akT"][tt]])

            for blk in range(2):
                wb = next_wb(blk * 512, 512)
                pipeline([tileQK(blk, wb, tt) for tt in range(NT)], 3)
            ckpt("C0")
            for (c0, kind) in ((1024, "av"), (1536, "z0"), (2560, "bv"), (3072, "z1"), (5120, "z2")):
                wb = next_wb(c0, 512)
                for tt in range(NT):
                    is_s = tt < NTS
                    ps = mm_tok(wb, tt, 512)
                    ob = b1.next()
                    if kind in ("av", "bv"):
                        S.op("act", "copy", R=[ps], W=[ob], out=ob[:], in_=ps[:])
                        dst, rn = (av, "av") if kind == "av" else (bv, "bv")
                        S.dma("pool", dst[tt * 128:(tt + 1) * 128, :], ob[:], R=[ob], W=[RX[rn][tt]])
                        if kind == "av" and not is_s:
                            of_ = f1.next()
                            S.op("dve", "tensor_copy", R=[ps], W=[of_], out=of_[:], in_=ps[:])
                            pi = (tt - NTS) // 2
                            r0 = ((tt - NTS) % 2) * 128
                            S.dma("pool", nv_out[pi, l, r0:r0 + 128, :], of_[:], R=[of_])
                    else:
                        zi = int(kind[1])
                        S.op("act", "activation", R=[ps], W=[ob], out=ob[:], in_=ps[:], func=AF.Silu)
                        S.dma("pool", zg[zi, tt * 128:(tt + 1) * 128, :], ob[:], R=[ob], W=[RX["zg%d" % zi][tt]])
                ckpt("C1_" + kind)
            ckpt("C1")
            wb = next_wb(2048, 512)
            for tt in range(NT):
                is_s = tt < NTS
                ps = mm_tok(wb, tt, 512)
                qk = f1.next()
                S.op("act", "copy", R=[ps], W=[qk], out=qk[:, 0:256], in_=ps[:, 0:256])
                S.op("act", "mul", R=[ps], W=[qk], out=qk[:, 256:512], in_=ps[:, 256:512], mul=0.125)
                if is_s:
                    qk = rope(qk, tt, 8, f1)
                qb = b1.next()
                S.op("act", "copy", R=[qk], W=[qb], out=qb[:], in_=qk[:])
                S.dma("pool", bk[tt * 128:(tt + 1) * 128, :], qb[:, 256:512], R=[qb], W=[RX["bk"][tt]])
                pb = psb()
                transposes_to(pb, qb, [(qb[:, g * 64:(g + 1) * 64], 64, g * 128) for g in range(8)])
                sg = stg.next()
                S.op("dve", "tensor_copy", R=[pb], W=[sg], out=sg[0:64, :], in_=pb[0:64, :])
                S.dma("pool", bqT[:, :, tt * 128:(tt + 1) * 128].rearrange("h p t -> p h t"),
                      sg[0:64, 0:512].rearrange("p (h t) -> p h t", h=4), R=[sg], W=[RX["bqT"][tt]])
                S.dma("pool", bkT[:, :, tt * 128:(tt + 1) * 128].rearrange("h p t -> p h t"),
                      sg[0:64, 512:1024].rearrange("p (h t) -> p h t", h=4), R=[sg], W=[RX["bkT"][tt]])
            ckpt("C2")
            for blk in range(3):
                wb = next_wb(3584 + blk * 512, 512)
                for cc in range(4):
                    for (t0, T, _) in seqs:
                        for g0 in range(0, T, 512):
                            n = min(512, T - g0)
                            ps = psf()
                            for c in range(8):
                                S.op("pe", "matmul", R=[hT, wb], W=[ps], out=ps[:, 0:n],
                                     lhsT=wb[:, c, cc * 128:(cc + 1) * 128], rhs=hT[:, c, t0 + g0:t0 + g0 + n],
                                     start=(c == 0), stop=(c == 7))
                            cf = cfp.next()
                            S.op("act", "copy", R=[ps], W=[cf], out=cf[:, 0:n], in_=ps[:, 0:n])
                            ch0 = (blk * 4 + cc) * 128
                            tiles = range((t0 + g0) // 128, (t0 + g0 + n) // 128)
                            S.dma("pool", cT[ch0:ch0 + 128, t0 + g0:t0 + g0 + n], cf[:, 0:n], R=[cf],
                                  W=[RX["cT"][i] for i in tiles])
            ckpt("C3")
            wb = next_wb(5632, 16)
            for tt in range(NT):
                ps = mm_tok(wb, tt, 16)
                o16 = st8.next()
                S.op("act", "activation", R=[ps], W=[o16], out=o16[:, 0:8], in_=ps[:, 0:8], func=AF.Sigmoid)
                S.op("dve", "tensor_tensor", R=[ps, dtb_bc], W=[o16], out=o16[:, 8:16], in0=ps[:, 8:16], in1=dtb_bc[:],
                     op=ALU.add)
                S.op("act", "activation", R=[o16], W=[o16], out=o16[:, 8:16], in_=o16[:, 8:16], func=AF.Exp)
                S.op("act", "activation", R=[o16], W=[o16], out=o16[:, 8:16], in_=o16[:, 8:16], func=AF.Ln, bias=1.0)
                S.op("dve", "tensor_tensor", R=[o16, negA_bc], W=[o16], out=o16[:, 8:16], in0=o16[:, 8:16],
                     in1=negA_bc[:], op=ALU.mult)
                S.dma("pool", bgs[tt * 128:(tt + 1) * 128, :], o16[:], R=[o16], W=[RX["bgs"][tt]])
            ckpt("C4")
            for blk in range(6):
                wb = next_wb(5648 + blk * 512, 512)
                for tt in range(NT):
                    ps = mm_tok(wb, tt, 512)
                    ob = b1.next()
                    S.op("act", "activation", R=[ps], W=[ob], out=ob[:], in_=ps[:], func=AF.Sigmoid)
                    S.dma("pool", mg[tt * 128:(tt + 1) * 128, blk * 512:(blk + 1) * 512], ob[:], R=[ob],
                          W=[RX["mg"][tt]] if blk == 5 else [])
            S.barrier()
        esH.close()
        ckpt("C")
        for si, (t0, T, is_s) in enumerate(seqs):
            if is_s and not os.environ.get("KNOOVL"):
                sample_mixers_overlapped(l, si, t0, T, is_s)
                ckpt("gpre%d" % si)
            elif os.environ.get("KNOOVL"):
                attention(l, si, t0, T, is_s)
                ckpt("att%d" % si)
                retention(l, si, t0, T, is_s)
                ckpt("ret%d" % si)
                gdn_pre(l, si, t0, T, is_s)
                ckpt("gpre%d" % si)
            gdn_scan(l, si, t0, T, is_s)
            ckpt("gscan%d" % si)
        phaseE(l, xsrc, xdst)

    def norm_gate_store(es_tiles, o_t, o_ap, w_bc, zi, mi, tt, rows, col_lo=None):
        sq, s4, zt, gb, sg = es_tiles
        tok0 = tt * 128 + (col_lo or 0)
        S.op("act", "activation", R=[o_t], W=[sq], out=sq[0:rows, :], in_=o_ap, func=AF.Square)
        S.op("dve", "tensor_reduce", R=[sq], W=[s4], out=s4[0:rows, 0:4],
             in_=sq[0:rows, :].rearrange("p (g d) -> p g d", d=128), axis=AX.X, op=ALU.add)
        rsqrt_inplace(s4, s4[0:rows, 0:4], 1.0 / 128, EPS)
        S.dma("sp", zt[0:rows, :], zg[zi, tok0:tok0 + rows, :], R=[RX["zg%d" % zi][tt]], W=[zt])
        S.op("dve", "tensor_tensor", R=[o_t, s4], W=[o_t], out=o_ap.rearrange("p (g d) -> p g d", d=128),
             in0=o_ap.rearrange("p (g d) -> p g d", d=128),
             in1=s4[0:rows, 0:4].unsqueeze(2).to_broadcast([rows, 4, 128]), op=ALU.mult)
        S.op("pool", "tensor_tensor", R=[o_t, w_bc], W=[o_t], out=o_ap.rearrange("p (g d) -> p g d", d=128),
             in0=o_ap.rearrange("p (g d) -> p g d", d=128),
             in1=w_bc[0:rows, :].unsqueeze(1).to_broadcast([rows, 4, 128]), op=ALU.mult)
        S.op("dve", "tensor_tensor", R=[o_t, zt], W=[gb], out=gb[0:rows, :], in0=o_ap, in1=zt[0:rows, :], op=ALU.mult)
        pb = psb()
        for g in range(4):
            S.op("pe", "transpose", R=[gb, ident_b], W=[pb], out=pb[:, g * 128:g * 128 + rows],
                 in_=gb[0:rows, g * 128:(g + 1) * 128], identity=ident_b[0:rows, 0:rows])
        pv = pb[:, 0:512].rearrange("p (g t) -> p g t", g=4)[:, :, 0:rows]
        S.op("act", "copy", R=[pb], W=[sg], out=sg[:, :, 0:rows], in_=pv)
        S.dma("pool", gT[mi, :, tok0:tok0 + rows].rearrange("(g p) t -> p g t", p=128), sg[:, :, 0:rows], R=[sg],
              W=[RX["gT%d" % mi][tt]])

    def ng_tiles(es, pfx):
        return (sb(es, pfx + "sq", [128, 512], F32), sb(es, pfx + "s4", [128, 4], F32),
                sb(es, pfx + "zt", [128, 512], BF16), sb(es, pfx + "gb", [128, 512], BF16),
                sb(es, pfx + "sg", [128, 4, 128], BF16))

    def attention_gen(l, si, t0, T, is_s, es, acc_sets, sc_banks=None):
        Sk = T + (PAST if is_s else 0)
        nst = Sk // 128
        ntl = T // 128
        QB = min(512, T)
        kT = sb(es, "at_kT", [128, 4, Sk], BF16)
        V1 = sb(es, "at_V1", [128, nst, 4, 130], BF16)
        qTp = Pool_(es, "at_qT", [128, 4, QB], BF16, 2)
        ex = Pool_(es, "at_ex", [128, QB], BF16, 3)
        osb = [sb(es, "at_os%d" % qs, [128, 512], F32) for qs in range(QB // 128)]
        rc = Pool_(es, "at_rc", [128, 2], F32, 4)
        ngt = ng_tiles(es, "at_")
        grp = [0]
        scn = [0]
        S.dma("sp", kT[:, :, 0:T], akT[:, :, t0:t0 + T].rearrange("h p t -> p h t"),
              R=[RX["akT"][i] for i in range(t0 // 128, (t0 + T) // 128)], W=[kT])
        S.op("pool", "memset", W=[V1], ap=V1[:, :, :, 128:130], constant=1.0)
        for i in range(ntl):
            S.dma("sp", V1[:, i, :, 0:128], av[t0 + i * 128:t0 + (i + 1) * 128, :].rearrange("p (h e) -> p h e", h=4),
                  R=[RX["av"][t0 // 128 + i]], W=[V1])
        if is_s:
            with ExitStack() as es2:
                ck = sb(es2, "at_ck", [128, 2, 512], F32)
                cv = sb(es2, "at_cv", [128, 2, 512], F32)
                ckb = sb(es2, "at_ckb", [128, 2, 512], BF16)
                S.dma("sp", ck[:], cache_k[l].rearrange("(n p) f -> p n f", p=128), W=[ck])
                S.dma("sp", cv[:], cache_v[l].rearrange("(n p) f -> p n f", p=128), W=[cv])
                S.op("dve", "tensor_copy", R=[ck], W=[ckb], out=ckb[:], in_=ck[:])
                for n in range(2):
                    S.op("pool", "tensor_copy", R=[cv], W=[V1], out=V1[:, ntl + n, :, 0:128],
                         in_=cv[:, n, :].rearrange("p (h e) -> p h e", h=4))
                    pb = psb()
                    transposes_to(pb, ckb, [(ckb[:, n, h * 128:(h + 1) * 128], 128, h * 128) for h in range(4)])
                    S.op("act", "copy", R=[pb], W=[kT], out=kT[:, :, T + n * 128:T + (n + 1) * 128],
                         in_=pb[:, 0:512].rearrange("p (h t) -> p h t", h=4))
                S.barrier()
        for qb0 in range(0, T, QB):
            qT = qTp.next()
            S.dma("sp", qT[:], aqT[:, :, t0 + qb0:t0 + qb0 + QB].rearrange("h p t -> p h t"),
                  R=[RX["aqT"][i] for i in range((t0 + qb0) // 128, (t0 + qb0 + QB) // 128)], W=[qT])
            nqs = QB // 128
            for h in range(4):
                for m in range(2):
                    grp[0] += 1
                    acc = acc_sets[grp[0] % len(acc_sets)]

                    def pv(st, e, acc=acc, h=h):
                        for qs in range(nqs):
                            a = acc[qs // 2]
                            S.op("pe", "matmul", R=[e, V1], W=[a], out=a[:, (qs % 2) * 256:(qs % 2) * 256 + 129],
                                 lhsT=e[:, qs * 128:(qs + 1) * 128], rhs=V1[:, st, h, 0:129],
                                 start=(st == 0 and qs % 2 == 0), stop=(st == nst - 1), skip_group_check=True)
                    pend = None
                    for st in range(nst):
                        scn[0] += 1
                        _sb = sc_banks or [PSF[0], PSF[1]]
                        ps = _sb[scn[0] % len(_sb)]
                        S.op("pe", "matmul", R=[kT, qT], W=[ps], out=ps[:, 0:QB],
                             lhsT=kT[m * 64:(m + 1) * 64, h, st * 128:(st + 1) * 128],
                             rhs=qT[m * 64:(m + 1) * 64, h, :], start=True, stop=True)
                        e = ex.next()
                        S.op("act", "activation", R=[ps], W=[e], out=e[:, 0:QB], in_=ps[:, 0:QB], func=AF.Exp)
                        if pend is not None:
                            pv(*pend)
                        pend = (st, e)
                        yield
                    pv(*pend)
                    for qs in range(nqs):
                        a = acc[qs // 2]
                        c0 = (qs % 2) * 256
                        r = rc.next()
                        S.op("dve", "reciprocal", R=[a], W=[r], out=r[:, 0:1], in_=a[:, c0 + 128:c0 + 129])
                        if m == 0:
                            S.op("dve", "tensor_scalar", R=[a, r], W=[osb[qs]], out=osb[qs][:, h * 128:(h + 1) * 128],
                                 in0=a[:, c0:c0 + 128], scalar1=r[:, 0:1], scalar2=None, op0=ALU.mult)
                        else:
                            S.op("dve", "tensor_tensor", R=[r, lamt], W=[r], out=r[:, 1:2], in0=r[:, 0:1],
                                 in1=lamt[:, 0:1], op=ALU.mult)
                            S.op("dve", "scalar_tensor_tensor", R=[a, r, osb[qs]], W=[osb[qs]],
                                 out=osb[qs][:, h * 128:(h + 1) * 128], in0=a[:, c0:c0 + 128], scalar=r[:, 1:2],
                                 in1=osb[qs][:, h * 128:(h + 1) * 128], op0=ALU.mult, op1=ALU.add)
            for qs in range(nqs):
                tt = (t0 + qb0) // 128 + qs
                norm_gate_store(ngt, osb[qs], osb[qs][:], subw_bc, 0, 0, tt, 128)

    def attention(l, si, t0, T, is_s):
        with ExitStack() as es:
            for _ in attention_gen(l, si, t0, T, is_s, es, [[PSF[2], PSF[3]], [PSF[4], PSF[5]]]):
                pass
            S.barrier()

    def retention(l, si, t0, T, is_s):
        with ExitStack() as es:
            run_rr([ret_chain(l, si, t0, T, is_s, d, es) for d in range(2)])
            S.barrier()
        combine(t0, T, retw_bc, 1, 1)

    def rr_gen(gens):
        gens = list(gens)
        while gens:
            for g in list(gens):
                try:
                    next(g)
                    yield
                except StopIteration:
                    gens.remove(g)

    def side_gen(l, si, t0, T, is_s):
        with ExitStack() as es2:
            yield from gdn_pre_gen(l, si, t0, T, is_s, es2, 256, 1)
            S.barrier()
        with ExitStack() as es3:
            yield from rr_gen([ret_chain(l, si, t0, T, is_s, d, es3) for d in range(2)])
            S.barrier()
        yield from combine_gen(t0, T, retw_bc, 1, 1)
        for pi in range(1, len(seqs)):
            (tp0, Tp, _) = seqs[pi]
            with ExitStack() as esp:
                yield from attention_gen(l, pi, tp0, Tp, False, esp, [[PSF[5]]], sc_banks=[PSF[4]])
                S.barrier()
            with ExitStack() as esp:
                yield from gdn_pre_gen(l, pi, tp0, Tp, False, esp, 256, 1)
                S.barrier()
            with ExitStack() as esp:
                yield from rr_gen([ret_chain(l, pi, tp0, Tp, False, d, esp) for d in range(2)])
                S.barrier()
            yield from combine_gen(tp0, Tp, retw_bc, 1, 1)

    def sample_mixers_overlapped(l, si, t0, T, is_s):
        with ExitStack() as es:
            att = attention_gen(l, si, t0, T, is_s, es, [[PSF[2], PSF[3]]])
            next(att)
            old = psf_banks[0]
            psf_banks[0] = [4, 5]
            n_att = (T // min(512, T)) * 8 * ((T + PAST) // 128)
            n_side = (T // 256) * 36 + (T // 128) * 5 + (len(seqs) - 1) * 70
            run_weighted(att, side_gen(l, si, t0, T, is_s), max(1, int(0.9 * n_att / n_side)))
            psf_banks[0] = old
            S.barrier()

    def ret_chain(l, si, t0, T, is_s, d, es):
        ntl = T // 128
        pf = "rt%d_" % d
        Sf = sb(es, pf + "S", [64, 4, 128], F32)
        Sb_ = sb(es, pf + "Sb", [64, 4, 128], BF16)
        qTp = Pool_(es, pf + "qT", [64, 4, 128], BF16, 2)
        kTp = Pool_(es, pf + "kT", [64, 4, 128], BF16, 2)
        ktp = Pool_(es, pf + "k", [128, 256], BF16, 2)
        vp = Pool_(es, pf + "v", [128, 512], BF16, 2)
        itp = Pool_(es, pf + "it", [128, 512], BF16, 2)
        qdp = Pool_(es, pf + "qd", [64, 4, 128], BF16, 2)
        kdp = Pool_(es, pf + "kd", [128, 256], BF16, 2)
        op_ = Pool_(es, pf + "o", [128, 512], F32, 2)
        odst, orn = (ofs, "ofs") if d == 0 else (ofb, "ofb")
        if is_s:
            S.dma("sp", Sf[:], st_ret[l, d].rearrange("h k e -> k h e"), W=[Sf])
        else:
            S.op("dve", "memset", W=[Sf], ap=Sf[:], constant=0.0)
        S.op("act", "copy", R=[Sf], W=[Sb_], out=Sb_[:], in_=Sf[:])
        order = range(ntl) if d == 0 else range(ntl - 1, -1, -1)
        for i in order:
            tt = t0 // 128 + i
            c0 = tt * 128
            qT = qTp.next(); kT = kTp.next(); kt = ktp.next(); v = vp.next()
            S.dma("sp", qT[:], bqT[:, :, c0:c0 + 128].rearrange("h p t -> p h t"), R=[RX["bqT"][tt]], W=[qT])
            S.dma("sp", kT[:], bkT[:, :, c0:c0 + 128].rearrange("h p t -> p h t"), R=[RX["bkT"][tt]], W=[kT])
            S.dma("sp", kt[:], bk[c0:c0 + 128, :], R=[RX["bk"][tt]], W=[kt])
            S.dma("sp", v[:], bv[c0:c0 + 128, :], R=[RX["bv"][tt]], W=[v])
            ps = psf()
            for h in range(4):
                S.op("pe", "matmul", R=[kT, qT], W=[ps], out=ps[:, h * 128:(h + 1) * 128], lhsT=kT[:, h, :],
                     rhs=qT[:, h, :], start=True, stop=True)
            it = itp.next()
            S.op("dve", "tensor_tensor", R=[ps, rdmat], W=[it], out=it[:], in0=ps[:],
                 in1=rdmat[:, d, :, :].rearrange("p h i -> p (h i)"), op=ALU.mult)
            qd = qdp.next()
            S.op("pool", "tensor_tensor", R=[qT, rqdec], W=[qd], out=qd[:], in0=qT[:], in1=rqdec[:, d, :, :],
                 op=ALU.mult)
            kd = kdp.next()
            S.op("pool", "tensor_tensor", R=[kt, rkdec], W=[kd], out=kd[:].rearrange("p (h e) -> p h e", h=4),
                 in0=kt[:].rearrange("p (h e) -> p h e", h=4),
                 in1=rkdec[:, d, :].unsqueeze(2).to_broadcast([128, 4, 64]), op=ALU.mult)
            yield
            po = psf()
            for h in range(4):
                S.op("pe", "matmul", R=[it, v], W=[po], out=po[:, h * 128:(h + 1) * 128],
                     lhsT=it[:, h * 128:(h + 1) * 128], rhs=v[:, h * 128:(h + 1) * 128], start=True, stop=False)
                S.op("pe", "matmul", R=[qd, Sb_], W=[po], out=po[:, h * 128:(h + 1) * 128], lhsT=qd[:, h, :],
                     rhs=Sb_[:, h, :], start=False, stop=True)
            pS = psf()
            for h in range(4):
                S.op("pe", "matmul", R=[kd, v], W=[pS], out=pS[0:64, h * 128:(h + 1) * 128],
                     lhsT=kd[:, h * 64:(h + 1) * 64], rhs=v[:, h * 128:(h + 1) * 128], start=True, stop=True)
            S.op("dve", "tensor_tensor", R=[Sf, rcdec], W=[Sf], out=Sf[:], in0=Sf[:],
                 in1=rcdec[0:64, d * 4:d * 4 + 4].unsqueeze(2).to_broadcast([64, 4, 128]), op=ALU.mult)
            S.op("dve", "tensor_tensor", R=[Sf, pS], W=[Sf], out=Sf[:].rearrange("p h e -> p (h e)"),
                 in0=Sf[:].rearrange("p h e -> p (h e)"), in1=pS[0:64, :], op=ALU.add)
            S.op("act", "copy", R=[Sf], W=[Sb_], out=Sb_[:], in_=Sf[:])
            o = op_.next()
            S.op("act", "copy", R=[po], W=[o], out=o[:], in_=po[:])
            S.dma("pool", odst[c0:c0 + 128, :], o[:], R=[o], W=[RX[orn][tt]])
            yield
        if not is_s:
            S.dma("pool", nret_out[si - 1, l, d].rearrange("h k e -> k h e"), Sf[:], R=[Sf])

    def pipeline_gen(gens, depth):
        gens = list(gens)
        active = []
        while gens or active:
            while gens and len(active) < depth:
                active.append(gens.pop(0))
            for g in list(active):
                try:
                    next(g)
                except StopIteration:
                    active.remove(g)
            yield

    def gdn_pre_gen(l, si, t0, T, is_s, es, G, nbuf):
        xin = Pool_(es, "gp_x", [128, 12, G + 4], F32, nbuf)
        acc = Pool_(es, "gp_a", [128, 12, G], F32, nbuf)
        sqp = Pool_(es, "gp_sq", [128, G], F32, 5)
        rsp = Pool_(es, "gp_rs", [128, G], F32, 5)
        nb = Pool_(es, "gp_nb", [128, 12, G], BF16, nbuf)
        sg = Pool_(es, "gp_sg", [128, 1024], BF16, 2)
        chunk_res = {}
        for g0 in range(0, T, G):
            x = xin.next()
            a = acc.next()
            lo = 2 if g0 == 0 else 0
            hi = 2 if g0 + G == T else 0
            if lo:
                S.op("pool", "memset", W=[x], ap=x[:, :, 0:2], constant=0.0)
            if hi:
                S.op("pool", "memset", W=[x], ap=x[:, :, G + 2:G + 4], constant=0.0)
            tl = [i for i in range((t0 + g0) // 128 - (0 if lo else 1), (t0 + g0 + G) // 128 + (0 if hi else 1))]
            S.dma("sp", x[:, :, lo:G + 4 - hi],
                  cT[:, t0 + g0 - 2 + lo:t0 + g0 + G + 2 - hi].rearrange("(c p) t -> p c t", p=128),
                  R=[RX["cT"][i] for i in tl], W=[x])
            yield
            yield
            ar = chunk_res.setdefault(("a", id(a)), [Res("gpa%d" % c) for c in range(12)])

            def convc(c):
                S.op("dve", "tensor_scalar", R=[x, convw], W=[ar[c]], out=a[:, c, :], in0=x[:, c, 0:G],
                     scalar1=convw[:, c, 0:1], scalar2=None, op0=ALU.mult)
                for k in range(1, 5):
                    S.op("dve", "scalar_tensor_tensor", R=[x, convw, ar[c]], W=[ar[c]], out=a[:, c, :],
                         in0=x[:, c, k:k + G], scalar=convw[:, c, k:k + 1], in1=a[:, c, :], op0=ALU.mult, op1=ALU.add)
                yield
                yield
                S.op("act", "activation", R=[ar[c]], W=[ar[c]], out=a[:, c, :], in_=a[:, c, :], func=AF.Silu)
            yield from pipeline_gen([convc(c) for c in range(12)], 3)
            n = nb.next()
            nr = chunk_res.setdefault(("n", id(n)), [Res("gpn%d" % c) for c in range(12)])

            def l2c(c):
                sq = sqp.next()
                S.op("act", "activation", R=[ar[c]], W=[sq], out=sq[:], in_=a[:, c, :], func=AF.Square)
                yield
                yield
                ps = psf()
                S.op("pe", "matmul", R=[ones_f, sq], W=[ps], out=ps[:, 0:G], lhsT=ones_f[:], rhs=sq[:], start=True,
                     stop=True)
                rs = rsp.next()
                S.op("dve", "tensor_scalar", R=[ps], W=[rs], out=rs[:], in0=ps[:, 0:G], scalar1=EPS, scalar2=None,
                     op0=ALU.add)
                yield
                yield
                S.op("act", "activation", R=[rs], W=[rs], out=rs[:], in_=rs[:], func=AF.Sqrt)
                yield
                yield
                S.op("dve", "reciprocal", R=[rs], W=[rs], out=rs[:], in_=rs[:])
                if c < 4:
                    S.op("dve", "scalar_tensor_tensor", R=[ar[c], rs], W=[nr[c]], out=n[:, c, :], in0=a[:, c, :],
                         scalar=float(128 ** -0.5), in1=rs[:], op0=ALU.mult, op1=ALU.mult)
                else:
                    S.op("dve", "tensor_tensor", R=[ar[c], rs], W=[nr[c]], out=n[:, c, :], in0=a[:, c, :], in1=rs[:],
                         op=ALU.mult)
            yield from pipeline_gen([l2c(c) for c in range(8)], 4)
            S.op("pool", "tensor_copy", R=ar[8:12], W=nr[8:12], out=n[:, 8:12, :], in_=a[:, 8:12, :])
            tiles = list(range((t0 + g0) // 128, (t0 + g0 + G) // 128))
            S.dma("pool", gqT[:, :, t0 + g0:t0 + g0 + G].rearrange("h p t -> p h t"), n[:, 0:4, :], R=nr[0:4],
                  W=[RX["gqT"][i] for i in tiles])
            S.dma("pool", gkT[:, :, t0 + g0:t0 + g0 + G].rearrange("h p t -> p h t"), n[:, 4:8, :], R=nr[4:8],
                  W=[RX["gkT"][i] for i in tiles])
            yield
            yield
            for ti in range(G // 128):
                tt = (t0 + g0) // 128 + ti
                pb = psb()
                transposes_to(pb, nr[4:12], [(n[:, 4 + c, ti * 128:(ti + 1) * 128], 128, c * 128) for c in range(8)])
                s = sg.next()
                S.op("act", "copy", R=[pb], W=[s], out=s[:], in_=pb[:])
                S.dma("pool", gk[tt * 128:(tt + 1) * 128, :], s[:, 0:512], R=[s], W=[RX["gk"][tt]])
                S.dma("pool", gv[tt * 128:(tt + 1) * 128, :], s[:, 512:1024], R=[s], W=[RX["gv"][tt]])
                yield

    def gdn_pre(l, si, t0, T, is_s):
        with ExitStack() as es:
            for _ in gdn_pre_gen(l, si, t0, T, is_s, es, min(512, T), 2):
                pass
            S.barrier()

    def pipeline(gens, depth):
        gens = list(gens)
        active = []
        while gens or active:
            while gens and len(active) < depth:
                active.append(gens.pop(0))
            for g in list(active):
                try:
                    next(g)
                except StopIteration:
                    active.remove(g)

    def run_weighted(main, side, ratio):
        main_alive = side_alive = True
        while main_alive or side_alive:
            if main_alive:
                for _ in range(ratio):
                    try:
                        next(main)
                    except StopIteration:
                        main_alive = False
                        break
            if side_alive:
                try:
                    next(side)
                except StopIteration:
                    side_alive = False

    def run_rr(gens):
        gens = list(gens)
        while gens:
            for g in list(gens):
                try:
                    next(g)
                except StopIteration:
                    gens.remove(g)

    def combine(t0, T, w_bc, zi, mi):
        for _ in combine_gen(t0, T, w_bc, zi, mi):
            pass

    def combine_gen(t0, T, w_bc, zi, mi):
        with ExitStack() as es:
            fa = Pool_(es, "cb_a", [128, 512], F32, 2)
            fb = Pool_(es, "cb_b", [128, 512], F32, 2)
            ngts = [ng_tiles(es, "cb%d_" % i) for i in range(2)]
            for i in range(T // 128):
                tt = t0 // 128 + i
                c0 = tt * 128
                a = fa.next(); b = fb.next()
                S.dma("sp", a[:], ofs[c0:c0 + 128, :], R=[RX["ofs"][tt]], W=[a])
                S.dma("sp", b[:], ofb[c0:c0 + 128, :], R=[RX["ofb"][tt]], W=[b])
                S.op("pool", "tensor_tensor", R=[a, b], W=[a], out=a[:], in0=a[:], in1=b[:], op=ALU.add)
                norm_gate_store(ngts[i % 2], a, a[:], w_bc, zi, mi, tt, 128)
                yield
            S.barrier()

    def gdn_scan(l, si, t0, T, is_s):
        with ExitStack() as es:
            run_rr([gdn_chain(l, si, t0, T, is_s, d, es) for d in range(2)])
            S.barrier()
        combine(t0, T, gdnw_bc, 2, 2)

    def gdn_chain(l, si, t0, T, is_s, d, es):
        ntl = T // 128
        pf = "gd%d_" % d
        Sf = sb(es, pf + "S", [128, 4, 128], F32)
        Sb_ = sb(es, pf + "Sb", [128, 4, 128], BF16)
        kTp = Pool_(es, pf + "kT", [128, 4, 128], BF16, 2)
        qTp = Pool_(es, pf + "qT", [128, 4, 128], BF16, 2)
        ktp = Pool_(es, pf + "k", [128, 512], BF16, 2)
        vtp = Pool_(es, pf + "v", [128, 512], BF16, 2)
        bgp = Pool_(es, pf + "bg", [128, 16], F32, 2)
        sm = Pool_(es, pf + "sm", [128, 8, 4], F32, 2)
        X1 = Pool_(es, pf + "X", [128, 4, 128], F32, 1)
        X2 = Pool_(es, pf + "X2", [128, 4, 128], F32, 1)
        Dn = Pool_(es, pf + "Dn", [128, 4, 128], F32, 1)
        Ea = Pool_(es, pf + "Ea", [128, 4, 128], F32, 1)
        Eb = Pool_(es, pf + "Eb", [128, 4, 128], F32, 1)
        Fq = Pool_(es, pf + "Fq", [128, 4, 128], F32, 1)
        Eg = Pool_(es, pf + "Eg", [128, 4, 128], F32, 1)
        Pp = Pool_(es, pf + "P", [128, 4, 128], F32, 2)
        PTp = Pool_(es, pf + "PT", [128, 4, 128], F32, 2)
        TTf = Pool_(es, pf + "TTf", [128, 4, 128], F32, 1)
        qkTp = Pool_(es, pf + "qkT", [128, 4, 128], BF16, 2)
        qgp = Pool_(es, pf + "qg", [128, 4, 128], BF16, 2)
        vbp = Pool_(es, pf + "vb", [128, 4, 128], F32, 1)
        kbp = Pool_(es, pf + "kb", [128, 4, 128], F32, 1)
        kdp = Pool_(es, pf + "kd", [128, 4, 128], BF16, 2)
        Up = Pool_(es, pf + "U", [128, 4, 128], F32, 2)
        WTp = Pool_(es, pf + "WT", [128, 4, 128], BF16, 2)
        vnp = Pool_(es, pf + "vn", [128, 128], BF16, 4)
        op_ = Pool_(es, pf + "o", [64, 512], F32, 2)
        odst, orn = (ofs, "ofs") if d == 0 else (ofb, "ofb")
        Sfr = [Res("gSf%d" % h) for h in range(4)]
        Sbr = [Res("gSb%d" % h) for h in range(4)]
        if is_s:
            S.dma("sp", Sf[:], st_gdn[l, d].rearrange("h k e -> k h e"), W=Sfr)
        else:
            S.op("dve", "memset", W=Sfr, ap=Sf[:], constant=0.0)
        S.op("act", "copy", R=Sfr, W=Sbr, out=Sb_[:], in_=Sf[:])
        idb = ident_f[:].unsqueeze(1).to_broadcast([128, 4, 128])
        v4 = lambda t: t[:].rearrange("p h b -> p (h b)")
        order = range(ntl) if d == 0 else range(ntl - 1, -1, -1)
        for i in order:
            tt = t0 // 128 + i
            c0 = tt * 128
            kT = kTp.next(); qT = qTp.next(); kt = ktp.next(); vt = vtp.next(); bg = bgp.next()
            S.dma("sp", kT[:], gkT[:, :, c0:c0 + 128].rearrange("h p t -> p h t"), R=[RX["gkT"][tt]], W=[kT])
            S.dma("sp", qT[:], gqT[:, :, c0:c0 + 128].rearrange("h p t -> p h t"), R=[RX["gqT"][tt]], W=[qT])
            S.dma("sp", kt[:], gk[c0:c0 + 128, :], R=[RX["gk"][tt]], W=[kt])
            S.dma("sp", vt[:], gv[c0:c0 + 128, :], R=[RX["gv"][tt]], W=[vt])
            S.dma("sp", bg[:], bgs[c0:c0 + 128, :], R=[RX["bgs"][tt]], W=[bg])
            s = sm.next()
            S.op("dve", "tensor_copy", R=[bg], W=[s], out=s[:, 0, :], in_=bg[:, 8 + d * 4:12 + d * 4])
            S.op("dve", "tensor_copy", R=[bg], W=[s], out=s[:, 1, :], in_=bg[:, d * 4:d * 4 + 4])
            pg = psf()
            S.op("pe", "matmul", R=[utri2, s], W=[pg], out=pg[:, 0:4], lhsT=utri2[:, d, :], rhs=s[:, 0, :],
                 start=True, stop=True)
            S.op("pe", "matmul", R=[bd1, s], W=[pg], out=pg[:, 4:8], lhsT=bd1[:], rhs=s[:, 0, :], start=True, stop=True)
            S.op("pe", "matmul", R=[selc, s], W=[pg], out=pg[:, 8:12], lhsT=selc[:, 0, :], rhs=s[:, 0, :],
                 start=True, stop=True)
            S.op("pe", "matmul", R=[selc, s], W=[pg], out=pg[:, 12:16], lhsT=selc[:, 1, :], rhs=s[:, 0, :],
                 start=True, stop=True)
            S.op("dve", "tensor_copy", R=[pg], W=[s], out=s[:, 2, :], in_=pg[:, 0:4])
            S.op("act", "activation", R=[pg], W=[s], out=s[:, 6:8, :].rearrange("p c h -> p (c h)"), in_=pg[:, 8:16],
                 func=AF.Exp)
            S.op("dve", "tensor_tensor", R=[pg], W=[s], out=s[:, 4, :], in0=pg[:, 4:8], in1=s[:, 2, :], op=ALU.subtract)
            S.op("act", "activation", R=[s], W=[s], out=s[:, 4, :], in_=s[:, 4, :], func=AF.Exp)
            S.op("act", "activation", R=[s], W=[s], out=s[:, 5, :], in_=s[:, 2, :], func=AF.Exp)
            S.op("dve", "tensor_tensor", R=[s], W=[s], out=s[:, 5, :], in0=s[:, 5, :], in1=s[:, 1, :], op=ALU.mult)
            ckpt('g0')
            yield
            x1 = X1.next(); x2 = X2.next()
            S.op("dve", "tensor_tensor", R=[ident_f, s], W=[x1], out=x1[:], in0=idb,
                 in1=s[:, 2, :].unsqueeze(2).to_broadcast([128, 4, 128]), op=ALU.mult)
            S.op("pool", "tensor_tensor", R=[ident_f, s], W=[x2], out=x2[:], in0=idb,
                 in1=s[:, 1, :].unsqueeze(2).to_broadcast([128, 4, 128]), op=ALU.mult)
            pR = psf(); pRb = psf()
            S.op("pe", "matmul", R=[ones_f, x1], W=[pR], out=pR[:], lhsT=ones_f[:], rhs=v4(x1), start=True, stop=True)
            S.op("pe", "matmul", R=[ones_f, x2], W=[pRb], out=pRb[:], lhsT=ones_f[:], rhs=v4(x2), start=True, stop=True)
            dn = Dn.next()
            S.op("dve", "tensor_tensor", R=[pR, s], W=[dn], out=dn[:], in0=pR[:].rearrange("p (h b) -> p h b", h=4),
                 in1=s[:, 2, :].unsqueeze(2).to_broadcast([128, 4, 128]), op=ALU.subtract)
            eg = Eg.next()
            S.op("act", "activation", R=[pR], W=[eg], out=v4(eg), in_=pR[:], func=AF.Exp)
            ea = Ea.next(); eb = Eb.next(); fq = Fq.next()
            S.op("dve", "tensor_scalar", R=[dn], W=[ea], out=ea[:], in0=dn[:], scalar1=-1.0, scalar2=0.0,
                 op0=ALU.mult, op1=ALU.min)
            S.op("dve", "tensor_scalar", R=[dn], W=[eb], out=eb[:], in0=dn[:], scalar1=0.0, scalar2=None, op0=ALU.min)
            S.op("act", "activation", R=[ea], W=[ea], out=ea[:], in_=ea[:], func=AF.Exp)
            S.op("act", "activation", R=[eb], W=[eb], out=eb[:], in_=eb[:], func=AF.Exp)
            S.op("dve", "tensor_tensor", R=[ea, gmask2], W=[ea], out=ea[:], in0=ea[:],
                 in1=gmask2[:, d, 0, :].unsqueeze(1).to_broadcast([128, 4, 128]), op=ALU.mult)
            S.op("dve", "tensor_tensor", R=[ea, s], W=[ea], out=ea[:], in0=ea[:],
                 in1=s[:, 1, :].unsqueeze(2).to_broadcast([128, 4, 128]), op=ALU.mult)
            S.op("pool", "tensor_tensor", R=[eb, gmask2], W=[fq], out=fq[:], in0=eb[:],
                 in1=gmask2[:, d, 2, :].unsqueeze(1).to_broadcast([128, 4, 128]), op=ALU.mult)
            S.op("pool", "tensor_tensor", R=[eb, gmask2], W=[eb], out=eb[:], in0=eb[:],
                 in1=gmask2[:, d, 1, :].unsqueeze(1).to_broadcast([128, 4, 128]), op=ALU.mult)
            S.op("dve", "tensor_tensor", R=[eb, pRb], W=[eb], out=eb[:], in0=eb[:],
                 in1=pRb[:].rearrange("p (h b) -> p h b", h=4), op=ALU.mult)
            ckpt('g1')
            yield
            qg = qgp.next()
            S.op("pool", "tensor_tensor", R=[qT, eg], W=[qg], out=qg[:], in0=qT[:], in1=eg[:], op=ALU.mult)
            pK = psf(); pQ = psf()
            for h in range(4):
                S.op("pe", "matmul", R=[kT], W=[pK], out=pK[:, h * 128:(h + 1) * 128], lhsT=kT[:, h, :], rhs=kT[:, h, :],
                     start=True, stop=True)
                S.op("pe", "matmul", R=[kT, qT], W=[pQ], out=pQ[:, h * 128:(h + 1) * 128], lhsT=kT[:, h, :],
                     rhs=qT[:, h, :], start=True, stop=True)
            P = Pp.next(); PT = PTp.next(); ttf = TTf.next(); qkT = qkTp.next()
            pKv = pK[:].rearrange("p (h b) -> p h b", h=4)
            S.op("dve", "tensor_tensor", R=[pK, ea], W=[P], out=P[:], in0=pKv, in1=ea[:], op=ALU.mult)
            S.op("dve", "tensor_tensor", R=[pK, eb], W=[PT], out=PT[:], in0=pKv, in1=eb[:], op=ALU.mult)
            S.op("pool", "tensor_tensor", R=[PT, ident_f], W=[ttf], out=ttf[:], in0=PT[:], in1=idb, op=ALU.add)
            S.op("dve", "tensor_tensor", R=[pQ, fq], W=[qkT], out=qkT[:], in0=pQ[:].rearrange("p (h b) -> p h b", h=4),
                 in1=fq[:], op=ALU.mult)
            ckpt('g2')
            yield
            for lev in range(1, 6):
                p1 = psf()
                for h in range(4):
                    S.op("pe", "matmul", R=[PT, P], W=[p1], out=p1[:, h * 128:(h + 1) * 128], lhsT=PT[:, h, :],
                         rhs=P[:, h, :], start=True, stop=True)
                Pn = Pp.next()
                S.op("dve", "tensor_copy", R=[p1], W=[Pn], out=v4(Pn), in_=p1[:])
                p3 = psf()
                for h in range(4):
                    S.op("pe", "matmul", R=[Pn, ttf], W=[p3], out=p3[:, h * 128:(h + 1) * 128], lhsT=Pn[:, h, :],
                         rhs=ttf[:, h, :], start=True, stop=True)
                if lev < 5:
                    p2 = psf()
                    for h in range(4):
                        if os.environ.get("KNOTR"):
                            S.op("pe", "matmul", R=[PT, P], W=[p2], out=p2[:, h * 128:(h + 1) * 128], lhsT=P[:, h, :],
                                 rhs=PT[:, h, :], start=True, stop=True)
                        else:
                            S.op("pe", "transpose", R=[Pn, ident_f], W=[p2], out=p2[:, h * 128:(h + 1) * 128],
                                 in_=Pn[:, h, :], identity=ident_f[:])
                    PTn = PTp.next()
                    S.op("act", "copy", R=[p2], W=[PTn], out=v4(PTn), in_=p2[:])
                S.op("dve", "tensor_tensor", R=[ttf, p3], W=[ttf], out=v4(ttf), in0=v4(ttf), in1=p3[:], op=ALU.add)
                P = Pn
                if lev < 5:
                    PT = PTn
                ckpt('g3')
                yield
            vb = vbp.next(); kb = kbp.next(); kd = kdp.next()
            S.op("dve", "tensor_tensor", R=[vt, s], W=[vb], out=vb[:], in0=vt[:].rearrange("p (h e) -> p h e", h=4),
                 in1=s[:, 1, :].unsqueeze(2).to_broadcast([128, 4, 128]), op=ALU.mult)
            S.op("pool", "tensor_tensor", R=[kt, s], W=[kb], out=kb[:], in0=kt[:].rearrange("p (h e) -> p h e", h=4),
                 in1=s[:, 5, :].unsqueeze(2).to_broadcast([128, 4, 128]), op=ALU.mult)
            S.op("pool", "tensor_tensor", R=[kt, s], W=[kd], out=kd[:], in0=kt[:].rearrange("p (h e) -> p h e", h=4),
                 in1=s[:, 4, :].unsqueeze(2).to_broadcast([128, 4, 128]), op=ALU.mult)
            U = Up.next(); WT = WTp.next()
            pU = psf(); pW = psf()
            for h in range(4):
                S.op("pe", "matmul", R=[ttf, vb], W=[pU], out=pU[:, h * 128:(h + 1) * 128], lhsT=ttf[:, h, :],
                     rhs=vb[:, h, :], start=True, stop=True)
                S.op("pe", "matmul", R=[kb, ttf], W=[pW], out=pW[:, h * 128:(h + 1) * 128], lhsT=kb[:, h, :],
                     rhs=ttf[:, h, :], start=True, stop=True)
            S.op("dve", "tensor_copy", R=[pU], W=[U], out=v4(U), in_=pU[:])
            S.op("act", "copy", R=[pW], W=[WT], out=v4(WT), in_=pW[:])
            ckpt('g4')
            yield
            for cb in ((0, 1) if d == 0 else (1, 0)):
                rows = slice(cb * 64, cb * 64 + 64)
                cols = slice(cb * 64, cb * 64 + 64)
                po = PSF[4 + d]
                for h in range(4):
                    pa = psf()
                    S.op("pe", "matmul", R=[WT, Sbr[h]], W=[pa], out=pa[:, 0:128], lhsT=WT[:, h, :], rhs=Sb_[:, h, :],
                         start=True, stop=True)
                    vn = vnp.next()
                    S.op("dve", "tensor_tensor", R=[U, pa], W=[vn], out=vn[rows, :], in0=U[rows, h, :],
                         in1=pa[rows, 0:128], op=ALU.subtract)
                    S.op("pe", "matmul", R=[qg, Sbr[h]], W=[po], out=po[0:64, h * 128:(h + 1) * 128], lhsT=qg[:, h, cols],
                         rhs=Sb_[:, h, :], start=True, stop=False)
                    S.op("pe", "matmul", R=[qkT, vn], W=[po], out=po[0:64, h * 128:(h + 1) * 128],
                         lhsT=qkT[rows, h, cols], rhs=vn[rows, :], start=False, stop=True)
                    pS = psf()
                    S.op("pe", "matmul", R=[kd, vn], W=[pS], out=pS[:, 0:128], lhsT=kd[rows, h, :], rhs=vn[rows, :],
                         start=True, stop=True)
                    S.op("dve", "scalar_tensor_tensor", R=[Sfr[h], s, pS], W=[Sfr[h]], out=Sf[:, h, :], in0=Sf[:, h, :],
                         scalar=s[:, 6 + cb, h:h + 1], in1=pS[:, 0:128], op0=ALU.mult, op1=ALU.add)
                    S.op("act", "copy", R=[Sfr[h]], W=[Sbr[h]], out=Sb_[:, h, :], in_=Sf[:, h, :])
                    if h % 2 == 1:
                        ckpt('g5')
                        yield
                r0 = c0 + cb * 64
                o = op_.next()
                S.op("act", "copy", R=[po], W=[o], out=o[:], in_=po[0:64, :])
                S.dma("pool", odst[r0:r0 + 64, :], o[:], R=[o], W=[RX[orn][tt]])
                ckpt('g6')
                yield
        if not is_s:
            S.dma("pool", ngdn_out[si - 1, l, d].rearrange("h k e -> k h e"), Sf[:], R=Sfr)

    def phaseE(l, xsrc, xdst):
        with ExitStack() as es:
            wbr = sb(es, "pe_wbr", [128, 12, D], BF16)
            wo = sb(es, "pe_wo", [128, 8, D], BF16)
            with ExitStack() as es0:
                wst = Pool_(es0, "pe_wst", [128, 4, D], F32, 2)
                for q in range(3):
                    w = wst.next()
                    S.dma("sp", w[:], w_branch[l, q].rearrange("(c p) n -> p c n", p=128), W=[w])
                    S.op("dve" if q % 2 == 0 else "pool", "tensor_copy", R=[w], W=[wbr], out=wbr[:, q * 4:q * 4 + 4, :],
                         in_=w[:])
                for q in range(2):
                    w = wst.next()
                    S.dma("sp", w[:], w_out[l].rearrange("(c p) n -> p c n", p=128)[:, q * 4:q * 4 + 4, :], W=[w])
                    S.op("dve" if q % 2 == 0 else "pool", "tensor_copy", R=[w], W=[wo], out=wo[:, q * 4:q * 4 + 4, :],
                         in_=w[:])
                S.barrier()
            gtp = Pool_(es, "pe_gt", [128, 12, 128], BF16, 4)
            mgp = Pool_(es, "pe_mg", [128, 3 * D], BF16, 4)
            mp = Pool_(es, "pe_m", [128, D], F32, 3)
            mbp = Pool_(es, "pe_mb", [128, D], BF16, 3)
            mTp = Pool_(es, "pe_mT", [128, 8, 128], BF16, 3)
            xp = Pool_(es, "pe_x", [128, D], F32, 4)
            tp = Pool_(es, "pe_t", [128, 512], F32, 4)

            def tileE(tt):
                j = 0 if tt < NTS else 1
                c0 = tt * 128
                gt = gtp.next(); mgt = mgp.next(); xt = xp.next()
                S.dma("sp", gt[:], gT[:, :, c0:c0 + 128].rearrange("m (c p) t -> p (m c) t", p=128),
                      R=[RX["gT0"][tt], RX["gT1"][tt], RX["gT2"][tt]], W=[gt])
                S.dma("sp", mgt[:], mg[c0:c0 + 128, :], R=[RX["mg"][tt]], W=[mgt])
                S.dma("sp", xt[:], xsrc[c0:c0 + 128, :], R=[RX["x"][tt]] if l > 0 else [], W=[xt])
                yield
                m = mp.next()
                for q in range(3):
                    for nb in range(2):
                        ps = psf()
                        for c in range(4):
                            S.op("pe", "matmul", R=[gt, wbr], W=[ps], out=ps[:], lhsT=gt[:, q * 4 + c, :],
                                 rhs=wbr[:, q * 4 + c, nb * 512:(nb + 1) * 512], start=(c == 0), stop=(c == 3))
                        cols = slice(nb * 512, (nb + 1) * 512)
                        if q == 0:
                            S.op("dve", "tensor_tensor", R=[ps, mgt], W=[m], out=m[:, cols], in0=ps[:],
                                 in1=mgt[:, cols], op=ALU.mult)
                        else:
                            t = tp.next()
                            S.op("dve", "tensor_tensor", R=[ps, mgt], W=[t], out=t[:], in0=ps[:],
                                 in1=mgt[:, q * D + nb * 512:q * D + (nb + 1) * 512], op=ALU.mult)
                            S.op("pool", "tensor_tensor", R=[t, m], W=[m], out=m[:, cols], in0=m[:, cols], in1=t[:],
                                 op=ALU.add)
                yield
                mb = mbp.next()
                S.op("act", "copy", R=[m], W=[mb], out=mb[:], in_=m[:])
                yield
                pb = psb()
                transposes_to(pb, mb, [(mb[:, c * 128:(c + 1) * 128], 128, c * 128) for c in range(8)])
                mT = mTp.next()
                S.op("act", "copy", R=[pb], W=[mT], out=mT[:].rearrange("p c t -> p (c t)"), in_=pb[:])
                yield
                for nb in range(2):
                    ps = psf()
                    for c in range(8):
                        S.op("pe", "matmul", R=[mT, wo], W=[ps], out=ps[:], lhsT=mT[:, c, :],
                             rhs=wo[:, c, nb * 512:(nb + 1) * 512], start=(c == 0), stop=(c == 7))
                    cols = slice(nb * 512, (nb + 1) * 512)
                    t = tp.next()
                    S.op("dve", "tensor_tensor", R=[ps, gate_bc], W=[t], out=t[:], in0=ps[:], in1=gate_bc[:, j, cols],
                         op=ALU.mult)
                    S.op("pool", "tensor_tensor", R=[t, xt], W=[xt], out=xt[:, cols], in0=xt[:, cols], in1=t[:],
                         op=ALU.add)
                S.dma("pool", xdst[c0:c0 + 128, :], xt[:], R=[xt], W=[RX["x"][tt]])

            gens = [tileE(tt) for tt in range(NT)]
            active = []
            while gens or active:
                if gens:
                    active.append(gens.pop(0))
                for g in list(active):
                    try:
                        next(g)
                    except StopIteration:
                        active.remove(g)
            S.barrier()

    try:
        for l in range(depth):
            layer(l)
    except StopBuild as e:
        print("build stopped at", e)
        root2 = None
    S.barrier()
    S.emit()
    build.stats = (S.n_instr, S.n_wait)
    return nc


_CACHE = {}


def _get_nc(T_S, depth, debug):
    key = (T_S, depth, debug)
    if key not in _CACHE:
        _CACHE[key] = build(T_S, depth, debug)
    return _CACHE[key]


def run_cfg(inp, T_S, depth, debug=False):
    f = lambda a: np.ascontiguousarray(np.asarray(a), dtype=np.float32)
    xs = f(inp["x_sample"])
    xp = f(inp["x_prompt"])
    n_core = 8
    nsb = xs.shape[0]
    consts = make_consts(T_S)
    shared = {
        "norm_w": f(inp["norm_w"])[:depth], "w_ada": f(inp["w_ada"])[:depth], "b_ada": f(inp["b_ada"])[:depth],
        "w_in": f(inp["w_in"])[:depth], "qk_norm_w": f(inp["qk_norm_w"])[:depth],
        "diff_lambda": f(inp["diff_lambda"])[:depth], "subln_w": f(inp["subln_w"])[:depth],
        "ret_decay": f(inp["ret_decay"])[:depth].reshape(depth, 8), "ret_norm_w": f(inp["ret_norm_w"])[:depth],
        "conv_w": f(inp["conv_w"])[:depth], "gdn_a_log": f(inp["gdn_a_log"])[:depth].reshape(depth, 8),
        "gdn_dt_bias": f(inp["gdn_dt_bias"])[:depth].reshape(depth, 8), "gdn_norm_w": f(inp["gdn_norm_w"])[:depth],
        "w_branch": f(inp["w_branch"])[:depth], "w_out": f(inp["w_out"])[:depth],
    }
    for k, v in consts.items():
        shared["c_" + k] = v
    ck = f(inp["cache_attn_k"])
    cv = f(inp["cache_attn_v"])
    sr = f(inp["state_ret"])
    sgd = f(inp["state_gdn"])
    c = f(inp["c"])
    cctx = f(inp["c_ctx"])
    in_maps = []
    for core in range(n_core):
        b = core % nsb
        m = dict(shared)
        m["x_in"] = np.ascontiguousarray(np.concatenate(
            [xs[b, :T_S]] + [xp[core * NPR + i] for i in range(NPR)], axis=0))
        m["cond"] = np.ascontiguousarray(np.stack([c[b], cctx], axis=0))
        m["cache_k"] = np.ascontiguousarray(ck[b, :depth].reshape(depth, PAST, 512))
        m["cache_v"] = np.ascontiguousarray(cv[b, :depth].reshape(depth, PAST, 512))
        m["st_ret"] = np.ascontiguousarray(sr[b, :depth])
        m["st_gdn"] = np.ascontiguousarray(sgd[b, :depth])
        in_maps.append(m)
    nc = _get_nc(T_S, depth, debug)
    res = run_bass_kernel_spmd(nc, in_maps, core_ids=list(range(n_core)))
    R = res.results
    y_s = np.stack([np.asarray(R[b]["y"])[:T_S] for b in range(nsb)], axis=0).astype(np.float32)
    y_p = np.stack([np.asarray(R[core]["y"])[T_S + i * TP:T_S + (i + 1) * TP]
                    for core in range(n_core) for i in range(NPR)], axis=0).astype(np.float32)
    nk = np.concatenate([np.asarray(R[core]["nk"]) for core in range(n_core)], axis=0).astype(np.float32)
    nv = np.concatenate([np.asarray(R[core]["nv"]) for core in range(n_core)], axis=0).astype(np.float32)
    nret = np.concatenate([np.asarray(R[core]["nret"]) for core in range(n_core)], axis=0).astype(np.float32)
    ngdn = np.concatenate([np.asarray(R[core]["ngdn"]) for core in range(n_core)], axis=0).astype(np.float32)
    nk = nk.reshape(n_core * NPR, depth, TP, 4, 2, 64)
    nv = nv.reshape(n_core * NPR, depth, TP, 4, 128)
    outs = (y_p, y_s, nk, nv, nret, ngdn)
    if debug:
        return outs, R
    return outs


def kernel(**inputs):
    return run_cfg(inputs, 4096, DEPTH, False)
```

```python
import math
from contextlib import ExitStack
import numpy as np
import ml_dtypes
import concourse.bass as bass
import concourse.mybir as mybir
from concourse.bass_utils import run_bass_kernel_spmd

F32 = mybir.dt.float32
BF16 = mybir.dt.bfloat16
AF = mybir.ActivationFunctionType
ALU = mybir.AluOpType
AX = mybir.AxisListType

D = 1024
DEPTH = 4
TP = 256
NPR = 2
PAST = 256
D_IN = 8720
EPS = 1e-6
CH = 64


import os


class StopBuild(Exception):
    pass


def ckpt(name):
    if os.environ.get("KSTOP", "") == name:
        raise StopBuild(name)


class Res:
    __slots__ = ("name", "w", "r", "excl")

    def __init__(self, name=""):
        self.name = name
        self.w = None
        self.r = {}
        self.excl = False


class Tile:
    def __init__(self, h, name, psum=False):
        self.h = h
        self.res = Res(name)
        self.res.excl = psum

    def __getitem__(self, k):
        return self.h[k]


def _res(x):
    out = []
    for t in x:
        if isinstance(t, (list, tuple)):
            out.extend(_res(t))
        elif isinstance(t, Res):
            out.append(t)
        else:
            out.append(t.res)
    return out


class Sched:
    ENG = ("pe", "act", "dve", "pool", "sp")

    def __init__(self, nc, n_dma_slots=8):
        self.nc = nc
        self.streams = {e: [] for e in self.ENG}
        self.sems = {}
        self.cnt = {}
        for e in ("pe", "act", "dve", "pool"):
            self.sems[e] = nc.alloc_semaphore("s_" + e)
            self.cnt[e] = 0
        self.nslots = n_dma_slots
        self.dq = {}
        for q in ("sp", "pool", "act"):
            slots = []
            for i in range(n_dma_slots):
                k = "d_%s%d" % (q, i)
                self.sems[k] = nc.alloc_semaphore(k)
                slots.append([k, 0])
            self.dq[q] = [slots, 0]
        self.seen = {e: {} for e in self.ENG}
        self.n_instr = 0
        self.n_wait = 0

    def _wait(self, eng, key, val):
        if val is None or val <= 0:
            return
        if eng == "pe" and key == "pe":
            return
        s = self.seen[eng]
        if s.get(key, 0) >= val:
            return
        s[key] = val
        sem = self.sems[key]
        self.streams[eng].append(lambda e, sem=sem, val=val: e.wait_ge(sem, val))
        self.n_wait += 1

    def _deps(self, eng, reads, writes, is_dma=False):
        for r in reads:
            if r.w is not None:
                self._wait(eng, r.w[0], r.w[1])
        for w in writes:
            if w.w is not None:
                if is_dma or not (w.w[0] == eng):
                    self._wait(eng, w.w[0], w.w[1])
            for k, v in w.r.items():
                if (not is_dma) and k == eng:
                    continue
                self._wait(eng, k, v)

    def _mark(self, key, val, reads, writes):
        for r in reads:
            if r.r.get(key, 0) < val:
                r.r[key] = val
        for w in writes:
            w.w = (key, val)
            w.r = {}

    def op(self, eng, method, R=(), W=(), **kw):
        reads = _res(R)
        writes = _res(W)
        ex = [r for r in reads if r.excl]
        if ex:
            reads = [r for r in reads if not r.excl]
            writes = writes + [r for r in ex if r not in writes]
        self._deps(eng, reads, writes)
        self.cnt[eng] += 1
        val = self.cnt[eng]
        sem = self.sems[eng]
        import traceback
        org = traceback.extract_stack(limit=3)[0]
        org = "%s:%d" % (org.name, org.lineno)

        def _f(e, m=method, kw=kw, sem=sem, org=org):
            try:
                return getattr(e, m)(**kw).then_inc(sem, 1)
            except Exception as ex:
                raise RuntimeError("emit failed at %s (%s): %s" % (org, m, ex)) from ex
        self.streams[eng].append(_f)
        self._mark(eng, val, reads, writes)
        self.n_instr += 1

    def dma(self, q, out, in_, R=(), W=(), **kw):
        reads = _res(R)
        writes = _res(W)
        slots, idx = self.dq[q]
        slot = slots[idx % self.nslots]
        self.dq[q][1] = idx + 1
        key = slot[0]
        if slot[1] > 0:
            self._wait(q, key, slot[1])
        self._deps(q, reads, writes, is_dma=True)
        slot[1] += 16
        val = slot[1]
        sem = self.sems[key]
        import traceback
        org = traceback.extract_stack(limit=3)[0]
        org = "%s:%d" % (org.name, org.lineno)

        def _f(e, out=out, in_=in_, sem=sem, kw=kw, org=org):
            try:
                return e.dma_start(out=out, in_=in_, **kw).then_inc(sem, 16)
            except Exception as ex:
                raise RuntimeError("dma emit failed at %s: %s" % (org, ex)) from ex
        self.streams[q].append(_f)
        self._mark(key, val, reads, writes)
        self.n_instr += 1

    def barrier(self):
        for e in self.ENG:
            for k in ("pe", "act", "dve", "pool"):
                if k != e:
                    self._wait(e, k, self.cnt[k])
            for q in self.dq:
                for slot in self.dq[q][0]:
                    if slot[1] > 0:
                        self._wait(e, slot[0], slot[1])

    def emit(self):
        nc = self.nc
        st = self.streams
        with nc.Block() as block:
            @block.sync
            def _(e):
                for f in st["sp"]:
                    f(e)

            @block.tensor
            def _(e):
                for f in st["pe"]:
                    f(e)

            @block.scalar
            def _(e):
                for f in st["act"]:
                    f(e)

            @block.vector
            def _(e):
                for f in st["dve"]:
                    f(e)

            @block.gpsimd
            def _(e):
                for f in st["pool"]:
                    f(e)


def make_consts(T_S):
    c = {}
    c["ident"] = np.eye(128, dtype=np.float32)
    n_rows = T_S // 64
    row = np.repeat(np.arange(n_rows, dtype=np.float32), 64)
    col = np.tile(np.arange(64, dtype=np.float32), n_rows)
    inv = (1.0 / (10000.0 ** (np.arange(16, dtype=np.float32) / 16))).astype(np.float32)
    ar = row[:, None] * inv
    ac = col[:, None] * inv
    ang = np.concatenate([ar, ar, ac, ac], axis=-1).astype(np.float32)
    cos = np.cos(ang).astype(np.float32)
    sin = np.sin(ang).astype(np.float32)
    sgn = np.tile(np.concatenate([-np.ones(16), np.ones(16)]), 2).astype(np.float32)
    c["ropec"] = cos
    c["ropes"] = (sin * sgn).astype(np.float32)
    a = np.arange(64)[:, None]
    b = np.arange(64)[None, :]
    low = (a > b).astype(np.float32)
    up = (b > a).astype(np.float32)
    upi = (b >= a).astype(np.float32)
    lowi = (a >= b).astype(np.float32)
    gm = np.zeros((64, 2, 3, 64), np.float32)
    gm[:, 0, 0] = -low
    gm[:, 0, 1] = -up
    gm[:, 0, 2] = upi
    gm[:, 1, 0] = -up
    gm[:, 1, 1] = -low
    gm[:, 1, 2] = lowi
    c["gmask"] = gm
    a2 = np.arange(128)[:, None]
    b2 = np.arange(128)[None, :]
    same = (a2 // 64 == b2 // 64)
    gm2 = np.zeros((128, 2, 3, 128), np.float32)
    gm2[:, 0, 0] = -1.0 * ((a2 > b2) & same)
    gm2[:, 0, 1] = -1.0 * ((b2 > a2) & same)
    gm2[:, 0, 2] = ((b2 >= a2) & same)
    gm2[:, 1, 0] = -1.0 * ((b2 > a2) & same)
    gm2[:, 1, 1] = -1.0 * ((a2 > b2) & same)
    gm2[:, 1, 2] = ((a2 >= b2) & same)
    c["gmask2"] = gm2
    ut2 = np.zeros((128, 2, 128), np.float32)
    ut2[:, 0] = ((a2 <= b2) & same)
    ut2[:, 1] = ((a2 >= b2) & same)
    c["utri2"] = ut2
    c["bd1"] = same.astype(np.float32)
    sel = np.zeros((128, 2, 128), np.float32)
    sel[0:64, 0, :] = 1.0
    sel[64:128, 1, :] = 1.0
    c["selc"] = sel
    ut = np.zeros((64, 2, 64), np.float32)
    ut[:, 0] = (a <= b).astype(np.float32)
    ut[:, 1] = (a >= b).astype(np.float32)
    c["utri"] = ut
    j = np.arange(128)[:, None].astype(np.float32)
    i = np.arange(128)[None, :].astype(np.float32)
    rr = np.zeros((128, 2, 128), np.float32)
    rm = np.zeros((128, 2, 128), np.float32)
    rr[:, 0] = np.maximum(i - j, 0)
    rm[:, 0] = (i >= j)
    rr[:, 1] = np.maximum(j - i, 0)
    rm[:, 1] = (j >= i)
    c["rrel"] = rr
    c["rmask"] = rm
    rq = np.zeros((64, 2, 128), np.float32)
    rq[:, 0] = (np.arange(128) + 1.0)[None, :]
    rq[:, 1] = (128.0 - np.arange(128))[None, :]
    c["rqexp"] = rq
    rk = np.zeros((128, 2), np.float32)
    rk[:, 0] = 127.0 - np.arange(128)
    rk[:, 1] = np.arange(128)
    c["rkexp"] = rk
    return c


CONST_SHAPES = lambda T_S: {k: v.shape for k, v in make_consts(T_S).items()}


def build(T_S=4096, depth=DEPTH, debug=False):
    nc = bass.Bass("TRN2", target_bir_lowering=False)
    S = Sched(nc)
    TT = T_S + NPR * TP
    NT = TT // 128
    NTS = T_S // 128
    seqs = [(0, T_S, True)] + [(T_S + i * TP, TP, False) for i in range(NPR)]
    lam_inits = [0.8 - 0.6 * math.exp(-0.3 * l) for l in range(depth)]

    def din(name, shape, dt=F32):
        return nc.dram_tensor(name, list(shape), dt, kind="ExternalInput").ap()

    def dout(name, shape, dt=F32):
        return nc.dram_tensor(name, list(shape), dt, kind="ExternalOutput").ap()

    def scr(name, shape, dt):
        if debug:
            return nc.dram_tensor(name, list(shape), dt, kind="ExternalOutput").ap()
        return nc.dram_tensor(name, list(shape), dt).ap()

    x_in = din("x_in", [TT, D])
    cond = din("cond", [2, D])
    cache_k = din("cache_k", [depth, PAST, 512])
    cache_v = din("cache_v", [depth, PAST, 512])
    st_ret = din("st_ret", [depth, 2, 4, 64, 128])
    st_gdn = din("st_gdn", [depth, 2, 4, 128, 128])
    norm_w = din("norm_w", [depth, D])
    w_ada = din("w_ada", [depth, D, 3 * D])
    b_ada = din("b_ada", [depth, 3 * D])
    w_in = din("w_in", [depth, D, D_IN])
    qk_norm_w = din("qk_norm_w", [depth, 2, 64])
    diff_lambda = din("diff_lambda", [depth, 4, 64])
    subln_w = din("subln_w", [depth, 128])
    ret_decay = din("ret_decay", [depth, 8])
    ret_norm_w = din("ret_norm_w", [depth, 128])
    conv_w = din("conv_w", [depth, 5, 1536])
    gdn_a_log = din("gdn_a_log", [depth, 8])
    gdn_dt_bias = din("gdn_dt_bias", [depth, 8])
    gdn_norm_w = din("gdn_norm_w", [depth, 128])
    w_branch = din("w_branch", [depth, 3, 512, D])
    w_out = din("w_out", [depth, D, D])
    cst = {k: din("c_" + k, shp) for k, shp in CONST_SHAPES(T_S).items()}
    y_out = dout("y", [TT, D])
    nk_out = dout("nk", [NPR, depth, TP, 512])
    nv_out = dout("nv", [NPR, depth, TP, 512])
    nret_out = dout("nret", [NPR, depth, 2, 4, 64, 128])
    ngdn_out = dout("ngdn", [NPR, depth, 2, 4, 128, 128])
    xcur = scr("xcur", [TT, D], F32)
    aqT = scr("aqT", [4, 128, TT], BF16)
    akT = scr("akT", [4, 128, TT], BF16)
    av = scr("av", [TT, 512], BF16)
    zg = scr("zg", [3, TT, 512], BF16)
    bqT = scr("bqT", [4, 64, TT], BF16)
    bkT = scr("bkT", [4, 64, TT], BF16)
    bk = scr("bk", [TT, 256], BF16)
    bv = scr("bv", [TT, 512], BF16)
    cT = scr("cT", [1536, TT], F32)
    gqT = scr("gqT", [4, 128, TT], BF16)
    gkT = scr("gkT", [4, 128, TT], BF16)
    gk = scr("gk", [TT, 512], BF16)
    gv = scr("gv", [TT, 512], BF16)
    bgs = scr("bgs", [TT, 16], F32)
    mg = scr("mg", [TT, 3 * D], BF16)
    ofs = scr("ofs", [TT, 512], F32)
    ofb = scr("ofb", [TT, 512], F32)
    gT = scr("gT", [3, 512, TT], BF16)

    def tres(n):
        return [Res("%s%d" % (n, i)) for i in range(NT)]
    RX = {n: tres(n) for n in ("x", "aqT", "akT", "av", "zg0", "zg1", "zg2", "bqT", "bkT", "bk", "bv", "cT",
                               "gqT", "gkT", "gk", "gv", "bgs", "mg", "ofs", "ofb", "gT0", "gT1", "gT2")}
    R_out = Res("outs")

    uid = [0]

    def sb(es, name, shape, dt):
        uid[0] += 1
        nm = "%s_u%d" % (name, uid[0])
        return Tile(es.enter_context(nc.sbuf_tensor(nm, list(shape), dt)), nm)

    class Pool_:
        def __init__(self, es, name, shape, dt, n):
            self.t = [sb(es, "%s_%d" % (name, i), shape, dt) for i in range(n)]
            self.i = 0

        def next(self):
            t = self.t[self.i % len(self.t)]
            self.i += 1
            return t

    root = ExitStack()
    PSF = [Tile(nc.alloc_psum_tensor("psf%d" % i, [128, 512], F32), "psf%d" % i, True) for i in range(6)]
    PSB = [Tile(nc.alloc_psum_tensor("psb%d" % i, [128, 1024], BF16), "psb%d" % i, True) for i in range(2)]
    psc = [0, 0]

    psf_banks = [[0, 1, 2, 3]]

    def psf():
        psc[0] += 1
        bk = psf_banks[0]
        return PSF[bk[psc[0] % len(bk)]]

    def psb():
        psc[1] += 1
        return PSB[psc[1] % 2]

    ident_f = sb(root, "ident_f", [128, 128], F32)
    ident_b = sb(root, "ident_b", [128, 128], BF16)
    ones_f = sb(root, "ones_f", [128, 128], F32)
    ropec = sb(root, "ropec", [128, NTS, 64], F32)
    ropes = sb(root, "ropes", [128, NTS, 64], F32)
    gmask = sb(root, "gmask", [64, 2, 3, 64], F32)
    utri = sb(root, "utri", [64, 2, 64], F32)
    gmask2 = sb(root, "gmask2", [128, 2, 3, 128], F32)
    utri2 = sb(root, "utri2", [128, 2, 128], F32)
    bd1 = sb(root, "bd1", [128, 128], F32)
    selc = sb(root, "selc", [128, 2, 128], F32)
    rrel = sb(root, "rrel", [128, 2, 128], F32)
    rmask = sb(root, "rmask", [128, 2, 128], F32)
    rqexp = sb(root, "rqexp", [64, 2, 128], F32)
    rkexp = sb(root, "rkexp", [128, 2], F32)
    gate_bc = sb(root, "gate_bc", [128, 2, D], F32)
    wq_bc = sb(root, "wq_bc", [128, 2, 64], F32)
    subw_bc = sb(root, "subw_bc", [128, 128], F32)
    retw_bc = sb(root, "retw_bc", [128, 128], F32)
    gdnw_bc = sb(root, "gdnw_bc", [128, 128], F32)
    lamt = sb(root, "lamt", [128, 8], F32)
    dl_bc = sb(root, "dl_bc", [128, 4, 64], F32)
    lg_bc = sb(root, "lg_bc", [128, 8], F32)
    rdmat = sb(root, "rdmat", [128, 2, 4, 128], F32)
    rqdec = sb(root, "rqdec", [64, 2, 4, 128], F32)
    rkdec = sb(root, "rkdec", [128, 2, 4], F32)
    rcdec = sb(root, "rcdec", [128, 8], F32)
    negA_bc = sb(root, "negA_bc", [128, 8], F32)
    dtb_bc = sb(root, "dtb_bc", [128, 8], F32)
    convw = sb(root, "convw", [128, 12, 5], F32)
    tmp8 = sb(root, "tmp8", [128, 8], F32)

    S.dma("sp", ident_f[:], cst["ident"], W=[ident_f])
    S.op("dve", "tensor_copy", R=[ident_f], W=[ident_b], out=ident_b[:], in_=ident_f[:])
    S.op("dve", "memset", W=[ones_f], ap=ones_f[:], constant=1.0)
    S.dma("sp", ropec[:], cst["ropec"].rearrange("(n p) d -> p n d", p=128), W=[ropec])
    S.dma("sp", ropes[:], cst["ropes"].rearrange("(n p) d -> p n d", p=128), W=[ropes])
    S.dma("sp", gmask[:], cst["gmask"], W=[gmask])
    S.dma("sp", utri[:], cst["utri"], W=[utri])
    S.dma("sp", gmask2[:], cst["gmask2"], W=[gmask2])
    S.dma("sp", utri2[:], cst["utri2"], W=[utri2])
    S.dma("sp", bd1[:], cst["bd1"], W=[bd1])
    S.dma("sp", selc[:], cst["selc"], W=[selc])
    S.dma("sp", rrel[:], cst["rrel"], W=[rrel])
    S.dma("sp", rmask[:], cst["rmask"], W=[rmask])
    S.dma("sp", rqexp[:], cst["rqexp"], W=[rqexp])
    S.dma("sp", rkexp[:], cst["rkexp"], W=[rkexp])

    def rsqrt_inplace(t, ap, scale, eps):
        S.op("dve", "tensor_scalar", R=[t], W=[t], out=ap, in0=ap, scalar1=scale, scalar2=eps,
             op0=ALU.mult, op1=ALU.add)
        S.op("act", "activation", R=[t], W=[t], out=ap, in_=ap, func=AF.Sqrt)
        S.op("dve", "reciprocal", R=[t], W=[t], out=ap, in_=ap)

    def transposes_to(ps_t, src_t, blocks, rows=128):
        for (sap, w, off) in blocks:
            S.op("pe", "transpose", R=[src_t, ident_b], W=[ps_t], out=ps_t[0:w, off:off + rows], in_=sap,
                 identity=ident_b[0:rows, 0:rows])

    def layer(l):
        last = (l == depth - 1)
        xsrc = x_in if l == 0 else xcur
        xdst = y_out if last else xcur
        esH = ExitStack()
        hT = sb(esH, "hT", [128, 8, TT], BF16)
        esA = ExitStack()
        A_bc = sb(esA, "A_bc", [128, 2, D], F32)
        sh_bc = sb(esA, "sh_bc", [128, 2, D], F32)
        with ExitStack() as es:
            nw_bc = sb(es, "nw_bc", [128, D], F32)
            cs = sb(es, "cs", [128, 2, 8], F32)
            rep = sb(es, "rep", [128, 16, 128], F32)
            bada = sb(es, "bada", [1, 3 * D], F32)
            wa = Pool_(es, "wa", [128, 8, 512], F32, 2)
            S.dma("sp", nw_bc[:], norm_w[l].partition_broadcast(128), W=[nw_bc])
            S.dma("sp", cs[:], cond.rearrange("j (p c) -> p j c", c=8), W=[cs])
            S.dma("sp", bada[:], b_ada[l:l + 1, :], W=[bada])
            S.dma("sp", wq_bc[:],
                  qk_norm_w[l].rearrange("a d -> (a d)").partition_broadcast(128).rearrange("p (a d) -> p a d", a=2),
                  W=[wq_bc])
            S.dma("sp", subw_bc[:], subln_w[l].partition_broadcast(128), W=[subw_bc])
            S.dma("sp", retw_bc[:], ret_norm_w[l].partition_broadcast(128), W=[retw_bc])
            S.dma("sp", gdnw_bc[:], gdn_norm_w[l].partition_broadcast(128), W=[gdnw_bc])
            S.dma("sp", dl_bc[:], diff_lambda[l].rearrange("a d -> (a d)").partition_broadcast(128)
                  .rearrange("p (a d) -> p a d", a=4), W=[dl_bc])
            S.dma("sp", lg_bc[:], ret_decay[l].partition_broadcast(128), W=[lg_bc])
            S.dma("sp", negA_bc[:], gdn_a_log[l].partition_broadcast(128), W=[negA_bc])
            S.dma("sp", dtb_bc[:], gdn_dt_bias[l].partition_broadcast(128), W=[dtb_bc])
            for k in range(5):
                S.dma("sp", convw[:, :, k:k + 1], conv_w[l, k].rearrange("(c p o) -> p c o", p=128, o=1), W=[convw],
                      allow_slow_non_contiguous=True)
            S.op("dve", "tensor_scalar", R=[wq_bc], W=[wq_bc], out=wq_bc[:, 0, :], in0=wq_bc[:, 0, :],
                 scalar1=0.125, scalar2=None, op0=ALU.mult)
            S.op("dve", "tensor_scalar", R=[subw_bc], W=[subw_bc], out=subw_bc[:], in0=subw_bc[:],
                 scalar1=float(1.0 - lam_inits[l]), scalar2=None, op0=ALU.mult)
            S.op("dve", "tensor_tensor", R=[dl_bc], W=[dl_bc], out=dl_bc[:, 0, :], in0=dl_bc[:, 0, :],
                 in1=dl_bc[:, 1, :], op=ALU.mult)
            S.op("dve", "tensor_tensor", R=[dl_bc], W=[dl_bc], out=dl_bc[:, 2, :], in0=dl_bc[:, 2, :],
                 in1=dl_bc[:, 3, :], op=ALU.mult)
            S.op("dve", "tensor_reduce", R=[dl_bc], W=[lamt], out=lamt[:, 0:4], in_=dl_bc[:], axis=AX.X, op=ALU.add)
            S.op("act", "activation", R=[lamt], W=[lamt], out=lamt[:, 4:8], in_=lamt[:, 0:4], func=AF.Exp)
            S.op("dve", "tensor_tensor", R=[lamt], W=[lamt], out=lamt[:, 1:2], in0=lamt[:, 6:7], in1=lamt[:, 4:5],
                 op=ALU.subtract)
            S.op("dve", "tensor_scalar", R=[lamt], W=[lamt], out=lamt[:, 0:1], in0=lamt[:, 1:2],
                 scalar1=float(-lam_inits[l]), scalar2=None, op0=ALU.add)
            S.op("act", "activation", R=[lg_bc], W=[lg_bc], out=lg_bc[:], in_=lg_bc[:], func=AF.Exp, scale=-1.0)
            S.op("act", "activation", R=[lg_bc], W=[lg_bc], out=lg_bc[:], in_=lg_bc[:], func=AF.Ln, bias=1.0)
            S.op("dve", "tensor_scalar", R=[lg_bc], W=[lg_bc], out=lg_bc[:], in0=lg_bc[:], scalar1=-1.0,
                 scalar2=None, op0=ALU.mult)
            for d in range(2):
                for h in range(4):
                    u = d * 4 + h
                    S.op("act", "activation", R=[rrel, lg_bc], W=[rdmat], out=rdmat[:, d, h, :], in_=rrel[:, d, :],
                         func=AF.Exp, scale=lg_bc[:, u:u + 1])
                    S.op("dve", "tensor_tensor", R=[rdmat, rmask], W=[rdmat], out=rdmat[:, d, h, :],
                         in0=rdmat[:, d, h, :], in1=rmask[:, d, :], op=ALU.mult)
                    S.op("act", "activation", R=[rqexp, lg_bc], W=[rqdec], out=rqdec[:, d, h, :], in_=rqexp[:, d, :],
                         func=AF.Exp, scale=lg_bc[0:64, u:u + 1])
                    S.op("act", "activation", R=[rkexp, lg_bc], W=[rkdec], out=rkdec[:, d, h:h + 1],
                         in_=rkexp[:, d:d + 1], func=AF.Exp, scale=lg_bc[:, u:u + 1])
            S.op("act", "activation", R=[lg_bc], W=[rcdec], out=rcdec[:], in_=lg_bc[:], func=AF.Exp, scale=128.0)
            S.op("act", "activation", R=[negA_bc], W=[negA_bc], out=negA_bc[:], in_=negA_bc[:], func=AF.Exp)
            S.op("dve", "tensor_scalar", R=[negA_bc], W=[negA_bc], out=negA_bc[:], in0=negA_bc[:], scalar1=-1.0,
                 scalar2=None, op0=ALU.mult)
            S.op("act", "activation", R=[cs], W=[cs], out=cs[:], in_=cs[:], func=AF.Silu)
            S.op("dve", "tensor_copy", R=[cs], W=[rep], out=rep[:],
                 in_=cs[:].rearrange("p j c -> p (j c)").unsqueeze(2).to_broadcast([128, 16, 128]))
            for nb in range(6):
                w = wa.next()
                S.dma("sp", w[:], w_ada[l].rearrange("(p c) n -> p c n", c=8)[:, :, nb * 512:(nb + 1) * 512], W=[w])
                for j in range(2):
                    ps = psf()
                    for c in range(8):
                        S.op("pe", "matmul", R=[rep, w], W=[ps], out=ps[:], lhsT=rep[:, j * 8 + c, :], rhs=w[:, c, :],
                             start=(c == 0), stop=False)
                    S.op("pe", "matmul", R=[ones_f, bada], W=[ps], out=ps[:], lhsT=ones_f[0:1, :],
                         rhs=bada[0:1, nb * 512:(nb + 1) * 512], start=False, stop=True)
                    cols = slice((nb % 2) * 512, (nb % 2) * 512 + 512)
                    if nb < 2:
                        S.op("act", "copy", R=[ps], W=[sh_bc], out=sh_bc[:, j, cols], in_=ps[:])
                    elif nb < 4:
                        S.op("dve", "scalar_tensor_tensor", R=[ps, nw_bc], W=[A_bc], out=A_bc[:, j, cols], in0=ps[:],
                             scalar=1.0, in1=nw_bc[:, cols], op0=ALU.add, op1=ALU.mult)
                    else:
                        S.op("act", "copy", R=[ps], W=[gate_bc], out=gate_bc[:, j, cols], in_=ps[:])
            S.barrier()
        ckpt("A")
        with ExitStack() as es:
            xp = Pool_(es, "xB", [128, D], F32, 4)
            hp = Pool_(es, "hB", [128, D], F32, 4)
            hbp = Pool_(es, "hbB", [128, D], BF16, 4)
            junk = sb(es, "junkB", [128, D], F32)
            ssp = Pool_(es, "ssB", [128, 1], F32, 4)
            def tileB(tt):
                j = 0 if tt < NTS else 1
                xt = xp.next()
                ht = hp.next()
                hb = hbp.next()
                ss = ssp.next()
                S.dma("sp", xt[:], xsrc[tt * 128:(tt + 1) * 128, :], R=[RX["x"][tt]] if l > 0 else [], W=[xt])
                S.op("act", "activation", R=[xt], W=[junk, ss], out=junk[:], in_=xt[:], func=AF.Square, accum_out=ss[:])
                yield
                rsqrt_inplace(ss, ss[:], 1.0 / D, EPS)
                S.op("dve", "scalar_tensor_tensor", R=[xt, ss, A_bc], W=[ht], out=ht[:], in0=xt[:], scalar=ss[:, 0:1],
                     in1=A_bc[:, j, :], op0=ALU.mult, op1=ALU.mult)
                S.op("dve", "tensor_tensor", R=[ht, sh_bc], W=[hb], out=hb[:], in0=ht[:], in1=sh_bc[:, j, :], op=ALU.add)
                yield
                pb = psb()
                transposes_to(pb, hb, [(hb[:, c * 128:(c + 1) * 128], 128, c * 128) for c in range(8)])
                S.op("act", "copy", R=[pb], W=[hT], out=hT[:, :, tt * 128:(tt + 1) * 128],
                     in_=pb[:].rearrange("p (c t) -> p c t", c=8))
            pipeline([tileB(tt) for tt in range(NT)], 3)
            S.barrier()
        esA.close()
        ckpt("B")
        with ExitStack() as es:
            wf = Pool_(es, "wfC", [128, 4, 512], F32, 2)
            wbp = Pool_(es, "wbC", [128, 8, 512], BF16, 2)
            f1 = Pool_(es, "f1C", [128, 512], F32, 9)
            f2 = Pool_(es, "f2C", [128, 512], F32, 3)
            b1 = Pool_(es, "b1C", [128, 512], BF16, 4)
            st8 = Pool_(es, "st8C", [128, 16], F32, 4)
            stg = Pool_(es, "stgC", [128, 1024], BF16, 3)
            cfp = Pool_(es, "cfC", [128, 512], F32, 3)

            def load_wblock(c0, ncols):
                wb = wbp.next()
                for half in range(2):
                    w = wf.next()
                    S.dma("sp", w[:, :, 0:ncols],
                          w_in[l].rearrange("(c p) n -> p c n", p=128)[:, half * 4:half * 4 + 4, c0:c0 + ncols], W=[w])
                    S.op("dve" if half == 0 else "pool", "tensor_copy", R=[w], W=[wb],
                         out=wb[:, half * 4:half * 4 + 4, 0:ncols], in_=w[:, :, 0:ncols])
                return (wb, (c0, ncols))

            wblocks = [(0, 512), (512, 512), (1024, 512), (1536, 512), (2560, 512), (3072, 512), (5120, 512),
                       (2048, 512), (3584, 512), (4096, 512), (4608, 512), (5632, 16)] + \
                      [(5648 + i * 512, 512) for i in range(6)]
            wq = []

            def next_wb(c0, ncols):
                if not wq:
                    wq.append(load_wblock(*wblocks.pop(0)))
                wb = wq.pop(0)
                assert wb[1] == (c0, ncols), (wb[1], c0, ncols)
                if wblocks:
                    wq.append(load_wblock(*wblocks.pop(0)))
                return wb[0]

            def mm_tok(wb, tt, ncols):
                ps = psf()
                for c in range(8):
                    S.op("pe", "matmul", R=[hT, wb], W=[ps], out=ps[:, 0:ncols], lhsT=hT[:, c, tt * 128:(tt + 1) * 128],
                         rhs=wb[:, c, 0:ncols], start=(c == 0), stop=(c == 7))
                return ps

            def rope(src, tt, ngrp, dst_pool):
                t1 = dst_pool.next()
                t2 = dst_pool.next()
                s4 = src[:, 0:ngrp * 64].rearrange("p (g a h f) -> p g a h f", a=2, h=2, f=16)
                S.op("dve", "tensor_tensor", R=[src, ropec], W=[t1],
                     out=t1[:, 0:ngrp * 64].rearrange("p (g d) -> p g d", d=64),
                     in0=src[:, 0:ngrp * 64].rearrange("p (g d) -> p g d", d=64),
                     in1=ropec[:, tt:tt + 1, :].to_broadcast([128, ngrp, 64]), op=ALU.mult)
                t24 = t2[:, 0:ngrp * 64].rearrange("p (g a h f) -> p g a h f", a=2, h=2, f=16)
                sn4 = ropes[:, tt, :].rearrange("p (a h f) -> p a h f", a=2, h=2)
                for hh in range(2):
                    S.op("pool", "tensor_tensor", R=[src, ropes], W=[t2], out=t24[:, :, :, hh, :],
                         in0=s4[:, :, :, 1 - hh, :],
                         in1=sn4[:, :, hh, :].unsqueeze(1).to_broadcast([128, ngrp, 2, 16]), op=ALU.mult)
                S.op("dve", "tensor_tensor", R=[t1, t2], W=[t1], out=t1[:, 0:ngrp * 64], in0=t1[:, 0:ngrp * 64],
                     in1=t2[:, 0:ngrp * 64], op=ALU.add)
                return t1

            def tileQK(blk, wb, tt):
                is_s = tt < NTS
                ps = mm_tok(wb, tt, 512)
                sq = f2.next()
                s8 = st8.next()
                qn = f1.next()
                S.op("act", "activation", R=[ps], W=[sq], out=sq[:], in_=ps[:], func=AF.Square)
                S.op("dve", "tensor_reduce", R=[sq], W=[s8], out=s8[:, 0:8],
                     in_=sq[:].rearrange("p (g d) -> p g d", d=64), axis=AX.X, op=ALU.add)
                yield
                rsqrt_inplace(s8, s8[:, 0:8], 1.0 / 64, EPS)
                S.op("dve", "tensor_tensor", R=[ps, s8], W=[qn], out=qn[:].rearrange("p (g d) -> p g d", d=64),
                     in0=ps[:].rearrange("p (g d) -> p g d", d=64),
                     in1=s8[:, 0:8].unsqueeze(2).to_broadcast([128, 8, 64]), op=ALU.mult)
                S.op("pool", "tensor_tensor", R=[qn, wq_bc], W=[qn], out=qn[:].rearrange("p (g d) -> p g d", d=64),
                     in0=qn[:].rearrange("p (g d) -> p g d", d=64),
                     in1=wq_bc[:, blk:blk + 1, :].to_broadcast([128, 8, 64]), op=ALU.mult)
                if blk == 1 and not is_s:
                    pi = (tt - NTS) // 2
                    r0 = ((tt - NTS) % 2) * 128
                    S.dma("pool", nk_out[pi, l, r0:r0 + 128, :], qn[:], R=[qn])
                yield
                if is_s:
                    qn = rope(qn, tt, 8, f1)
                qb = b1.next()
                S.op("act", "copy", R=[qn], W=[qb], out=qb[:], in_=qn[:])
                yield
                pb = psb()
                transposes_to(pb, qb, [(qb[:, h * 128:(h + 1) * 128], 128, h * 128) for h in range(4)])
                sg = stg.next()
                S.op("dve", "tensor_copy", R=[pb], W=[sg], out=sg[:, 0:512], in_=pb[:, 0:512])
                dst = aqT if blk == 0 else akT
                S.dma("pool", dst[:, :, tt * 128:(tt + 1) * 128].rearrange("h p t -> p h t"),
                      sg[:, 0:512].rearrange("p (h t) -> p h t", h=4), R=[sg],
                      W=[RX["aqT" if blk == 0 else "akT"][tt]])

            for blk in range(2):
                wb = next_wb(blk * 512, 512)
                pipeline([tileQK(blk, wb, tt) for tt in range(NT)], 4)
            ckpt("C0")
            for (c0, kind) in ((1024, "av"), (1536, "z0"), (2560, "bv"), (3072, "z1"), (5120, "z2")):
                wb = next_wb(c0, 512)
                for tt in range(NT):
                    is_s = tt < NTS
                    ps = mm_tok(wb, tt, 512)
                    ob = b1.next()
                    if kind in ("av", "bv"):
                        S.op("act", "copy", R=[ps], W=[ob], out=ob[:], in_=ps[:])
                        dst, rn = (av, "av") if kind == "av" else (bv, "bv")
                        S.dma("pool", dst[tt * 128:(tt + 1) * 128, :], ob[:], R=[ob], W=[RX[rn][tt]])
                        if kind == "av" and not is_s:
                            of_ = f1.next()
                            S.op("dve", "tensor_copy", R=[ps], W=[of_], out=of_[:], in_=ps[:])
                            pi = (tt - NTS) // 2
                            r0 = ((tt - NTS) % 2) * 128
                            S.dma("pool", nv_out[pi, l, r0:r0 + 128, :], of_[:], R=[of_])
                    else:
                        zi = int(kind[1])
                        S.op("act", "activation", R=[ps], W=[ob], out=ob[:], in_=ps[:], func=AF.Silu)
                        S.dma("pool", zg[zi, tt * 128:(tt + 1) * 128, :], ob[:], R=[ob], W=[RX["zg%d" % zi][tt]])
                ckpt("C1_" + kind)
            ckpt("C1")
            wb = next_wb(2048, 512)

            def tileBQK(wb, tt):
                is_s = tt < NTS
                ps = mm_tok(wb, tt, 512)
                qk = f1.next()
                S.op("act", "copy", R=[ps], W=[qk], out=qk[:, 0:256], in_=ps[:, 0:256])
                S.op("act", "mul", R=[ps], W=[qk], out=qk[:, 256:512], in_=ps[:, 256:512], mul=0.125)
                yield
                if is_s:
                    qk = rope(qk, tt, 8, f1)
                qb = b1.next()
                S.op("act", "copy", R=[qk], W=[qb], out=qb[:], in_=qk[:])
                S.dma("pool", bk[tt * 128:(tt + 1) * 128, :], qb[:, 256:512], R=[qb], W=[RX["bk"][tt]])
                yield
                pb = psb()
                transposes_to(pb, qb, [(qb[:, g * 64:(g + 1) * 64], 64, g * 128) for g in range(8)])
                sg = stg.next()
                S.op("dve", "tensor_copy", R=[pb], W=[sg], out=sg[0:64, :], in_=pb[0:64, :])
                S.dma("pool", bqT[:, :, tt * 128:(tt + 1) * 128].rearrange("h p t -> p h t"),
                      sg[0:64, 0:512].rearrange("p (h t) -> p h t", h=4), R=[sg], W=[RX["bqT"][tt]])
                S.dma("pool", bkT[:, :, tt * 128:(tt + 1) * 128].rearrange("h p t -> p h t"),
                      sg[0:64, 512:1024].rearrange("p (h t) -> p h t", h=4), R=[sg], W=[RX["bkT"][tt]])
            pipeline([tileBQK(wb, tt) for tt in range(NT)], 3)
            ckpt("C2")
            for blk in range(3):
                wb = next_wb(3584 + blk * 512, 512)
                for cc in range(4):
                    for (t0, T, _) in seqs:
                        for g0 in range(0, T, 512):
                            n = min(512, T - g0)
                            ps = psf()
                            for c in range(8):
                                S.op("pe", "matmul", R=[hT, wb], W=[ps], out=ps[:, 0:n],
                                     lhsT=wb[:, c, cc * 128:(cc + 1) * 128], rhs=hT[:, c, t0 + g0:t0 + g0 + n],
                                     start=(c == 0), stop=(c == 7))
                            cf = cfp.next()
                            S.op("act", "copy", R=[ps], W=[cf], out=cf[:, 0:n], in_=ps[:, 0:n])
                            ch0 = (blk * 4 + cc) * 128
                            tiles = range((t0 + g0) // 128, (t0 + g0 + n) // 128)
                            S.dma("pool", cT[ch0:ch0 + 128, t0 + g0:t0 + g0 + n], cf[:, 0:n], R=[cf],
                                  W=[RX["cT"][i] for i in tiles])
            ckpt("C3")
            wb = next_wb(5632, 16)
            for tt in range(NT):
                ps = mm_tok(wb, tt, 16)
                o16 = st8.next()
                S.op("act", "activation", R=[ps], W=[o16], out=o16[:, 0:8], in_=ps[:, 0:8], func=AF.Sigmoid)
                S.op("dve", "tensor_tensor", R=[ps, dtb_bc], W=[o16], out=o16[:, 8:16], in0=ps[:, 8:16], in1=dtb_bc[:],
                     op=ALU.add)
                S.op("act", "activation", R=[o16], W=[o16], out=o16[:, 8:16], in_=o16[:, 8:16], func=AF.Exp)
                S.op("act", "activation", R=[o16], W=[o16], out=o16[:, 8:16], in_=o16[:, 8:16], func=AF.Ln, bias=1.0)
                S.op("dve", "tensor_tensor", R=[o16, negA_bc], W=[o16], out=o16[:, 8:16], in0=o16[:, 8:16],
                     in1=negA_bc[:], op=ALU.mult)
                S.dma("pool", bgs[tt * 128:(tt + 1) * 128, :], o16[:], R=[o16], W=[RX["bgs"][tt]])
            ckpt("C4")
            for blk in range(6):
                wb = next_wb(5648 + blk * 512, 512)
                for tt in range(NT):
                    ps = mm_tok(wb, tt, 512)
                    ob = b1.next()
                    S.op("act", "activation", R=[ps], W=[ob], out=ob[:], in_=ps[:], func=AF.Sigmoid)
                    S.dma("pool", mg[tt * 128:(tt + 1) * 128, blk * 512:(blk + 1) * 512], ob[:], R=[ob],
                          W=[RX["mg"][tt]] if blk == 5 else [])
            S.barrier()
        esH.close()
        ckpt("C")
        for si, (t0, T, is_s) in enumerate(seqs):
            if is_s and not os.environ.get("KNOOVL"):
                sample_mixers_overlapped(l, si, t0, T, is_s)
                ckpt("gpre%d" % si)
            elif os.environ.get("KNOOVL"):
                attention(l, si, t0, T, is_s)
                ckpt("att%d" % si)
                retention(l, si, t0, T, is_s)
                ckpt("ret%d" % si)
                gdn_pre(l, si, t0, T, is_s)
                ckpt("gpre%d" % si)
            gdn_scan(l, si, t0, T, is_s)
            ckpt("gscan%d" % si)
        phaseE(l, xsrc, xdst)

    def norm_gate_store(es_tiles, o_t, o_ap, w_bc, zi, mi, tt, rows, col_lo=None):
        sq, s4, zt, gb, sg = es_tiles
        tok0 = tt * 128 + (col_lo or 0)
        S.op("act", "activation", R=[o_t], W=[sq], out=sq[0:rows, :], in_=o_ap, func=AF.Square)
        S.op("dve", "tensor_reduce", R=[sq], W=[s4], out=s4[0:rows, 0:4],
             in_=sq[0:rows, :].rearrange("p (g d) -> p g d", d=128), axis=AX.X, op=ALU.add)
        rsqrt_inplace(s4, s4[0:rows, 0:4], 1.0 / 128, EPS)
        S.dma("sp", zt[0:rows, :], zg[zi, tok0:tok0 + rows, :], R=[RX["zg%d" % zi][tt]], W=[zt])
        S.op("dve", "tensor_tensor", R=[o_t, s4], W=[o_t], out=o_ap.rearrange("p (g d) -> p g d", d=128),
             in0=o_ap.rearrange("p (g d) -> p g d", d=128),
             in1=s4[0:rows, 0:4].unsqueeze(2).to_broadcast([rows, 4, 128]), op=ALU.mult)
        S.op("pool", "tensor_tensor", R=[o_t, w_bc], W=[o_t], out=o_ap.rearrange("p (g d) -> p g d", d=128),
             in0=o_ap.rearrange("p (g d) -> p g d", d=128),
             in1=w_bc[0:rows, :].unsqueeze(1).to_broadcast([rows, 4, 128]), op=ALU.mult)
        S.op("dve", "tensor_tensor", R=[o_t, zt], W=[gb], out=gb[0:rows, :], in0=o_ap, in1=zt[0:rows, :], op=ALU.mult)
        pb = psb()
        for g in range(4):
            S.op("pe", "transpose", R=[gb, ident_b], W=[pb], out=pb[:, g * 128:g * 128 + rows],
                 in_=gb[0:rows, g * 128:(g + 1) * 128], identity=ident_b[0:rows, 0:rows])
        pv = pb[:, 0:512].rearrange("p (g t) -> p g t", g=4)[:, :, 0:rows]
        S.op("act", "copy", R=[pb], W=[sg], out=sg[:, :, 0:rows], in_=pv)
        S.dma("pool", gT[mi, :, tok0:tok0 + rows].rearrange("(g p) t -> p g t", p=128), sg[:, :, 0:rows], R=[sg],
              W=[RX["gT%d" % mi][tt]])

    def ng_tiles(es, pfx):
        return (sb(es, pfx + "sq", [128, 512], F32), sb(es, pfx + "s4", [128, 4], F32),
                sb(es, pfx + "zt", [128, 512], BF16), sb(es, pfx + "gb", [128, 512], BF16),
                sb(es, pfx + "sg", [128, 4, 128], BF16))

    def attention_gen(l, si, t0, T, is_s, es, acc_sets, sc_banks=None):
        Sk = T + (PAST if is_s else 0)
        nst = Sk // 128
        ntl = T // 128
        QB = min(512, T)
        kT = sb(es, "at_kT", [128, 4, Sk], BF16)
        V1 = sb(es, "at_V1", [128, nst, 4, 130], BF16)
        qTp = Pool_(es, "at_qT", [128, 4, QB], BF16, 2)
        ex = Pool_(es, "at_ex", [128, QB], BF16, 3)
        osb = [sb(es, "at_os%d" % qs, [128, 512], F32) for qs in range(QB // 128)]
        rc = Pool_(es, "at_rc", [128, 2], F32, 4)
        ngt = ng_tiles(es, "at_")
        grp = [0]
        scn = [0]
        S.dma("sp", kT[:, :, 0:T], akT[:, :, t0:t0 + T].rearrange("h p t -> p h t"),
              R=[RX["akT"][i] for i in range(t0 // 128, (t0 + T) // 128)], W=[kT])
        S.op("pool", "memset", W=[V1], ap=V1[:, :, :, 128:130], constant=1.0)
        for i in range(ntl):
            S.dma("sp", V1[:, i, :, 0:128], av[t0 + i * 128:t0 + (i + 1) * 128, :].rearrange("p (h e) -> p h e", h=4),
                  R=[RX["av"][t0 // 128 + i]], W=[V1])
        if is_s:
            with ExitStack() as es2:
                ck = sb(es2, "at_ck", [128, 2, 512], F32)
                cv = sb(es2, "at_cv", [128, 2, 512], F32)
                ckb = sb(es2, "at_ckb", [128, 2, 512], BF16)
                S.dma("sp", ck[:], cache_k[l].rearrange("(n p) f -> p n f", p=128), W=[ck])
                S.dma("sp", cv[:], cache_v[l].rearrange("(n p) f -> p n f", p=128), W=[cv])
                S.op("dve", "tensor_copy", R=[ck], W=[ckb], out=ckb[:], in_=ck[:])
                for n in range(2):
                    S.op("pool", "tensor_copy", R=[cv], W=[V1], out=V1[:, ntl + n, :, 0:128],
                         in_=cv[:, n, :].rearrange("p (h e) -> p h e", h=4))
                    pb = psb()
                    transposes_to(pb, ckb, [(ckb[:, n, h * 128:(h + 1) * 128], 128, h * 128) for h in range(4)])
                    S.op("act", "copy", R=[pb], W=[kT], out=kT[:, :, T + n * 128:T + (n + 1) * 128],
                         in_=pb[:, 0:512].rearrange("p (h t) -> p h t", h=4))
                S.barrier()
        for qb0 in range(0, T, QB):
            qT = qTp.next()
            S.dma("sp", qT[:], aqT[:, :, t0 + qb0:t0 + qb0 + QB].rearrange("h p t -> p h t"),
                  R=[RX["aqT"][i] for i in range((t0 + qb0) // 128, (t0 + qb0 + QB) // 128)], W=[qT])
            nqs = QB // 128
            for h in range(4):
                for m in range(2):
                    grp[0] += 1
                    acc = acc_sets[grp[0] % len(acc_sets)]

                    def pv(st, e, acc=acc, h=h):
                        for qs in range(nqs):
                            a = acc[qs // 2]
                            S.op("pe", "matmul", R=[e, V1], W=[a], out=a[:, (qs % 2) * 256:(qs % 2) * 256 + 129],
                                 lhsT=e[:, qs * 128:(qs + 1) * 128], rhs=V1[:, st, h, 0:129],
                                 start=(st == 0 and qs % 2 == 0), stop=(st == nst - 1), skip_group_check=True)
                    pend = None
                    for st in range(nst):
                        scn[0] += 1
                        _sb = sc_banks or [PSF[0], PSF[1]]
                        ps = _sb[scn[0] % len(_sb)]
                        S.op("pe", "matmul", R=[kT, qT], W=[ps], out=ps[:, 0:QB],
                             lhsT=kT[m * 64:(m + 1) * 64, h, st * 128:(st + 1) * 128],
                             rhs=qT[m * 64:(m + 1) * 64, h, :], start=True, stop=True)
                        e = ex.next()
                        S.op("act", "activation", R=[ps], W=[e], out=e[:, 0:QB], in_=ps[:, 0:QB], func=AF.Exp)
                        if pend is not None:
                            pv(*pend)
                        pend = (st, e)
                        yield
                    pv(*pend)
                    for qs in range(nqs):
                        a = acc[qs // 2]
                        c0 = (qs % 2) * 256
                        r = rc.next()
                        S.op("dve", "reciprocal", R=[a], W=[r], out=r[:, 0:1], in_=a[:, c0 + 128:c0 + 129])
                        if m == 0:
                            S.op("dve", "tensor_scalar", R=[a, r], W=[osb[qs]], out=osb[qs][:, h * 128:(h + 1) * 128],
                                 in0=a[:, c0:c0 + 128], scalar1=r[:, 0:1], scalar2=None, op0=ALU.mult)
                        else:
                            S.op("dve", "tensor_tensor", R=[r, lamt], W=[r], out=r[:, 1:2], in0=r[:, 0:1],
                                 in1=lamt[:, 0:1], op=ALU.mult)
                            S.op("dve", "scalar_tensor_tensor", R=[a, r, osb[qs]], W=[osb[qs]],
                                 out=osb[qs][:, h * 128:(h + 1) * 128], in0=a[:, c0:c0 + 128], scalar=r[:, 1:2],
                                 in1=osb[qs][:, h * 128:(h + 1) * 128], op0=ALU.mult, op1=ALU.add)
            for qs in range(nqs):
                tt = (t0 + qb0) // 128 + qs
                norm_gate_store(ngt, osb[qs], osb[qs][:], subw_bc, 0, 0, tt, 128)

    def attention(l, si, t0, T, is_s):
        with ExitStack() as es:
            for _ in attention_gen(l, si, t0, T, is_s, es, [[PSF[2], PSF[3]], [PSF[4], PSF[5]]]):
                pass
            S.barrier()

    def retention(l, si, t0, T, is_s):
        with ExitStack() as es:
            run_rr([ret_chain(l, si, t0, T, is_s, d, es) for d in range(2)])
            S.barrier()
        combine(t0, T, retw_bc, 1, 1)

    def rr_gen(gens):
        gens = list(gens)
        while gens:
            for g in list(gens):
                try:
                    next(g)
                    yield
                except StopIteration:
                    gens.remove(g)

    def side_gen(l, si, t0, T, is_s):
        with ExitStack() as es2:
            yield from gdn_pre_gen(l, si, t0, T, is_s, es2, 256, 1)
            S.barrier()
        with ExitStack() as es3:
            yield from rr_gen([ret_chain(l, si, t0, T, is_s, d, es3) for d in range(2)])
            S.barrier()
        yield from combine_gen(t0, T, retw_bc, 1, 1)
        for pi in range(1, len(seqs)):
            (tp0, Tp, _) = seqs[pi]
            with ExitStack() as esp:
                yield from attention_gen(l, pi, tp0, Tp, False, esp, [[PSF[5]]], sc_banks=[PSF[4]])
                S.barrier()
            with ExitStack() as esp:
                yield from gdn_pre_gen(l, pi, tp0, Tp, False, esp, 256, 1)
                S.barrier()
            with ExitStack() as esp:
                yield from rr_gen([ret_chain(l, pi, tp0, Tp, False, d, esp) for d in range(2)])
                S.barrier()
            yield from combine_gen(tp0, Tp, retw_bc, 1, 1)

    def sample_mixers_overlapped(l, si, t0, T, is_s):
        with ExitStack() as es:
            att = attention_gen(l, si, t0, T, is_s, es, [[PSF[2], PSF[3]]])
            next(att)
            old = psf_banks[0]
            psf_banks[0] = [4, 5]
            n_att = (T // min(512, T)) * 8 * ((T + PAST) // 128)
            n_side = (T // 256) * 36 + (T // 128) * 5 + (len(seqs) - 1) * 70
            run_weighted(att, side_gen(l, si, t0, T, is_s), max(1, int(0.9 * n_att / n_side)))
            psf_banks[0] = old
            S.barrier()

    def ret_chain(l, si, t0, T, is_s, d, es):
        ntl = T // 128
        pf = "rt%d_" % d
        Sf = sb(es, pf + "S", [64, 4, 128], F32)
        Sb_ = sb(es, pf + "Sb", [64, 4, 128], BF16)
        qTp = Pool_(es, pf + "qT", [64, 4, 128], BF16, 2)
        kTp = Pool_(es, pf + "kT", [64, 4, 128], BF16, 2)
        ktp = Pool_(es, pf + "k", [128, 256], BF16, 2)
        vp = Pool_(es, pf + "v", [128, 512], BF16, 2)
        itp = Pool_(es, pf + "it", [128, 512], BF16, 2)
        qdp = Pool_(es, pf + "qd", [64, 4, 128], BF16, 2)
        kdp = Pool_(es, pf + "kd", [128, 256], BF16, 2)
        op_ = Pool_(es, pf + "o", [128, 512], F32, 2)
        odst, orn = (ofs, "ofs") if d == 0 else (ofb, "ofb")
        if is_s:
            S.dma("sp", Sf[:], st_ret[l, d].rearrange("h k e -> k h e"), W=[Sf])
        else:
            S.op("dve", "memset", W=[Sf], ap=Sf[:], constant=0.0)
        S.op("act", "copy", R=[Sf], W=[Sb_], out=Sb_[:], in_=Sf[:])
        order = range(ntl) if d == 0 else range(ntl - 1, -1, -1)
        for i in order:
            tt = t0 // 128 + i
            c0 = tt * 128
            qT = qTp.next(); kT = kTp.next(); kt = ktp.next(); v = vp.next()
            S.dma("sp", qT[:], bqT[:, :, c0:c0 + 128].rearrange("h p t -> p h t"), R=[RX["bqT"][tt]], W=[qT])
            S.dma("sp", kT[:], bkT[:, :, c0:c0 + 128].rearrange("h p t -> p h t"), R=[RX["bkT"][tt]], W=[kT])
            S.dma("sp", kt[:], bk[c0:c0 + 128, :], R=[RX["bk"][tt]], W=[kt])
            S.dma("sp", v[:], bv[c0:c0 + 128, :], R=[RX["bv"][tt]], W=[v])
            ps = psf()
            for h in range(4):
                S.op("pe", "matmul", R=[kT, qT], W=[ps], out=ps[:, h * 128:(h + 1) * 128], lhsT=kT[:, h, :],
                     rhs=qT[:, h, :], start=True, stop=True)
            it = itp.next()
            S.op("dve", "tensor_tensor", R=[ps, rdmat], W=[it], out=it[:], in0=ps[:],
                 in1=rdmat[:, d, :, :].rearrange("p h i -> p (h i)"), op=ALU.mult)
            qd = qdp.next()
            S.op("pool", "tensor_tensor", R=[qT, rqdec], W=[qd], out=qd[:], in0=qT[:], in1=rqdec[:, d, :, :],
                 op=ALU.mult)
            kd = kdp.next()
            S.op("pool", "tensor_tensor", R=[kt, rkdec], W=[kd], out=kd[:].rearrange("p (h e) -> p h e", h=4),
                 in0=kt[:].rearrange("p (h e) -> p h e", h=4),
                 in1=rkdec[:, d, :].unsqueeze(2).to_broadcast([128, 4, 64]), op=ALU.mult)
            yield
            po = psf()
            for h in range(4):
                S.op("pe", "matmul", R=[it, v], W=[po], out=po[:, h * 128:(h + 1) * 128],
                     lhsT=it[:, h * 128:(h + 1) * 128], rhs=v[:, h * 128:(h + 1) * 128], start=True, stop=False)
                S.op("pe", "matmul", R=[qd, Sb_], W=[po], out=po[:, h * 128:(h + 1) * 128], lhsT=qd[:, h, :],
                     rhs=Sb_[:, h, :], start=False, stop=True)
            pS = psf()
            for h in range(4):
                S.op("pe", "matmul", R=[kd, v], W=[pS], out=pS[0:64, h * 128:(h + 1) * 128],
                     lhsT=kd[:, h * 64:(h + 1) * 64], rhs=v[:, h * 128:(h + 1) * 128], start=True, stop=True)
            S.op("dve", "tensor_tensor", R=[Sf, rcdec], W=[Sf], out=Sf[:], in0=Sf[:],
                 in1=rcdec[0:64, d * 4:d * 4 + 4].unsqueeze(2).to_broadcast([64, 4, 128]), op=ALU.mult)
            S.op("dve", "tensor_tensor", R=[Sf, pS], W=[Sf], out=Sf[:].rearrange("p h e -> p (h e)"),
                 in0=Sf[:].rearrange("p h e -> p (h e)"), in1=pS[0:64, :], op=ALU.add)
            S.op("act", "copy", R=[Sf], W=[Sb_], out=Sb_[:], in_=Sf[:])
            o = op_.next()
            S.op("act", "copy", R=[po], W=[o], out=o[:], in_=po[:])
            S.dma("pool", odst[c0:c0 + 128, :], o[:], R=[o], W=[RX[orn][tt]])
            yield
        if not is_s:
            S.dma("pool", nret_out[si - 1, l, d].rearrange("h k e -> k h e"), Sf[:], R=[Sf])

    def pipeline_gen(gens, depth):
        gens = list(gens)
        active = []
        while gens or active:
            while gens and len(active) < depth:
                active.append(gens.pop(0))
            for g in list(active):
                try:
                    next(g)
                except StopIteration:
                    active.remove(g)
            yield

    def gdn_pre_gen(l, si, t0, T, is_s, es, G, nbuf):
        xin = Pool_(es, "gp_x", [128, 12, G + 4], F32, nbuf)
        acc = Pool_(es, "gp_a", [128, 12, G], F32, nbuf)
        sqp = Pool_(es, "gp_sq", [128, G], F32, 5)
        rsp = Pool_(es, "gp_rs", [128, G], F32, 5)
        nb = Pool_(es, "gp_nb", [128, 12, G], BF16, nbuf)
        sg = Pool_(es, "gp_sg", [128, 1024], BF16, 2)
        chunk_res = {}
        for g0 in range(0, T, G):
            x = xin.next()
            a = acc.next()
            lo = 2 if g0 == 0 else 0
            hi = 2 if g0 + G == T else 0
            if lo:
                S.op("pool", "memset", W=[x], ap=x[:, :, 0:2], constant=0.0)
            if hi:
                S.op("pool", "memset", W=[x], ap=x[:, :, G + 2:G + 4], constant=0.0)
            tl = [i for i in range((t0 + g0) // 128 - (0 if lo else 1), (t0 + g0 + G) // 128 + (0 if hi else 1))]
            S.dma("sp", x[:, :, lo:G + 4 - hi],
                  cT[:, t0 + g0 - 2 + lo:t0 + g0 + G + 2 - hi].rearrange("(c p) t -> p c t", p=128),
                  R=[RX["cT"][i] for i in tl], W=[x])
            yield
            yield
            ar = chunk_res.setdefault(("a", id(a)), [Res("gpa%d" % c) for c in range(12)])

            def convc(c):
                S.op("dve", "tensor_scalar", R=[x, convw], W=[ar[c]], out=a[:, c, :], in0=x[:, c, 0:G],
                     scalar1=convw[:, c, 0:1], scalar2=None, op0=ALU.mult)
                for k in range(1, 5):
                    S.op("dve", "scalar_tensor_tensor", R=[x, convw, ar[c]], W=[ar[c]], out=a[:, c, :],
                         in0=x[:, c, k:k + G], scalar=convw[:, c, k:k + 1], in1=a[:, c, :], op0=ALU.mult, op1=ALU.add)
                yield
                yield
                S.op("act", "activation", R=[ar[c]], W=[ar[c]], out=a[:, c, :], in_=a[:, c, :], func=AF.Silu)
            yield from pipeline_gen([convc(c) for c in range(12)], 3)
            n = nb.next()
            nr = chunk_res.setdefault(("n", id(n)), [Res("gpn%d" % c) for c in range(12)])

            def l2c(c):
                sq = sqp.next()
                S.op("act", "activation", R=[ar[c]], W=[sq], out=sq[:], in_=a[:, c, :], func=AF.Square)
                yield
                yield
                ps = psf()
                S.op("pe", "matmul", R=[ones_f, sq], W=[ps], out=ps[:, 0:G], lhsT=ones_f[:], rhs=sq[:], start=True,
                     stop=True)
                rs = rsp.next()
                S.op("dve", "tensor_scalar", R=[ps], W=[rs], out=rs[:], in0=ps[:, 0:G], scalar1=EPS, scalar2=None,
                     op0=ALU.add)
                yield
                yield
                S.op("act", "activation", R=[rs], W=[rs], out=rs[:], in_=rs[:], func=AF.Sqrt)
                yield
                yield
                S.op("dve", "reciprocal", R=[rs], W=[rs], out=rs[:], in_=rs[:])
                if c < 4:
                    S.op("dve", "scalar_tensor_tensor", R=[ar[c], rs], W=[nr[c]], out=n[:, c, :], in0=a[:, c, :],
                         scalar=float(128 ** -0.5), in1=rs[:], op0=ALU.mult, op1=ALU.mult)
                else:
                    S.op("dve", "tensor_tensor", R=[ar[c], rs], W=[nr[c]], out=n[:, c, :], in0=a[:, c, :], in1=rs[:],
                         op=ALU.mult)
            yield from pipeline_gen([l2c(c) for c in range(8)], 4)
            S.op("pool", "tensor_copy", R=ar[8:12], W=nr[8:12], out=n[:, 8:12, :], in_=a[:, 8:12, :])
            tiles = list(range((t0 + g0) // 128, (t0 + g0 + G) // 128))
            S.dma("pool", gqT[:, :, t0 + g0:t0 + g0 + G].rearrange("h p t -> p h t"), n[:, 0:4, :], R=nr[0:4],
                  W=[RX["gqT"][i] for i in tiles])
            S.dma("pool", gkT[:, :, t0 + g0:t0 + g0 + G].rearrange("h p t -> p h t"), n[:, 4:8, :], R=nr[4:8],
                  W=[RX["gkT"][i] for i in tiles])
            yield
            yield
            for ti in range(G // 128):
                tt = (t0 + g0) // 128 + ti
                pb = psb()
                transposes_to(pb, nr[4:12], [(n[:, 4 + c, ti * 128:(ti + 1) * 128], 128, c * 128) for c in range(8)])
                s = sg.next()
                S.op("act", "copy", R=[pb], W=[s], out=s[:], in_=pb[:])
                S.dma("pool", gk[tt * 128:(tt + 1) * 128, :], s[:, 0:512], R=[s], W=[RX["gk"][tt]])
                S.dma("pool", gv[tt * 128:(tt + 1) * 128, :], s[:, 512:1024], R=[s], W=[RX["gv"][tt]])
                yield

    def gdn_pre(l, si, t0, T, is_s):
        with ExitStack() as es:
            for _ in gdn_pre_gen(l, si, t0, T, is_s, es, min(512, T), 2):
                pass
            S.barrier()

    def pipeline(gens, depth):
        gens = list(gens)
        active = []
        while gens or active:
            if gens and len(active) < depth:
                active.append(gens.pop(0))
            for g in list(active):
                try:
                    next(g)
                except StopIteration:
                    active.remove(g)

    def run_weighted(main, side, ratio):
        main_alive = side_alive = True
        while main_alive or side_alive:
            if main_alive:
                for _ in range(ratio):
                    try:
                        next(main)
                    except StopIteration:
                        main_alive = False
                        break
            if side_alive:
                try:
                    next(side)
                except StopIteration:
                    side_alive = False

    def run_rr(gens):
        gens = list(gens)
        while gens:
            for g in list(gens):
                try:
                    next(g)
                except StopIteration:
                    gens.remove(g)

    def combine(t0, T, w_bc, zi, mi):
        for _ in combine_gen(t0, T, w_bc, zi, mi):
            pass

    def combine_gen(t0, T, w_bc, zi, mi):
        with ExitStack() as es:
            fa = Pool_(es, "cb_a", [128, 512], F32, 2)
            fb = Pool_(es, "cb_b", [128, 512], F32, 2)
            ngts = [ng_tiles(es, "cb%d_" % i) for i in range(2)]
            for i in range(T // 128):
                tt = t0 // 128 + i
                c0 = tt * 128
                a = fa.next(); b = fb.next()
                S.dma("sp", a[:], ofs[c0:c0 + 128, :], R=[RX["ofs"][tt]], W=[a])
                S.dma("sp", b[:], ofb[c0:c0 + 128, :], R=[RX["ofb"][tt]], W=[b])
                S.op("pool", "tensor_tensor", R=[a, b], W=[a], out=a[:], in0=a[:], in1=b[:], op=ALU.add)
                norm_gate_store(ngts[i % 2], a, a[:], w_bc, zi, mi, tt, 128)
                yield
            S.barrier()

    def gdn_scan(l, si, t0, T, is_s):
        with ExitStack() as es:
            run_rr([gdn_chain(l, si, t0, T, is_s, d, es) for d in range(2)])
            S.barrier()
        combine(t0, T, gdnw_bc, 2, 2)

    def gdn_chain(l, si, t0, T, is_s, d, es):
        ntl = T // 128
        pf = "gd%d_" % d
        Sf = sb(es, pf + "S", [128, 4, 128], F32)
        Sb_ = sb(es, pf + "Sb", [128, 4, 128], BF16)
        kTp = Pool_(es, pf + "kT", [128, 4, 128], BF16, 2)
        qTp = Pool_(es, pf + "qT", [128, 4, 128], BF16, 2)
        ktp = Pool_(es, pf + "k", [128, 512], BF16, 2)
        vtp = Pool_(es, pf + "v", [128, 512], BF16, 2)
        bgp = Pool_(es, pf + "bg", [128, 16], F32, 2)
        sm = Pool_(es, pf + "sm", [128, 8, 4], F32, 2)
        X1 = Pool_(es, pf + "X", [128, 4, 128], F32, 1)
        X2 = Pool_(es, pf + "X2", [128, 4, 128], F32, 1)
        Dn = Pool_(es, pf + "Dn", [128, 4, 128], F32, 1)
        Ea = Pool_(es, pf + "Ea", [128, 4, 128], F32, 1)
        Eb = Pool_(es, pf + "Eb", [128, 4, 128], F32, 1)
        Fq = Pool_(es, pf + "Fq", [128, 4, 128], F32, 1)
        Eg = Pool_(es, pf + "Eg", [128, 4, 128], F32, 1)
        Pp = Pool_(es, pf + "P", [128, 4, 128], F32, 2)
        PTp = Pool_(es, pf + "PT", [128, 4, 128], F32, 2)
        TTf = Pool_(es, pf + "TTf", [128, 4, 128], F32, 1)
        qkTp = Pool_(es, pf + "qkT", [128, 4, 128], BF16, 2)
        qgp = Pool_(es, pf + "qg", [128, 4, 128], BF16, 2)
        vbp = Pool_(es, pf + "vb", [128, 4, 128], F32, 1)
        kbp = Pool_(es, pf + "kb", [128, 4, 128], F32, 1)
        kdp = Pool_(es, pf + "kd", [128, 4, 128], BF16, 2)
        Up = Pool_(es, pf + "U", [128, 4, 128], F32, 2)
        WTp = Pool_(es, pf + "WT", [128, 4, 128], BF16, 2)
        vnp = Pool_(es, pf + "vn", [128, 128], BF16, 4)
        op_ = Pool_(es, pf + "o", [64, 512], F32, 2)
        odst, orn = (ofs, "ofs") if d == 0 else (ofb, "ofb")
        Sfr = [Res("gSf%d" % h) for h in range(4)]
        Sbr = [Res("gSb%d" % h) for h in range(4)]
        if is_s:
            S.dma("sp", Sf[:], st_gdn[l, d].rearrange("h k e -> k h e"), W=Sfr)
        else:
            S.op("dve", "memset", W=Sfr, ap=Sf[:], constant=0.0)
        S.op("act", "copy", R=Sfr, W=Sbr, out=Sb_[:], in_=Sf[:])
        idb = ident_f[:].unsqueeze(1).to_broadcast([128, 4, 128])
        v4 = lambda t: t[:].rearrange("p h b -> p (h b)")
        order = range(ntl) if d == 0 else range(ntl - 1, -1, -1)
        for i in order:
            tt = t0 // 128 + i
            c0 = tt * 128
            kT = kTp.next(); qT = qTp.next(); kt = ktp.next(); vt = vtp.next(); bg = bgp.next()
            S.dma("sp", kT[:], gkT[:, :, c0:c0 + 128].rearrange("h p t -> p h t"), R=[RX["gkT"][tt]], W=[kT])
            S.dma("sp", qT[:], gqT[:, :, c0:c0 + 128].rearrange("h p t -> p h t"), R=[RX["gqT"][tt]], W=[qT])
            S.dma("sp", kt[:], gk[c0:c0 + 128, :], R=[RX["gk"][tt]], W=[kt])
            S.dma("sp", vt[:], gv[c0:c0 + 128, :], R=[RX["gv"][tt]], W=[vt])
            S.dma("sp", bg[:], bgs[c0:c0 + 128, :], R=[RX["bgs"][tt]], W=[bg])
            s = sm.next()
            S.op("dve", "tensor_copy", R=[bg], W=[s], out=s[:, 0, :], in_=bg[:, 8 + d * 4:12 + d * 4])
            S.op("dve", "tensor_copy", R=[bg], W=[s], out=s[:, 1, :], in_=bg[:, d * 4:d * 4 + 4])
            pg = psf()
            S.op("pe", "matmul", R=[utri2, s], W=[pg], out=pg[:, 0:4], lhsT=utri2[:, d, :], rhs=s[:, 0, :],
                 start=True, stop=True)
            S.op("pe", "matmul", R=[bd1, s], W=[pg], out=pg[:, 4:8], lhsT=bd1[:], rhs=s[:, 0, :], start=True, stop=True)
            S.op("pe", "matmul", R=[selc, s], W=[pg], out=pg[:, 8:12], lhsT=selc[:, 0, :], rhs=s[:, 0, :],
                 start=True, stop=True)
            S.op("pe", "matmul", R=[selc, s], W=[pg], out=pg[:, 12:16], lhsT=selc[:, 1, :], rhs=s[:, 0, :],
                 start=True, stop=True)
            S.op("dve", "tensor_copy", R=[pg], W=[s], out=s[:, 2, :], in_=pg[:, 0:4])
            S.op("act", "activation", R=[pg], W=[s], out=s[:, 6:8, :].rearrange("p c h -> p (c h)"), in_=pg[:, 8:16],
                 func=AF.Exp)
            S.op("dve", "tensor_tensor", R=[pg], W=[s], out=s[:, 4, :], in0=pg[:, 4:8], in1=s[:, 2, :], op=ALU.subtract)
            S.op("act", "activation", R=[s], W=[s], out=s[:, 4, :], in_=s[:, 4, :], func=AF.Exp)
            S.op("act", "activation", R=[s], W=[s], out=s[:, 5, :], in_=s[:, 2, :], func=AF.Exp)
            S.op("dve", "tensor_tensor", R=[s], W=[s], out=s[:, 5, :], in0=s[:, 5, :], in1=s[:, 1, :], op=ALU.mult)
            ckpt('g0')
            yield
            x1 = X1.next(); x2 = X2.next()
            S.op("dve", "tensor_tensor", R=[ident_f, s], W=[x1], out=x1[:], in0=idb,
                 in1=s[:, 2, :].unsqueeze(2).to_broadcast([128, 4, 128]), op=ALU.mult)
            S.op("pool", "tensor_tensor", R=[ident_f, s], W=[x2], out=x2[:], in0=idb,
                 in1=s[:, 1, :].unsqueeze(2).to_broadcast([128, 4, 128]), op=ALU.mult)
            pR = psf(); pRb = psf()
            S.op("pe", "matmul", R=[ones_f, x1], W=[pR], out=pR[:], lhsT=ones_f[:], rhs=v4(x1), start=True, stop=True)
            S.op("pe", "matmul", R=[ones_f, x2], W=[pRb], out=pRb[:], lhsT=ones_f[:], rhs=v4(x2), start=True, stop=True)
            dn = Dn.next()
            S.op("dve", "tensor_tensor", R=[pR, s], W=[dn], out=dn[:], in0=pR[:].rearrange("p (h b) -> p h b", h=4),
                 in1=s[:, 2, :].unsqueeze(2).to_broadcast([128, 4, 128]), op=ALU.subtract)
            eg = Eg.next()
            S.op("act", "activation", R=[pR], W=[eg], out=v4(eg), in_=pR[:], func=AF.Exp)
            ea = Ea.next(); eb = Eb.next(); fq = Fq.next()
            S.op("dve", "tensor_scalar", R=[dn], W=[ea], out=ea[:], in0=dn[:], scalar1=-1.0, scalar2=0.0,
                 op0=ALU.mult, op1=ALU.min)
            S.op("dve", "tensor_scalar", R=[dn], W=[eb], out=eb[:], in0=dn[:], scalar1=0.0, scalar2=None, op0=ALU.min)
            S.op("act", "activation", R=[ea], W=[ea], out=ea[:], in_=ea[:], func=AF.Exp)
            S.op("act", "activation", R=[eb], W=[eb], out=eb[:], in_=eb[:], func=AF.Exp)
            S.op("dve", "tensor_tensor", R=[ea, gmask2], W=[ea], out=ea[:], in0=ea[:],
                 in1=gmask2[:, d, 0, :].unsqueeze(1).to_broadcast([128, 4, 128]), op=ALU.mult)
            S.op("dve", "tensor_tensor", R=[ea, s], W=[ea], out=ea[:], in0=ea[:],
                 in1=s[:, 1, :].unsqueeze(2).to_broadcast([128, 4, 128]), op=ALU.mult)
            S.op("pool", "tensor_tensor", R=[eb, gmask2], W=[fq], out=fq[:], in0=eb[:],
                 in1=gmask2[:, d, 2, :].unsqueeze(1).to_broadcast([128, 4, 128]), op=ALU.mult)
            S.op("pool", "tensor_tensor", R=[eb, gmask2], W=[eb], out=eb[:], in0=eb[:],
                 in1=gmask2[:, d, 1, :].unsqueeze(1).to_broadcast([128, 4, 128]), op=ALU.mult)
            S.op("dve", "tensor_tensor", R=[eb, pRb], W=[eb], out=eb[:], in0=eb[:],
                 in1=pRb[:].rearrange("p (h b) -> p h b", h=4), op=ALU.mult)
            ckpt('g1')
            yield
            qg = qgp.next()
            S.op("pool", "tensor_tensor", R=[qT, eg], W=[qg], out=qg[:], in0=qT[:], in1=eg[:], op=ALU.mult)
            pK = psf(); pQ = psf()
            for h in range(4):
                S.op("pe", "matmul", R=[kT], W=[pK], out=pK[:, h * 128:(h + 1) * 128], lhsT=kT[:, h, :], rhs=kT[:, h, :],
                     start=True, stop=True)
                S.op("pe", "matmul", R=[kT, qT], W=[pQ], out=pQ[:, h * 128:(h + 1) * 128], lhsT=kT[:, h, :],
                     rhs=qT[:, h, :], start=True, stop=True)
            P = Pp.next(); PT = PTp.next(); ttf = TTf.next(); qkT = qkTp.next()
            pKv = pK[:].rearrange("p (h b) -> p h b", h=4)
            S.op("dve", "tensor_tensor", R=[pK, ea], W=[P], out=P[:], in0=pKv, in1=ea[:], op=ALU.mult)
            S.op("dve", "tensor_tensor", R=[pK, eb], W=[PT], out=PT[:], in0=pKv, in1=eb[:], op=ALU.mult)
            S.op("pool", "tensor_tensor", R=[PT, ident_f], W=[ttf], out=ttf[:], in0=PT[:], in1=idb, op=ALU.add)
            S.op("dve", "tensor_tensor", R=[pQ, fq], W=[qkT], out=qkT[:], in0=pQ[:].rearrange("p (h b) -> p h b", h=4),
                 in1=fq[:], op=ALU.mult)
            ckpt('g2')
            yield
            for lev in range(1, 6):
                p1 = psf()
                for h in range(4):
                    S.op("pe", "matmul", R=[PT, P], W=[p1], out=p1[:, h * 128:(h + 1) * 128], lhsT=PT[:, h, :],
                         rhs=P[:, h, :], start=True, stop=True)
                Pn = Pp.next()
                S.op("dve", "tensor_copy", R=[p1], W=[Pn], out=v4(Pn), in_=p1[:])
                p3 = psf()
                for h in range(4):
                    S.op("pe", "matmul", R=[Pn, ttf], W=[p3], out=p3[:, h * 128:(h + 1) * 128], lhsT=Pn[:, h, :],
                         rhs=ttf[:, h, :], start=True, stop=True)
                if lev < 5:
                    p2 = psf()
                    for h in range(4):
                        if os.environ.get("KNOTR"):
                            S.op("pe", "matmul", R=[PT, P], W=[p2], out=p2[:, h * 128:(h + 1) * 128], lhsT=P[:, h, :],
                                 rhs=PT[:, h, :], start=True, stop=True)
                        else:
                            S.op("pe", "transpose", R=[Pn, ident_f], W=[p2], out=p2[:, h * 128:(h + 1) * 128],
                                 in_=Pn[:, h, :], identity=ident_f[:])
                    PTn = PTp.next()
                    S.op("act", "copy", R=[p2], W=[PTn], out=v4(PTn), in_=p2[:])
                S.op("dve", "tensor_tensor", R=[ttf, p3], W=[ttf], out=v4(ttf), in0=v4(ttf), in1=p3[:], op=ALU.add)
                P = Pn
                if lev < 5:
                    PT = PTn
                ckpt('g3')
                yield
            vb = vbp.next(); kb = kbp.next(); kd = kdp.next()
            S.op("dve", "tensor_tensor", R=[vt, s], W=[vb], out=vb[:], in0=vt[:].rearrange("p (h e) -> p h e", h=4),
                 in1=s[:, 1, :].unsqueeze(2).to_broadcast([128, 4, 128]), op=ALU.mult)
            S.op("pool", "tensor_tensor", R=[kt, s], W=[kb], out=kb[:], in0=kt[:].rearrange("p (h e) -> p h e", h=4),
                 in1=s[:, 5, :].unsqueeze(2).to_broadcast([128, 4, 128]), op=ALU.mult)
            S.op("pool", "tensor_tensor", R=[kt, s], W=[kd], out=kd[:], in0=kt[:].rearrange("p (h e) -> p h e", h=4),
                 in1=s[:, 4, :].unsqueeze(2).to_broadcast([128, 4, 128]), op=ALU.mult)
            U = Up.next(); WT = WTp.next()
            pU = psf(); pW = psf()
            for h in range(4):
                S.op("pe", "matmul", R=[ttf, vb], W=[pU], out=pU[:, h * 128:(h + 1) * 128], lhsT=ttf[:, h, :],
                     rhs=vb[:, h, :], start=True, stop=True)
                S.op("pe", "matmul", R=[kb, ttf], W=[pW], out=pW[:, h * 128:(h + 1) * 128], lhsT=kb[:, h, :],
                     rhs=ttf[:, h, :], start=True, stop=True)
            S.op("dve", "tensor_copy", R=[pU], W=[U], out=v4(U), in_=pU[:])
            S.op("act", "copy", R=[pW], W=[WT], out=v4(WT), in_=pW[:])
            ckpt('g4')
            yield
            for cb in ((0, 1) if d == 0 else (1, 0)):
                rows = slice(cb * 64, cb * 64 + 64)
                cols = slice(cb * 64, cb * 64 + 64)
                po = PSF[4 + d]
                for h in range(4):
                    pa = psf()
                    S.op("pe", "matmul", R=[WT, Sbr[h]], W=[pa], out=pa[:, 0:128], lhsT=WT[:, h, :], rhs=Sb_[:, h, :],
                         start=True, stop=True)
                    vn = vnp.next()
                    S.op("dve", "tensor_tensor", R=[U, pa], W=[vn], out=vn[rows, :], in0=U[rows, h, :],
                         in1=pa[rows, 0:128], op=ALU.subtract)
                    S.op("pe", "matmul", R=[qg, Sbr[h]], W=[po], out=po[0:64, h * 128:(h + 1) * 128], lhsT=qg[:, h, cols],
                         rhs=Sb_[:, h, :], start=True, stop=False)
                    S.op("pe", "matmul", R=[qkT, vn], W=[po], out=po[0:64, h * 128:(h + 1) * 128],
                         lhsT=qkT[rows, h, cols], rhs=vn[rows, :], start=False, stop=True)
                    pS = psf()
                    S.op("pe", "matmul", R=[kd, vn], W=[pS], out=pS[:, 0:128], lhsT=kd[rows, h, :], rhs=vn[rows, :],
                         start=True, stop=True)
                    S.op("dve", "scalar_tensor_tensor", R=[Sfr[h], s, pS], W=[Sfr[h]], out=Sf[:, h, :], in0=Sf[:, h, :],
                         scalar=s[:, 6 + cb, h:h + 1], in1=pS[:, 0:128], op0=ALU.mult, op1=ALU.add)
                    S.op("act", "copy", R=[Sfr[h]], W=[Sbr[h]], out=Sb_[:, h, :], in_=Sf[:, h, :])
                    if h % 2 == 1:
                        ckpt('g5')
                        yield
                r0 = c0 + cb * 64
                o = op_.next()
                S.op("act", "copy", R=[po], W=[o], out=o[:], in_=po[0:64, :])
                S.dma("pool", odst[r0:r0 + 64, :], o[:], R=[o], W=[RX[orn][tt]])
                ckpt('g6')
                yield
        if not is_s:
            S.dma("pool", ngdn_out[si - 1, l, d].rearrange("h k e -> k h e"), Sf[:], R=Sfr)

    def phaseE(l, xsrc, xdst):
        with ExitStack() as es:
            wbr = sb(es, "pe_wbr", [128, 12, D], BF16)
            wo = sb(es, "pe_wo", [128, 8, D], BF16)
            with ExitStack() as es0:
                wst = Pool_(es0, "pe_wst", [128, 4, D], F32, 2)
                for q in range(3):
                    w = wst.next()
                    S.dma("sp", w[:], w_branch[l, q].rearrange("(c p) n -> p c n", p=128), W=[w])
                    S.op("dve" if q % 2 == 0 else "pool", "tensor_copy", R=[w], W=[wbr], out=wbr[:, q * 4:q * 4 + 4, :],
                         in_=w[:])
                for q in range(2):
                    w = wst.next()
                    S.dma("sp", w[:], w_out[l].rearrange("(c p) n -> p c n", p=128)[:, q * 4:q * 4 + 4, :], W=[w])
                    S.op("dve" if q % 2 == 0 else "pool", "tensor_copy", R=[w], W=[wo], out=wo[:, q * 4:q * 4 + 4, :],
                         in_=w[:])
                S.barrier()
            gtp = Pool_(es, "pe_gt", [128, 12, 128], BF16, 4)
            mgp = Pool_(es, "pe_mg", [128, 3 * D], BF16, 4)
            mp = Pool_(es, "pe_m", [128, D], F32, 3)
            mbp = Pool_(es, "pe_mb", [128, D], BF16, 3)
            mTp = Pool_(es, "pe_mT", [128, 8, 128], BF16, 3)
            xp = Pool_(es, "pe_x", [128, D], F32, 4)
            tp = Pool_(es, "pe_t", [128, 512], F32, 4)

            def tileE(tt):
                j = 0 if tt < NTS else 1
                c0 = tt * 128
                gt = gtp.next(); mgt = mgp.next(); xt = xp.next()
                S.dma("sp", gt[:], gT[:, :, c0:c0 + 128].rearrange("m (c p) t -> p (m c) t", p=128),
                      R=[RX["gT0"][tt], RX["gT1"][tt], RX["gT2"][tt]], W=[gt])
                S.dma("sp", mgt[:], mg[c0:c0 + 128, :], R=[RX["mg"][tt]], W=[mgt])
                S.dma("sp", xt[:], xsrc[c0:c0 + 128, :], R=[RX["x"][tt]] if l > 0 else [], W=[xt])
                yield
                m = mp.next()
                for q in range(3):
                    for nb in range(2):
                        ps = psf()
                        for c in range(4):
                            S.op("pe", "matmul", R=[gt, wbr], W=[ps], out=ps[:], lhsT=gt[:, q * 4 + c, :],
                                 rhs=wbr[:, q * 4 + c, nb * 512:(nb + 1) * 512], start=(c == 0), stop=(c == 3))
                        cols = slice(nb * 512, (nb + 1) * 512)
                        if q == 0:
                            S.op("dve", "tensor_tensor", R=[ps, mgt], W=[m], out=m[:, cols], in0=ps[:],
                                 in1=mgt[:, cols], op=ALU.mult)
                        else:
                            t = tp.next()
                            S.op("dve", "tensor_tensor", R=[ps, mgt], W=[t], out=t[:], in0=ps[:],
                                 in1=mgt[:, q * D + nb * 512:q * D + (nb + 1) * 512], op=ALU.mult)
                            S.op("pool", "tensor_tensor", R=[t, m], W=[m], out=m[:, cols], in0=m[:, cols], in1=t[:],
                                 op=ALU.add)
                yield
                mb = mbp.next()
                S.op("act", "copy", R=[m], W=[mb], out=mb[:], in_=m[:])
                yield
                pb = psb()
                transposes_to(pb, mb, [(mb[:, c * 128:(c + 1) * 128], 128, c * 128) for c in range(8)])
                mT = mTp.next()
                S.op("act", "copy", R=[pb], W=[mT], out=mT[:].rearrange("p c t -> p (c t)"), in_=pb[:])
                yield
                for nb in range(2):
                    ps = psf()
                    for c in range(8):
                        S.op("pe", "matmul", R=[mT, wo], W=[ps], out=ps[:], lhsT=mT[:, c, :],
                             rhs=wo[:, c, nb * 512:(nb + 1) * 512], start=(c == 0), stop=(c == 7))
                    cols = slice(nb * 512, (nb + 1) * 512)
                    t = tp.next()
                    S.op("dve", "tensor_tensor", R=[ps, gate_bc], W=[t], out=t[:], in0=ps[:], in1=gate_bc[:, j, cols],
                         op=ALU.mult)
                    S.op("pool", "tensor_tensor", R=[t, xt], W=[xt], out=xt[:, cols], in0=xt[:, cols], in1=t[:],
                         op=ALU.add)
                S.dma("pool", xdst[c0:c0 + 128, :], xt[:], R=[xt], W=[RX["x"][tt]])

            gens = [tileE(tt) for tt in range(NT)]
            active = []
            while gens or active:
                if gens:
                    active.append(gens.pop(0))
                for g in list(active):
                    try:
                        next(g)
                    except StopIteration:
                        active.remove(g)
            S.barrier()

    try:
        for l in range(depth):
            layer(l)
    except StopBuild as e:
        print("build stopped at", e)
        root2 = None
    S.barrier()
    S.emit()
    build.stats = (S.n_instr, S.n_wait)
    return nc


_CACHE = {}


def _get_nc(T_S, depth, debug):
    key = (T_S, depth, debug)
    if key not in _CACHE:
        _CACHE[key] = build(T_S, depth, debug)
    return _CACHE[key]


def run_cfg(inp, T_S, depth, debug=False):
    f = lambda a: np.ascontiguousarray(np.asarray(a), dtype=np.float32)
    xs = f(inp["x_sample"])
    xp = f(inp["x_prompt"])
    n_core = 8
    nsb = xs.shape[0]
    consts = make_consts(T_S)
    shared = {
        "norm_w": f(inp["norm_w"])[:depth], "w_ada": f(inp["w_ada"])[:depth], "b_ada": f(inp["b_ada"])[:depth],
        "w_in": f(inp["w_in"])[:depth], "qk_norm_w": f(inp["qk_norm_w"])[:depth],
        "diff_lambda": f(inp["diff_lambda"])[:depth], "subln_w": f(inp["subln_w"])[:depth],
        "ret_decay": f(inp["ret_decay"])[:depth].reshape(depth, 8), "ret_norm_w": f(inp["ret_norm_w"])[:depth],
        "conv_w": f(inp["conv_w"])[:depth], "gdn_a_log": f(inp["gdn_a_log"])[:depth].reshape(depth, 8),
        "gdn_dt_bias": f(inp["gdn_dt_bias"])[:depth].reshape(depth, 8), "gdn_norm_w": f(inp["gdn_norm_w"])[:depth],
        "w_branch": f(inp["w_branch"])[:depth], "w_out": f(inp["w_out"])[:depth],
    }
    for k, v in consts.items():
        shared["c_" + k] = v
    ck = f(inp["cache_attn_k"])
    cv = f(inp["cache_attn_v"])
    sr = f(inp["state_ret"])
    sgd = f(inp["state_gdn"])
    c = f(inp["c"])
    cctx = f(inp["c_ctx"])
    in_maps = []
    for core in range(n_core):
        b = core % nsb
        m = dict(shared)
        m["x_in"] = np.ascontiguousarray(np.concatenate(
            [xs[b, :T_S]] + [xp[core * NPR + i] for i in range(NPR)], axis=0))
        m["cond"] = np.ascontiguousarray(np.stack([c[b], cctx], axis=0))
        m["cache_k"] = np.ascontiguousarray(ck[b, :depth].reshape(depth, PAST, 512))
        m["cache_v"] = np.ascontiguousarray(cv[b, :depth].reshape(depth, PAST, 512))
        m["st_ret"] = np.ascontiguousarray(sr[b, :depth])
        m["st_gdn"] = np.ascontiguousarray(sgd[b, :depth])
        in_maps.append(m)
    nc = _get_nc(T_S, depth, debug)
    res = run_bass_kernel_spmd(nc, in_maps, core_ids=list(range(n_core)))
    R = res.results
    y_s = np.stack([np.asarray(R[b]["y"])[:T_S] for b in range(nsb)], axis=0).astype(np.float32)
    y_p = np.stack([np.asarray(R[core]["y"])[T_S + i * TP:T_S + (i + 1) * TP]
                    for core in range(n_core) for i in range(NPR)], axis=0).astype(np.float32)
    nk = np.concatenate([np.asarray(R[core]["nk"]) for core in range(n_core)], axis=0).astype(np.float32)
    nv = np.concatenate([np.asarray(R[core]["nv"]) for core in range(n_core)], axis=0).astype(np.float32)
    nret = np.concatenate([np.asarray(R[core]["nret"]) for core in range(n_core)], axis=0).astype(np.float32)
    ngdn = np.concatenate([np.asarray(R[core]["ngdn"]) for core in range(n_core)], axis=0).astype(np.float32)
    nk = nk.reshape(n_core * NPR, depth, TP, 4, 2, 64)
    nv = nv.reshape(n_core * NPR, depth, TP, 4, 128)
    outs = (y_p, y_s, nk, nv, nret, ngdn)
    if debug:
        return outs, R
    return outs


def kernel(**inputs):
    return run_cfg(inputs, 4096, DEPTH, False)
```

```python
import math
from contextlib import ExitStack
import numpy as np
import ml_dtypes
import concourse.bass as bass
import concourse.mybir as mybir
from concourse.bass_utils import run_bass_kernel_spmd

F32 = mybir.dt.float32
BF16 = mybir.dt.bfloat16
AF = mybir.ActivationFunctionType
ALU = mybir.AluOpType
AX = mybir.AxisListType

D = 1024
DEPTH = 4
TP = 256
NPR = 2
PAST = 256
D_IN = 8720
EPS = 1e-6
CH = 64


import os


class StopBuild(Exception):
    pass


def ckpt(name):
    if os.environ.get("KSTOP", "") == name:
        raise StopBuild(name)


class Res:
    __slots__ = ("name", "w", "r", "excl")

    def __init__(self, name=""):
        self.name = name
        self.w = None
        self.r = {}
        self.excl = False


class Tile:
    def __init__(self, h, name, psum=False):
        self.h = h
        self.res = Res(name)
        self.res.excl = psum

    def __getitem__(self, k):
        return self.h[k]


def _res(x):
    out = []
    for t in x:
        if isinstance(t, (list, tuple)):
            out.extend(_res(t))
        elif isinstance(t, Res):
            out.append(t)
        else:
            out.append(t.res)
    return out


class Sched:
    ENG = ("pe", "act", "dve", "pool", "sp")

    def __init__(self, nc, n_dma_slots=8):
        self.nc = nc
        self.streams = {e: [] for e in self.ENG}
        self.sems = {}
        self.cnt = {}
        for e in ("pe", "act", "dve", "pool"):
            self.sems[e] = nc.alloc_semaphore("s_" + e)
            self.cnt[e] = 0
        self.nslots = n_dma_slots
        self.dq = {}
        for q in ("sp", "pool", "act"):
            slots = []
            for i in range(n_dma_slots):
                k = "d_%s%d" % (q, i)
                self.sems[k] = nc.alloc_semaphore(k)
                slots.append([k, 0])
            self.dq[q] = [slots, 0]
        self.seen = {e: {} for e in self.ENG}
        self.n_instr = 0
        self.n_wait = 0

    def _wait(self, eng, key, val):
        if val is None or val <= 0:
            return
        if eng == "pe" and key == "pe":
            return
        s = self.seen[eng]
        if s.get(key, 0) >= val:
            return
        s[key] = val
        sem = self.sems[key]
        self.streams[eng].append(lambda e, sem=sem, val=val: e.wait_ge(sem, val))
        self.n_wait += 1

    def _deps(self, eng, reads, writes, is_dma=False):
        for r in reads:
            if r.w is not None:
                self._wait(eng, r.w[0], r.w[1])
        for w in writes:
            if w.w is not None:
                if is_dma or not (w.w[0] == eng):
                    self._wait(eng, w.w[0], w.w[1])
            for k, v in w.r.items():
                if (not is_dma) and k == eng:
                    continue
                self._wait(eng, k, v)

    def _mark(self, key, val, reads, writes):
        for r in reads:
            if r.r.get(key, 0) < val:
                r.r[key] = val
        for w in writes:
            w.w = (key, val)
            w.r = {}

    def op(self, eng, method, R=(), W=(), **kw):
        reads = _res(R)
        writes = _res(W)
        ex = [r for r in reads if r.excl]
        if ex:
            reads = [r for r in reads if not r.excl]
            writes = writes + [r for r in ex if r not in writes]
        self._deps(eng, reads, writes)
        self.cnt[eng] += 1
        val = self.cnt[eng]
        sem = self.sems[eng]
        import traceback
        org = traceback.extract_stack(limit=3)[0]
        org = "%s:%d" % (org.name, org.lineno)

        def _f(e, m=method, kw=kw, sem=sem, org=org):
            try:
                return getattr(e, m)(**kw).then_inc(sem, 1)
            except Exception as ex:
                raise RuntimeError("emit failed at %s (%s): %s" % (org, m, ex)) from ex
        self.streams[eng].append(_f)
        self._mark(eng, val, reads, writes)
        self.n_instr += 1

    def dma(self, q, out, in_, R=(), W=(), **kw):
        reads = _res(R)
        writes = _res(W)
        slots, idx = self.dq[q]
        slot = slots[idx % self.nslots]
        self.dq[q][1] = idx + 1
        key = slot[0]
        if slot[1] > 0:
            self._wait(q, key, slot[1])
        self._deps(q, reads, writes, is_dma=True)
        slot[1] += 16
        val = slot[1]
        sem = self.sems[key]
        import traceback
        org = traceback.extract_stack(limit=3)[0]
        org = "%s:%d" % (org.name, org.lineno)

        def _f(e, out=out, in_=in_, sem=sem, kw=kw, org=org):
            try:
                return e.dma_start(out=out, in_=in_, **kw).then_inc(sem, 16)
            except Exception as ex:
                raise RuntimeError("dma emit failed at %s: %s" % (org, ex)) from ex
        self.streams[q].append(_f)
        self._mark(key, val, reads, writes)
        self.n_instr += 1

    def barrier(self):
        for e in self.ENG:
            for k in ("pe", "act", "dve", "pool"):
                self._wait(e, k, self.cnt[k])
            for q in self.dq:
                for slot in self.dq[q][0]:
                    if slot[1] > 0:
                        self._wait(e, slot[0], slot[1])

    def emit(self):
        nc = self.nc
        st = self.streams
        with nc.Block() as block:
            @block.sync
            def _(e):
                for f in st["sp"]:
                    f(e)

            @block.tensor
            def _(e):
                for f in st["pe"]:
                    f(e)

            @block.scalar
            def _(e):
                for f in st["act"]:
                    f(e)

            @block.vector
            def _(e):
                for f in st["dve"]:
                    f(e)

            @block.gpsimd
            def _(e):
                for f in st["pool"]:
                    f(e)


def make_consts(T_S):
    c = {}
    c["ident"] = np.eye(128, dtype=np.float32)
    n_rows = T_S // 64
    row = np.repeat(np.arange(n_rows, dtype=np.float32), 64)
    col = np.tile(np.arange(64, dtype=np.float32), n_rows)
    inv = (1.0 / (10000.0 ** (np.arange(16, dtype=np.float32) / 16))).astype(np.float32)
    ar = row[:, None] * inv
    ac = col[:, None] * inv
    ang = np.concatenate([ar, ar, ac, ac], axis=-1).astype(np.float32)
    cos = np.cos(ang).astype(np.float32)
    sin = np.sin(ang).astype(np.float32)
    sgn = np.tile(np.concatenate([-np.ones(16), np.ones(16)]), 2).astype(np.float32)
    c["ropec"] = cos
    c["ropes"] = (sin * sgn).astype(np.float32)
    a = np.arange(64)[:, None]
    b = np.arange(64)[None, :]
    low = (a > b).astype(np.float32)
    up = (b > a).astype(np.float32)
    upi = (b >= a).astype(np.float32)
    lowi = (a >= b).astype(np.float32)
    gm = np.zeros((64, 2, 3, 64), np.float32)
    gm[:, 0, 0] = -low
    gm[:, 0, 1] = -up
    gm[:, 0, 2] = upi
    gm[:, 1, 0] = -up
    gm[:, 1, 1] = -low
    gm[:, 1, 2] = lowi
    c["gmask"] = gm
    a2 = np.arange(128)[:, None]
    b2 = np.arange(128)[None, :]
    same = (a2 // 64 == b2 // 64)
    gm2 = np.zeros((128, 2, 3, 128), np.float32)
    gm2[:, 0, 0] = -1.0 * ((a2 > b2) & same)
    gm2[:, 0, 1] = -1.0 * ((b2 > a2) & same)
    gm2[:, 0, 2] = ((b2 >= a2) & same)
    gm2[:, 1, 0] = -1.0 * ((b2 > a2) & same)
    gm2[:, 1, 1] = -1.0 * ((a2 > b2) & same)
    gm2[:, 1, 2] = ((a2 >= b2) & same)
    c["gmask2"] = gm2
    ut2 = np.zeros((128, 2, 128), np.float32)
    ut2[:, 0] = ((a2 <= b2) & same)
    ut2[:, 1] = ((a2 >= b2) & same)
    c["utri2"] = ut2
    c["bd1"] = same.astype(np.float32)
    sel = np.zeros((128, 2, 128), np.float32)
    sel[0:64, 0, :] = 1.0
    sel[64:128, 1, :] = 1.0
    c["selc"] = sel
    ut = np.zeros((64, 2, 64), np.float32)
    ut[:, 0] = (a <= b).astype(np.float32)
    ut[:, 1] = (a >= b).astype(np.float32)
    c["utri"] = ut
    j = np.arange(128)[:, None].astype(np.float32)
    i = np.arange(128)[None, :].astype(np.float32)
    rr = np.zeros((128, 2, 128), np.float32)
    rm = np.zeros((128, 2, 128), np.float32)
    rr[:, 0] = np.maximum(i - j, 0)
    rm[:, 0] = (i >= j)
    rr[:, 1] = np.maximum(j - i, 0)
    rm[:, 1] = (j >= i)
    c["rrel"] = rr
    c["rmask"] = rm
    rq = np.zeros((64, 2, 128), np.float32)
    rq[:, 0] = (np.arange(128) + 1.0)[None, :]
    rq[:, 1] = (128.0 - np.arange(128))[None, :]
    c["rqexp"] = rq
    rk = np.zeros((128, 2), np.float32)
    rk[:, 0] = 127.0 - np.arange(128)
    rk[:, 1] = np.arange(128)
    c["rkexp"] = rk
    return c


CONST_SHAPES = lambda T_S: {k: v.shape for k, v in make_consts(T_S).items()}


def build(T_S=4096, depth=DEPTH, debug=False):
    nc = bass.Bass("TRN2", target_bir_lowering=False)
    S = Sched(nc)
    TT = T_S + NPR * TP
    NT = TT // 128
    NTS = T_S // 128
    seqs = [(0, T_S, True)] + [(T_S + i * TP, TP, False) for i in range(NPR)]
    lam_inits = [0.8 - 0.6 * math.exp(-0.3 * l) for l in range(depth)]

    def din(name, shape, dt=F32):
        return nc.dram_tensor(name, list(shape), dt, kind="ExternalInput").ap()

    def dout(name, shape, dt=F32):
        return nc.dram_tensor(name, list(shape), dt, kind="ExternalOutput").ap()

    def scr(name, shape, dt):
        if debug:
            return nc.dram_tensor(name, list(shape), dt, kind="ExternalOutput").ap()
        return nc.dram_tensor(name, list(shape), dt).ap()

    x_in = din("x_in", [TT, D])
    cond = din("cond", [2, D])
    cache_k = din("cache_k", [depth, PAST, 512])
    cache_v = din("cache_v", [depth, PAST, 512])
    st_ret = din("st_ret", [depth, 2, 4, 64, 128])
    st_gdn = din("st_gdn", [depth, 2, 4, 128, 128])
    norm_w = din("norm_w", [depth, D])
    w_ada = din("w_ada", [depth, D, 3 * D])
    b_ada = din("b_ada", [depth, 3 * D])
    w_in = din("w_in", [depth, D, D_IN])
    qk_norm_w = din("qk_norm_w", [depth, 2, 64])
    diff_lambda = din("diff_lambda", [depth, 4, 64])
    subln_w = din("subln_w", [depth, 128])
    ret_decay = din("ret_decay", [depth, 8])
    ret_norm_w = din("ret_norm_w", [depth, 128])
    conv_w = din("conv_w", [depth, 5, 1536])
    gdn_a_log = din("gdn_a_log", [depth, 8])
    gdn_dt_bias = din("gdn_dt_bias", [depth, 8])
    gdn_norm_w = din("gdn_norm_w", [depth, 128])
    w_branch = din("w_branch", [depth, 3, 512, D])
    w_out = din("w_out", [depth, D, D])
    cst = {k: din("c_" + k, shp) for k, shp in CONST_SHAPES(T_S).items()}
    y_out = dout("y", [TT, D])
    nk_out = dout("nk", [NPR, depth, TP, 512])
    nv_out = dout("nv", [NPR, depth, TP, 512])
    nret_out = dout("nret", [NPR, depth, 2, 4, 64, 128])
    ngdn_out = dout("ngdn", [NPR, depth, 2, 4, 128, 128])
    xcur = scr("xcur", [TT, D], F32)
    aqT = scr("aqT", [4, 128, TT], BF16)
    akT = scr("akT", [4, 128, TT], BF16)
    av = scr("av", [TT, 512], BF16)
    zg = scr("zg", [3, TT, 512], BF16)
    bqT = scr("bqT", [4, 64, TT], BF16)
    bkT = scr("bkT", [4, 64, TT], BF16)
    bk = scr("bk", [TT, 256], BF16)
    bv = scr("bv", [TT, 512], BF16)
    cT = scr("cT", [1536, TT], F32)
    gqT = scr("gqT", [4, 128, TT], BF16)
    gkT = scr("gkT", [4, 128, TT], BF16)
    gk = scr("gk", [TT, 512], BF16)
    gv = scr("gv", [TT, 512], BF16)
    bgs = scr("bgs", [TT, 16], F32)
    mg = scr("mg", [TT, 3 * D], BF16)
    ofs = scr("ofs", [TT, 512], F32)
    ofb = scr("ofb", [TT, 512], F32)
    gT = scr("gT", [3, 512, TT], BF16)

    def tres(n):
        return [Res("%s%d" % (n, i)) for i in range(NT)]
    RX = {n: tres(n) for n in ("x", "aqT", "akT", "av", "zg0", "zg1", "zg2", "bqT", "bkT", "bk", "bv", "cT",
                               "gqT", "gkT", "gk", "gv", "bgs", "mg", "ofs", "ofb", "gT0", "gT1", "gT2")}
    R_out = Res("outs")

    uid = [0]

    def sb(es, name, shape, dt):
        uid[0] += 1
        nm = "%s_u%d" % (name, uid[0])
        return Tile(es.enter_context(nc.sbuf_tensor(nm, list(shape), dt)), nm)

    class Pool_:
        def __init__(self, es, name, shape, dt, n):
            self.t = [sb(es, "%s_%d" % (name, i), shape, dt) for i in range(n)]
            self.i = 0

        def next(self):
            t = self.t[self.i % len(self.t)]
            self.i += 1
            return t

    root = ExitStack()
    PSF = [Tile(nc.alloc_psum_tensor("psf%d" % i, [128, 512], F32), "psf%d" % i, True) for i in range(6)]
    PSB = [Tile(nc.alloc_psum_tensor("psb%d" % i, [128, 1024], BF16), "psb%d" % i, True) for i in range(2)]
    psc = [0, 0]

    psf_banks = [[0, 1, 2, 3]]

    def psf():
        psc[0] += 1
        bk = psf_banks[0]
        return PSF[bk[psc[0] % len(bk)]]

    def psb():
        psc[1] += 1
        return PSB[psc[1] % 2]

    ident_f = sb(root, "ident_f", [128, 128], F32)
    ident_b = sb(root, "ident_b", [128, 128], BF16)
    ones_f = sb(root, "ones_f", [128, 128], F32)
    ropec = sb(root, "ropec", [128, NTS, 64], F32)
    ropes = sb(root, "ropes", [128, NTS, 64], F32)
    gmask = sb(root, "gmask", [64, 2, 3, 64], F32)
    utri = sb(root, "utri", [64, 2, 64], F32)
    gmask2 = sb(root, "gmask2", [128, 2, 3, 128], F32)
    utri2 = sb(root, "utri2", [128, 2, 128], F32)
    bd1 = sb(root, "bd1", [128, 128], F32)
    selc = sb(root, "selc", [128, 2, 128], F32)
    rrel = sb(root, "rrel", [128, 2, 128], F32)
    rmask = sb(root, "rmask", [128, 2, 128], F32)
    rqexp = sb(root, "rqexp", [64, 2, 128], F32)
    rkexp = sb(root, "rkexp", [128, 2], F32)
    gate_bc = sb(root, "gate_bc", [128, 2, D], F32)
    wq_bc = sb(root, "wq_bc", [128, 2, 64], F32)
    subw_bc = sb(root, "subw_bc", [128, 128], F32)
    retw_bc = sb(root, "retw_bc", [128, 128], F32)
    gdnw_bc = sb(root, "gdnw_bc", [128, 128], F32)
    lamt = sb(root, "lamt", [128, 8], F32)
    dl_bc = sb(root, "dl_bc", [128, 4, 64], F32)
    lg_bc = sb(root, "lg_bc", [128, 8], F32)
    rdmat = sb(root, "rdmat", [128, 2, 4, 128], F32)
    rqdec = sb(root, "rqdec", [64, 2, 4, 128], F32)
    rkdec = sb(root, "rkdec", [128, 2, 4], F32)
    rcdec = sb(root, "rcdec", [128, 8], F32)
    negA_bc = sb(root, "negA_bc", [128, 8], F32)
    dtb_bc = sb(root, "dtb_bc", [128, 8], F32)
    convw = sb(root, "convw", [128, 12, 5], F32)
    tmp8 = sb(root, "tmp8", [128, 8], F32)

    S.dma("sp", ident_f[:], cst["ident"], W=[ident_f])
    S.op("dve", "tensor_copy", R=[ident_f], W=[ident_b], out=ident_b[:], in_=ident_f[:])
    S.op("dve", "memset", W=[ones_f], ap=ones_f[:], constant=1.0)
    S.dma("sp", ropec[:], cst["ropec"].rearrange("(n p) d -> p n d", p=128), W=[ropec])
    S.dma("sp", ropes[:], cst["ropes"].rearrange("(n p) d -> p n d", p=128), W=[ropes])
    S.dma("sp", gmask[:], cst["gmask"], W=[gmask])
    S.dma("sp", utri[:], cst["utri"], W=[utri])
    S.dma("sp", gmask2[:], cst["gmask2"], W=[gmask2])
    S.dma("sp", utri2[:], cst["utri2"], W=[utri2])
    S.dma("sp", bd1[:], cst["bd1"], W=[bd1])
    S.dma("sp", selc[:], cst["selc"], W=[selc])
    S.dma("sp", rrel[:], cst["rrel"], W=[rrel])
    S.dma("sp", rmask[:], cst["rmask"], W=[rmask])
    S.dma("sp", rqexp[:], cst["rqexp"], W=[rqexp])
    S.dma("sp", rkexp[:], cst["rkexp"], W=[rkexp])

    def rsqrt_inplace(t, ap, scale, eps):
        S.op("dve", "tensor_scalar", R=[t], W=[t], out=ap, in0=ap, scalar1=scale, scalar2=eps,
             op0=ALU.mult, op1=ALU.add)
        S.op("act", "activation", R=[t], W=[t], out=ap, in_=ap, func=AF.Sqrt)
        S.op("dve", "reciprocal", R=[t], W=[t], out=ap, in_=ap)

    def transposes_to(ps_t, src_t, blocks, rows=128):
        for (sap, w, off) in blocks:
            S.op("pe", "transpose", R=[src_t, ident_b], W=[ps_t], out=ps_t[0:w, off:off + rows], in_=sap,
                 identity=ident_b[0:rows, 0:rows])

    def layer(l):
        last = (l == depth - 1)
        xsrc = x_in if l == 0 else xcur
        xdst = y_out if last else xcur
        esH = ExitStack()
        hT = sb(esH, "hT", [128, 8, TT], BF16)
        esA = ExitStack()
        A_bc = sb(esA, "A_bc", [128, 2, D], F32)
        sh_bc = sb(esA, "sh_bc", [128, 2, D], F32)
        with ExitStack() as es:
            nw_bc = sb(es, "nw_bc", [128, D], F32)
            cs = sb(es, "cs", [128, 2, 8], F32)
            rep = sb(es, "rep", [128, 16, 128], F32)
            bada = sb(es, "bada", [1, 3 * D], F32)
            wa = Pool_(es, "wa", [128, 8, 512], F32, 2)
            S.dma("sp", nw_bc[:], norm_w[l].partition_broadcast(128), W=[nw_bc])
            S.dma("sp", cs[:], cond.rearrange("j (p c) -> p j c", c=8), W=[cs])
            S.dma("sp", bada[:], b_ada[l:l + 1, :], W=[bada])
            S.dma("sp", wq_bc[:],
                  qk_norm_w[l].rearrange("a d -> (a d)").partition_broadcast(128).rearrange("p (a d) -> p a d", a=2),
                  W=[wq_bc])
            S.dma("sp", subw_bc[:], subln_w[l].partition_broadcast(128), W=[subw_bc])
            S.dma("sp", retw_bc[:], ret_norm_w[l].partition_broadcast(128), W=[retw_bc])
            S.dma("sp", gdnw_bc[:], gdn_norm_w[l].partition_broadcast(128), W=[gdnw_bc])
            S.dma("sp", dl_bc[:], diff_lambda[l].rearrange("a d -> (a d)").partition_broadcast(128)
                  .rearrange("p (a d) -> p a d", a=4), W=[dl_bc])
            S.dma("sp", lg_bc[:], ret_decay[l].partition_broadcast(128), W=[lg_bc])
            S.dma("sp", negA_bc[:], gdn_a_log[l].partition_broadcast(128), W=[negA_bc])
            S.dma("sp", dtb_bc[:], gdn_dt_bias[l].partition_broadcast(128), W=[dtb_bc])
            for k in range(5):
                S.dma("sp", convw[:, :, k:k + 1], conv_w[l, k].rearrange("(c p o) -> p c o", p=128, o=1), W=[convw],
                      allow_slow_non_contiguous=True)
            S.op("dve", "tensor_scalar", R=[wq_bc], W=[wq_bc], out=wq_bc[:, 0, :], in0=wq_bc[:, 0, :],
                 scalar1=0.125, scalar2=None, op0=ALU.mult)
            S.op("dve", "tensor_scalar", R=[subw_bc], W=[subw_bc], out=subw_bc[:], in0=subw_bc[:],
                 scalar1=float(1.0 - lam_inits[l]), scalar2=None, op0=ALU.mult)
            S.op("dve", "tensor_tensor", R=[dl_bc], W=[dl_bc], out=dl_bc[:, 0, :], in0=dl_bc[:, 0, :],
                 in1=dl_bc[:, 1, :], op=ALU.mult)
            S.op("dve", "tensor_tensor", R=[dl_bc], W=[dl_bc], out=dl_bc[:, 2, :], in0=dl_bc[:, 2, :],
                 in1=dl_bc[:, 3, :], op=ALU.mult)
            S.op("dve", "tensor_reduce", R=[dl_bc], W=[lamt], out=lamt[:, 0:4], in_=dl_bc[:], axis=AX.X, op=ALU.add)
            S.op("act", "activation", R=[lamt], W=[lamt], out=lamt[:, 4:8], in_=lamt[:, 0:4], func=AF.Exp)
            S.op("dve", "tensor_tensor", R=[lamt], W=[lamt], out=lamt[:, 1:2], in0=lamt[:, 6:7], in1=lamt[:, 4:5],
                 op=ALU.subtract)
            S.op("dve", "tensor_scalar", R=[lamt], W=[lamt], out=lamt[:, 0:1], in0=lamt[:, 1:2],
                 scalar1=float(-lam_inits[l]), scalar2=None, op0=ALU.add)
            S.op("act", "activation", R=[lg_bc], W=[lg_bc], out=lg_bc[:], in_=lg_bc[:], func=AF.Exp, scale=-1.0)
            S.op("act", "activation", R=[lg_bc], W=[lg_bc], out=lg_bc[:], in_=lg_bc[:], func=AF.Ln, bias=1.0)
            S.op("dve", "tensor_scalar", R=[lg_bc], W=[lg_bc], out=lg_bc[:], in0=lg_bc[:], scalar1=-1.0,
                 scalar2=None, op0=ALU.mult)
            for d in range(2):
                for h in range(4):
                    u = d * 4 + h
                    S.op("act", "activation", R=[rrel, lg_bc], W=[rdmat], out=rdmat[:, d, h, :], in_=rrel[:, d, :],
                         func=AF.Exp, scale=lg_bc[:, u:u + 1])
                    S.op("dve", "tensor_tensor", R=[rdmat, rmask], W=[rdmat], out=rdmat[:, d, h, :],
                         in0=rdmat[:, d, h, :], in1=rmask[:, d, :], op=ALU.mult)
                    S.op("act", "activation", R=[rqexp, lg_bc], W=[rqdec], out=rqdec[:, d, h, :], in_=rqexp[:, d, :],
                         func=AF.Exp, scale=lg_bc[0:64, u:u + 1])
                    S.op("act", "activation", R=[rkexp, lg_bc], W=[rkdec], out=rkdec[:, d, h:h + 1],
                         in_=rkexp[:, d:d + 1], func=AF.Exp, scale=lg_bc[:, u:u + 1])
            S.op("act", "activation", R=[lg_bc], W=[rcdec], out=rcdec[:], in_=lg_bc[:], func=AF.Exp, scale=128.0)
            S.op("act", "activation", R=[negA_bc], W=[negA_bc], out=negA_bc[:], in_=negA_bc[:], func=AF.Exp)
            S.op("dve", "tensor_scalar", R=[negA_bc], W=[negA_bc], out=negA_bc[:], in0=negA_bc[:], scalar1=-1.0,
                 scalar2=None, op0=ALU.mult)
            S.op("act", "activation", R=[cs], W=[cs], out=cs[:], in_=cs[:], func=AF.Silu)
            S.op("dve", "tensor_copy", R=[cs], W=[rep], out=rep[:],
                 in_=cs[:].rearrange("p j c -> p (j c)").unsqueeze(2).to_broadcast([128, 16, 128]))
            for nb in range(6):
                w = wa.next()
                S.dma("sp", w[:], w_ada[l].rearrange("(p c) n -> p c n", c=8)[:, :, nb * 512:(nb + 1) * 512], W=[w])
                for j in range(2):
                    ps = psf()
                    for c in range(8):
                        S.op("pe", "matmul", R=[rep, w], W=[ps], out=ps[:], lhsT=rep[:, j * 8 + c, :], rhs=w[:, c, :],
                             start=(c == 0), stop=False)
                    S.op("pe", "matmul", R=[ones_f, bada], W=[ps], out=ps[:], lhsT=ones_f[0:1, :],
                         rhs=bada[0:1, nb * 512:(nb + 1) * 512], start=False, stop=True)
                    cols = slice((nb % 2) * 512, (nb % 2) * 512 + 512)
                    if nb < 2:
                        S.op("act", "copy", R=[ps], W=[sh_bc], out=sh_bc[:, j, cols], in_=ps[:])
                    elif nb < 4:
                        S.op("dve", "scalar_tensor_tensor", R=[ps, nw_bc], W=[A_bc], out=A_bc[:, j, cols], in0=ps[:],
                             scalar=1.0, in1=nw_bc[:, cols], op0=ALU.add, op1=ALU.mult)
                    else:
                        S.op("act", "copy", R=[ps], W=[gate_bc], out=gate_bc[:, j, cols], in_=ps[:])
            S.barrier()
        ckpt("A")
        with ExitStack() as es:
            xp = Pool_(es, "xB", [128, D], F32, 4)
            hp = Pool_(es, "hB", [128, D], F32, 4)
            hbp = Pool_(es, "hbB", [128, D], BF16, 4)
            junk = sb(es, "junkB", [128, D], F32)
            ssp = Pool_(es, "ssB", [128, 1], F32, 4)
            def tileB(tt):
                j = 0 if tt < NTS else 1
                xt = xp.next()
                ht = hp.next()
                hb = hbp.next()
                ss = ssp.next()
                S.dma("sp", xt[:], xsrc[tt * 128:(tt + 1) * 128, :], R=[RX["x"][tt]] if l > 0 else [], W=[xt])
                S.op("act", "activation", R=[xt], W=[junk, ss], out=junk[:], in_=xt[:], func=AF.Square, accum_out=ss[:])
                yield
                rsqrt_inplace(ss, ss[:], 1.0 / D, EPS)
                S.op("dve", "scalar_tensor_tensor", R=[xt, ss, A_bc], W=[ht], out=ht[:], in0=xt[:], scalar=ss[:, 0:1],
                     in1=A_bc[:, j, :], op0=ALU.mult, op1=ALU.mult)
                S.op("dve", "tensor_tensor", R=[ht, sh_bc], W=[hb], out=hb[:], in0=ht[:], in1=sh_bc[:, j, :], op=ALU.add)
                yield
                pb = psb()
                transposes_to(pb, hb, [(hb[:, c * 128:(c + 1) * 128], 128, c * 128) for c in range(8)])
                S.op("act", "copy", R=[pb], W=[hT], out=hT[:, :, tt * 128:(tt + 1) * 128],
                     in_=pb[:].rearrange("p (c t) -> p c t", c=8))
            pipeline([tileB(tt) for tt in range(NT)], 3)
            S.barrier()
        esA.close()
        ckpt("B")
        with ExitStack() as es:
            wf = Pool_(es, "wfC", [128, 4, 512], F32, 2)
            wbp = Pool_(es, "wbC", [128, 8, 512], BF16, 2)
            f1 = Pool_(es, "f1C", [128, 512], F32, 9)
            f2 = Pool_(es, "f2C", [128, 512], F32, 3)
            b1 = Pool_(es, "b1C", [128, 512], BF16, 4)
            st8 = Pool_(es, "st8C", [128, 16], F32, 4)
            stg = Pool_(es, "stgC", [128, 1024], BF16, 3)
            cfp = Pool_(es, "cfC", [128, 512], F32, 3)

            def load_wblock(c0, ncols):
                wb = wbp.next()
                for half in range(2):
                    w = wf.next()
                    S.dma("sp", w[:, :, 0:ncols],
                          w_in[l].rearrange("(c p) n -> p c n", p=128)[:, half * 4:half * 4 + 4, c0:c0 + ncols], W=[w])
                    S.op("dve" if half == 0 else "pool", "tensor_copy", R=[w], W=[wb],
                         out=wb[:, half * 4:half * 4 + 4, 0:ncols], in_=w[:, :, 0:ncols])
                return (wb, (c0, ncols))

            wblocks = [(0, 512), (512, 512), (1024, 512), (1536, 512), (2560, 512), (3072, 512), (5120, 512),
                       (2048, 512), (3584, 512), (4096, 512), (4608, 512), (5632, 16)] + \
                      [(5648 + i * 512, 512) for i in range(6)]
            wq = []

            def next_wb(c0, ncols):
                if not wq:
                    wq.append(load_wblock(*wblocks.pop(0)))
                wb = wq.pop(0)
                assert wb[1] == (c0, ncols), (wb[1], c0, ncols)
                if wblocks:
                    wq.append(load_wblock(*wblocks.pop(0)))
                return wb[0]

            def mm_tok(wb, tt, ncols):
                ps = psf()
                for c in range(8):
                    S.op("pe", "matmul", R=[hT, wb], W=[ps], out=ps[:, 0:ncols], lhsT=hT[:, c, tt * 128:(tt + 1) * 128],
                         rhs=wb[:, c, 0:ncols], start=(c == 0), stop=(c == 7))
                return ps

            def rope(src, tt, ngrp, dst_pool):
                t1 = dst_pool.next()
                t2 = dst_pool.next()
                s4 = src[:, 0:ngrp * 64].rearrange("p (g a h f) -> p g a h f", a=2, h=2, f=16)
                S.op("dve", "tensor_tensor", R=[src, ropec], W=[t1],
                     out=t1[:, 0:ngrp * 64].rearrange("p (g d) -> p g d", d=64),
                     in0=src[:, 0:ngrp * 64].rearrange("p (g d) -> p g d", d=64),
                     in1=ropec[:, tt:tt + 1, :].to_broadcast([128, ngrp, 64]), op=ALU.mult)
                t24 = t2[:, 0:ngrp * 64].rearrange("p (g a h f) -> p g a h f", a=2, h=2, f=16)
                sn4 = ropes[:, tt, :].rearrange("p (a h f) -> p a h f", a=2, h=2)
                for hh in range(2):
                    S.op("pool", "tensor_tensor", R=[src, ropes], W=[t2], out=t24[:, :, :, hh, :],
                         in0=s4[:, :, :, 1 - hh, :],
                         in1=sn4[:, :, hh, :].unsqueeze(1).to_broadcast([128, ngrp, 2, 16]), op=ALU.mult)
                S.op("dve", "tensor_tensor", R=[t1, t2], W=[t1], out=t1[:, 0:ngrp * 64], in0=t1[:, 0:ngrp * 64],
                     in1=t2[:, 0:ngrp * 64], op=ALU.add)
                return t1

            def tileQK(blk, wb, tt):
                is_s = tt < NTS
                ps = mm_tok(wb, tt, 512)
                sq = f2.next()
                s8 = st8.next()
                qn = f1.next()
                S.op("act", "activation", R=[ps], W=[sq], out=sq[:], in_=ps[:], func=AF.Square)
                S.op("dve", "tensor_reduce", R=[sq], W=[s8], out=s8[:, 0:8],
                     in_=sq[:].rearrange("p (g d) -> p g d", d=64), axis=AX.X, op=ALU.add)
                yield
                rsqrt_inplace(s8, s8[:, 0:8], 1.0 / 64, EPS)
                S.op("dve", "tensor_tensor", R=[ps, s8], W=[qn], out=qn[:].rearrange("p (g d) -> p g d", d=64),
                     in0=ps[:].rearrange("p (g d) -> p g d", d=64),
                     in1=s8[:, 0:8].unsqueeze(2).to_broadcast([128, 8, 64]), op=ALU.mult)
                S.op("pool", "tensor_tensor", R=[qn, wq_bc], W=[qn], out=qn[:].rearrange("p (g d) -> p g d", d=64),
                     in0=qn[:].rearrange("p (g d) -> p g d", d=64),
                     in1=wq_bc[:, blk:blk + 1, :].to_broadcast([128, 8, 64]), op=ALU.mult)
                if blk == 1 and not is_s:
                    pi = (tt - NTS) // 2
                    r0 = ((tt - NTS) % 2) * 128
                    S.dma("pool", nk_out[pi, l, r0:r0 + 128, :], qn[:], R=[qn])
                yield
                if is_s:
                    qn = rope(qn, tt, 8, f1)
                qb = b1.next()
                S.op("act", "copy", R=[qn], W=[qb], out=qb[:], in_=qn[:])
                yield
                pb = psb()
                transposes_to(pb, qb, [(qb[:, h * 128:(h + 1) * 128], 128, h * 128) for h in range(4)])
                sg = stg.next()
                S.op("dve", "tensor_copy", R=[pb], W=[sg], out=sg[:, 0:512], in_=pb[:, 0:512])
                dst = aqT if blk == 0 else akT
                S.dma("pool", dst[:, :, tt * 128:(tt + 1) * 128].rearrange("h p t -> p h t"),
                      sg[:, 0:512].rearrange("p (h t) -> p h t", h=4), R=[sg],
                      W=[RX["aqT" if blk == 0 else "akT"][tt]])

            for blk in range(2):
                wb = next_wb(blk * 512, 512)
                pipeline([tileQK(blk, wb, tt) for tt in range(NT)], 4)
            ckpt("C0")
            for (c0, kind) in ((1024, "av"), (1536, "z0"), (2560, "bv"), (3072, "z1"), (5120, "z2")):
                wb = next_wb(c0, 512)
                for tt in range(NT):
                    is_s = tt < NTS
                    ps = mm_tok(wb, tt, 512)
                    ob = b1.next()
                    if kind in ("av", "bv"):
                        S.op("act", "copy", R=[ps], W=[ob], out=ob[:], in_=ps[:])
                        dst, rn = (av, "av") if kind == "av" else (bv, "bv")
                        S.dma("pool", dst[tt * 128:(tt + 1) * 128, :], ob[:], R=[ob], W=[RX[rn][tt]])
                        if kind == "av" and not is_s:
                            of_ = f1.next()
                            S.op("dve", "tensor_copy", R=[ps], W=[of_], out=of_[:], in_=ps[:])
                            pi = (tt - NTS) // 2
                            r0 = ((tt - NTS) % 2) * 128
                            S.dma("pool", nv_out[pi, l, r0:r0 + 128, :], of_[:], R=[of_])
                    else:
                        zi = int(kind[1])
                        S.op("act", "activation", R=[ps], W=[ob], out=ob[:], in_=ps[:], func=AF.Silu)
                        S.dma("pool", zg[zi, tt * 128:(tt + 1) * 128, :], ob[:], R=[ob], W=[RX["zg%d" % zi][tt]])
                ckpt("C1_" + kind)
            ckpt("C1")
            wb = next_wb(2048, 512)

            def tileBQK(wb, tt):
                is_s = tt < NTS
                ps = mm_tok(wb, tt, 512)
                qk = f1.next()
                S.op("act", "copy", R=[ps], W=[qk], out=qk[:, 0:256], in_=ps[:, 0:256])
                S.op("act", "mul", R=[ps], W=[qk], out=qk[:, 256:512], in_=ps[:, 256:512], mul=0.125)
                yield
                if is_s:
                    qk = rope(qk, tt, 8, f1)
                qb = b1.next()
                S.op("act", "copy", R=[qk], W=[qb], out=qb[:], in_=qk[:])
                S.dma("pool", bk[tt * 128:(tt + 1) * 128, :], qb[:, 256:512], R=[qb], W=[RX["bk"][tt]])
                yield
                pb = psb()
                transposes_to(pb, qb, [(qb[:, g * 64:(g + 1) * 64], 64, g * 128) for g in range(8)])
                sg = stg.next()
                S.op("dve", "tensor_copy", R=[pb], W=[sg], out=sg[0:64, :], in_=pb[0:64, :])
                S.dma("pool", bqT[:, :, tt * 128:(tt + 1) * 128].rearrange("h p t -> p h t"),
                      sg[0:64, 0:512].rearrange("p (h t) -> p h t", h=4), R=[sg], W=[RX["bqT"][tt]])
                S.dma("pool", bkT[:, :, tt * 128:(tt + 1) * 128].rearrange("h p t -> p h t"),
                      sg[0:64, 512:1024].rearrange("p (h t) -> p h t", h=4), R=[sg], W=[RX["bkT"][tt]])
            pipeline([tileBQK(wb, tt) for tt in range(NT)], 3)
            ckpt("C2")
            for blk in range(3):
                wb = next_wb(3584 + blk * 512, 512)
                for cc in range(4):
                    for (t0, T, _) in seqs:
                        for g0 in range(0, T, 512):
                            n = min(512, T - g0)
                            ps = psf()
                            for c in range(8):
                                S.op("pe", "matmul", R=[hT, wb], W=[ps], out=ps[:, 0:n],
                                     lhsT=wb[:, c, cc * 128:(cc + 1) * 128], rhs=hT[:, c, t0 + g0:t0 + g0 + n],
                                     start=(c == 0), stop=(c == 7))
                            cf = cfp.next()
                            S.op("act", "copy", R=[ps], W=[cf], out=cf[:, 0:n], in_=ps[:, 0:n])
                            ch0 = (blk * 4 + cc) * 128
                            tiles = range((t0 + g0) // 128, (t0 + g0 + n) // 128)
                            S.dma("pool", cT[ch0:ch0 + 128, t0 + g0:t0 + g0 + n], cf[:, 0:n], R=[cf],
                                  W=[RX["cT"][i] for i in tiles])
            ckpt("C3")
            wb = next_wb(5632, 16)
            for tt in range(NT):
                ps = mm_tok(wb, tt, 16)
                o16 = st8.next()
                S.op("act", "activation", R=[ps], W=[o16], out=o16[:, 0:8], in_=ps[:, 0:8], func=AF.Sigmoid)
                S.op("dve", "tensor_tensor", R=[ps, dtb_bc], W=[o16], out=o16[:, 8:16], in0=ps[:, 8:16], in1=dtb_bc[:],
                     op=ALU.add)
                S.op("act", "activation", R=[o16], W=[o16], out=o16[:, 8:16], in_=o16[:, 8:16], func=AF.Exp)
                S.op("act", "activation", R=[o16], W=[o16], out=o16[:, 8:16], in_=o16[:, 8:16], func=AF.Ln, bias=1.0)
                S.op("dve", "tensor_tensor", R=[o16, negA_bc], W=[o16], out=o16[:, 8:16], in0=o16[:, 8:16],
                     in1=negA_bc[:], op=ALU.mult)
                S.dma("pool", bgs[tt * 128:(tt + 1) * 128, :], o16[:], R=[o16], W=[RX["bgs"][tt]])
            ckpt("C4")
            for blk in range(6):
                wb = next_wb(5648 + blk * 512, 512)
                for tt in range(NT):
                    ps = mm_tok(wb, tt, 512)
                    ob = b1.next()
                    S.op("act", "activation", R=[ps], W=[ob], out=ob[:], in_=ps[:], func=AF.Sigmoid)
                    S.dma("pool", mg[tt * 128:(tt + 1) * 128, blk * 512:(blk + 1) * 512], ob[:], R=[ob],
                          W=[RX["mg"][tt]] if blk == 5 else [])
            S.barrier()
        esH.close()
        ckpt("C")
        for si, (t0, T, is_s) in enumerate(seqs):
            if is_s and not os.environ.get("KNOOVL"):
                sample_mixers_overlapped(l, si, t0, T, is_s)
                ckpt("gpre%d" % si)
            elif os.environ.get("KNOOVL"):
                attention(l, si, t0, T, is_s)
                ckpt("att%d" % si)
                retention(l, si, t0, T, is_s)
                ckpt("ret%d" % si)
                gdn_pre(l, si, t0, T, is_s)
                ckpt("gpre%d" % si)
            gdn_scan(l, si, t0, T, is_s)
            ckpt("gscan%d" % si)
        phaseE(l, xsrc, xdst)

    def norm_gate_store(es_tiles, o_t, o_ap, w_bc, zi, mi, tt, rows, col_lo=None):
        sq, s4, zt, gb, sg = es_tiles
        tok0 = tt * 128 + (col_lo or 0)
        S.op("act", "activation", R=[o_t], W=[sq], out=sq[0:rows, :], in_=o_ap, func=AF.Square)
        S.op("dve", "tensor_reduce", R=[sq], W=[s4], out=s4[0:rows, 0:4],
             in_=sq[0:rows, :].rearrange("p (g d) -> p g d", d=128), axis=AX.X, op=ALU.add)
        rsqrt_inplace(s4, s4[0:rows, 0:4], 1.0 / 128, EPS)
        S.dma("sp", zt[0:rows, :], zg[zi, tok0:tok0 + rows, :], R=[RX["zg%d" % zi][tt]], W=[zt])
        S.op("dve", "tensor_tensor", R=[o_t, s4], W=[o_t], out=o_ap.rearrange("p (g d) -> p g d", d=128),
             in0=o_ap.rearrange("p (g d) -> p g d", d=128),
             in1=s4[0:rows, 0:4].unsqueeze(2).to_broadcast([rows, 4, 128]), op=ALU.mult)
        S.op("pool", "tensor_tensor", R=[o_t, w_bc], W=[o_t], out=o_ap.rearrange("p (g d) -> p g d", d=128),
             in0=o_ap.rearrange("p (g d) -> p g d", d=128),
             in1=w_bc[0:rows, :].unsqueeze(1).to_broadcast([rows, 4, 128]), op=ALU.mult)
        S.op("dve", "tensor_tensor", R=[o_t, zt], W=[gb], out=gb[0:rows, :], in0=o_ap, in1=zt[0:rows, :], op=ALU.mult)
        pb = psb()
        for g in range(4):
            S.op("pe", "transpose", R=[gb, ident_b], W=[pb], out=pb[:, g * 128:g * 128 + rows],
                 in_=gb[0:rows, g * 128:(g + 1) * 128], identity=ident_b[0:rows, 0:rows])
        pv = pb[:, 0:512].rearrange("p (g t) -> p g t", g=4)[:, :, 0:rows]
        S.op("act", "copy", R=[pb], W=[sg], out=sg[:, :, 0:rows], in_=pv)
        S.dma("pool", gT[mi, :, tok0:tok0 + rows].rearrange("(g p) t -> p g t", p=128), sg[:, :, 0:rows], R=[sg],
              W=[RX["gT%d" % mi][tt]])

    def ng_tiles(es, pfx):
        return (sb(es, pfx + "sq", [128, 512], F32), sb(es, pfx + "s4", [128, 4], F32),
                sb(es, pfx + "zt", [128, 512], BF16), sb(es, pfx + "gb", [128, 512], BF16),
                sb(es, pfx + "sg", [128, 4, 128], BF16))

    def attention_gen(l, si, t0, T, is_s, es, acc_sets, sc_banks=None):
        Sk = T + (PAST if is_s else 0)
        nst = Sk // 128
        ntl = T // 128
        QB = min(512, T)
        kT = sb(es, "at_kT", [128, 4, Sk], BF16)
        V1 = sb(es, "at_V1", [128, nst, 4, 130], BF16)
        qTp = Pool_(es, "at_qT", [128, 4, QB], BF16, 2)
        ex = Pool_(es, "at_ex", [128, QB], BF16, 3)
        osb = [sb(es, "at_os%d" % qs, [128, 512], F32) for qs in range(QB // 128)]
        rc = Pool_(es, "at_rc", [128, 2], F32, 4)
        ngt = ng_tiles(es, "at_")
        grp = [0]
        scn = [0]
        S.dma("sp", kT[:, :, 0:T], akT[:, :, t0:t0 + T].rearrange("h p t -> p h t"),
              R=[RX["akT"][i] for i in range(t0 // 128, (t0 + T) // 128)], W=[kT])
        S.op("pool", "memset", W=[V1], ap=V1[:, :, :, 128:130], constant=1.0)
        for i in range(ntl):
            S.dma("sp", V1[:, i, :, 0:128], av[t0 + i * 128:t0 + (i + 1) * 128, :].rearrange("p (h e) -> p h e", h=4),
                  R=[RX["av"][t0 // 128 + i]], W=[V1])
        if is_s:
            with ExitStack() as es2:
                ck = sb(es2, "at_ck", [128, 2, 512], F32)
                cv = sb(es2, "at_cv", [128, 2, 512], F32)
                ckb = sb(es2, "at_ckb", [128, 2, 512], BF16)
                S.dma("sp", ck[:], cache_k[l].rearrange("(n p) f -> p n f", p=128), W=[ck])
                S.dma("sp", cv[:], cache_v[l].rearrange("(n p) f -> p n f", p=128), W=[cv])
                S.op("dve", "tensor_copy", R=[ck], W=[ckb], out=ckb[:], in_=ck[:])
                for n in range(2):
                    S.op("pool", "tensor_copy", R=[cv], W=[V1], out=V1[:, ntl + n, :, 0:128],
                         in_=cv[:, n, :].rearrange("p (h e) -> p h e", h=4))
                    pb = psb()
                    transposes_to(pb, ckb, [(ckb[:, n, h * 128:(h + 1) * 128], 128, h * 128) for h in range(4)])
                    S.op("act", "copy", R=[pb], W=[kT], out=kT[:, :, T + n * 128:T + (n + 1) * 128],
                         in_=pb[:, 0:512].rearrange("p (h t) -> p h t", h=4))
                S.barrier()
        for qb0 in range(0, T, QB):
            qT = qTp.next()
            S.dma("sp", qT[:], aqT[:, :, t0 + qb0:t0 + qb0 + QB].rearrange("h p t -> p h t"),
                  R=[RX["aqT"][i] for i in range((t0 + qb0) // 128, (t0 + qb0 + QB) // 128)], W=[qT])
            nqs = QB // 128
            for h in range(4):
                for m in range(2):
                    grp[0] += 1
                    acc = acc_sets[grp[0] % len(acc_sets)]

                    def pv(st, e, acc=acc, h=h):
                        for qs in range(nqs):
                            a = acc[qs // 2]
                            S.op("pe", "matmul", R=[e, V1], W=[a], out=a[:, (qs % 2) * 256:(qs % 2) * 256 + 129],
                                 lhsT=e[:, qs * 128:(qs + 1) * 128], rhs=V1[:, st, h, 0:129],
                                 start=(st == 0 and qs % 2 == 0), stop=(st == nst - 1), skip_group_check=True)
                    pend = None
                    for st in range(nst):
                        scn[0] += 1
                        _sb = sc_banks or [PSF[0], PSF[1]]
                        ps = _sb[scn[0] % len(_sb)]
                        S.op("pe", "matmul", R=[kT, qT], W=[ps], out=ps[:, 0:QB],
                             lhsT=kT[m * 64:(m + 1) * 64, h, st * 128:(st + 1) * 128],
                             rhs=qT[m * 64:(m + 1) * 64, h, :], start=True, stop=True)
                        e = ex.next()
                        S.op("act", "activation", R=[ps], W=[e], out=e[:, 0:QB], in_=ps[:, 0:QB], func=AF.Exp)
                        if pend is not None:
                            pv(*pend)
                        pend = (st, e)
                        yield
                    pv(*pend)
                    for qs in range(nqs):
                        a = acc[qs // 2]
                        c0 = (qs % 2) * 256
                        r = rc.next()
                        S.op("dve", "reciprocal", R=[a], W=[r], out=r[:, 0:1], in_=a[:, c0 + 128:c0 + 129])
                        if m == 0:
                            S.op("dve", "tensor_scalar", R=[a, r], W=[osb[qs]], out=osb[qs][:, h * 128:(h + 1) * 128],
                                 in0=a[:, c0:c0 + 128], scalar1=r[:, 0:1], scalar2=None, op0=ALU.mult)
                        else:
                            S.op("dve", "tensor_tensor", R=[r, lamt], W=[r], out=r[:, 1:2], in0=r[:, 0:1],
                                 in1=lamt[:, 0:1], op=ALU.mult)
                            S.op("dve", "scalar_tensor_tensor", R=[a, r, osb[qs]], W=[osb[qs]],
                                 out=osb[qs][:, h * 128:(h + 1) * 128], in0=a[:, c0:c0 + 128], scalar=r[:, 1:2],
                                 in1=osb[qs][:, h * 128:(h + 1) * 128], op0=ALU.mult, op1=ALU.add)
            for qs in range(nqs):
                tt = (t0 + qb0) // 128 + qs
                norm_gate_store(ngt, osb[qs], osb[qs][:], subw_bc, 0, 0, tt, 128)

    def attention(l, si, t0, T, is_s):
        with ExitStack() as es:
            for _ in attention_gen(l, si, t0, T, is_s, es, [[PSF[2], PSF[3]], [PSF[4], PSF[5]]]):
                pass
            S.barrier()

    def retention(l, si, t0, T, is_s):
        with ExitStack() as es:
            run_rr([ret_chain(l, si, t0, T, is_s, d, es) for d in range(2)])
            S.barrier()
        combine(t0, T, retw_bc, 1, 1)

    def rr_gen(gens):
        gens = list(gens)
        while gens:
            for g in list(gens):
                try:
                    next(g)
                    yield
                except StopIteration:
                    gens.remove(g)

    def side_gen(l, si, t0, T, is_s):
        with ExitStack() as es2:
            yield from gdn_pre_gen(l, si, t0, T, is_s, es2, 256, 1)
            S.barrier()
        with ExitStack() as es3:
            yield from rr_gen([ret_chain(l, si, t0, T, is_s, d, es3) for d in range(2)])
            S.barrier()
        yield from combine_gen(t0, T, retw_bc, 1, 1)
        for pi in range(1, len(seqs)):
            (tp0, Tp, _) = seqs[pi]
            with ExitStack() as esp:
                yield from attention_gen(l, pi, tp0, Tp, False, esp, [[PSF[5]]], sc_banks=[PSF[4]])
                S.barrier()
            with ExitStack() as esp:
                yield from gdn_pre_gen(l, pi, tp0, Tp, False, esp, 256, 1)
                S.barrier()
            with ExitStack() as esp:
                yield from rr_gen([ret_chain(l, pi, tp0, Tp, False, d, esp) for d in range(2)])
                S.barrier()
            yield from combine_gen(tp0, Tp, retw_bc, 1, 1)

    def sample_mixers_overlapped(l, si, t0, T, is_s):
        with ExitStack() as es:
            att = attention_gen(l, si, t0, T, is_s, es, [[PSF[2], PSF[3]]])
            next(att)
            old = psf_banks[0]
            psf_banks[0] = [4, 5]
            n_att = (T // min(512, T)) * 8 * ((T + PAST) // 128)
            n_side = (T // 256) * 36 + (T // 128) * 5 + (len(seqs) - 1) * 70
            run_weighted(att, side_gen(l, si, t0, T, is_s), max(1, int(0.9 * n_att / n_side)))
            psf_banks[0] = old
            S.barrier()

    def ret_chain(l, si, t0, T, is_s, d, es):
        ntl = T // 128
        pf = "rt%d_" % d
        Sf = sb(es, pf + "S", [64, 4, 128], F32)
        Sb_ = sb(es, pf + "Sb", [64, 4, 128], BF16)
        qTp = Pool_(es, pf + "qT", [64, 4, 128], BF16, 2)
        kTp = Pool_(es, pf + "kT", [64, 4, 128], BF16, 2)
        ktp = Pool_(es, pf + "k", [128, 256], BF16, 2)
        vp = Pool_(es, pf + "v", [128, 512], BF16, 2)
        itp = Pool_(es, pf + "it", [128, 512], BF16, 2)
        qdp = Pool_(es, pf + "qd", [64, 4, 128], BF16, 2)
        kdp = Pool_(es, pf + "kd", [128, 256], BF16, 2)
        op_ = Pool_(es, pf + "o", [128, 512], F32, 2)
        odst, orn = (ofs, "ofs") if d == 0 else (ofb, "ofb")
        if is_s:
            S.dma("sp", Sf[:], st_ret[l, d].rearrange("h k e -> k h e"), W=[Sf])
        else:
            S.op("dve", "memset", W=[Sf], ap=Sf[:], constant=0.0)
        S.op("act", "copy", R=[Sf], W=[Sb_], out=Sb_[:], in_=Sf[:])
        order = range(ntl) if d == 0 else range(ntl - 1, -1, -1)
        for i in order:
            tt = t0 // 128 + i
            c0 = tt * 128
            qT = qTp.next(); kT = kTp.next(); kt = ktp.next(); v = vp.next()
            S.dma("sp", qT[:], bqT[:, :, c0:c0 + 128].rearrange("h p t -> p h t"), R=[RX["bqT"][tt]], W=[qT])
            S.dma("sp", kT[:], bkT[:, :, c0:c0 + 128].rearrange("h p t -> p h t"), R=[RX["bkT"][tt]], W=[kT])
            S.dma("sp", kt[:], bk[c0:c0 + 128, :], R=[RX["bk"][tt]], W=[kt])
            S.dma("sp", v[:], bv[c0:c0 + 128, :], R=[RX["bv"][tt]], W=[v])
            ps = psf()
            for h in range(4):
                S.op("pe", "matmul", R=[kT, qT], W=[ps], out=ps[:, h * 128:(h + 1) * 128], lhsT=kT[:, h, :],
                     rhs=qT[:, h, :], start=True, stop=True)
            it = itp.next()
            S.op("dve", "tensor_tensor", R=[ps, rdmat], W=[it], out=it[:], in0=ps[:],
                 in1=rdmat[:, d, :, :].rearrange("p h i -> p (h i)"), op=ALU.mult)
            qd = qdp.next()
            S.op("pool", "tensor_tensor", R=[qT, rqdec], W=[qd], out=qd[:], in0=qT[:], in1=rqdec[:, d, :, :],
                 op=ALU.mult)
            kd = kdp.next()
            S.op("pool", "tensor_tensor", R=[kt, rkdec], W=[kd], out=kd[:].rearrange("p (h e) -> p h e", h=4),
                 in0=kt[:].rearrange("p (h e) -> p h e", h=4),
                 in1=rkdec[:, d, :].unsqueeze(2).to_broadcast([128, 4, 64]), op=ALU.mult)
            yield
            po = psf()
            for h in range(4):
                S.op("pe", "matmul", R=[it, v], W=[po], out=po[:, h * 128:(h + 1) * 128],
                     lhsT=it[:, h * 128:(h + 1) * 128], rhs=v[:, h * 128:(h + 1) * 128], start=True, stop=False)
                S.op("pe", "matmul", R=[qd, Sb_], W=[po], out=po[:, h * 128:(h + 1) * 128], lhsT=qd[:, h, :],
                     rhs=Sb_[:, h, :], start=False, stop=True)
            pS = psf()
            for h in range(4):
                S.op("pe", "matmul", R=[kd, v], W=[pS], out=pS[0:64, h * 128:(h + 1) * 128],
                     lhsT=kd[:, h * 64:(h + 1) * 64], rhs=v[:, h * 128:(h + 1) * 128], start=True, stop=True)
            S.op("dve", "tensor_tensor", R=[Sf, rcdec], W=[Sf], out=Sf[:], in0=Sf[:],
                 in1=rcdec[0:64, d * 4:d * 4 + 4].unsqueeze(2).to_broadcast([64, 4, 128]), op=ALU.mult)
            S.op("dve", "tensor_tensor", R=[Sf, pS], W=[Sf], out=Sf[:].rearrange("p h e -> p (h e)"),
                 in0=Sf[:].rearrange("p h e -> p (h e)"), in1=pS[0:64, :], op=ALU.add)
            S.op("act", "copy", R=[Sf], W=[Sb_], out=Sb_[:], in_=Sf[:])
            o = op_.next()
            S.op("act", "copy", R=[po], W=[o], out=o[:], in_=po[:])
            S.dma("pool", odst[c0:c0 + 128, :], o[:], R=[o], W=[RX[orn][tt]])
            yield
        if not is_s:
            S.dma("pool", nret_out[si - 1, l, d].rearrange("h k e -> k h e"), Sf[:], R=[Sf])

    def pipeline_gen(gens, depth):
        gens = list(gens)
        active = []
        while gens or active:
            while gens and len(active) < depth:
                active.append(gens.pop(0))
            for g in list(active):
                try:
                    next(g)
                except StopIteration:
                    active.remove(g)
            yield

    def gdn_pre_gen(l, si, t0, T, is_s, es, G, nbuf):
        xin = Pool_(es, "gp_x", [128, 12, G + 4], F32, nbuf)
        acc = Pool_(es, "gp_a", [128, 12, G], F32, nbuf)
        sqp = Pool_(es, "gp_sq", [128, G], F32, 5)
        rsp = Pool_(es, "gp_rs", [128, G], F32, 5)
        nb = Pool_(es, "gp_nb", [128, 12, G], BF16, nbuf)
        sg = Pool_(es, "gp_sg", [128, 1024], BF16, 2)
        chunk_res = {}
        for g0 in range(0, T, G):
            x = xin.next()
            a = acc.next()
            lo = 2 if g0 == 0 else 0
            hi = 2 if g0 + G == T else 0
            if lo:
                S.op("pool", "memset", W=[x], ap=x[:, :, 0:2], constant=0.0)
            if hi:
                S.op("pool", "memset", W=[x], ap=x[:, :, G + 2:G + 4], constant=0.0)
            tl = [i for i in range((t0 + g0) // 128 - (0 if lo else 1), (t0 + g0 + G) // 128 + (0 if hi else 1))]
            S.dma("sp", x[:, :, lo:G + 4 - hi],
                  cT[:, t0 + g0 - 2 + lo:t0 + g0 + G + 2 - hi].rearrange("(c p) t -> p c t", p=128),
                  R=[RX["cT"][i] for i in tl], W=[x])
            yield
            yield
            ar = chunk_res.setdefault(("a", id(a)), [Res("gpa%d" % c) for c in range(12)])

            def convc(c):
                S.op("dve", "tensor_scalar", R=[x, convw], W=[ar[c]], out=a[:, c, :], in0=x[:, c, 0:G],
                     scalar1=convw[:, c, 0:1], scalar2=None, op0=ALU.mult)
                for k in range(1, 5):
                    S.op("dve", "scalar_tensor_tensor", R=[x, convw, ar[c]], W=[ar[c]], out=a[:, c, :],
                         in0=x[:, c, k:k + G], scalar=convw[:, c, k:k + 1], in1=a[:, c, :], op0=ALU.mult, op1=ALU.add)
                yield
                yield
                S.op("act", "activation", R=[ar[c]], W=[ar[c]], out=a[:, c, :], in_=a[:, c, :], func=AF.Silu)
            yield from pipeline_gen([convc(c) for c in range(12)], 3)
            n = nb.next()
            nr = chunk_res.setdefault(("n", id(n)), [Res("gpn%d" % c) for c in range(12)])

            def l2c(c):
                sq = sqp.next()
                S.op("act", "activation", R=[ar[c]], W=[sq], out=sq[:], in_=a[:, c, :], func=AF.Square)
                yield
                yield
                ps = psf()
                S.op("pe", "matmul", R=[ones_f, sq], W=[ps], out=ps[:, 0:G], lhsT=ones_f[:], rhs=sq[:], start=True,
                     stop=True)
                rs = rsp.next()
                S.op("dve", "tensor_scalar", R=[ps], W=[rs], out=rs[:], in0=ps[:, 0:G], scalar1=EPS, scalar2=None,
                     op0=ALU.add)
                yield
                yield
                S.op("act", "activation", R=[rs], W=[rs], out=rs[:], in_=rs[:], func=AF.Sqrt)
                yield
                yield
                S.op("dve", "reciprocal", R=[rs], W=[rs], out=rs[:], in_=rs[:])
                if c < 4:
                    S.op("dve", "scalar_tensor_tensor", R=[ar[c], rs], W=[nr[c]], out=n[:, c, :], in0=a[:, c, :],
                         scalar=float(128 ** -0.5), in1=rs[:], op0=ALU.mult, op1=ALU.mult)
                else:
                    S.op("dve", "tensor_tensor", R=[ar[c], rs], W=[nr[c]], out=n[:, c, :], in0=a[:, c, :], in1=rs[:],
                         op=ALU.mult)
            yield from pipeline_gen([l2c(c) for c in range(8)], 4)
            S.op("pool", "tensor_copy", R=ar[8:12], W=nr[8:12], out=n[:, 8:12, :], in_=a[:, 8:12, :])
            tiles = list(range((t0 + g0) // 128, (t0 + g0 + G) // 128))
            S.dma("pool", gqT[:, :, t0 + g0:t0 + g0 + G].rearrange("h p t -> p h t"), n[:, 0:4, :], R=nr[0:4],
                  W=[RX["gqT"][i] for i in tiles])
            S.dma("pool", gkT[:, :, t0 + g0:t0 + g0 + G].rearrange("h p t -> p h t"), n[:, 4:8, :], R=nr[4:8],
                  W=[RX["gkT"][i] for i in tiles])
            yield
            yield
            for ti in range(G // 128):
                tt = (t0 + g0) // 128 + ti
                pb = psb()
                transposes_to(pb, nr[4:12], [(n[:, 4 + c, ti * 128:(ti + 1) * 128], 128, c * 128) for c in range(8)])
                s = sg.next()
                S.op("act", "copy", R=[pb], W=[s], out=s[:], in_=pb[:])
                S.dma("pool", gk[tt * 128:(tt + 1) * 128, :], s[:, 0:512], R=[s], W=[RX["gk"][tt]])
                S.dma("pool", gv[tt * 128:(tt + 1) * 128, :], s[:, 512:1024], R=[s], W=[RX["gv"][tt]])
                yield

    def gdn_pre(l, si, t0, T, is_s):
        with ExitStack() as es:
            for _ in gdn_pre_gen(l, si, t0, T, is_s, es, min(512, T), 2):
                pass
            S.barrier()

    def pipeline(gens, depth):
        gens = list(gens)
        active = []
        while gens or active:
            if gens and len(active) < depth:
                active.append(gens.pop(0))
            for g in list(active):
                try:
                    next(g)
                except StopIteration:
                    active.remove(g)

    def run_weighted(main, side, ratio):
        main_alive = side_alive = True
        while main_alive or side_alive:
            if main_alive:
                for _ in range(ratio):
                    try:
                        next(main)
                    except StopIteration:
                        main_alive = False
                        break
            if side_alive:
                try:
                    next(side)
                except StopIteration:
                    side_alive = False

    def run_rr(gens):
        gens = list(gens)
        while gens:
            for g in list(gens):
                try:
                    next(g)
                except StopIteration:
                    gens.remove(g)

    def combine(t0, T, w_bc, zi, mi):
        for _ in combine_gen(t0, T, w_bc, zi, mi):
            pass

    def combine_gen(t0, T, w_bc, zi, mi):
        with ExitStack() as es:
            fa = Pool_(es, "cb_a", [128, 512], F32, 2)
            fb = Pool_(es, "cb_b", [128, 512], F32, 2)
            ngts = [ng_tiles(es, "cb%d_" % i) for i in range(2)]
            for i in range(T // 128):
                tt = t0 // 128 + i
                c0 = tt * 128
                a = fa.next(); b = fb.next()
                S.dma("sp", a[:], ofs[c0:c0 + 128, :], R=[RX["ofs"][tt]], W=[a])
                S.dma("sp", b[:], ofb[c0:c0 + 128, :], R=[RX["ofb"][tt]], W=[b])
                S.op("pool", "tensor_tensor", R=[a, b], W=[a], out=a[:], in0=a[:], in1=b[:], op=ALU.add)
                norm_gate_store(ngts[i % 2], a, a[:], w_bc, zi, mi, tt, 128)
                yield
            S.barrier()

    def gdn_scan(l, si, t0, T, is_s):
        with ExitStack() as es:
            run_rr([gdn_chain(l, si, t0, T, is_s, d, es) for d in range(2)])
            S.barrier()
        combine(t0, T, gdnw_bc, 2, 2)

    def gdn_chain(l, si, t0, T, is_s, d, es):
        ntl = T // 128
        pf = "gd%d_" % d
        Sf = sb(es, pf + "S", [128, 4, 128], F32)
        Sb_ = sb(es, pf + "Sb", [128, 4, 128], BF16)
        kTp = Pool_(es, pf + "kT", [128, 4, 128], BF16, 2)
        qTp = Pool_(es, pf + "qT", [128, 4, 128], BF16, 2)
        ktp = Pool_(es, pf + "k", [128, 512], BF16, 2)
        vtp = Pool_(es, pf + "v", [128, 512], BF16, 2)
        bgp = Pool_(es, pf + "bg", [128, 16], F32, 2)
        sm = Pool_(es, pf + "sm", [128, 8, 4], F32, 2)
        X1 = Pool_(es, pf + "X", [128, 4, 128], F32, 1)
        X2 = Pool_(es, pf + "X2", [128, 4, 128], F32, 1)
        Dn = Pool_(es, pf + "Dn", [128, 4, 128], F32, 1)
        Ea = Pool_(es, pf + "Ea", [128, 4, 128], F32, 1)
        Eb = Pool_(es, pf + "Eb", [128, 4, 128], F32, 1)
        Fq = Pool_(es, pf + "Fq", [128, 4, 128], F32, 1)
        Eg = Pool_(es, pf + "Eg", [128, 4, 128], F32, 1)
        Pp = Pool_(es, pf + "P", [128, 4, 128], F32, 2)
        PTp = Pool_(es, pf + "PT", [128, 4, 128], F32, 2)
        TTf = Pool_(es, pf + "TTf", [128, 4, 128], F32, 1)
        qkTp = Pool_(es, pf + "qkT", [128, 4, 128], BF16, 2)
        qgp = Pool_(es, pf + "qg", [128, 4, 128], BF16, 2)
        vbp = Pool_(es, pf + "vb", [128, 4, 128], F32, 1)
        kbp = Pool_(es, pf + "kb", [128, 4, 128], F32, 1)
        kdp = Pool_(es, pf + "kd", [128, 4, 128], BF16, 2)
        Up = Pool_(es, pf + "U", [128, 4, 128], F32, 2)
        WTp = Pool_(es, pf + "WT", [128, 4, 128], BF16, 2)
        vnp = Pool_(es, pf + "vn", [128, 128], BF16, 4)
        op_ = Pool_(es, pf + "o", [64, 512], F32, 2)
        odst, orn = (ofs, "ofs") if d == 0 else (ofb, "ofb")
        Sfr = [Res("gSf%d" % h) for h in range(4)]
        Sbr = [Res("gSb%d" % h) for h in range(4)]
        if is_s:
            S.dma("sp", Sf[:], st_gdn[l, d].rearrange("h k e -> k h e"), W=Sfr)
        else:
            S.op("dve", "memset", W=Sfr, ap=Sf[:], constant=0.0)
        S.op("act", "copy", R=Sfr, W=Sbr, out=Sb_[:], in_=Sf[:])
        idb = ident_f[:].unsqueeze(1).to_broadcast([128, 4, 128])
        v4 = lambda t: t[:].rearrange("p h b -> p (h b)")
        order = range(ntl) if d == 0 else range(ntl - 1, -1, -1)
        for i in order:
            tt = t0 // 128 + i
            c0 = tt * 128
            kT = kTp.next(); qT = qTp.next(); kt = ktp.next(); vt = vtp.next(); bg = bgp.next()
            S.dma("sp", kT[:], gkT[:, :, c0:c0 + 128].rearrange("h p t -> p h t"), R=[RX["gkT"][tt]], W=[kT])
            S.dma("sp", qT[:], gqT[:, :, c0:c0 + 128].rearrange("h p t -> p h t"), R=[RX["gqT"][tt]], W=[qT])
            S.dma("sp", kt[:], gk[c0:c0 + 128, :], R=[RX["gk"][tt]], W=[kt])
            S.dma("sp", vt[:], gv[c0:c0 + 128, :], R=[RX["gv"][tt]], W=[vt])
            S.dma("sp", bg[:], bgs[c0:c0 + 128, :], R=[RX["bgs"][tt]], W=[bg])
            s = sm.next()
            S.op("dve", "tensor_copy", R=[bg], W=[s], out=s[:, 0, :], in_=bg[:, 8 + d * 4:12 + d * 4])
            S.op("dve", "tensor_copy", R=[bg], W=[s], out=s[:, 1, :], in_=bg[:, d * 4:d * 4 + 4])
            pg = psf()
            S.op("pe", "matmul", R=[utri2, s], W=[pg], out=pg[:, 0:4], lhsT=utri2[:, d, :], rhs=s[:, 0, :],
                 start=True, stop=True)
            S.op("pe", "matmul", R=[bd1, s], W=[pg], out=pg[:, 4:8], lhsT=bd1[:], rhs=s[:, 0, :], start=True, stop=True)
            S.op("pe", "matmul", R=[selc, s], W=[pg], out=pg[:, 8:12], lhsT=selc[:, 0, :], rhs=s[:, 0, :],
                 start=True, stop=True)
            S.op("pe", "matmul", R=[selc, s], W=[pg], out=pg[:, 12:16], lhsT=selc[:, 1, :], rhs=s[:, 0, :],
                 start=True, stop=True)
            S.op("dve", "tensor_copy", R=[pg], W=[s], out=s[:, 2, :], in_=pg[:, 0:4])
            S.op("act", "activation", R=[pg], W=[s], out=s[:, 6:8, :].rearrange("p c h -> p (c h)"), in_=pg[:, 8:16],
                 func=AF.Exp)
            S.op("dve", "tensor_tensor", R=[pg], W=[s], out=s[:, 4, :], in0=pg[:, 4:8], in1=s[:, 2, :], op=ALU.subtract)
            S.op("act", "activation", R=[s], W=[s], out=s[:, 4, :], in_=s[:, 4, :], func=AF.Exp)
            S.op("act", "activation", R=[s], W=[s], out=s[:, 5, :], in_=s[:, 2, :], func=AF.Exp)
            S.op("dve", "tensor_tensor", R=[s], W=[s], out=s[:, 5, :], in0=s[:, 5, :], in1=s[:, 1, :], op=ALU.mult)
            ckpt('g0')
            yield
            x1 = X1.next(); x2 = X2.next()
            S.op("dve", "tensor_tensor", R=[ident_f, s], W=[x1], out=x1[:], in0=idb,
                 in1=s[:, 2, :].unsqueeze(2).to_broadcast([128, 4, 128]), op=ALU.mult)
            S.op("pool", "tensor_tensor", R=[ident_f, s], W=[x2], out=x2[:], in0=idb,
                 in1=s[:, 1, :].unsqueeze(2).to_broadcast([128, 4, 128]), op=ALU.mult)
            pR = psf(); pRb = psf()
            S.op("pe", "matmul", R=[ones_f, x1], W=[pR], out=pR[:], lhsT=ones_f[:], rhs=v4(x1), start=True, stop=True)
            S.op("pe", "matmul", R=[ones_f, x2], W=[pRb], out=pRb[:], lhsT=ones_f[:], rhs=v4(x2), start=True, stop=True)
            dn = Dn.next()
            S.op("dve", "tensor_tensor", R=[pR, s], W=[dn], out=dn[:], in0=pR[:].rearrange("p (h b) -> p h b", h=4),
                 in1=s[:, 2, :].unsqueeze(2).to_broadcast([128, 4, 128]), op=ALU.subtract)
            eg = Eg.next()
            S.op("act", "activation", R=[pR], W=[eg], out=v4(eg), in_=pR[:], func=AF.Exp)
            ea = Ea.next(); eb = Eb.next(); fq = Fq.next()
            S.op("dve", "tensor_scalar", R=[dn], W=[ea], out=ea[:], in0=dn[:], scalar1=-1.0, scalar2=0.0,
                 op0=ALU.mult, op1=ALU.min)
            S.op("dve", "tensor_scalar", R=[dn], W=[eb], out=eb[:], in0=dn[:], scalar1=0.0, scalar2=None, op0=ALU.min)
            S.op("act", "activation", R=[ea], W=[ea], out=ea[:], in_=ea[:], func=AF.Exp)
            S.op("act", "activation", R=[eb], W=[eb], out=eb[:], in_=eb[:], func=AF.Exp)
            S.op("dve", "tensor_tensor", R=[ea, gmask2], W=[ea], out=ea[:], in0=ea[:],
                 in1=gmask2[:, d, 0, :].unsqueeze(1).to_broadcast([128, 4, 128]), op=ALU.mult)
            S.op("dve", "tensor_tensor", R=[ea, s], W=[ea], out=ea[:], in0=ea[:],
                 in1=s[:, 1, :].unsqueeze(2).to_broadcast([128, 4, 128]), op=ALU.mult)
            S.op("pool", "tensor_tensor", R=[eb, gmask2], W=[fq], out=fq[:], in0=eb[:],
                 in1=gmask2[:, d, 2, :].unsqueeze(1).to_broadcast([128, 4, 128]), op=ALU.mult)
            S.op("pool", "tensor_tensor", R=[eb, gmask2], W=[eb], out=eb[:], in0=eb[:],
                 in1=gmask2[:, d, 1, :].unsqueeze(1).to_broadcast([128, 4, 128]), op=ALU.mult)
            S.op("dve", "tensor_tensor", R=[eb, pRb], W=[eb], out=eb[:], in0=eb[:],
                 in1=pRb[:].rearrange("p (h b) -> p h b", h=4), op=ALU.mult)
            ckpt('g1')
            yield
            qg = qgp.next()
            S.op("pool", "tensor_tensor", R=[qT, eg], W=[qg], out=qg[:], in0=qT[:], in1=eg[:], op=ALU.mult)
            pK = psf(); pQ = psf()
            for h in range(4):
                S.op("pe", "matmul", R=[kT], W=[pK], out=pK[:, h * 128:(h + 1) * 128], lhsT=kT[:, h, :], rhs=kT[:, h, :],
                     start=True, stop=True)
                S.op("pe", "matmul", R=[kT, qT], W=[pQ], out=pQ[:, h * 128:(h + 1) * 128], lhsT=kT[:, h, :],
                     rhs=qT[:, h, :], start=True, stop=True)
            P = Pp.next(); PT = PTp.next(); ttf = TTf.next(); qkT = qkTp.next()
            pKv = pK[:].rearrange("p (h b) -> p h b", h=4)
            S.op("dve", "tensor_tensor", R=[pK, ea], W=[P], out=P[:], in0=pKv, in1=ea[:], op=ALU.mult)
            S.op("dve", "tensor_tensor", R=[pK, eb], W=[PT], out=PT[:], in0=pKv, in1=eb[:], op=ALU.mult)
            S.op("pool", "tensor_tensor", R=[PT, ident_f], W=[ttf], out=ttf[:], in0=PT[:], in1=idb, op=ALU.add)
            S.op("dve", "tensor_tensor", R=[pQ, fq], W=[qkT], out=qkT[:], in0=pQ[:].rearrange("p (h b) -> p h b", h=4),
                 in1=fq[:], op=ALU.mult)
            ckpt('g2')
            yield
            for lev in range(1, 6):
                p1 = psf()
                for h in range(4):
                    S.op("pe", "matmul", R=[PT, P], W=[p1], out=p1[:, h * 128:(h + 1) * 128], lhsT=PT[:, h, :],
                         rhs=P[:, h, :], start=True, stop=True)
                Pn = Pp.next()
                S.op("dve", "tensor_copy", R=[p1], W=[Pn], out=v4(Pn), in_=p1[:])
                p3 = psf()
                for h in range(4):
                    S.op("pe", "matmul", R=[Pn, ttf], W=[p3], out=p3[:, h * 128:(h + 1) * 128], lhsT=Pn[:, h, :],
                         rhs=ttf[:, h, :], start=True, stop=True)
                if lev < 5:
                    p2 = psf()
                    for h in range(4):
                        if os.environ.get("KNOTR"):
                            S.op("pe", "matmul", R=[PT, P], W=[p2], out=p2[:, h * 128:(h + 1) * 128], lhsT=P[:, h, :],
                                 rhs=PT[:, h, :], start=True, stop=True)
                        else:
                            S.op("pe", "transpose", R=[Pn, ident_f], W=[p2], out=p2[:, h * 128:(h + 1) * 128],
                                 in_=Pn[:, h, :], identity=ident_f[:])
                    PTn = PTp.next()
                    S.op("act", "copy", R=[p2], W=[PTn], out=v4(PTn), in_=p2[:])
                S.op("dve", "tensor_tensor", R=[ttf, p3], W=[ttf], out=v4(ttf), in0=v4(ttf), in1=p3[:], op=ALU.add)
                P = Pn
                if lev < 5:
                    PT = PTn
                ckpt('g3')
                yield
            vb = vbp.next(); kb = kbp.next(); kd = kdp.next()
            S.op("dve", "tensor_tensor", R=[vt, s], W=[vb], out=vb[:], in0=vt[:].rearrange("p (h e) -> p h e", h=4),
                 in1=s[:, 1, :].unsqueeze(2).to_broadcast([128, 4, 128]), op=ALU.mult)
            S.op("pool", "tensor_tensor", R=[kt, s], W=[kb], out=kb[:], in0=kt[:].rearrange("p (h e) -> p h e", h=4),
                 in1=s[:, 5, :].unsqueeze(2).to_broadcast([128, 4, 128]), op=ALU.mult)
            S.op("pool", "tensor_tensor", R=[kt, s], W=[kd], out=kd[:], in0=kt[:].rearrange("p (h e) -> p h e", h=4),
                 in1=s[:, 4, :].unsqueeze(2).to_broadcast([128, 4, 128]), op=ALU.mult)
            U = Up.next(); WT = WTp.next()
            pU = psf(); pW = psf()
            for h in range(4):
                S.op("pe", "matmul", R=[ttf, vb], W=[pU], out=pU[:, h * 128:(h + 1) * 128], lhsT=ttf[:, h, :],
                     rhs=vb[:, h, :], start=True, stop=True)
                S.op("pe", "matmul", R=[kb, ttf], W=[pW], out=pW[:, h * 128:(h + 1) * 128], lhsT=kb[:, h, :],
                     rhs=ttf[:, h, :], start=True, stop=True)
            S.op("dve", "tensor_copy", R=[pU], W=[U], out=v4(U), in_=pU[:])
            S.op("act", "copy", R=[pW], W=[WT], out=v4(WT), in_=pW[:])
            ckpt('g4')
            yield
            for cb in ((0, 1) if d == 0 else (1, 0)):
                rows = slice(cb * 64, cb * 64 + 64)
                cols = slice(cb * 64, cb * 64 + 64)
                po = PSF[4 + d]
                for h in range(4):
                    pa = psf()
                    S.op("pe", "matmul", R=[WT, Sbr[h]], W=[pa], out=pa[:, 0:128], lhsT=WT[:, h, :], rhs=Sb_[:, h, :],
                         start=True, stop=True)
                    vn = vnp.next()
                    S.op("dve", "tensor_tensor", R=[U, pa], W=[vn], out=vn[rows, :], in0=U[rows, h, :],
                         in1=pa[rows, 0:128], op=ALU.subtract)
                    S.op("pe", "matmul", R=[qg, Sbr[h]], W=[po], out=po[0:64, h * 128:(h + 1) * 128], lhsT=qg[:, h, cols],
                         rhs=Sb_[:, h, :], start=True, stop=False)
                    S.op("pe", "matmul", R=[qkT, vn], W=[po], out=po[0:64, h * 128:(h + 1) * 128],
                         lhsT=qkT[rows, h, cols], rhs=vn[rows, :], start=False, stop=True)
                    pS = psf()
                    S.op("pe", "matmul", R=[kd, vn], W=[pS], out=pS[:, 0:128], lhsT=kd[rows, h, :], rhs=vn[rows, :],
                         start=True, stop=True)
                    S.op("dve", "scalar_tensor_tensor", R=[Sfr[h], s, pS], W=[Sfr[h]], out=Sf[:, h, :], in0=Sf[:, h, :],
                         scalar=s[:, 6 + cb, h:h + 1], in1=pS[:, 0:128], op0=ALU.mult, op1=ALU.add)
                    S.op("act", "copy", R=[Sfr[h]], W=[Sbr[h]], out=Sb_[:, h, :], in_=Sf[:, h, :])
                    if h % 2 == 1:
                        ckpt('g5')
                        yield
                r0 = c0 + cb * 64
                o = op_.next()
                S.op("act", "copy", R=[po], W=[o], out=o[:], in_=po[0:64, :])
                S.dma("pool", odst[r0:r0 + 64, :], o[:], R=[o], W=[RX[orn][tt]])
                ckpt('g6')
                yield
        if not is_s:
            S.dma("pool", ngdn_out[si - 1, l, d].rearrange("h k e -> k h e"), Sf[:], R=Sfr)

    def phaseE(l, xsrc, xdst):
        with ExitStack() as es:
            wbr = sb(es, "pe_wbr", [128, 12, D], BF16)
            wo = sb(es, "pe_wo", [128, 8, D], BF16)
            with ExitStack() as es0:
                wst = Pool_(es0, "pe_wst", [128, 4, D], F32, 2)
                for q in range(3):
                    w = wst.next()
                    S.dma("sp", w[:], w_branch[l, q].rearrange("(c p) n -> p c n", p=128), W=[w])
                    S.op("dve" if q % 2 == 0 else "pool", "tensor_copy", R=[w], W=[wbr], out=wbr[:, q * 4:q * 4 + 4, :],
                         in_=w[:])
                for q in range(2):
                    w = wst.next()
                    S.dma("sp", w[:], w_out[l].rearrange("(c p) n -> p c n", p=128)[:, q * 4:q * 4 + 4, :], W=[w])
                    S.op("dve" if q % 2 == 0 else "pool", "tensor_copy", R=[w], W=[wo], out=wo[:, q * 4:q * 4 + 4, :],
                         in_=w[:])
                S.barrier()
            gtp = Pool_(es, "pe_gt", [128, 12, 128], BF16, 4)
            mgp = Pool_(es, "pe_mg", [128, 3 * D], BF16, 4)
            mp = Pool_(es, "pe_m", [128, D], F32, 3)
            mbp = Pool_(es, "pe_mb", [128, D], BF16, 3)
            mTp = Pool_(es, "pe_mT", [128, 8, 128], BF16, 3)
            xp = Pool_(es, "pe_x", [128, D], F32, 4)
            tp = Pool_(es, "pe_t", [128, 512], F32, 4)

            def tileE(tt):
                j = 0 if tt < NTS else 1
                c0 = tt * 128
                gt = gtp.next(); mgt = mgp.next(); xt = xp.next()
                S.dma("sp", gt[:], gT[:, :, c0:c0 + 128].rearrange("m (c p) t -> p (m c) t", p=128),
                      R=[RX["gT0"][tt], RX["gT1"][tt], RX["gT2"][tt]], W=[gt])
                S.dma("sp", mgt[:], mg[c0:c0 + 128, :], R=[RX["mg"][tt]], W=[mgt])
                S.dma("sp", xt[:], xsrc[c0:c0 + 128, :], R=[RX["x"][tt]] if l > 0 else [], W=[xt])
                yield
                m = mp.next()
                for q in range(3):
                    for nb in range(2):
                        ps = psf()
                        for c in range(4):
                            S.op("pe", "matmul", R=[gt, wbr], W=[ps], out=ps[:], lhsT=gt[:, q * 4 + c, :],
                                 rhs=wbr[:, q * 4 + c, nb * 512:(nb + 1) * 512], start=(c == 0), stop=(c == 3))
                        cols = slice(nb * 512, (nb + 1) * 512)
                        if q == 0:
                            S.op("dve", "tensor_tensor", R=[ps, mgt], W=[m], out=m[:, cols], in0=ps[:],
                                 in1=mgt[:, cols], op=ALU.mult)
                        else:
                            t = tp.next()
                            S.op("dve", "tensor_tensor", R=[ps, mgt], W=[t], out=t[:], in0=ps[:],
                                 in1=mgt[:, q * D + nb * 512:q * D + (nb + 1) * 512], op=ALU.mult)
                            S.op("pool", "tensor_tensor", R=[t, m], W=[m], out=m[:, cols], in0=m[:, cols], in1=t[:],
                                 op=ALU.add)
                yield
                mb = mbp.next()
                S.op("act", "copy", R=[m], W=[mb], out=mb[:], in_=m[:])
                yield
                pb = psb()
                transposes_to(pb, mb, [(mb[:, c * 128:(c + 1) * 128], 128, c * 128) for c in range(8)])
                mT = mTp.next()
                S.op("act", "copy", R=[pb], W=[mT], out=mT[:].rearrange("p c t -> p (c t)"), in_=pb[:])
                yield
                for nb in range(2):
                    ps = psf()
                    for c in range(8):
                        S.op("pe", "matmul", R=[mT, wo], W=[ps], out=ps[:], lhsT=mT[:, c, :],
                             rhs=wo[:, c, nb * 512:(nb + 1) * 512], start=(c == 0), stop=(c == 7))
                    cols = slice(nb * 512, (nb + 1) * 512)
                    t = tp.next()
                    S.op("dve", "tensor_tensor", R=[ps, gate_bc], W=[t], out=t[:], in0=ps[:], in1=gate_bc[:, j, cols],
                         op=ALU.mult)
                    S.op("pool", "tensor_tensor", R=[t, xt], W=[xt], out=xt[:, cols], in0=xt[:, cols], in1=t[:],
                         op=ALU.add)
                S.dma("pool", xdst[c0:c0 + 128, :], xt[:], R=[xt], W=[RX["x"][tt]])

            gens = [tileE(tt) for tt in range(NT)]
            active = []
            while gens or active:
                if gens:
                    active.append(gens.pop(0))
                for g in list(active):
                    try:
                        next(g)
                    except StopIteration:
                        active.remove(g)
            S.barrier()

    try:
        for l in range(depth):
            layer(l)
    except StopBuild as e:
        print("build stopped at", e)
        root2 = None
    S.barrier()
    S.emit()
    build.stats = (S.n_instr, S.n_wait)
    return nc


_CACHE = {}


def _get_nc(T_S, depth, debug):
    key = (T_S, depth, debug)
    if key not in _CACHE:
        _CACHE[key] = build(T_S, depth, debug)
    return _CACHE[key]


def run_cfg(inp, T_S, depth, debug=False):
    f = lambda a: np.ascontiguousarray(np.asarray(a), dtype=np.float32)
    xs = f(inp["x_sample"])
    xp = f(inp["x_prompt"])
    n_core = 8
    nsb = xs.shape[0]
    consts = make_consts(T_S)
    shared = {
        "norm_w": f(inp["norm_w"])[:depth], "w_ada": f(inp["w_ada"])[:depth], "b_ada": f(inp["b_ada"])[:depth],
        "w_in": f(inp["w_in"])[:depth], "qk_norm_w": f(inp["qk_norm_w"])[:depth],
        "diff_lambda": f(inp["diff_lambda"])[:depth], "subln_w": f(inp["subln_w"])[:depth],
        "ret_decay": f(inp["ret_decay"])[:depth].reshape(depth, 8), "ret_norm_w": f(inp["ret_norm_w"])[:depth],
        "conv_w": f(inp["conv_w"])[:depth], "gdn_a_log": f(inp["gdn_a_log"])[:depth].reshape(depth, 8),
        "gdn_dt_bias": f(inp["gdn_dt_bias"])[:depth].reshape(depth, 8), "gdn_norm_w": f(inp["gdn_norm_w"])[:depth],
        "w_branch": f(inp["w_branch"])[:depth], "w_out": f(inp["w_out"])[:depth],
    }
    for k, v in consts.items():
        shared["c_" + k] = v
    ck = f(inp["cache_attn_k"])
    cv = f(inp["cache_attn_v"])
    sr = f(inp["state_ret"])
    sgd = f(inp["state_gdn"])
    c = f(inp["c"])
    cctx = f(inp["c_ctx"])
    in_maps = []
    for core in range(n_core):
        b = core % nsb
        m = dict(shared)
        m["x_in"] = np.ascontiguousarray(np.concatenate(
            [xs[b, :T_S]] + [xp[core * NPR + i] for i in range(NPR)], axis=0))
        m["cond"] = np.ascontiguousarray(np.stack([c[b], cctx], axis=0))
        m["cache_k"] = np.ascontiguousarray(ck[b, :depth].reshape(depth, PAST, 512))
        m["cache_v"] = np.ascontiguousarray(cv[b, :depth].reshape(depth, PAST, 512))
        m["st_ret"] = np.ascontiguousarray(sr[b, :depth])
        m["st_gdn"] = np.ascontiguousarray(sgd[b, :depth])
        in_maps.append(m)
    nc = _get_nc(T_S, depth, debug)
    res = run_bass_kernel_spmd(nc, in_maps, core_ids=list(range(n_core)))
    R = res.results
    y_s = np.stack([np.asarray(R[b]["y"])[:T_S] for b in range(nsb)], axis=0).astype(np.float32)
    y_p = np.stack([np.asarray(R[core]["y"])[T_S + i * TP:T_S + (i + 1) * TP]
                    for core in range(n_core) for i in range(NPR)], axis=0).astype(np.float32)
    nk = np.concatenate([np.asarray(R[core]["nk"]) for core in range(n_core)], axis=0).astype(np.float32)
    nv = np.concatenate([np.asarray(R[core]["nv"]) for core in range(n_core)], axis=0).astype(np.float32)
    nret = np.concatenate([np.asarray(R[core]["nret"]) for core in range(n_core)], axis=0).astype(np.float32)
    ngdn = np.concatenate([np.asarray(R[core]["ngdn"]) for core in range(n_core)], axis=0).astype(np.float32)
    nk = nk.reshape(n_core * NPR, depth, TP, 4, 2, 64)
    nv = nv.reshape(n_core * NPR, depth, TP, 4, 128)
    outs = (y_p, y_s, nk, nv, nret, ngdn)
    if debug:
        return outs, R
    return outs


def kernel(**inputs):
    return run_cfg(inputs, 4096, DEPTH, False)
```

```python
import math
from contextlib import ExitStack
import numpy as np
import ml_dtypes
import concourse.bass as bass
import concourse.mybir as mybir
from concourse.bass_utils import run_bass_kernel_spmd

F32 = mybir.dt.float32
BF16 = mybir.dt.bfloat16
AF = mybir.ActivationFunctionType
ALU = mybir.AluOpType
AX = mybir.AxisListType

D = 1024
DEPTH = 4
TP = 256
NPR = 2
PAST = 256
D_IN = 8720
EPS = 1e-6
CH = 64


import os


class StopBuild(Exception):
    pass


def ckpt(name):
    if os.environ.get("KSTOP", "") == name:
        raise StopBuild(name)


class Res:
    __slots__ = ("name", "w", "r", "excl")

    def __init__(self, name=""):
        self.name = name
        self.w = None
        self.r = {}
        self.excl = False


class Tile:
    def __init__(self, h, name, psum=False):
        self.h = h
        self.res = Res(name)
        self.res.excl = psum

    def __getitem__(self, k):
        return self.h[k]


def _res(x):
    out = []
    for t in x:
        if isinstance(t, (list, tuple)):
            out.extend(_res(t))
        elif isinstance(t, Res):
            out.append(t)
        else:
            out.append(t.res)
    return out


class Sched:
    ENG = ("pe", "act", "dve", "pool", "sp")

    def __init__(self, nc, n_dma_slots=8):
        self.nc = nc
        self.streams = {e: [] for e in self.ENG}
        self.sems = {}
        self.cnt = {}
        for e in ("pe", "act", "dve", "pool"):
            self.sems[e] = nc.alloc_semaphore("s_" + e)
            self.cnt[e] = 0
        self.nslots = n_dma_slots
        self.dq = {}
        for q in ("sp", "pool", "act"):
            slots = []
            for i in range(n_dma_slots):
                k = "d_%s%d" % (q, i)
                self.sems[k] = nc.alloc_semaphore(k)
                slots.append([k, 0])
            self.dq[q] = [slots, 0]
        self.seen = {e: {} for e in self.ENG}
        self.n_instr = 0
        self.n_wait = 0

    def _wait(self, eng, key, val):
        if val is None or val <= 0:
            return
        if eng == "pe" and key == "pe":
            return
        s = self.seen[eng]
        if s.get(key, 0) >= val:
            return
        s[key] = val
        sem = self.sems[key]
        self.streams[eng].append(lambda e, sem=sem, val=val: e.wait_ge(sem, val))
        self.n_wait += 1

    def _deps(self, eng, reads, writes, is_dma=False):
        for r in reads:
            if r.w is not None:
                self._wait(eng, r.w[0], r.w[1])
        for w in writes:
            if w.w is not None:
                if is_dma or not (w.w[0] == eng):
                    self._wait(eng, w.w[0], w.w[1])
            for k, v in w.r.items():
                if (not is_dma) and k == eng:
                    continue
                self._wait(eng, k, v)

    def _mark(self, key, val, reads, writes):
        for r in reads:
            if r.r.get(key, 0) < val:
                r.r[key] = val
        for w in writes:
            w.w = (key, val)
            w.r = {}

    def op(self, eng, method, R=(), W=(), **kw):
        reads = _res(R)
        writes = _res(W)
        ex = [r for r in reads if r.excl]
        if ex:
            reads = [r for r in reads if not r.excl]
            writes = writes + [r for r in ex if r not in writes]
        self._deps(eng, reads, writes)
        self.cnt[eng] += 1
        val = self.cnt[eng]
        sem = self.sems[eng]
        import traceback
        org = traceback.extract_stack(limit=3)[0]
        org = "%s:%d" % (org.name, org.lineno)

        def _f(e, m=method, kw=kw, sem=sem, org=org):
            try:
                return getattr(e, m)(**kw).then_inc(sem, 1)
            except Exception as ex:
                raise RuntimeError("emit failed at %s (%s): %s" % (org, m, ex)) from ex
        self.streams[eng].append(_f)
        self._mark(eng, val, reads, writes)
        self.n_instr += 1

    def dma(self, q, out, in_, R=(), W=(), **kw):
        reads = _res(R)
        writes = _res(W)
        slots, idx = self.dq[q]
        slot = slots[idx % self.nslots]
        self.dq[q][1] = idx + 1
        key = slot[0]
        if slot[1] > 0:
            self._wait(q, key, slot[1])
        self._deps(q, reads, writes, is_dma=True)
        slot[1] += 16
        val = slot[1]
        sem = self.sems[key]
        import traceback
        org = traceback.extract_stack(limit=3)[0]
        org = "%s:%d" % (org.name, org.lineno)

        def _f(e, out=out, in_=in_, sem=sem, kw=kw, org=org):
            try:
                return e.dma_start(out=out, in_=in_, **kw).then_inc(sem, 16)
            except Exception as ex:
                raise RuntimeError("dma emit failed at %s: %s" % (org, ex)) from ex
        self.streams[q].append(_f)
        self._mark(key, val, reads, writes)
        self.n_instr += 1

    def barrier(self):
        for e in self.ENG:
            for k in ("pe", "act", "dve", "pool"):
                self._wait(e, k, self.cnt[k])
            for q in self.dq:
                for slot in self.dq[q][0]:
                    if slot[1] > 0:
                        self._wait(e, slot[0], slot[1])

    def emit(self):
        nc = self.nc
        st = self.streams
        with nc.Block() as block:
            @block.sync
            def _(e):
                for f in st["sp"]:
                    f(e)

            @block.tensor
            def _(e):
                for f in st["pe"]:
                    f(e)

            @block.scalar
            def _(e):
                for f in st["act"]:
                    f(e)

            @block.vector
            def _(e):
                for f in st["dve"]:
                    f(e)

            @block.gpsimd
            def _(e):
                for f in st["pool"]:
                    f(e)


def make_consts(T_S):
    c = {}
    c["ident"] = np.eye(128, dtype=np.float32)
    n_rows = T_S // 64
    row = np.repeat(np.arange(n_rows, dtype=np.float32), 64)
    col = np.tile(np.arange(64, dtype=np.float32), n_rows)
    inv = (1.0 / (10000.0 ** (np.arange(16, dtype=np.float32) / 16))).astype(np.float32)
    ar = row[:, None] * inv
    ac = col[:, None] * inv
    ang = np.concatenate([ar, ar, ac, ac], axis=-1).astype(np.float32)
    cos = np.cos(ang).astype(np.float32)
    sin = np.sin(ang).astype(np.float32)
    sgn = np.tile(np.concatenate([-np.ones(16), np.ones(16)]), 2).astype(np.float32)
    c["ropec"] = cos
    c["ropes"] = (sin * sgn).astype(np.float32)
    a = np.arange(64)[:, None]
    b = np.arange(64)[None, :]
    low = (a > b).astype(np.float32)
    up = (b > a).astype(np.float32)
    upi = (b >= a).astype(np.float32)
    lowi = (a >= b).astype(np.float32)
    gm = np.zeros((64, 2, 3, 64), np.float32)
    gm[:, 0, 0] = -low
    gm[:, 0, 1] = -up
    gm[:, 0, 2] = upi
    gm[:, 1, 0] = -up
    gm[:, 1, 1] = -low
    gm[:, 1, 2] = lowi
    c["gmask"] = gm
    a2 = np.arange(128)[:, None]
    b2 = np.arange(128)[None, :]
    same = (a2 // 64 == b2 // 64)
    gm2 = np.zeros((128, 2, 3, 128), np.float32)
    gm2[:, 0, 0] = -1.0 * ((a2 > b2) & same)
    gm2[:, 0, 1] = -1.0 * ((b2 > a2) & same)
    gm2[:, 0, 2] = ((b2 >= a2) & same)
    gm2[:, 1, 0] = -1.0 * ((b2 > a2) & same)
    gm2[:, 1, 1] = -1.0 * ((a2 > b2) & same)
    gm2[:, 1, 2] = ((a2 >= b2) & same)
    c["gmask2"] = gm2
    ut2 = np.zeros((128, 2, 128), np.float32)
    ut2[:, 0] = ((a2 <= b2) & same)
    ut2[:, 1] = ((a2 >= b2) & same)
    c["utri2"] = ut2
    c["bd1"] = same.astype(np.float32)
    sel = np.zeros((128, 2, 128), np.float32)
    sel[0:64, 0, :] = 1.0
    sel[64:128, 1, :] = 1.0
    c["selc"] = sel
    ut = np.zeros((64, 2, 64), np.float32)
    ut[:, 0] = (a <= b).astype(np.float32)
    ut[:, 1] = (a >= b).astype(np.float32)
    c["utri"] = ut
    j = np.arange(128)[:, None].astype(np.float32)
    i = np.arange(128)[None, :].astype(np.float32)
    rr = np.zeros((128, 2, 128), np.float32)
    rm = np.zeros((128, 2, 128), np.float32)
    rr[:, 0] = np.maximum(i - j, 0)
    rm[:, 0] = (i >= j)
    rr[:, 1] = np.maximum(j - i, 0)
    rm[:, 1] = (j >= i)
    c["rrel"] = rr
    c["rmask"] = rm
    rq = np.zeros((64, 2, 128), np.float32)
    rq[:, 0] = (np.arange(128) + 1.0)[None, :]
    rq[:, 1] = (128.0 - np.arange(128))[None, :]
    c["rqexp"] = rq
    rk = np.zeros((128, 2), np.float32)
    rk[:, 0] = 127.0 - np.arange(128)
    rk[:, 1] = np.arange(128)
    c["rkexp"] = rk
    return c


CONST_SHAPES = lambda T_S: {k: v.shape for k, v in make_consts(T_S).items()}


def build(T_S=4096, depth=DEPTH, debug=False):
    nc = bass.Bass("TRN2", target_bir_lowering=False)
    S = Sched(nc)
    TT = T_S + NPR * TP
    NT = TT // 128
    NTS = T_S // 128
    seqs = [(0, T_S, True)] + [(T_S + i * TP, TP, False) for i in range(NPR)]
    lam_inits = [0.8 - 0.6 * math.exp(-0.3 * l) for l in range(depth)]

    def din(name, shape, dt=F32):
        return nc.dram_tensor(name, list(shape), dt, kind="ExternalInput").ap()

    def dout(name, shape, dt=F32):
        return nc.dram_tensor(name, list(shape), dt, kind="ExternalOutput").ap()

    def scr(name, shape, dt):
        if debug:
            return nc.dram_tensor(name, list(shape), dt, kind="ExternalOutput").ap()
        return nc.dram_tensor(name, list(shape), dt).ap()

    x_in = din("x_in", [TT, D])
    cond = din("cond", [2, D])
    cache_k = din("cache_k", [depth, PAST, 512])
    cache_v = din("cache_v", [depth, PAST, 512])
    st_ret = din("st_ret", [depth, 2, 4, 64, 128])
    st_gdn = din("st_gdn", [depth, 2, 4, 128, 128])
    norm_w = din("norm_w", [depth, D])
    w_ada = din("w_ada", [depth, D, 3 * D])
    b_ada = din("b_ada", [depth, 3 * D])
    w_in = din("w_in", [depth, D, D_IN])
    qk_norm_w = din("qk_norm_w", [depth, 2, 64])
    diff_lambda = din("diff_lambda", [depth, 4, 64])
    subln_w = din("subln_w", [depth, 128])
    ret_decay = din("ret_decay", [depth, 8])
    ret_norm_w = din("ret_norm_w", [depth, 128])
    conv_w = din("conv_w", [depth, 5, 1536])
    gdn_a_log = din("gdn_a_log", [depth, 8])
    gdn_dt_bias = din("gdn_dt_bias", [depth, 8])
    gdn_norm_w = din("gdn_norm_w", [depth, 128])
    w_branch = din("w_branch", [depth, 3, 512, D])
    w_out = din("w_out", [depth, D, D])
    cst = {k: din("c_" + k, shp) for k, shp in CONST_SHAPES(T_S).items()}
    y_out = dout("y", [TT, D])
    nk_out = dout("nk", [NPR, depth, TP, 512])
    nv_out = dout("nv", [NPR, depth, TP, 512])
    nret_out = dout("nret", [NPR, depth, 2, 4, 64, 128])
    ngdn_out = dout("ngdn", [NPR, depth, 2, 4, 128, 128])
    xcur = scr("xcur", [TT, D], F32)
    aqT = scr("aqT", [4, 128, TT], BF16)
    akT = scr("akT", [4, 128, TT], BF16)
    av = scr("av", [TT, 512], BF16)
    zg = scr("zg", [3, TT, 512], BF16)
    bqT = scr("bqT", [4, 64, TT], BF16)
    bkT = scr("bkT", [4, 64, TT], BF16)
    bk = scr("bk", [TT, 256], BF16)
    bv = scr("bv", [TT, 512], BF16)
    cT = scr("cT", [1536, TT], F32)
    gqT = scr("gqT", [4, 128, TT], BF16)
    gkT = scr("gkT", [4, 128, TT], BF16)
    gk = scr("gk", [TT, 512], BF16)
    gv = scr("gv", [TT, 512], BF16)
    bgs = scr("bgs", [TT, 16], F32)
    mg = scr("mg", [TT, 3 * D], BF16)
    ofs = scr("ofs", [TT, 512], F32)
    ofb = scr("ofb", [TT, 512], F32)
    gT = scr("gT", [3, 512, TT], BF16)

    def tres(n):
        return [Res("%s%d" % (n, i)) for i in range(NT)]
    RX = {n: tres(n) for n in ("x", "aqT", "akT", "av", "zg0", "zg1", "zg2", "bqT", "bkT", "bk", "bv", "cT",
                               "gqT", "gkT", "gk", "gv", "bgs", "mg", "ofs", "ofb", "gT0", "gT1", "gT2")}
    R_out = Res("outs")

    uid = [0]

    def sb(es, name, shape, dt):
        uid[0] += 1
        nm = "%s_u%d" % (name, uid[0])
        return Tile(es.enter_context(nc.sbuf_tensor(nm, list(shape), dt)), nm)

    class Pool_:
        def __init__(self, es, name, shape, dt, n):
            self.t = [sb(es, "%s_%d" % (name, i), shape, dt) for i in range(n)]
            self.i = 0

        def next(self):
            t = self.t[self.i % len(self.t)]
            self.i += 1
            return t

    root = ExitStack()
    PSF = [Tile(nc.alloc_psum_tensor("psf%d" % i, [128, 512], F32), "psf%d" % i, True) for i in range(6)]
    PSB = [Tile(nc.alloc_psum_tensor("psb%d" % i, [128, 1024], BF16), "psb%d" % i, True) for i in range(2)]
    psc = [0, 0]

    psf_banks = [[0, 1, 2, 3]]

    def psf():
        psc[0] += 1
        bk = psf_banks[0]
        return PSF[bk[psc[0] % len(bk)]]

    def psb():
        psc[1] += 1
        return PSB[psc[1] % 2]

    ident_f = sb(root, "ident_f", [128, 128], F32)
    ident_b = sb(root, "ident_b", [128, 128], BF16)
    ones_f = sb(root, "ones_f", [128, 128], F32)
    ropec = sb(root, "ropec", [128, NTS, 64], F32)
    ropes = sb(root, "ropes", [128, NTS, 64], F32)
    gmask = sb(root, "gmask", [64, 2, 3, 64], F32)
    utri = sb(root, "utri", [64, 2, 64], F32)
    gmask2 = sb(root, "gmask2", [128, 2, 3, 128], F32)
    utri2 = sb(root, "utri2", [128, 2, 128], F32)
    bd1 = sb(root, "bd1", [128, 128], F32)
    selc = sb(root, "selc", [128, 2, 128], F32)
    rrel = sb(root, "rrel", [128, 2, 128], F32)
    rmask = sb(root, "rmask", [128, 2, 128], F32)
    rqexp = sb(root, "rqexp", [64, 2, 128], F32)
    rkexp = sb(root, "rkexp", [128, 2], F32)
    gate_bc = sb(root, "gate_bc", [128, 2, D], F32)
    wq_bc = sb(root, "wq_bc", [128, 2, 64], F32)
    subw_bc = sb(root, "subw_bc", [128, 128], F32)
    retw_bc = sb(root, "retw_bc", [128, 128], F32)
    gdnw_bc = sb(root, "gdnw_bc", [128, 128], F32)
    lamt = sb(root, "lamt", [128, 8], F32)
    dl_bc = sb(root, "dl_bc", [128, 4, 64], F32)
    lg_bc = sb(root, "lg_bc", [128, 8], F32)
    rdmat = sb(root, "rdmat", [128, 2, 4, 128], F32)
    rqdec = sb(root, "rqdec", [64, 2, 4, 128], F32)
    rkdec = sb(root, "rkdec", [128, 2, 4], F32)
    rcdec = sb(root, "rcdec", [128, 8], F32)
    negA_bc = sb(root, "negA_bc", [128, 8], F32)
    dtb_bc = sb(root, "dtb_bc", [128, 8], F32)
    convw = sb(root, "convw", [128, 12, 5], F32)
    tmp8 = sb(root, "tmp8", [128, 8], F32)

    S.dma("sp", ident_f[:], cst["ident"], W=[ident_f])
    S.op("dve", "tensor_copy", R=[ident_f], W=[ident_b], out=ident_b[:], in_=ident_f[:])
    S.op("dve", "memset", W=[ones_f], ap=ones_f[:], constant=1.0)
    S.dma("sp", ropec[:], cst["ropec"].rearrange("(n p) d -> p n d", p=128), W=[ropec])
    S.dma("sp", ropes[:], cst["ropes"].rearrange("(n p) d -> p n d", p=128), W=[ropes])
    S.dma("sp", gmask[:], cst["gmask"], W=[gmask])
    S.dma("sp", utri[:], cst["utri"], W=[utri])
    S.dma("sp", gmask2[:], cst["gmask2"], W=[gmask2])
    S.dma("sp", utri2[:], cst["utri2"], W=[utri2])
    S.dma("sp", bd1[:], cst["bd1"], W=[bd1])
    S.dma("sp", selc[:], cst["selc"], W=[selc])
    S.dma("sp", rrel[:], cst["rrel"], W=[rrel])
    S.dma("sp", rmask[:], cst["rmask"], W=[rmask])
    S.dma("sp", rqexp[:], cst["rqexp"], W=[rqexp])
    S.dma("sp", rkexp[:], cst["rkexp"], W=[rkexp])

    def rsqrt_inplace(t, ap, scale, eps):
        S.op("dve", "tensor_scalar", R=[t], W=[t], out=ap, in0=ap, scalar1=scale, scalar2=eps,
             op0=ALU.mult, op1=ALU.add)
        S.op("act", "activation", R=[t], W=[t], out=ap, in_=ap, func=AF.Sqrt)
        S.op("dve", "reciprocal", R=[t], W=[t], out=ap, in_=ap)

    def transposes_to(ps_t, src_t, blocks, rows=128):
        for (sap, w, off) in blocks:
            S.op("pe", "transpose", R=[src_t, ident_b], W=[ps_t], out=ps_t[0:w, off:off + rows], in_=sap,
                 identity=ident_b[0:rows, 0:rows])

    def layer(l):
        last = (l == depth - 1)
        xsrc = x_in if l == 0 else xcur
        xdst = y_out if last else xcur
        esH = ExitStack()
        hT = sb(esH, "hT", [128, 8, TT], BF16)
        esA = ExitStack()
        A_bc = sb(esA, "A_bc", [128, 2, D], F32)
        sh_bc = sb(esA, "sh_bc", [128, 2, D], F32)
        with ExitStack() as es:
            nw_bc = sb(es, "nw_bc", [128, D], F32)
            cs = sb(es, "cs", [128, 2, 8], F32)
            rep = sb(es, "rep", [128, 16, 128], F32)
            bada = sb(es, "bada", [1, 3 * D], F32)
            wa = Pool_(es, "wa", [128, 8, 512], F32, 2)
            S.dma("sp", nw_bc[:], norm_w[l].partition_broadcast(128), W=[nw_bc])
            S.dma("sp", cs[:], cond.rearrange("j (p c) -> p j c", c=8), W=[cs])
            S.dma("sp", bada[:], b_ada[l:l + 1, :], W=[bada])
            S.dma("sp", wq_bc[:],
                  qk_norm_w[l].rearrange("a d -> (a d)").partition_broadcast(128).rearrange("p (a d) -> p a d", a=2),
                  W=[wq_bc])
            S.dma("sp", subw_bc[:], subln_w[l].partition_broadcast(128), W=[subw_bc])
            S.dma("sp", retw_bc[:], ret_norm_w[l].partition_broadcast(128), W=[retw_bc])
            S.dma("sp", gdnw_bc[:], gdn_norm_w[l].partition_broadcast(128), W=[gdnw_bc])
            S.dma("sp", dl_bc[:], diff_lambda[l].rearrange("a d -> (a d)").partition_broadcast(128)
                  .rearrange("p (a d) -> p a d", a=4), W=[dl_bc])
            S.dma("sp", lg_bc[:], ret_decay[l].partition_broadcast(128), W=[lg_bc])
            S.dma("sp", negA_bc[:], gdn_a_log[l].partition_broadcast(128), W=[negA_bc])
            S.dma("sp", dtb_bc[:], gdn_dt_bias[l].partition_broadcast(128), W=[dtb_bc])
            for k in range(5):
                S.dma("sp", convw[:, :, k:k + 1], conv_w[l, k].rearrange("(c p o) -> p c o", p=128, o=1), W=[convw],
                      allow_slow_non_contiguous=True)
            S.op("dve", "tensor_scalar", R=[wq_bc], W=[wq_bc], out=wq_bc[:, 0, :], in0=wq_bc[:, 0, :],
                 scalar1=0.125, scalar2=None, op0=ALU.mult)
            S.op("dve", "tensor_scalar", R=[subw_bc], W=[subw_bc], out=subw_bc[:], in0=subw_bc[:],
                 scalar1=float(1.0 - lam_inits[l]), scalar2=None, op0=ALU.mult)
            S.op("dve", "tensor_tensor", R=[dl_bc], W=[dl_bc], out=dl_bc[:, 0, :], in0=dl_bc[:, 0, :],
                 in1=dl_bc[:, 1, :], op=ALU.mult)
            S.op("dve", "tensor_tensor", R=[dl_bc], W=[dl_bc], out=dl_bc[:, 2, :], in0=dl_bc[:, 2, :],
                 in1=dl_bc[:, 3, :], op=ALU.mult)
            S.op("dve", "tensor_reduce", R=[dl_bc], W=[lamt], out=lamt[:, 0:4], in_=dl_bc[:], axis=AX.X, op=ALU.add)
            S.op("act", "activation", R=[lamt], W=[lamt], out=lamt[:, 4:8], in_=lamt[:, 0:4], func=AF.Exp)
            S.op("dve", "tensor_tensor", R=[lamt], W=[lamt], out=lamt[:, 1:2], in0=lamt[:, 6:7], in1=lamt[:, 4:5],
                 op=ALU.subtract)
            S.op("dve", "tensor_scalar", R=[lamt], W=[lamt], out=lamt[:, 0:1], in0=lamt[:, 1:2],
                 scalar1=float(-lam_inits[l]), scalar2=None, op0=ALU.add)
            S.op("act", "activation", R=[lg_bc], W=[lg_bc], out=lg_bc[:], in_=lg_bc[:], func=AF.Exp, scale=-1.0)
            S.op("act", "activation", R=[lg_bc], W=[lg_bc], out=lg_bc[:], in_=lg_bc[:], func=AF.Ln, bias=1.0)
            S.op("dve", "tensor_scalar", R=[lg_bc], W=[lg_bc], out=lg_bc[:], in0=lg_bc[:], scalar1=-1.0,
                 scalar2=None, op0=ALU.mult)
            for d in range(2):
                for h in range(4):
                    u = d * 4 + h
                    S.op("act", "activation", R=[rrel, lg_bc], W=[rdmat], out=rdmat[:, d, h, :], in_=rrel[:, d, :],
                         func=AF.Exp, scale=lg_bc[:, u:u + 1])
                    S.op("dve", "tensor_tensor", R=[rdmat, rmask], W=[rdmat], out=rdmat[:, d, h, :],
                         in0=rdmat[:, d, h, :], in1=rmask[:, d, :], op=ALU.mult)
                    S.op("act", "activation", R=[rqexp, lg_bc], W=[rqdec], out=rqdec[:, d, h, :], in_=rqexp[:, d, :],
                         func=AF.Exp, scale=lg_bc[0:64, u:u + 1])
                    S.op("act", "activation", R=[rkexp, lg_bc], W=[rkdec], out=rkdec[:, d, h:h + 1],
                         in_=rkexp[:, d:d + 1], func=AF.Exp, scale=lg_bc[:, u:u + 1])
            S.op("act", "activation", R=[lg_bc], W=[rcdec], out=rcdec[:], in_=lg_bc[:], func=AF.Exp, scale=128.0)
            S.op("act", "activation", R=[negA_bc], W=[negA_bc], out=negA_bc[:], in_=negA_bc[:], func=AF.Exp)
            S.op("dve", "tensor_scalar", R=[negA_bc], W=[negA_bc], out=negA_bc[:], in0=negA_bc[:], scalar1=-1.0,
                 scalar2=None, op0=ALU.mult)
            S.op("act", "activation", R=[cs], W=[cs], out=cs[:], in_=cs[:], func=AF.Silu)
            S.op("dve", "tensor_copy", R=[cs], W=[rep], out=rep[:],
                 in_=cs[:].rearrange("p j c -> p (j c)").unsqueeze(2).to_broadcast([128, 16, 128]))
            for nb in range(6):
                w = wa.next()
                S.dma("sp", w[:], w_ada[l].rearrange("(p c) n -> p c n", c=8)[:, :, nb * 512:(nb + 1) * 512], W=[w])
                for j in range(2):
                    ps = psf()
                    for c in range(8):
                        S.op("pe", "matmul", R=[rep, w], W=[ps], out=ps[:], lhsT=rep[:, j * 8 + c, :], rhs=w[:, c, :],
                             start=(c == 0), stop=False)
                    S.op("pe", "matmul", R=[ones_f, bada], W=[ps], out=ps[:], lhsT=ones_f[0:1, :],
                         rhs=bada[0:1, nb * 512:(nb + 1) * 512], start=False, stop=True)
                    cols = slice((nb % 2) * 512, (nb % 2) * 512 + 512)
                    if nb < 2:
                        S.op("act", "copy", R=[ps], W=[sh_bc], out=sh_bc[:, j, cols], in_=ps[:])
                    elif nb < 4:
                        S.op("dve", "scalar_tensor_tensor", R=[ps, nw_bc], W=[A_bc], out=A_bc[:, j, cols], in0=ps[:],
                             scalar=1.0, in1=nw_bc[:, cols], op0=ALU.add, op1=ALU.mult)
                    else:
                        S.op("act", "copy", R=[ps], W=[gate_bc], out=gate_bc[:, j, cols], in_=ps[:])
            S.barrier()
        ckpt("A")
        with ExitStack() as es:
            xp = Pool_(es, "xB", [128, D], F32, 4)
            hp = Pool_(es, "hB", [128, D], F32, 4)
            hbp = Pool_(es, "hbB", [128, D], BF16, 4)
            junk = sb(es, "junkB", [128, D], F32)
            ssp = Pool_(es, "ssB", [128, 1], F32, 4)
            def tileB(tt):
                j = 0 if tt < NTS else 1
                xt = xp.next()
                ht = hp.next()
                hb = hbp.next()
                ss = ssp.next()
                S.dma("sp", xt[:], xsrc[tt * 128:(tt + 1) * 128, :], R=[RX["x"][tt]] if l > 0 else [], W=[xt])
                S.op("act", "activation", R=[xt], W=[junk, ss], out=junk[:], in_=xt[:], func=AF.Square, accum_out=ss[:])
                yield
                rsqrt_inplace(ss, ss[:], 1.0 / D, EPS)
                S.op("dve", "scalar_tensor_tensor", R=[xt, ss, A_bc], W=[ht], out=ht[:], in0=xt[:], scalar=ss[:, 0:1],
                     in1=A_bc[:, j, :], op0=ALU.mult, op1=ALU.mult)
                S.op("dve", "tensor_tensor", R=[ht, sh_bc], W=[hb], out=hb[:], in0=ht[:], in1=sh_bc[:, j, :], op=ALU.add)
                yield
                pb = psb()
                transposes_to(pb, hb, [(hb[:, c * 128:(c + 1) * 128], 128, c * 128) for c in range(8)])
                S.op("act", "copy", R=[pb], W=[hT], out=hT[:, :, tt * 128:(tt + 1) * 128],
                     in_=pb[:].rearrange("p (c t) -> p c t", c=8))
            pipeline([tileB(tt) for tt in range(NT)], 3)
            S.barrier()
        esA.close()
        ckpt("B")
        with ExitStack() as es:
            wf = Pool_(es, "wfC", [128, 4, 512], F32, 2)
            wbp = Pool_(es, "wbC", [128, 8, 512], BF16, 2)
            f1 = Pool_(es, "f1C", [128, 512], F32, 9)
            f2 = Pool_(es, "f2C", [128, 512], F32, 3)
            b1 = Pool_(es, "b1C", [128, 512], BF16, 4)
            st8 = Pool_(es, "st8C", [128, 16], F32, 4)
            stg = Pool_(es, "stgC", [128, 1024], BF16, 3)
            cfp = Pool_(es, "cfC", [128, 512], F32, 3)

            def load_wblock(c0, ncols):
                wb = wbp.next()
                for half in range(2):
                    w = wf.next()
                    S.dma("sp", w[:, :, 0:ncols],
                          w_in[l].rearrange("(c p) n -> p c n", p=128)[:, half * 4:half * 4 + 4, c0:c0 + ncols], W=[w])
                    S.op("dve" if half == 0 else "pool", "tensor_copy", R=[w], W=[wb],
                         out=wb[:, half * 4:half * 4 + 4, 0:ncols], in_=w[:, :, 0:ncols])
                return (wb, (c0, ncols))

            wblocks = [(0, 512), (512, 512), (1024, 512), (1536, 512), (2560, 512), (3072, 512), (5120, 512),
                       (2048, 512), (3584, 512), (4096, 512), (4608, 512), (5632, 16)] + \
                      [(5648 + i * 512, 512) for i in range(6)]
            wq = []

            def next_wb(c0, ncols):
                if not wq:
                    wq.append(load_wblock(*wblocks.pop(0)))
                wb = wq.pop(0)
                assert wb[1] == (c0, ncols), (wb[1], c0, ncols)
                if wblocks:
                    wq.append(load_wblock(*wblocks.pop(0)))
                return wb[0]

            def mm_tok(wb, tt, ncols):
                ps = psf()
                for c in range(8):
                    S.op("pe", "matmul", R=[hT, wb], W=[ps], out=ps[:, 0:ncols], lhsT=hT[:, c, tt * 128:(tt + 1) * 128],
                         rhs=wb[:, c, 0:ncols], start=(c == 0), stop=(c == 7))
                return ps

            def rope(src, tt, ngrp, dst_pool):
                t1 = dst_pool.next()
                t2 = dst_pool.next()
                s4 = src[:, 0:ngrp * 64].rearrange("p (g a h f) -> p g a h f", a=2, h=2, f=16)
                S.op("dve", "tensor_tensor", R=[src, ropec], W=[t1],
                     out=t1[:, 0:ngrp * 64].rearrange("p (g d) -> p g d", d=64),
                     in0=src[:, 0:ngrp * 64].rearrange("p (g d) -> p g d", d=64),
                     in1=ropec[:, tt:tt + 1, :].to_broadcast([128, ngrp, 64]), op=ALU.mult)
                t24 = t2[:, 0:ngrp * 64].rearrange("p (g a h f) -> p g a h f", a=2, h=2, f=16)
                sn4 = ropes[:, tt, :].rearrange("p (a h f) -> p a h f", a=2, h=2)
                for hh in range(2):
                    S.op("pool", "tensor_tensor", R=[src, ropes], W=[t2], out=t24[:, :, :, hh, :],
                         in0=s4[:, :, :, 1 - hh, :],
                         in1=sn4[:, :, hh, :].unsqueeze(1).to_broadcast([128, ngrp, 2, 16]), op=ALU.mult)
                S.op("dve", "tensor_tensor", R=[t1, t2], W=[t1], out=t1[:, 0:ngrp * 64], in0=t1[:, 0:ngrp * 64],
                     in1=t2[:, 0:ngrp * 64], op=ALU.add)
                return t1

            def tileQK(blk, wb, tt):
                is_s = tt < NTS
                ps = mm_tok(wb, tt, 512)
                sq = f2.next()
                s8 = st8.next()
                qn = f1.next()
                S.op("act", "activation", R=[ps], W=[sq], out=sq[:], in_=ps[:], func=AF.Square)
                S.op("dve", "tensor_reduce", R=[sq], W=[s8], out=s8[:, 0:8],
                     in_=sq[:].rearrange("p (g d) -> p g d", d=64), axis=AX.X, op=ALU.add)
                yield
                rsqrt_inplace(s8, s8[:, 0:8], 1.0 / 64, EPS)
                S.op("dve", "tensor_tensor", R=[ps, s8], W=[qn], out=qn[:].rearrange("p (g d) -> p g d", d=64),
                     in0=ps[:].rearrange("p (g d) -> p g d", d=64),
                     in1=s8[:, 0:8].unsqueeze(2).to_broadcast([128, 8, 64]), op=ALU.mult)
                S.op("pool", "tensor_tensor", R=[qn, wq_bc], W=[qn], out=qn[:].rearrange("p (g d) -> p g d", d=64),
                     in0=qn[:].rearrange("p (g d) -> p g d", d=64),
                     in1=wq_bc[:, blk:blk + 1, :].to_broadcast([128, 8, 64]), op=ALU.mult)
                if blk == 1 and not is_s:
                    pi = (tt - NTS) // 2
                    r0 = ((tt - NTS) % 2) * 128
                    S.dma("pool", nk_out[pi, l, r0:r0 + 128, :], qn[:], R=[qn])
                yield
                if is_s:
                    qn = rope(qn, tt, 8, f1)
                qb = b1.next()
                S.op("act", "copy", R=[qn], W=[qb], out=qb[:], in_=qn[:])
                yield
                pb = psb()
                transposes_to(pb, qb, [(qb[:, h * 128:(h + 1) * 128], 128, h * 128) for h in range(4)])
                sg = stg.next()
                S.op("dve", "tensor_copy", R=[pb], W=[sg], out=sg[:, 0:512], in_=pb[:, 0:512])
                dst = aqT if blk == 0 else akT
                S.dma("pool", dst[:, :, tt * 128:(tt + 1) * 128].rearrange("h p t -> p h t"),
                      sg[:, 0:512].rearrange("p (h t) -> p h t", h=4), R=[sg],
                      W=[RX["aqT" if blk == 0 else "akT"][tt]])

            for blk in range(2):
                wb = next_wb(blk * 512, 512)
                pipeline([tileQK(blk, wb, tt) for tt in range(NT)], 4)
            ckpt("C0")
            for (c0, kind) in ((1024, "av"), (1536, "z0"), (2560, "bv"), (3072, "z1"), (5120, "z2")):
                wb = next_wb(c0, 512)
                for tt in range(NT):
                    is_s = tt < NTS
                    ps = mm_tok(wb, tt, 512)
                    ob = b1.next()
                    if kind in ("av", "bv"):
                        S.op("act", "copy", R=[ps], W=[ob], out=ob[:], in_=ps[:])
                        dst, rn = (av, "av") if kind == "av" else (bv, "bv")
                        S.dma("pool", dst[tt * 128:(tt + 1) * 128, :], ob[:], R=[ob], W=[RX[rn][tt]])
                        if kind == "av" and not is_s:
                            of_ = f1.next()
                            S.op("dve", "tensor_copy", R=[ps], W=[of_], out=of_[:], in_=ps[:])
                            pi = (tt - NTS) // 2
                            r0 = ((tt - NTS) % 2) * 128
                            S.dma("pool", nv_out[pi, l, r0:r0 + 128, :], of_[:], R=[of_])
                    else:
                        zi = int(kind[1])
                        S.op("act", "activation", R=[ps], W=[ob], out=ob[:], in_=ps[:], func=AF.Silu)
                        S.dma("pool", zg[zi, tt * 128:(tt + 1) * 128, :], ob[:], R=[ob], W=[RX["zg%d" % zi][tt]])
                ckpt("C1_" + kind)
            ckpt("C1")
            wb = next_wb(2048, 512)

            def tileBQK(wb, tt):
                is_s = tt < NTS
                ps = mm_tok(wb, tt, 512)
                qk = f1.next()
                S.op("act", "copy", R=[ps], W=[qk], out=qk[:, 0:256], in_=ps[:, 0:256])
                S.op("act", "mul", R=[ps], W=[qk], out=qk[:, 256:512], in_=ps[:, 256:512], mul=0.125)
                yield
                if is_s:
                    qk = rope(qk, tt, 8, f1)
                qb = b1.next()
                S.op("act", "copy", R=[qk], W=[qb], out=qb[:], in_=qk[:])
                S.dma("pool", bk[tt * 128:(tt + 1) * 128, :], qb[:, 256:512], R=[qb], W=[RX["bk"][tt]])
                yield
                pb = psb()
                transposes_to(pb, qb, [(qb[:, g * 64:(g + 1) * 64], 64, g * 128) for g in range(8)])
                sg = stg.next()
                S.op("dve", "tensor_copy", R=[pb], W=[sg], out=sg[0:64, :], in_=pb[0:64, :])
                S.dma("pool", bqT[:, :, tt * 128:(tt + 1) * 128].rearrange("h p t -> p h t"),
                      sg[0:64, 0:512].rearrange("p (h t) -> p h t", h=4), R=[sg], W=[RX["bqT"][tt]])
                S.dma("pool", bkT[:, :, tt * 128:(tt + 1) * 128].rearrange("h p t -> p h t"),
                      sg[0:64, 512:1024].rearrange("p (h t) -> p h t", h=4), R=[sg], W=[RX["bkT"][tt]])
            pipeline([tileBQK(wb, tt) for tt in range(NT)], 3)
            ckpt("C2")
            for blk in range(3):
                wb = next_wb(3584 + blk * 512, 512)
                for cc in range(4):
                    for (t0, T, _) in seqs:
                        for g0 in range(0, T, 512):
                            n = min(512, T - g0)
                            ps = psf()
                            for c in range(8):
                                S.op("pe", "matmul", R=[hT, wb], W=[ps], out=ps[:, 0:n],
                                     lhsT=wb[:, c, cc * 128:(cc + 1) * 128], rhs=hT[:, c, t0 + g0:t0 + g0 + n],
                                     start=(c == 0), stop=(c == 7))
                            cf = cfp.next()
                            S.op("act", "copy", R=[ps], W=[cf], out=cf[:, 0:n], in_=ps[:, 0:n])
                            ch0 = (blk * 4 + cc) * 128
                            tiles = range((t0 + g0) // 128, (t0 + g0 + n) // 128)
                            S.dma("pool", cT[ch0:ch0 + 128, t0 + g0:t0 + g0 + n], cf[:, 0:n], R=[cf],
                                  W=[RX["cT"][i] for i in tiles])
            ckpt("C3")
            wb = next_wb(5632, 16)
            for tt in range(NT):
                ps = mm_tok(wb, tt, 16)
                o16 = st8.next()
                S.op("act", "activation", R=[ps], W=[o16], out=o16[:, 0:8], in_=ps[:, 0:8], func=AF.Sigmoid)
                S.op("dve", "tensor_tensor", R=[ps, dtb_bc], W=[o16], out=o16[:, 8:16], in0=ps[:, 8:16], in1=dtb_bc[:],
                     op=ALU.add)
                S.op("act", "activation", R=[o16], W=[o16], out=o16[:, 8:16], in_=o16[:, 8:16], func=AF.Exp)
                S.op("act", "activation", R=[o16], W=[o16], out=o16[:, 8:16], in_=o16[:, 8:16], func=AF.Ln, bias=1.0)
                S.op("dve", "tensor_tensor", R=[o16, negA_bc], W=[o16], out=o16[:, 8:16], in0=o16[:, 8:16],
                     in1=negA_bc[:], op=ALU.mult)
                S.dma("pool", bgs[tt * 128:(tt + 1) * 128, :], o16[:], R=[o16], W=[RX["bgs"][tt]])
            ckpt("C4")
            for blk in range(6):
                wb = next_wb(5648 + blk * 512, 512)
                for tt in range(NT):
                    ps = mm_tok(wb, tt, 512)
                    ob = b1.next()
                    S.op("act", "activation", R=[ps], W=[ob], out=ob[:], in_=ps[:], func=AF.Sigmoid)
                    S.dma("pool", mg[tt * 128:(tt + 1) * 128, blk * 512:(blk + 1) * 512], ob[:], R=[ob],
                          W=[RX["mg"][tt]] if blk == 5 else [])
            S.barrier()
        esH.close()
        ckpt("C")
        for si, (t0, T, is_s) in enumerate(seqs):
            if is_s and not os.environ.get("KNOOVL"):
                sample_mixers_overlapped(l, si, t0, T, is_s)
                ckpt("gpre%d" % si)
            elif os.environ.get("KNOOVL"):
                attention(l, si, t0, T, is_s)
                ckpt("att%d" % si)
                retention(l, si, t0, T, is_s)
                ckpt("ret%d" % si)
                gdn_pre(l, si, t0, T, is_s)
                ckpt("gpre%d" % si)
            gdn_scan(l, si, t0, T, is_s)
            ckpt("gscan%d" % si)
        phaseE(l, xsrc, xdst)

    def norm_gate_store(es_tiles, o_t, o_ap, w_bc, zi, mi, tt, rows, col_lo=None):
        sq, s4, zt, gb, sg = es_tiles
        tok0 = tt * 128 + (col_lo or 0)
        S.op("act", "activation", R=[o_t], W=[sq], out=sq[0:rows, :], in_=o_ap, func=AF.Square)
        S.op("dve", "tensor_reduce", R=[sq], W=[s4], out=s4[0:rows, 0:4],
             in_=sq[0:rows, :].rearrange("p (g d) -> p g d", d=128), axis=AX.X, op=ALU.add)
        rsqrt_inplace(s4, s4[0:rows, 0:4], 1.0 / 128, EPS)
        S.dma("sp", zt[0:rows, :], zg[zi, tok0:tok0 + rows, :], R=[RX["zg%d" % zi][tt]], W=[zt])
        S.op("dve", "tensor_tensor", R=[o_t, s4], W=[o_t], out=o_ap.rearrange("p (g d) -> p g d", d=128),
             in0=o_ap.rearrange("p (g d) -> p g d", d=128),
             in1=s4[0:rows, 0:4].unsqueeze(2).to_broadcast([rows, 4, 128]), op=ALU.mult)
        S.op("pool", "tensor_tensor", R=[o_t, w_bc], W=[o_t], out=o_ap.rearrange("p (g d) -> p g d", d=128),
             in0=o_ap.rearrange("p (g d) -> p g d", d=128),
             in1=w_bc[0:rows, :].unsqueeze(1).to_broadcast([rows, 4, 128]), op=ALU.mult)
        S.op("dve", "tensor_tensor", R=[o_t, zt], W=[gb], out=gb[0:rows, :], in0=o_ap, in1=zt[0:rows, :], op=ALU.mult)
        pb = psb()
        for g in range(4):
            S.op("pe", "transpose", R=[gb, ident_b], W=[pb], out=pb[:, g * 128:g * 128 + rows],
                 in_=gb[0:rows, g * 128:(g + 1) * 128], identity=ident_b[0:rows, 0:rows])
        pv = pb[:, 0:512].rearrange("p (g t) -> p g t", g=4)[:, :, 0:rows]
        S.op("act", "copy", R=[pb], W=[sg], out=sg[:, :, 0:rows], in_=pv)
        S.dma("pool", gT[mi, :, tok0:tok0 + rows].rearrange("(g p) t -> p g t", p=128), sg[:, :, 0:rows], R=[sg],
              W=[RX["gT%d" % mi][tt]])

    def ng_tiles(es, pfx):
        return (sb(es, pfx + "sq", [128, 512], F32), sb(es, pfx + "s4", [128, 4], F32),
                sb(es, pfx + "zt", [128, 512], BF16), sb(es, pfx + "gb", [128, 512], BF16),
                sb(es, pfx + "sg", [128, 4, 128], BF16))

    def attention_gen(l, si, t0, T, is_s, es, acc_sets, sc_banks=None):
        Sk = T + (PAST if is_s else 0)
        nst = Sk // 128
        ntl = T // 128
        QB = min(512, T)
        kT = sb(es, "at_kT", [128, 4, Sk], BF16)
        V1 = sb(es, "at_V1", [128, nst, 4, 130], BF16)
        qTp = Pool_(es, "at_qT", [128, 4, QB], BF16, 2)
        ex = Pool_(es, "at_ex", [128, QB], BF16, 3)
        osb = [sb(es, "at_os%d" % qs, [128, 512], F32) for qs in range(QB // 128)]
        rc = Pool_(es, "at_rc", [128, 2], F32, 4)
        ngt = ng_tiles(es, "at_")
        grp = [0]
        scn = [0]
        S.dma("sp", kT[:, :, 0:T], akT[:, :, t0:t0 + T].rearrange("h p t -> p h t"),
              R=[RX["akT"][i] for i in range(t0 // 128, (t0 + T) // 128)], W=[kT])
        S.op("pool", "memset", W=[V1], ap=V1[:, :, :, 128:130], constant=1.0)
        for i in range(ntl):
            S.dma("sp", V1[:, i, :, 0:128], av[t0 + i * 128:t0 + (i + 1) * 128, :].rearrange("p (h e) -> p h e", h=4),
                  R=[RX["av"][t0 // 128 + i]], W=[V1])
        if is_s:
            with ExitStack() as es2:
                ck = sb(es2, "at_ck", [128, 2, 512], F32)
                cv = sb(es2, "at_cv", [128, 2, 512], F32)
                ckb = sb(es2, "at_ckb", [128, 2, 512], BF16)
                S.dma("sp", ck[:], cache_k[l].rearrange("(n p) f -> p n f", p=128), W=[ck])
                S.dma("sp", cv[:], cache_v[l].rearrange("(n p) f -> p n f", p=128), W=[cv])
                S.op("dve", "tensor_copy", R=[ck], W=[ckb], out=ckb[:], in_=ck[:])
                for n in range(2):
                    S.op("pool", "tensor_copy", R=[cv], W=[V1], out=V1[:, ntl + n, :, 0:128],
                         in_=cv[:, n, :].rearrange("p (h e) -> p h e", h=4))
                    pb = psb()
                    transposes_to(pb, ckb, [(ckb[:, n, h * 128:(h + 1) * 128], 128, h * 128) for h in range(4)])
                    S.op("act", "copy", R=[pb], W=[kT], out=kT[:, :, T + n * 128:T + (n + 1) * 128],
                         in_=pb[:, 0:512].rearrange("p (h t) -> p h t", h=4))
                S.barrier()
        for qb0 in range(0, T, QB):
            qT = qTp.next()
            S.dma("sp", qT[:], aqT[:, :, t0 + qb0:t0 + qb0 + QB].rearrange("h p t -> p h t"),
                  R=[RX["aqT"][i] for i in range((t0 + qb0) // 128, (t0 + qb0 + QB) // 128)], W=[qT])
            nqs = QB // 128
            for h in range(4):
                for m in range(2):
                    grp[0] += 1
                    acc = acc_sets[grp[0] % len(acc_sets)]

                    def pv(st, e, acc=acc, h=h):
                        for qs in range(nqs):
                            a = acc[qs // 2]
                            S.op("pe", "matmul", R=[e, V1], W=[a], out=a[:, (qs % 2) * 256:(qs % 2) * 256 + 129],
                                 lhsT=e[:, qs * 128:(qs + 1) * 128], rhs=V1[:, st, h, 0:129],
                                 start=(st == 0 and qs % 2 == 0), stop=(st == nst - 1), skip_group_check=True)
                    pend = None
                    for st in range(nst):
                        scn[0] += 1
                        _sb = sc_banks or [PSF[0], PSF[1]]
                        ps = _sb[scn[0] % len(_sb)]
                        S.op("pe", "matmul", R=[kT, qT], W=[ps], out=ps[:, 0:QB],
                             lhsT=kT[m * 64:(m + 1) * 64, h, st * 128:(st + 1) * 128],
                             rhs=qT[m * 64:(m + 1) * 64, h, :], start=True, stop=True)
                        e = ex.next()
                        S.op("act", "activation", R=[ps], W=[e], out=e[:, 0:QB], in_=ps[:, 0:QB], func=AF.Exp)
                        if pend is not None:
                            pv(*pend)
                        pend = (st, e)
                        yield
                    pv(*pend)
                    for qs in range(nqs):
                        a = acc[qs // 2]
                        c0 = (qs % 2) * 256
                        r = rc.next()
                        S.op("dve", "reciprocal", R=[a], W=[r], out=r[:, 0:1], in_=a[:, c0 + 128:c0 + 129])
                        if m == 0:
                            S.op("dve", "tensor_scalar", R=[a, r], W=[osb[qs]], out=osb[qs][:, h * 128:(h + 1) * 128],
                                 in0=a[:, c0:c0 + 128], scalar1=r[:, 0:1], scalar2=None, op0=ALU.mult)
                        else:
                            S.op("dve", "tensor_tensor", R=[r, lamt], W=[r], out=r[:, 1:2], in0=r[:, 0:1],
                                 in1=lamt[:, 0:1], op=ALU.mult)
                            S.op("dve", "scalar_tensor_tensor", R=[a, r, osb[qs]], W=[osb[qs]],
                                 out=osb[qs][:, h * 128:(h + 1) * 128], in0=a[:, c0:c0 + 128], scalar=r[:, 1:2],
                                 in1=osb[qs][:, h * 128:(h + 1) * 128], op0=ALU.mult, op1=ALU.add)
            for qs in range(nqs):
                tt = (t0 + qb0) // 128 + qs
                norm_gate_store(ngt, osb[qs], osb[qs][:], subw_bc, 0, 0, tt, 128)

    def attention(l, si, t0, T, is_s):
        with ExitStack() as es:
            for _ in attention_gen(l, si, t0, T, is_s, es, [[PSF[2], PSF[3]], [PSF[4], PSF[5]]]):
                pass
            S.barrier()

    def retention(l, si, t0, T, is_s):
        with ExitStack() as es:
            run_rr([ret_chain(l, si, t0, T, is_s, d, es) for d in range(2)])
            S.barrier()
        combine(t0, T, retw_bc, 1, 1)

    def rr_gen(gens):
        gens = list(gens)
        while gens:
            for g in list(gens):
                try:
                    next(g)
                    yield
                except StopIteration:
                    gens.remove(g)

    def side_gen(l, si, t0, T, is_s):
        with ExitStack() as es2:
            yield from gdn_pre_gen(l, si, t0, T, is_s, es2, 256, 1)
            S.barrier()
        with ExitStack() as es3:
            yield from rr_gen([ret_chain(l, si, t0, T, is_s, d, es3) for d in range(2)])
            S.barrier()
        yield from combine_gen(t0, T, retw_bc, 1, 1)
        for pi in range(1, len(seqs)):
            (tp0, Tp, _) = seqs[pi]
            with ExitStack() as esp:
                yield from attention_gen(l, pi, tp0, Tp, False, esp, [[PSF[5]]], sc_banks=[PSF[4]])
                S.barrier()
            with ExitStack() as esp:
                yield from gdn_pre_gen(l, pi, tp0, Tp, False, esp, 256, 1)
                S.barrier()
            with ExitStack() as esp:
                yield from rr_gen([ret_chain(l, pi, tp0, Tp, False, d, esp) for d in range(2)])
                S.barrier()
            yield from combine_gen(tp0, Tp, retw_bc, 1, 1)

    def sample_mixers_overlapped(l, si, t0, T, is_s):
        with ExitStack() as es:
            att = attention_gen(l, si, t0, T, is_s, es, [[PSF[2], PSF[3]]])
            next(att)
            old = psf_banks[0]
            psf_banks[0] = [4, 5]
            n_att = (T // min(512, T)) * 8 * ((T + PAST) // 128)
            n_side = (T // 256) * 36 + (T // 128) * 5 + (len(seqs) - 1) * 70
            run_weighted(att, side_gen(l, si, t0, T, is_s), max(1, int(0.9 * n_att / n_side)))
            psf_banks[0] = old
            S.barrier()

    def ret_chain(l, si, t0, T, is_s, d, es):
        ntl = T // 128
        pf = "rt%d_" % d
        Sf = sb(es, pf + "S", [64, 4, 128], F32)
        Sb_ = sb(es, pf + "Sb", [64, 4, 128], BF16)
        qTp = Pool_(es, pf + "qT", [64, 4, 128], BF16, 2)
        kTp = Pool_(es, pf + "kT", [64, 4, 128], BF16, 2)
        ktp = Pool_(es, pf + "k", [128, 256], BF16, 2)
        vp = Pool_(es, pf + "v", [128, 512], BF16, 2)
        itp = Pool_(es, pf + "it", [128, 512], BF16, 2)
        qdp = Pool_(es, pf + "qd", [64, 4, 128], BF16, 2)
        kdp = Pool_(es, pf + "kd", [128, 256], BF16, 2)
        op_ = Pool_(es, pf + "o", [128, 512], F32, 2)
        odst, orn = (ofs, "ofs") if d == 0 else (ofb, "ofb")
        if is_s:
            S.dma("sp", Sf[:], st_ret[l, d].rearrange("h k e -> k h e"), W=[Sf])
        else:
            S.op("dve", "memset", W=[Sf], ap=Sf[:], constant=0.0)
        S.op("act", "copy", R=[Sf], W=[Sb_], out=Sb_[:], in_=Sf[:])
        order = range(ntl) if d == 0 else range(ntl - 1, -1, -1)
        for i in order:
            tt = t0 // 128 + i
            c0 = tt * 128
            qT = qTp.next(); kT = kTp.next(); kt = ktp.next(); v = vp.next()
            S.dma("sp", qT[:], bqT[:, :, c0:c0 + 128].rearrange("h p t -> p h t"), R=[RX["bqT"][tt]], W=[qT])
            S.dma("sp", kT[:], bkT[:, :, c0:c0 + 128].rearrange("h p t -> p h t"), R=[RX["bkT"][tt]], W=[kT])
            S.dma("sp", kt[:], bk[c0:c0 + 128, :], R=[RX["bk"][tt]], W=[kt])
            S.dma("sp", v[:], bv[c0:c0 + 128, :], R=[RX["bv"][tt]], W=[v])
            ps = psf()
            for h in range(4):
                S.op("pe", "matmul", R=[kT, qT], W=[ps], out=ps[:, h * 128:(h + 1) * 128], lhsT=kT[:, h, :],
                     rhs=qT[:, h, :], start=True, stop=True)
            it = itp.next()
            S.op("dve", "tensor_tensor", R=[ps, rdmat], W=[it], out=it[:], in0=ps[:],
                 in1=rdmat[:, d, :, :].rearrange("p h i -> p (h i)"), op=ALU.mult)
            qd = qdp.next()
            S.op("pool", "tensor_tensor", R=[qT, rqdec], W=[qd], out=qd[:], in0=qT[:], in1=rqdec[:, d, :, :],
                 op=ALU.mult)
            kd = kdp.next()
            S.op("pool", "tensor_tensor", R=[kt, rkdec], W=[kd], out=kd[:].rearrange("p (h e) -> p h e", h=4),
                 in0=kt[:].rearrange("p (h e) -> p h e", h=4),
                 in1=rkdec[:, d, :].unsqueeze(2).to_broadcast([128, 4, 64]), op=ALU.mult)
            yield
            po = psf()
            for h in range(4):
                S.op("pe", "matmul", R=[it, v], W=[po], out=po[:, h * 128:(h + 1) * 128],
                     lhsT=it[:, h * 128:(h + 1) * 128], rhs=v[:, h * 128:(h + 1) * 128], start=True, stop=False)
                S.op("pe", "matmul", R=[qd, Sb_], W=[po], out=po[:, h * 128:(h + 1) * 128], lhsT=qd[:, h, :],
                     rhs=Sb_[:, h, :], start=False, stop=True)
            pS = psf()
            for h in range(4):
                S.op("pe", "matmul", R=[kd, v], W=[pS], out=pS[0:64, h * 128:(h + 1) * 128],
                     lhsT=kd[:, h * 64:(h + 1) * 64], rhs=v[:, h * 128:(h + 1) * 128], start=True, stop=True)
            S.op("dve", "tensor_tensor", R=[Sf, rcdec], W=[Sf], out=Sf[:], in0=Sf[:],
                 in1=rcdec[0:64, d * 4:d * 4 + 4].unsqueeze(2).to_broadcast([64, 4, 128]), op=ALU.mult)
            S.op("dve", "tensor_tensor", R=[Sf, pS], W=[Sf], out=Sf[:].rearrange("p h e -> p (h e)"),
                 in0=Sf[:].rearrange("p h e -> p (h e)"), in1=pS[0:64, :], op=ALU.add)
            S.op("act", "copy", R=[Sf], W=[Sb_], out=Sb_[:], in_=Sf[:])
            o = op_.next()
            S.op("act", "copy", R=[po], W=[o], out=o[:], in_=po[:])
            S.dma("pool", odst[c0:c0 + 128, :], o[:], R=[o], W=[RX[orn][tt]])
            yield
        if not is_s:
            S.dma("pool", nret_out[si - 1, l, d].rearrange("h k e -> k h e"), Sf[:], R=[Sf])

    def pipeline_gen(gens, depth):
        gens = list(gens)
        active = []
        while gens or active:
            while gens and len(active) < depth:
                active.append(gens.pop(0))
            for g in list(active):
                try:
                    next(g)
                except StopIteration:
                    active.remove(g)
            yield

    def gdn_pre_gen(l, si, t0, T, is_s, es, G, nbuf):
        xin = Pool_(es, "gp_x", [128, 12, G + 4], F32, nbuf)
        acc = Pool_(es, "gp_a", [128, 12, G], F32, nbuf)
        sqp = Pool_(es, "gp_sq", [128, G], F32, 5)
        rsp = Pool_(es, "gp_rs", [128, G], F32, 5)
        nb = Pool_(es, "gp_nb", [128, 12, G], BF16, nbuf)
        sg = Pool_(es, "gp_sg", [128, 1024], BF16, 2)
        chunk_res = {}
        for g0 in range(0, T, G):
            x = xin.next()
            a = acc.next()
            lo = 2 if g0 == 0 else 0
            hi = 2 if g0 + G == T else 0
            if lo:
                S.op("pool", "memset", W=[x], ap=x[:, :, 0:2], constant=0.0)
            if hi:
                S.op("pool", "memset", W=[x], ap=x[:, :, G + 2:G + 4], constant=0.0)
            tl = [i for i in range((t0 + g0) // 128 - (0 if lo else 1), (t0 + g0 + G) // 128 + (0 if hi else 1))]
            S.dma("sp", x[:, :, lo:G + 4 - hi],
                  cT[:, t0 + g0 - 2 + lo:t0 + g0 + G + 2 - hi].rearrange("(c p) t -> p c t", p=128),
                  R=[RX["cT"][i] for i in tl], W=[x])
            yield
            yield
            ar = chunk_res.setdefault(("a", id(a)), [Res("gpa%d" % c) for c in range(12)])

            def convc(c):
                S.op("dve", "tensor_scalar", R=[x, convw], W=[ar[c]], out=a[:, c, :], in0=x[:, c, 0:G],
                     scalar1=convw[:, c, 0:1], scalar2=None, op0=ALU.mult)
                for k in range(1, 5):
                    S.op("dve", "scalar_tensor_tensor", R=[x, convw, ar[c]], W=[ar[c]], out=a[:, c, :],
                         in0=x[:, c, k:k + G], scalar=convw[:, c, k:k + 1], in1=a[:, c, :], op0=ALU.mult, op1=ALU.add)
                yield
                yield
                S.op("act", "activation", R=[ar[c]], W=[ar[c]], out=a[:, c, :], in_=a[:, c, :], func=AF.Silu)
            yield from pipeline_gen([convc(c) for c in range(12)], 3)
            n = nb.next()
            nr = chunk_res.setdefault(("n", id(n)), [Res("gpn%d" % c) for c in range(12)])

            def l2c(c):
                sq = sqp.next()
                S.op("act", "activation", R=[ar[c]], W=[sq], out=sq[:], in_=a[:, c, :], func=AF.Square)
                yield
                yield
                ps = psf()
                S.op("pe", "matmul", R=[ones_f, sq], W=[ps], out=ps[:, 0:G], lhsT=ones_f[:], rhs=sq[:], start=True,
                     stop=True)
                rs = rsp.next()
                S.op("dve", "tensor_scalar", R=[ps], W=[rs], out=rs[:], in0=ps[:, 0:G], scalar1=EPS, scalar2=None,
                     op0=ALU.add)
                yield
                yield
                S.op("act", "activation", R=[rs], W=[rs], out=rs[:], in_=rs[:], func=AF.Sqrt)
                yield
                yield
                S.op("dve", "reciprocal", R=[rs], W=[rs], out=rs[:], in_=rs[:])
                if c < 4:
                    S.op("dve", "scalar_tensor_tensor", R=[ar[c], rs], W=[nr[c]], out=n[:, c, :], in0=a[:, c, :],
                         scalar=float(128 ** -0.5), in1=rs[:], op0=ALU.mult, op1=ALU.mult)
                else:
                    S.op("dve", "tensor_tensor", R=[ar[c], rs], W=[nr[c]], out=n[:, c, :], in0=a[:, c, :], in1=rs[:],
                         op=ALU.mult)
            yield from pipeline_gen([l2c(c) for c in range(8)], 4)
            S.op("pool", "tensor_copy", R=ar[8:12], W=nr[8:12], out=n[:, 8:12, :], in_=a[:, 8:12, :])
            tiles = list(range((t0 + g0) // 128, (t0 + g0 + G) // 128))
            S.dma("pool", gqT[:, :, t0 + g0:t0 + g0 + G].rearrange("h p t -> p h t"), n[:, 0:4, :], R=nr[0:4],
                  W=[RX["gqT"][i] for i in tiles])
            S.dma("pool", gkT[:, :, t0 + g0:t0 + g0 + G].rearrange("h p t -> p h t"), n[:, 4:8, :], R=nr[4:8],
                  W=[RX["gkT"][i] for i in tiles])
            yield
            yield
            for ti in range(G // 128):
                tt = (t0 + g0) // 128 + ti
                pb = psb()
                transposes_to(pb, nr[4:12], [(n[:, 4 + c, ti * 128:(ti + 1) * 128], 128, c * 128) for c in range(8)])
                s = sg.next()
                S.op("act", "copy", R=[pb], W=[s], out=s[:], in_=pb[:])
                S.dma("pool", gk[tt * 128:(tt + 1) * 128, :], s[:, 0:512], R=[s], W=[RX["gk"][tt]])
                S.dma("pool", gv[tt * 128:(tt + 1) * 128, :], s[:, 512:1024], R=[s], W=[RX["gv"][tt]])
                yield

    def gdn_pre(l, si, t0, T, is_s):
        with ExitStack() as es:
            for _ in gdn_pre_gen(l, si, t0, T, is_s, es, min(512, T), 2):
                pass
            S.barrier()

    def pipeline(gens, depth):
        gens = list(gens)
        active = []
        while gens or active:
            if gens and len(active) < depth:
                active.append(gens.pop(0))
            for g in list(active):
                try:
                    next(g)
                except StopIteration:
                    active.remove(g)

    def run_weighted(main, side, ratio):
        main_alive = side_alive = True
        while main_alive or side_alive:
            if main_alive:
                for _ in range(ratio):
                    try:
                        next(main)
                    except StopIteration:
                        main_alive = False
                        break
            if side_alive:
                try:
                    next(side)
                except StopIteration:
                    side_alive = False

    def run_rr(gens):
        gens = list(gens)
        while gens:
            for g in list(gens):
                try:
                    next(g)
                except StopIteration:
                    gens.remove(g)

    def combine(t0, T, w_bc, zi, mi):
        for _ in combine_gen(t0, T, w_bc, zi, mi):
            pass

    def combine_gen(t0, T, w_bc, zi, mi):
        with ExitStack() as es:
            fa = Pool_(es, "cb_a", [128, 512], F32, 2)
            fb = Pool_(es, "cb_b", [128, 512], F32, 2)
            ngts = [ng_tiles(es, "cb%d_" % i) for i in range(2)]
            for i in range(T // 128):
                tt = t0 // 128 + i
                c0 = tt * 128
                a = fa.next(); b = fb.next()
                S.dma("sp", a[:], ofs[c0:c0 + 128, :], R=[RX["ofs"][tt]], W=[a])
                S.dma("sp", b[:], ofb[c0:c0 + 128, :], R=[RX["ofb"][tt]], W=[b])
                S.op("pool", "tensor_tensor", R=[a, b], W=[a], out=a[:], in0=a[:], in1=b[:], op=ALU.add)
                norm_gate_store(ngts[i % 2], a, a[:], w_bc, zi, mi, tt, 128)
                yield
            S.barrier()

    def gdn_scan(l, si, t0, T, is_s):
        with ExitStack() as es:
            run_rr([gdn_chain(l, si, t0, T, is_s, d, es) for d in range(2)])
            S.barrier()
        combine(t0, T, gdnw_bc, 2, 2)

    def gdn_chain(l, si, t0, T, is_s, d, es):
        ntl = T // 128
        pf = "gd%d_" % d
        Sf = sb(es, pf + "S", [128, 4, 128], F32)
        Sb_ = sb(es, pf + "Sb", [128, 4, 128], BF16)
        kTp = Pool_(es, pf + "kT", [128, 4, 128], BF16, 2)
        qTp = Pool_(es, pf + "qT", [128, 4, 128], BF16, 2)
        ktp = Pool_(es, pf + "k", [128, 512], BF16, 2)
        vtp = Pool_(es, pf + "v", [128, 512], BF16, 2)
        bgp = Pool_(es, pf + "bg", [128, 16], F32, 2)
        sm = Pool_(es, pf + "sm", [128, 8, 4], F32, 2)
        X1 = Pool_(es, pf + "X", [128, 4, 128], F32, 1)
        X2 = Pool_(es, pf + "X2", [128, 4, 128], F32, 1)
        Dn = Pool_(es, pf + "Dn", [128, 4, 128], F32, 1)
        Ea = Pool_(es, pf + "Ea", [128, 4, 128], F32, 1)
        Eb = Pool_(es, pf + "Eb", [128, 4, 128], F32, 1)
        Fq = Pool_(es, pf + "Fq", [128, 4, 128], F32, 1)
        Eg = Pool_(es, pf + "Eg", [128, 4, 128], F32, 1)
        Pp = Pool_(es, pf + "P", [128, 4, 128], F32, 2)
        PTp = Pool_(es, pf + "PT", [128, 4, 128], F32, 2)
        TTf = Pool_(es, pf + "TTf", [128, 4, 128], F32, 1)
        qkTp = Pool_(es, pf + "qkT", [128, 4, 128], BF16, 2)
        qgp = Pool_(es, pf + "qg", [128, 4, 128], BF16, 2)
        vbp = Pool_(es, pf + "vb", [128, 4, 128], F32, 1)
        kbp = Pool_(es, pf + "kb", [128, 4, 128], F32, 1)
        kdp = Pool_(es, pf + "kd", [128, 4, 128], BF16, 2)
        Up = Pool_(es, pf + "U", [128, 4, 128], F32, 2)
        WTp = Pool_(es, pf + "WT", [128, 4, 128], BF16, 2)
        vnp = Pool_(es, pf + "vn", [128, 128], BF16, 4)
        op_ = Pool_(es, pf + "o", [64, 512], F32, 2)
        odst, orn = (ofs, "ofs") if d == 0 else (ofb, "ofb")
        Sfr = [Res("gSf%d" % h) for h in range(4)]
        Sbr = [Res("gSb%d" % h) for h in range(4)]
        if is_s:
            S.dma("sp", Sf[:], st_gdn[l, d].rearrange("h k e -> k h e"), W=Sfr)
        else:
            S.op("dve", "memset", W=Sfr, ap=Sf[:], constant=0.0)
        S.op("act", "copy", R=Sfr, W=Sbr, out=Sb_[:], in_=Sf[:])
        idb = ident_f[:].unsqueeze(1).to_broadcast([128, 4, 128])
        v4 = lambda t: t[:].rearrange("p h b -> p (h b)")
        order = range(ntl) if d == 0 else range(ntl - 1, -1, -1)
        for i in order:
            tt = t0 // 128 + i
            c0 = tt * 128
            kT = kTp.next(); qT = qTp.next(); kt = ktp.next(); vt = vtp.next(); bg = bgp.next()
            S.dma("sp", kT[:], gkT[:, :, c0:c0 + 128].rearrange("h p t -> p h t"), R=[RX["gkT"][tt]], W=[kT])
            S.dma("sp", qT[:], gqT[:, :, c0:c0 + 128].rearrange("h p t -> p h t"), R=[RX["gqT"][tt]], W=[qT])
            S.dma("sp", kt[:], gk[c0:c0 + 128, :], R=[RX["gk"][tt]], W=[kt])
            S.dma("sp", vt[:], gv[c0:c0 + 128, :], R=[RX["gv"][tt]], W=[vt])
            S.dma("sp", bg[:], bgs[c0:c0 + 128, :], R=[RX["bgs"][tt]], W=[bg])
            s = sm.next()
            S.op("dve", "tensor_copy", R=[bg], W=[s], out=s[:, 0, :], in_=bg[:, 8 + d * 4:12 + d * 4])
            S.op("dve", "tensor_copy", R=[bg], W=[s], out=s[:, 1, :], in_=bg[:, d * 4:d * 4 + 4])
            pg = psf()
            S.op("pe", "matmul", R=[utri2, s], W=[pg], out=pg[:, 0:4], lhsT=utri2[:, d, :], rhs=s[:, 0, :],
                 start=True, stop=True)
            S.op("pe", "matmul", R=[bd1, s], W=[pg], out=pg[:, 4:8], lhsT=bd1[:], rhs=s[:, 0, :], start=True, stop=True)
            S.op("pe", "matmul", R=[selc, s], W=[pg], out=pg[:, 8:12], lhsT=selc[:, 0, :], rhs=s[:, 0, :],
                 start=True, stop=True)
            S.op("pe", "matmul", R=[selc, s], W=[pg], out=pg[:, 12:16], lhsT=selc[:, 1, :], rhs=s[:, 0, :],
                 start=True, stop=True)
            S.op("dve", "tensor_copy", R=[pg], W=[s], out=s[:, 2, :], in_=pg[:, 0:4])
            S.op("act", "activation", R=[pg], W=[s], out=s[:, 6:8, :].rearrange("p c h -> p (c h)"), in_=pg[:, 8:16],
                 func=AF.Exp)
            S.op("dve", "tensor_tensor", R=[pg], W=[s], out=s[:, 4, :], in0=pg[:, 4:8], in1=s[:, 2, :], op=ALU.subtract)
            S.op("act", "activation", R=[s], W=[s], out=s[:, 4, :], in_=s[:, 4, :], func=AF.Exp)
            S.op("act", "activation", R=[s], W=[s], out=s[:, 5, :], in_=s[:, 2, :], func=AF.Exp)
            S.op("dve", "tensor_tensor", R=[s], W=[s], out=s[:, 5, :], in0=s[:, 5, :], in1=s[:, 1, :], op=ALU.mult)
            ckpt('g0')
            yield
            x1 = X1.next(); x2 = X2.next()
            S.op("dve", "tensor_tensor", R=[ident_f, s], W=[x1], out=x1[:], in0=idb,
                 in1=s[:, 2, :].unsqueeze(2).to_broadcast([128, 4, 128]), op=ALU.mult)
            S.op("pool", "tensor_tensor", R=[ident_f, s], W=[x2], out=x2[:], in0=idb,
                 in1=s[:, 1, :].unsqueeze(2).to_broadcast([128, 4, 128]), op=ALU.mult)
            pR = psf(); pRb = psf()
            S.op("pe", "matmul", R=[ones_f, x1], W=[pR], out=pR[:], lhsT=ones_f[:], rhs=v4(x1), start=True, stop=True)
            S.op("pe", "matmul", R=[ones_f, x2], W=[pRb], out=pRb[:], lhsT=ones_f[:], rhs=v4(x2), start=True, stop=True)
            dn = Dn.next()
            S.op("dve", "tensor_tensor", R=[pR, s], W=[dn], out=dn[:], in0=pR[:].rearrange("p (h b) -> p h b", h=4),
                 in1=s[:, 2, :].unsqueeze(2).to_broadcast([128, 4, 128]), op=ALU.subtract)
            eg = Eg.next()
            S.op("act", "activation", R=[pR], W=[eg], out=v4(eg), in_=pR[:], func=AF.Exp)
            ea = Ea.next(); eb = Eb.next(); fq = Fq.next()
            S.op("dve", "tensor_scalar", R=[dn], W=[ea], out=ea[:], in0=dn[:], scalar1=-1.0, scalar2=0.0,
                 op0=ALU.mult, op1=ALU.min)
            S.op("dve", "tensor_scalar", R=[dn], W=[eb], out=eb[:], in0=dn[:], scalar1=0.0, scalar2=None, op0=ALU.min)
            S.op("act", "activation", R=[ea], W=[ea], out=ea[:], in_=ea[:], func=AF.Exp)
            S.op("act", "activation", R=[eb], W=[eb], out=eb[:], in_=eb[:], func=AF.Exp)
            S.op("dve", "tensor_tensor", R=[ea, gmask2], W=[ea], out=ea[:], in0=ea[:],
                 in1=gmask2[:, d, 0, :].unsqueeze(1).to_broadcast([128, 4, 128]), op=ALU.mult)
            S.op("dve", "tensor_tensor", R=[ea, s], W=[ea], out=ea[:], in0=ea[:],
                 in1=s[:, 1, :].unsqueeze(2).to_broadcast([128, 4, 128]), op=ALU.mult)
            S.op("pool", "tensor_tensor", R=[eb, gmask2], W=[fq], out=fq[:], in0=eb[:],
                 in1=gmask2[:, d, 2, :].unsqueeze(1).to_broadcast([128, 4, 128]), op=ALU.mult)
            S.op("pool", "tensor_tensor", R=[eb, gmask2], W=[eb], out=eb[:], in0=eb[:],
                 in1=gmask2[:, d, 1, :].unsqueeze(1).to_broadcast([128, 4, 128]), op=ALU.mult)
            S.op("dve", "tensor_tensor", R=[eb, pRb], W=[eb], out=eb[:], in0=eb[:],
                 in1=pRb[:].rearrange("p (h b) -> p h b", h=4), op=ALU.mult)
            ckpt('g1')
            yield
            qg = qgp.next()
            S.op("pool", "tensor_tensor", R=[qT, eg], W=[qg], out=qg[:], in0=qT[:], in1=eg[:], op=ALU.mult)
            pK = psf(); pQ = psf()
            for h in range(4):
                S.op("pe", "matmul", R=[kT], W=[pK], out=pK[:, h * 128:(h + 1) * 128], lhsT=kT[:, h, :], rhs=kT[:, h, :],
                     start=True, stop=True)
                S.op("pe", "matmul", R=[kT, qT], W=[pQ], out=pQ[:, h * 128:(h + 1) * 128], lhsT=kT[:, h, :],
                     rhs=qT[:, h, :], start=True, stop=True)
            P = Pp.next(); PT = PTp.next(); ttf = TTf.next(); qkT = qkTp.next()
            pKv = pK[:].rearrange("p (h b) -> p h b", h=4)
            S.op("dve", "tensor_tensor", R=[pK, ea], W=[P], out=P[:], in0=pKv, in1=ea[:], op=ALU.mult)
            S.op("dve", "tensor_tensor", R=[pK, eb], W=[PT], out=PT[:], in0=pKv, in1=eb[:], op=ALU.mult)
            S.op("pool", "tensor_tensor", R=[PT, ident_f], W=[ttf], out=ttf[:], in0=PT[:], in1=idb, op=ALU.add)
            S.op("dve", "tensor_tensor", R=[pQ, fq], W=[qkT], out=qkT[:], in0=pQ[:].rearrange("p (h b) -> p h b", h=4),
                 in1=fq[:], op=ALU.mult)
            ckpt('g2')
            yield
            for lev in range(1, 6):
                p1 = psf()
                for h in range(4):
                    S.op("pe", "matmul", R=[PT, P], W=[p1], out=p1[:, h * 128:(h + 1) * 128], lhsT=PT[:, h, :],
                         rhs=P[:, h, :], start=True, stop=True)
                Pn = Pp.next()
                S.op("dve", "tensor_copy", R=[p1], W=[Pn], out=v4(Pn), in_=p1[:])
                p3 = psf()
                for h in range(4):
                    S.op("pe", "matmul", R=[Pn, ttf], W=[p3], out=p3[:, h * 128:(h + 1) * 128], lhsT=Pn[:, h, :],
                         rhs=ttf[:, h, :], start=True, stop=True)
                if lev < 5:
                    p2 = psf()
                    for h in range(4):
                        if os.environ.get("KNOTR"):
                            S.op("pe", "matmul", R=[PT, P], W=[p2], out=p2[:, h * 128:(h + 1) * 128], lhsT=P[:, h, :],
                                 rhs=PT[:, h, :], start=True, stop=True)
                        else:
                            S.op("pe", "transpose", R=[Pn, ident_f], W=[p2], out=p2[:, h * 128:(h + 1) * 128],
                                 in_=Pn[:, h, :], identity=ident_f[:])
                    PTn = PTp.next()
                    S.op("act", "copy", R=[p2], W=[PTn], out=v4(PTn), in_=p2[:])
                S.op("dve", "tensor_tensor", R=[ttf, p3], W=[ttf], out=v4(ttf), in0=v4(ttf), in1=p3[:], op=ALU.add)
                P = Pn
                if lev < 5:
                    PT = PTn
                ckpt('g3')
                yield
            vb = vbp.next(); kb = kbp.next(); kd = kdp.next()
            S.op("dve", "tensor_tensor", R=[vt, s], W=[vb], out=vb[:], in0=vt[:].rearrange("p (h e) -> p h e", h=4),
                 in1=s[:, 1, :].unsqueeze(2).to_broadcast([128, 4, 128]), op=ALU.mult)
            S.op("pool", "tensor_tensor", R=[kt, s], W=[kb], out=kb[:], in0=kt[:].rearrange("p (h e) -> p h e", h=4),
                 in1=s[:, 5, :].unsqueeze(2).to_broadcast([128, 4, 128]), op=ALU.mult)
            S.op("pool", "tensor_tensor", R=[kt, s], W=[kd], out=kd[:], in0=kt[:].rearrange("p (h e) -> p h e", h=4),
                 in1=s[:, 4, :].unsqueeze(2).to_broadcast([128, 4, 128]), op=ALU.mult)
            U = Up.next(); WT = WTp.next()
            pU = psf(); pW = psf()
            for h in range(4):
                S.op("pe", "matmul", R=[ttf, vb], W=[pU], out=pU[:, h * 128:(h + 1) * 128], lhsT=ttf[:, h, :],
                     rhs=vb[:, h, :], start=True, stop=True)
                S.op("pe", "matmul", R=[kb, ttf], W=[pW], out=pW[:, h * 128:(h + 1) * 128], lhsT=kb[:, h, :],
                     rhs=ttf[:, h, :], start=True, stop=True)
            S.op("dve", "tensor_copy", R=[pU], W=[U], out=v4(U), in_=pU[:])
            S.op("act", "copy", R=[pW], W=[WT], out=v4(WT), in_=pW[:])
            ckpt('g4')
            yield
            for cb in ((0, 1) if d == 0 else (1, 0)):
                rows = slice(cb * 64, cb * 64 + 64)
                cols = slice(cb * 64, cb * 64 + 64)
                po = PSF[4 + d]
                for h in range(4):
                    pa = psf()
                    S.op("pe", "matmul", R=[WT, Sbr[h]], W=[pa], out=pa[:, 0:128], lhsT=WT[:, h, :], rhs=Sb_[:, h, :],
                         start=True, stop=True)
                    vn = vnp.next()
                    S.op("dve", "tensor_tensor", R=[U, pa], W=[vn], out=vn[rows, :], in0=U[rows, h, :],
                         in1=pa[rows, 0:128], op=ALU.subtract)
                    S.op("pe", "matmul", R=[qg, Sbr[h]], W=[po], out=po[0:64, h * 128:(h + 1) * 128], lhsT=qg[:, h, cols],
                         rhs=Sb_[:, h, :], start=True, stop=False)
                    S.op("pe", "matmul", R=[qkT, vn], W=[po], out=po[0:64, h * 128:(h + 1) * 128],
                         lhsT=qkT[rows, h, cols], rhs=vn[rows, :], start=False, stop=True)
                    pS = psf()
                    S.op("pe", "matmul", R=[kd, vn], W=[pS], out=pS[:, 0:128], lhsT=kd[rows, h, :], rhs=vn[rows, :],
                         start=True, stop=True)
                    S.op("dve", "scalar_tensor_tensor", R=[Sfr[h], s, pS], W=[Sfr[h]], out=Sf[:, h, :], in0=Sf[:, h, :],
                         scalar=s[:, 6 + cb, h:h + 1], in1=pS[:, 0:128], op0=ALU.mult, op1=ALU.add)
                    S.op("act", "copy", R=[Sfr[h]], W=[Sbr[h]], out=Sb_[:, h, :], in_=Sf[:, h, :])
                    if h % 2 == 1:
                        ckpt('g5')
                        yield
                r0 = c0 + cb * 64
                o = op_.next()
                S.op("act", "copy", R=[po], W=[o], out=o[:], in_=po[0:64, :])
                S.dma("pool", odst[r0:r0 + 64, :], o[:], R=[o], W=[RX[orn][tt]])
                ckpt('g6')
                yield
        if not is_s:
            S.dma("pool", ngdn_out[si - 1, l, d].rearrange("h k e -> k h e"), Sf[:], R=Sfr)

    def phaseE(l, xsrc, xdst):
        with ExitStack() as es:
            wbr = sb(es, "pe_wbr", [128, 12, D], BF16)
            wo = sb(es, "pe_wo", [128, 8, D], BF16)
            with ExitStack() as es0:
                wst = Pool_(es0, "pe_wst", [128, 4, D], F32, 2)
                for q in range(3):
                    w = wst.next()
                    S.dma("sp", w[:], w_branch[l, q].rearrange("(c p) n -> p c n", p=128), W=[w])
                    S.op("dve" if q % 2 == 0 else "pool", "tensor_copy", R=[w], W=[wbr], out=wbr[:, q * 4:q * 4 + 4, :],
                         in_=w[:])
                for q in range(2):
                    w = wst.next()
                    S.dma("sp", w[:], w_out[l].rearrange("(c p) n -> p c n", p=128)[:, q * 4:q * 4 + 4, :], W=[w])
                    S.op("dve" if q % 2 == 0 else "pool", "tensor_copy", R=[w], W=[wo], out=wo[:, q * 4:q * 4 + 4, :],
                         in_=w[:])
                S.barrier()
            gtp = Pool_(es, "pe_gt", [128, 12, 128], BF16, 4)
            mgp = Pool_(es, "pe_mg", [128, 3 * D], BF16, 4)
            mp = Pool_(es, "pe_m", [128, D], F32, 3)
            mbp = Pool_(es, "pe_mb", [128, D], BF16, 3)
            mTp = Pool_(es, "pe_mT", [128, 8, 128], BF16, 3)
            xp = Pool_(es, "pe_x", [128, D], F32, 4)
            tp = Pool_(es, "pe_t", [128, 512], F32, 4)

            def tileE(tt):
                j = 0 if tt < NTS else 1
                c0 = tt * 128
                gt = gtp.next(); mgt = mgp.next(); xt = xp.next()
                S.dma("sp", gt[:], gT[:, :, c0:c0 + 128].rearrange("m (c p) t -> p (m c) t", p=128),
                      R=[RX["gT0"][tt], RX["gT1"][tt], RX["gT2"][tt]], W=[gt])
                S.dma("sp", mgt[:], mg[c0:c0 + 128, :], R=[RX["mg"][tt]], W=[mgt])
                S.dma("sp", xt[:], xsrc[c0:c0 + 128, :], R=[RX["x"][tt]] if l > 0 else [], W=[xt])
                yield
                m = mp.next()
                for q in range(3):
                    for nb in range(2):
                        ps = psf()
                        for c in range(4):
                            S.op("pe", "matmul", R=[gt, wbr], W=[ps], out=ps[:], lhsT=gt[:, q * 4 + c, :],
                                 rhs=wbr[:, q * 4 + c, nb * 512:(nb + 1) * 512], start=(c == 0), stop=(c == 3))
                        cols = slice(nb * 512, (nb + 1) * 512)
                        if q == 0:
                            S.op("dve", "tensor_tensor", R=[ps, mgt], W=[m], out=m[:, cols], in0=ps[:],
                                 in1=mgt[:, cols], op=ALU.mult)
                        else:
                            t = tp.next()
                            S.op("dve", "tensor_tensor", R=[ps, mgt], W=[t], out=t[:], in0=ps[:],
                                 in1=mgt[:, q * D + nb * 512:q * D + (nb + 1) * 512], op=ALU.mult)
                            S.op("pool" if nb == 0 else "dve", "tensor_tensor", R=[t, m], W=[m], out=m[:, cols],
                                 in0=m[:, cols], in1=t[:], op=ALU.add)
                yield
                mb = mbp.next()
                S.op("act", "copy", R=[m], W=[mb], out=mb[:], in_=m[:])
                yield
                pb = psb()
                transposes_to(pb, mb, [(mb[:, c * 128:(c + 1) * 128], 128, c * 128) for c in range(8)])
                mT = mTp.next()
                S.op("act", "copy", R=[pb], W=[mT], out=mT[:].rearrange("p c t -> p (c t)"), in_=pb[:])
                yield
                for nb in range(2):
                    ps = psf()
                    for c in range(8):
                        S.op("pe", "matmul", R=[mT, wo], W=[ps], out=ps[:], lhsT=mT[:, c, :],
                             rhs=wo[:, c, nb * 512:(nb + 1) * 512], start=(c == 0), stop=(c == 7))
                    cols = slice(nb * 512, (nb + 1) * 512)
                    t = tp.next()
                    S.op("dve", "tensor_tensor", R=[ps, gate_bc], W=[t], out=t[:], in0=ps[:], in1=gate_bc[:, j, cols],
                         op=ALU.mult)
                    S.op("pool" if nb == 0 else "dve", "tensor_tensor", R=[t, xt], W=[xt], out=xt[:, cols],
                         in0=xt[:, cols], in1=t[:], op=ALU.add)
                S.dma("pool", xdst[c0:c0 + 128, :], xt[:], R=[xt], W=[RX["x"][tt]])

            gens = [tileE(tt) for tt in range(NT)]
            active = []
            while gens or active:
                if gens:
                    active.append(gens.pop(0))
                for g in list(active):
                    try:
                        next(g)
                    except StopIteration:
                        active.remove(g)
            S.barrier()

    try:
        for l in range(depth):
            layer(l)
    except StopBuild as e:
        print("build stopped at", e)
        root2 = None
    S.barrier()
    S.emit()
    build.stats = (S.n_instr, S.n_wait)
    return nc


_CACHE = {}


def _get_nc(T_S, depth, debug):
    key = (T_S, depth, debug)
    if key not in _CACHE:
        _CACHE[key] = build(T_S, depth, debug)
    return _CACHE[key]


def run_cfg(inp, T_S, depth, debug=False):
    f = lambda a: np.ascontiguousarray(np.asarray(a), dtype=np.float32)
    xs = f(inp["x_sample"])
    xp = f(inp["x_prompt"])
    n_core = 8
    nsb = xs.shape[0]
    consts = make_consts(T_S)
    shared = {
        "norm_w": f(inp["norm_w"])[:depth], "w_ada": f(inp["w_ada"])[:depth], "b_ada": f(inp["b_ada"])[:depth],
        "w_in": f(inp["w_in"])[:depth], "qk_norm_w": f(inp["qk_norm_w"])[:depth],
        "diff_lambda": f(inp["diff_lambda"])[:depth], "subln_w": f(inp["subln_w"])[:depth],
        "ret_decay": f(inp["ret_decay"])[:depth].reshape(depth, 8), "ret_norm_w": f(inp["ret_norm_w"])[:depth],
        "conv_w": f(inp["conv_w"])[:depth], "gdn_a_log": f(inp["gdn_a_log"])[:depth].reshape(depth, 8),
        "gdn_dt_bias": f(inp["gdn_dt_bias"])[:depth].reshape(depth, 8), "gdn_norm_w": f(inp["gdn_norm_w"])[:depth],
        "w_branch": f(inp["w_branch"])[:depth], "w_out": f(inp["w_out"])[:depth],
    }
    for k, v in consts.items():
        shared["c_" + k] = v
    ck = f(inp["cache_attn_k"])
    cv = f(inp["cache_attn_v"])
    sr = f(inp["state_ret"])
    sgd = f(inp["state_gdn"])
    c = f(inp["c"])
    cctx = f(inp["c_ctx"])
    in_maps = []
    for core in range(n_core):
        b = core % nsb
        m = dict(shared)
        m["x_in"] = np.ascontiguousarray(np.concatenate(
            [xs[b, :T_S]] + [xp[core * NPR + i] for i in range(NPR)], axis=0))
        m["cond"] = np.ascontiguousarray(np.stack([c[b], cctx], axis=0))
        m["cache_k"] = np.ascontiguousarray(ck[b, :depth].reshape(depth, PAST, 512))
        m["cache_v"] = np.ascontiguousarray(cv[b, :depth].reshape(depth, PAST, 512))
        m["st_ret"] = np.ascontiguousarray(sr[b, :depth])
        m["st_gdn"] = np.ascontiguousarray(sgd[b, :depth])
        in_maps.append(m)
    nc = _get_nc(T_S, depth, debug)
    res = run_bass_kernel_spmd(nc, in_maps, core_ids=list(range(n_core)))
    R = res.results
    y_s = np.stack([np.asarray(R[b]["y"])[:T_S] for b in range(nsb)], axis=0).astype(np.float32)
    y_p = np.stack([np.asarray(R[core]["y"])[T_S + i * TP:T_S + (i + 1) * TP]
                    for core in range(n_core) for i in range(NPR)], axis=0).astype(np.float32)
    nk = np.concatenate([np.asarray(R[core]["nk"]) for core in range(n_core)], axis=0).astype(np.float32)
    nv = np.concatenate([np.asarray(R[core]["nv"]) for core in range(n_core)], axis=0).astype(np.float32)
    nret = np.concatenate([np.asarray(R[core]["nret"]) for core in range(n_core)], axis=0).astype(np.float32)
    ngdn = np.concatenate([np.asarray(R[core]["ngdn"]) for core in range(n_core)], axis=0).astype(np.float32)
    nk = nk.reshape(n_core * NPR, depth, TP, 4, 2, 64)
    nv = nv.reshape(n_core * NPR, depth, TP, 4, 128)
    outs = (y_p, y_s, nk, nv, nret, ngdn)
    if debug:
        return outs, R
    return outs


def kernel(**inputs):
    return run_cfg(inputs, 4096, DEPTH, False)
```
